# Optimizing a Trainium2 kernel written in Bass

```python
import jax, jax.numpy as jnp
from jax import lax
import numpy as np

D_MODEL = 1024
BATCH = 8
SEQ = 4096
DEPTH = 4

N_META = 16
CHUNK = 64
META_PAD = (-N_META) % CHUNK
GLA_H = 4
GLA_DK = D_MODEL // 2 // GLA_H
GLA_DV = D_MODEL // GLA_H
GLA_LR = 16
GLA_TAU = 16.0
ML_H = 4
ML_DK = D_MODEL // 2 // ML_H
ML_DV = D_MODEL // ML_H
ML_CONV = 4
ML_F_BIAS = 3.0
D_FF = ((8 * D_MODEL // 3) + 127) // 128 * 128
FFN_CONV = 3
EPS = 1e-6

GLA_QK = GLA_H * GLA_DK
GLA_V = GLA_H * GLA_DV
ML_QK = ML_H * ML_DK
ML_V = ML_H * ML_DV
IN_SIZES = (GLA_QK, GLA_QK, GLA_V, GLA_V, GLA_LR,
            ML_QK, ML_QK, ML_V, ML_V, ML_H, ML_H,
            D_MODEL, D_MODEL)
N_IN = sum(IN_SIZES)

kernel_name = 'hybrid_gla_mlstm_convffn_meta'


def rms_norm(x, g):
    xf = x.astype(jnp.float32)
    y = xf * lax.rsqrt(jnp.mean(xf * xf, axis=-1, keepdims=True) + EPS)
    return (y * g.astype(jnp.float32)).astype(x.dtype)


def causal_dwconv(x, w, b):
    k = w.shape[0]
    y = lax.conv_general_dilated(x, w[:, None, :].astype(x.dtype), window_strides=(1,),
                                 padding=[(k - 1, 0)], dimension_numbers=('NWC', 'WIO', 'NWC'),
                                 feature_group_count=x.shape[-1])
    return y + b.astype(x.dtype)


def to_chunks(x, n_heads):
    b, l, hd = x.shape
    x = jnp.pad(x, ((0, 0), (META_PAD, 0), (0, 0)))
    n = x.shape[1] // CHUNK
    return x.reshape(b, n, CHUNK, n_heads, hd // n_heads).transpose(1, 0, 3, 2, 4)


def from_chunks(y):
    n, b, h, c, d = y.shape
    return y.transpose(1, 0, 3, 2, 4).reshape(b, n * c, h * d)[:, META_PAD:, :]


def head_rms(y, g):
    return y * lax.rsqrt(jnp.mean(y * y, axis=-1, keepdims=True) + EPS) * g.astype(jnp.float32)


def gla_chunked(q, k, v, log_a):
    causal = jnp.tril(jnp.ones((CHUNK, CHUNK), dtype=bool))

    def step(S, inp):
        qc, kc, vc, la = inp
        bc = jnp.cumsum(la, axis=-2)
        inter = jnp.einsum('bhtk,bhkv->bhtv', qc * jnp.exp(bc), S)
        diff = bc[:, :, :, None, :] - bc[:, :, None, :, :]
        decay = jnp.exp(jnp.where(causal[:, :, None], diff, -jnp.inf))
        scores = jnp.einsum('bhtk,bhsk,bhtsk->bhts', qc, kc, decay)
        out = inter + jnp.einsum('bhts,bhsv->bhtv', scores, vc)
        b_last = bc[:, :, -1:, :]
        S_new = jnp.exp(b_last[:, :, 0, :])[..., None] * S + jnp.einsum(
            'bhsk,bhsv->bhkv', kc * jnp.exp(b_last - bc), vc)
        return S_new, out

    n, b, h, c, dk = q.shape
    S0 = jnp.zeros((b, h, dk, v.shape[-1]), jnp.float32)
    _, out = lax.scan(step, S0, (q, k, v, log_a))
    return out


def mlstm_chunked(q, k, v, log_i, log_f):
    causal = jnp.tril(jnp.ones((CHUNK, CHUNK), dtype=bool))

    def step(carry, inp):
        C_prev, n_prev, m_prev = carry
        qc, kc, vc, li, lf = inp
        F = jnp.cumsum(lf, axis=-1)
        d_intra = jnp.where(causal, F[..., :, None] - F[..., None, :] + li[..., None, :], -jnp.inf)
        d_inter = F + m_prev[..., None]
        m = jnp.maximum(d_inter, jnp.max(d_intra, axis=-1))
        w_intra = jnp.exp(d_intra - m[..., None])
        w_inter = jnp.exp(d_inter - m)
        qk = jnp.einsum('bhtk,bhsk->bhts', qc, kc) * w_intra
        num = w_inter[..., None] * jnp.einsum('bhtk,bhkv->bhtv', qc, C_prev) + jnp.einsum('bhts,bhsv->bhtv', qk, vc)
        den = w_inter * jnp.einsum('bhtk,bhk->bht', qc, n_prev) + jnp.sum(qk, axis=-1)
        h = num / jnp.maximum(jnp.abs(den), jnp.exp(-m))[..., None]
        F_last = F[..., -1]
        d_state = F_last[..., None] - F + li
        m_new = jnp.maximum(F_last + m_prev, jnp.max(d_state, axis=-1))
        w_state = jnp.exp(d_state - m_new[..., None])
        g_prev = jnp.exp(F_last + m_prev - m_new)
        C_new = g_prev[..., None, None] * C_prev + jnp.einsum('bhs,bhsk,bhsv->bhkv', w_state, kc, vc)
        n_new = g_prev[..., None] * n_prev + jnp.einsum('bhs,bhsk->bhk', w_state, kc)
        return (C_new, n_new, m_new), h

    n, b, hh, c, dk = q.shape
    init = (jnp.zeros((b, hh, dk, v.shape[-1]), jnp.float32),
            jnp.zeros((b, hh, dk), jnp.float32),
            jnp.zeros((b, hh), jnp.float32))
    _, out = lax.scan(step, init, (q, k, v, log_i, log_f))
    return out


def mixer_block(h, w_in, b_in, gla_w_gate, gla_b_gate, gla_norm, ml_conv_w, ml_conv_b, ml_norm,
                w_branch_gla, w_branch_ml, w_out):
    f32 = jnp.float32
    proj = h @ w_in + b_in.astype(h.dtype)
    splits = [int(s) for s in np.cumsum(IN_SIZES)[:-1]]
    (g_q, g_k, g_v, g_r, g_lr, m_q, m_k, m_v, m_o, m_i, m_f, gate_a, gate_b) = jnp.split(proj, splits, axis=-1)

    log_a = jax.nn.log_sigmoid((g_lr @ gla_w_gate + gla_b_gate.astype(h.dtype)).astype(f32)) / GLA_TAU
    o_gla = gla_chunked(to_chunks(g_q.astype(f32) * (GLA_DK ** -0.5), GLA_H),
                        to_chunks(g_k.astype(f32), GLA_H),
                        to_chunks(g_v.astype(f32), GLA_H),
                        to_chunks(log_a, GLA_H))
    o_gla = from_chunks(head_rms(o_gla, gla_norm)).astype(h.dtype) * jax.nn.silu(g_r)

    qk = jax.nn.silu(causal_dwconv(jnp.concatenate([m_q, m_k], axis=-1), ml_conv_w, ml_conv_b))
    mq, mk = jnp.split(qk, [ML_QK], axis=-1)
    o_ml = mlstm_chunked(to_chunks(mq.astype(f32) * (ML_DK ** -0.5), ML_H),
                         to_chunks(mk.astype(f32), ML_H),
                         to_chunks(m_v.astype(f32), ML_H),
                         to_chunks(m_i.astype(f32), ML_H)[..., 0],
                         to_chunks(jax.nn.log_sigmoid(m_f.astype(f32)), ML_H)[..., 0])
    o_ml = from_chunks(head_rms(o_ml, ml_norm.reshape(ML_H, 1, ML_DV))).astype(h.dtype) * jax.nn.sigmoid(m_o)

    y = jax.nn.sigmoid(gate_a) * (o_gla @ w_branch_gla) + jax.nn.sigmoid(gate_b) * (o_ml @ w_branch_ml)
    return y @ w_out


def conv_ffn(h, w_up, conv_w, conv_b, w_down):
    u = causal_dwconv(h @ w_up, conv_w, conv_b)
    gate, up = jnp.split(u, [D_FF], axis=-1)
    return (jax.nn.silu(gate) * up) @ w_down


def setup_inputs(seed: int = 0) -> dict:
    key = jax.random.key(seed)
    ks = jax.random.split(key, 24)
    nrm = lambda k, shape, s: jax.random.normal(k, shape, jnp.float32) * s
    f_off = sum(IN_SIZES[:10])
    b_in = nrm(ks[4], (DEPTH, N_IN), 0.02)
    b_in = b_in.at[:, f_off:f_off + ML_H].add(ML_F_BIAS)
    return {
        'x': jax.random.normal(ks[0], (BATCH, SEQ, D_MODEL), jnp.float32),
        'meta': nrm(ks[1], (N_META, D_MODEL), 1.0),
        'norm_mix': 1.0 + nrm(ks[2], (DEPTH, D_MODEL), 0.02),
        'w_in': nrm(ks[3], (DEPTH, D_MODEL, N_IN), D_MODEL ** -0.5),
        'b_in': b_in,
        'gla_w_gate': nrm(ks[5], (DEPTH, GLA_LR, GLA_QK), GLA_LR ** -0.5),
        'gla_b_gate': nrm(ks[6], (DEPTH, GLA_QK), 0.02),
        'gla_norm': 1.0 + nrm(ks[7], (DEPTH, GLA_DV), 0.02),
        'ml_conv_w': nrm(ks[8], (DEPTH, ML_CONV, 2 * ML_QK), ML_CONV ** -0.5),
        'ml_conv_b': nrm(ks[9], (DEPTH, 2 * ML_QK), 0.02),
        'ml_norm': 1.0 + nrm(ks[10], (DEPTH, ML_V), 0.02),
        'w_branch_gla': nrm(ks[11], (DEPTH, GLA_V, D_MODEL), GLA_V ** -0.5),
        'w_branch_ml': nrm(ks[12], (DEPTH, ML_V, D_MODEL), ML_V ** -0.5),
        'w_out': nrm(ks[13], (DEPTH, D_MODEL, D_MODEL), D_MODEL ** -0.5),
        'norm_ffn': 1.0 + nrm(ks[14], (DEPTH, D_MODEL), 0.02),
        'ffn_w_up': nrm(ks[15], (DEPTH, D_MODEL, 2 * D_FF), D_MODEL ** -0.5),
        'ffn_conv_w': nrm(ks[16], (DEPTH, FFN_CONV, 2 * D_FF), FFN_CONV ** -0.5),
        'ffn_conv_b': nrm(ks[17], (DEPTH, 2 * D_FF), 0.02),
        'ffn_w_down': nrm(ks[18], (DEPTH, D_FF, D_MODEL), D_FF ** -0.5),
        'norm_final': 1.0 + nrm(ks[19], (D_MODEL,), 0.02),
    }


def reference(x, meta, norm_mix, w_in, b_in, gla_w_gate, gla_b_gate, gla_norm, ml_conv_w, ml_conv_b,
              ml_norm, w_branch_gla, w_branch_ml, w_out, norm_ffn, ffn_w_up, ffn_conv_w, ffn_conv_b,
              ffn_w_down, norm_final):
    b = x.shape[0]
    meta_tok = jnp.broadcast_to(meta[None].astype(x.dtype), (b, N_META, D_MODEL))
    h = jnp.concatenate([meta_tok, x], axis=1)
    for l in range(DEPTH):
        h = h + mixer_block(rms_norm(h, norm_mix[l]), w_in[l], b_in[l], gla_w_gate[l], gla_b_gate[l],
                            gla_norm[l], ml_conv_w[l], ml_conv_b[l], ml_norm[l],
                            w_branch_gla[l], w_branch_ml[l], w_out[l])
        h = h + conv_ffn(rms_norm(h, norm_ffn[l]), ffn_w_up[l], ffn_conv_w[l], ffn_conv_b[l], ffn_w_down[l])
    h = rms_norm(h, norm_final)
    return h[:, N_META:, :]
```

```python
import math
from contextlib import ExitStack

import numpy as np
import concourse.bass as bass
import concourse.mybir as mybir
from concourse.bass_utils import run_bass_kernel_spmd

F32 = mybir.dt.float32
BF16 = mybir.dt.bfloat16
AF = mybir.ActivationFunctionType
ALU = mybir.AluOpType

D = 1024
DEPTH = 4
SEQ = 4096
NMETA = 16
DFF = 2816
NFC = 44
EPS = 1e-6
LNS = math.log(128.0 ** -0.5)
TAU = 16.0

PP_NM, PP_NF, PP_BFM, PP_NBG, PP_GN, PP_MN, PP_CW, PP_CB, PP_FW, PP_FB, PP_BIF = (
    0, 8, 16, 64, 68, 70, 78, 110, 118, 250, 294)
PPL = 302

SLABS = [(8, 24)] + [(8, 512)] * 12 + [(8, 512)] * 4 + [(8, 512)] * 6 + [(8, 512)] * 11 + [(22, 128)] * 8
SLAB_OFF = np.concatenate([[0], np.cumsum([k * n for k, n in SLABS])]).astype(int)
PW = int(SLAB_OFF[-1])
NSLAB = len(SLABS)
SLAB_MAX = 4096


class Prog:
    ENGS = ("pe", "act", "dve", "pool", "sp")
    SEM_LIMIT = 30000

    def __init__(self):
        self.ops = {e: [] for e in self.ENGS}
        self.state = {}
        self.know = {e: {} for e in self.ENGS}
        self.snap = {e: [] for e in self.ENGS}
        self.dsnap = {}
        self.dma_cnt = {}

    def _learn(self, eng, dep):
        k = self.know[eng]
        if dep[0] == "e":
            _, e2, pos = dep
            sn = self.snap[e2][pos - 1]
        else:
            sn = self.dsnap[dep]
        for key, v in sn.items():
            if k.get(key, 0) < v:
                k[key] = v
        key = (dep[0], dep[1])
        if k.get(key, 0) < dep[2]:
            k[key] = dep[2]

    def op(self, eng, fn, reads=(), writes=(), dma=None):
        deps = []
        for t in reads:
            st = self.state.get(t)
            if st is not None and st[0] is not None:
                deps.append((st[0], True))
        for t in writes:
            st = self.state.get(t)
            if st is not None:
                if st[0] is not None:
                    deps.append((st[0], False))
                for r in st[1].values():
                    deps.append((r, False))
        mypos = len(self.ops[eng]) + 1
        waits = []
        know = self.know[eng]
        for dep, is_raw in deps:
            key = (dep[0], dep[1])
            if dep[0] == "e" and dep[1] == eng:
                if eng in ("pe", "sp"):
                    continue
                if not is_raw or mypos - dep[2] > 1:
                    continue
            if know.get(key, 0) >= dep[2]:
                continue
            waits.append(dep)
            self._learn(eng, dep)
        best = {}
        for dpp in waits:
            key = (dpp[0], dpp[1])
            if key not in best or best[key][2] < dpp[2]:
                best[key] = dpp
        waits = list(best.values())
        for dpp in waits:
            if dpp[0] == "e":
                self.ops[dpp[1]][dpp[2] - 1]["sig"] = True
        rec = {"fn": fn, "waits": waits, "sig": False, "dma": None}
        self.ops[eng].append(rec)
        self.snap[eng].append(dict(know))
        if dma is not None:
            cnt = self.dma_cnt.get(dma, 0) + 16
            self.dma_cnt[dma] = cnt
            me = ("d", dma, cnt)
            rec["dma"] = dma
            self.dsnap[me] = dict(know)
        else:
            me = ("e", eng, mypos)
        for t in reads:
            st = self.state.setdefault(t, [None, {}])
            st[1][(me[0], me[1])] = me
        for t in writes:
            self.state[t] = [me, {}]
        return me

    def emit(self, nc, es, final_waits):
        semmap = {}
        nsem = {}
        for e in self.ENGS:
            cnt = 0
            idx = 0
            m = []
            for rec in self.ops[e]:
                if rec["sig"]:
                    if cnt >= self.SEM_LIMIT:
                        idx += 1
                        cnt = 0
                    cnt += 1
                m.append((idx, cnt))
            semmap[e] = m
            nsem[e] = idx + 1
        esems = {e: [es.enter_context(nc.semaphore(f"s_{e}{i}")) for i in range(nsem[e])] for e in self.ENGS}
        dsems = {k: es.enter_context(nc.semaphore(f"d_{k}")) for k in self.dma_cnt}
        block = es.enter_context(nc.Block())

        def run(engname, eh):
            m = semmap[engname]
            for i, rec in enumerate(self.ops[engname]):
                for dpp in rec["waits"]:
                    if dpp[0] == "e":
                        si, sv = semmap[dpp[1]][dpp[2] - 1]
                        eh.wait_ge(esems[dpp[1]][si], sv)
                    else:
                        eh.wait_ge(dsems[dpp[1]], dpp[2])
                ins = rec["fn"](eh)
                if rec["dma"] is not None:
                    ins.then_inc(dsems[rec["dma"]], 16)
                elif rec["sig"]:
                    ins.then_inc(esems[engname][m[i][0]], 1)
            for dpp in final_waits.get(engname, []):
                eh.wait_ge(dsems[dpp[1]], dpp[2])

        @block.tensor
        def _(t):
            run("pe", t)

        @block.scalar
        def _(a):
            run("act", a)

        @block.vector
        def _(v):
            run("dve", v)

        @block.gpsimd
        def _(g):
            run("pool", g)

        @block.sync
        def _(s):
            run("sp", s)


DEBUG = {"on": False, "names": []}


def build_program(depth, tiles, seq):
    nc = bass.Bass("TRN2", target_bir_lowering=False)
    DEBUG["names"] = []
    xT_d = nc.dram_tensor("xT", [D, seq], F32, kind="ExternalInput").ap()
    metaT_d = nc.dram_tensor("metaT", [D, NMETA], F32, kind="ExternalInput").ap()
    w32_d = nc.dram_tensor("w32", [depth, 128, PW], F32, kind="ExternalInput").ap()
    pp_d = nc.dram_tensor("pp", [128, depth * PPL + 8], F32, kind="ExternalInput").ap()
    p16_d = nc.dram_tensor("p16", [16, depth * 512 + depth], F32, kind="ExternalInput").ap()
    prow_d = nc.dram_tensor("prow", [4, 2048], F32, kind="ExternalInput").ap()
    hc_d = nc.dram_tensor("hc", [128, 768], F32, kind="ExternalInput").ap()
    outT_d = nc.dram_tensor("outT", [D, seq], F32, kind="ExternalOutput").ap()
    w16_d = nc.dram_tensor("w16", [depth, 128, PW], BF16, kind="Internal").ap()

    P = Prog()
    es = ExitStack()
    TM = 512

    def sb(name, shape, dt):
        return es.enter_context(nc.sbuf_tensor("sb_" + name, shape, dt))

    ident_bf = sb("ident_bf", [128, 128], BF16)
    ones_bf = sb("ones_bf", [128, 128], BF16)
    triu_bf = sb("triu_bf", [128, 128], BF16)
    triu_f = sb("triu_f", [128, 128], F32)
    ones_f = sb("ones_f", [128, 512], F32)
    pp = sb("pp", [128, depth * PPL + 8], F32)
    nbg = sb("nbg", [128, depth * 4], F32)
    wg_bf = sb("wg_bf", [16, depth * 512], BF16)
    blr = sb("blr", [16, depth], F32)
    brow_bf = sb("brow_bf", [4, 2048], BF16)
    sel_bf = sb("sel_bf", [4, 512], BF16)
    hT = sb("hT", [128, 8, TM], F32)
    NAR = 48
    arena = sb("arena", [128, NAR, TM], BF16)
    AR_XN, AR_SGA, AR_SR, AR_SO, AR_SGB, AR_QE, AR_KD = 0, 8, 16, 24, 32, 40, 44
    AR_AT = 16
    mqk_pre = sb("mqk_pre", [128, 8, TM + 3], BF16)
    NSB = 3
    slabbuf = [sb(f"slab{i}", [128, SLAB_MAX], BF16) for i in range(NSB)]
    S_st = [sb(f"S{l}", [128, 4, 256], F32) for l in range(depth)]
    C_st = [sb(f"C{l}", [128, 4, 256], F32) for l in range(depth)]
    n_st = [sb(f"n{l}", [128, 4], F32) for l in range(depth)]
    qk_hist = [sb(f"qkh{l}", [128, 8, 3], BF16) for l in range(depth)]
    u_hist = [sb(f"uh{l}", [128, 22, 2, 2], BF16) for l in range(depth)]
    S_bf = sb("S_bf", [128, 4, 256], BF16)
    C_bf = sb("C_bf", [128, 4, 256], BF16)
    nb_bf = sb("nb_bf", [128, 512], BF16)
    gv = sb("gv", [128, 4, 1024], BF16)
    mv = sb("mv", [128, 4, 1024], BF16)
    NF32 = 8
    f32r = [sb(f"f32r{i}", [128, 512], F32) for i in range(NF32)]
    NB16 = 8
    b16r = [sb(f"b16r{i}", [128, 512], BF16) for i in range(NB16)]
    ubuf = [sb(f"ubuf{i}", [128, 2, TM + 2], BF16) for i in range(2)]
    ifg = sb("ifg", [128, 4, 8], F32)
    lfn = sb("lfn", [128, 4, 4], F32)
    iftmp = sb("iftmp", [128, 4, 4], F32)
    A_t = sb("A_t", [128, 4, 4], F32)
    NSM = 8
    smr = [sb(f"smr{i}", [128, 4], F32) for i in range(NSM)]
    lrT = sb("lrT", [16, TM], BF16)
    rstd = sb("rstd", [128, TM], F32)
    NPS = 7
    psb = [es.enter_context(nc.psum_tensor(f"ps{i}", [128, 512], F32)) for i in range(NPS)]
    pst = es.enter_context(nc.psum_tensor("pst", [128, 1024], BF16))

    ctr = {"ps": 0, "f32": 0, "b16": 0, "sm": 0, "tp": 0, "ub": 0, "slab": 0}

    def ps_next():
        i = ctr["ps"] % NPS
        ctr["ps"] += 1
        return psb[i], ("ps", i)

    def tp_next():
        i = ctr["tp"] % 2
        ctr["tp"] += 1
        return pst[:, i * 512:(i + 1) * 512], ("pst", i)

    def f32_next():
        i = ctr["f32"] % NF32
        ctr["f32"] += 1
        return f32r[i], ("f32r", i)

    def b16_next():
        i = ctr["b16"] % NB16
        ctr["b16"] += 1
        return b16r[i], ("b16r", i)

    def sm_next():
        i = ctr["sm"] % NSM
        ctr["sm"] += 1
        return smr[i], ("smr", i)

    def ar(i):
        return ("ar", i)

    def mm(out, lhsT, rhs, start, stop, reads, writes):
        P.op("pe", lambda e: e.matmul(out, lhsT=lhsT, rhs=rhs, start=start, stop=stop), reads, writes)

    def tpose(out, in_, reads, writes):
        P.op("pe", lambda e: e.transpose(out, in_, ident_bf[:, :]), reads, writes)

    def act(out, in_, func, reads, writes, bias=None, scale=None):
        kw = {}
        if bias is not None:
            kw["bias"] = bias
        if scale is not None:
            kw["scale"] = scale
        P.op("act", lambda e: e.activation(out, in_, func, **kw), reads, writes)

    def tt(eng, out, in0, in1, op, reads, writes):
        P.op(eng, lambda e: e.tensor_tensor(out=out, in0=in0, in1=in1, op=op), reads, writes)

    def ts(eng, out, in0, s1, op0, reads, writes, s2=None, op1=None):
        if op1 is None:
            P.op(eng, lambda e: e.tensor_scalar(out=out, in0=in0, scalar1=s1, scalar2=None, op0=op0), reads, writes)
        else:
            P.op(eng, lambda e: e.tensor_scalar(out=out, in0=in0, scalar1=s1, scalar2=s2, op0=op0, op1=op1), reads, writes)

    def stt(out, in0, scalar, in1, op0, op1, reads, writes):
        P.op("dve", lambda e: e.scalar_tensor_tensor(out=out, in0=in0, scalar=scalar, in1=in1, op0=op0, op1=op1),
             reads, writes)

    def cp(eng, out, in_, reads, writes):
        if eng == "act":
            P.op(eng, lambda e: e.activation(out, in_, AF.Copy), reads, writes)
        else:
            P.op(eng, lambda e: e.tensor_copy(out=out, in_=in_), reads, writes)

    def scan(out, d0, d1, reads, writes):
        P.op("dve", lambda e: e.tensor_tensor_scan(out=out, data0=d0, data1=d1, initial=0.0, op0=ALU.mult,
                                                   op1=ALU.add), reads, writes)

    def recip(out, in_, reads, writes):
        P.op("dve", lambda e: e.reciprocal(out=out, in_=in_), reads, writes)

    def mset(eng, ap, val, writes):
        P.op(eng, lambda e: e.memset(ap, val), (), writes)

    def dma(eng, out, in_, reads, writes, sem, slow=False):
        if slow:
            return P.op(eng, lambda e: e.dma_start(out=out, in_=in_, allow_slow_non_contiguous=True), reads, writes,
                        dma=sem)
        return P.op(eng, lambda e: e.dma_start(out=out, in_=in_), reads, writes, dma=sem)

    def dbg(name, ap, reads):
        if not DEBUG["on"]:
            return
        shp = list(ap.shape)
        d = nc.dram_tensor("dbg_" + name, shp, ap.dtype, kind="ExternalOutput").ap()
        DEBUG["names"].append("dbg_" + name)
        final_dbg.append(dma("act", d, ap, reads, (), f"dbg{len(final_dbg)}", slow=True))

    final_dbg = []

    def v4(ap, n):
        return ap[:, :].rearrange("p (h t) -> p h t", h=4)[:, :, :n]

    W_PIECE = 8192
    for l in range(depth):
        a = 0
        while a < PW:
            b = min(PW, a + W_PIECE)
            dma("pool", w16_d[l, :, a:b], w32_d[l, :, a:b], (), [("w16", l)], f"pre{l}")
            a = b
    cst_toks = ["pp", "triu_f", "blr"] + [("hT", c) for c in range(6)] + [("f32r", l) for l in range(depth)]
    dma("act", pp[:, :], pp_d, (), ["pp"], "cst")
    dma("act", triu_f[:, :], hc_d[:, 128:256], (), ["triu_f"], "cst")
    dma("act", blr[:, :], p16_d[:, depth * 512:depth * 512 + depth], (), ["blr"], "cst", slow=True)
    dma("act", hT[:, 0, 0:128], hc_d[:, 0:128], (), [("hT", 0)], "cst")
    dma("act", hT[0:4, 1, 0:512], hc_d[0:4, 256:768], (), [("hT", 1)], "cst")
    for q in range(4):
        dma("act", hT[0:4, 2 + q, 0:512], prow_d[:, q * 512:(q + 1) * 512], (), [("hT", 2 + q)], "cst")
    for l in range(depth):
        last_cst = dma("act", f32r[l][0:16, 0:512], p16_d[:, l * 512:l * 512 + 512], (), [("f32r", l)], "cst")
    for tk in cst_toks:
        P.state[tk][0] = last_cst
    mset("dve", ones_bf[:, :], 1.0, ["ones_bf"])
    mset("dve", ones_f[:, :], 1.0, ["ones_f"])
    cp("dve", ident_bf[:, :], hT[:, 0, 0:128], [("hT", 0)], ["ident_bf"])
    cp("dve", triu_bf[:, :], triu_f[:, :], ["triu_f"], ["triu_bf"])
    cp("dve", sel_bf[:, :], hT[0:4, 1, 0:512], [("hT", 1)], ["sel_bf"])
    for q in range(4):
        cp("dve", brow_bf[:, q * 512:(q + 1) * 512], hT[0:4, 2 + q, 0:512], [("hT", 2 + q)], ["brow_bf"])
    for l in range(depth):
        cp("dve", wg_bf[:, l * 512:(l + 1) * 512], f32r[l][0:16, 0:512], [("f32r", l)], ["wg_bf"])
        ts("dve", nbg[:, l * 4:(l + 1) * 4], pp[:, l * PPL + PP_NBG:l * PPL + PP_NBG + 4], -1.0, ALU.mult,
           ["pp"], ["nbg"])
    for l in range(depth):
        for h in range(4):
            mset("pool", S_st[l][:, h, :], 0.0, [("S", l, h)])
            mset("pool", C_st[l][:, h, :], 0.0, [("C", l, h)])
        mset("pool", n_st[l][:, :], 0.0, [("n", l)])
        mset("pool", qk_hist[l][:, :, :], 0.0, [("qkh", l)])
        mset("pool", u_hist[l][:, :, :, :], 0.0, [("uh", l)])

    slab_seq = {"n": 0}

    def next_slab(l, s):
        i = slab_seq["n"] % NSB
        slab_seq["n"] += 1
        kc, ncols = SLABS[s]
        n = kc * ncols
        off = int(SLAB_OFF[s])
        dma("sp", slabbuf[i][:, 0:n], w16_d[l, :, off:off + n], [("w16", l)], [("slab", i)], f"sl{i}")
        return slabbuf[i][:, 0:n].rearrange("p (k n) -> p k n", k=kc), ("slab", i)

    def rms_to(l_gcol, T, dst_fn):
        for c in range(8):
            act(arena[:, AR_SGA + c, :T], hT[:, c, :T], AF.Square, [("hT", c)], [ar(AR_SGA + c)])
        ss, sst = ps_next()
        for c in range(8):
            mm(ss[:, :T], ones_bf[:, :], arena[:, AR_SGA + c, :T], c == 0, c == 7, ["ones_bf", ar(AR_SGA + c)], [sst])
        lnv, lnt = f32_next()
        act(lnv[:, :T], ss[:, :T], AF.Ln, [sst], [lnt], bias=EPS, scale=1.0 / D)
        act(rstd[:, :T], lnv[:, :T], AF.Exp, [lnt], ["rstd"], scale=-0.5)
        for c in range(8):
            o, otk = dst_fn(c)
            stt(o, hT[:, c, :T], pp[:, l_gcol + c:l_gcol + c + 1], rstd[:, :T], ALU.mult, ALU.mult,
                [("hT", c), "pp", "rstd"], otk)

    def tile_layer(l, T, cs, nch, ti=0):
        pb = l * PPL
        tag = f"t{ti}l{l}"
        ALLAR = [ar(i) for i in range(NAR)]
        xn = lambda c: arena[:, AR_XN + c, :T]
        rms_to(pb + PP_NM, T, lambda c: (xn(c), [ar(AR_XN + c)]))
        XN = [ar(AR_XN + c) for c in range(8)]
        dbg("xn_" + tag, arena[:, AR_XN:AR_XN + 8, :T], XN)

        sl, slt = next_slab(l, 0)
        ps, pst_ = ps_next()
        for kc in range(8):
            mm(ps[:16, :T], sl[:, kc, 0:16], xn(kc), kc == 0, kc == 7, [slt, ar(AR_XN + kc)], [pst_])
        act(lrT[:, :T], ps[:16, :T], AF.Identity, [pst_, "blr"], ["lrT"], bias=blr[:, l:l + 1])
        ps2, ps2t = ps_next()
        for c in range(nch):
            for kc in range(8):
                mm(ps2[:cs, c * 8:(c + 1) * 8], arena[:, AR_XN + kc, c * cs:(c + 1) * cs], sl[:, kc, 16:24],
                   kc == 0, kc == 7, [slt, ar(AR_XN + kc)], [ps2t])
        tt("dve", ifg[:cs, :nch, :], ps2[:cs, 0:nch * 8].rearrange("p (c e) -> p c e", e=8),
           pp[:cs, pb + PP_BIF:pb + PP_BIF + 8].unsqueeze(1).to_broadcast([cs, nch, 8]), ALU.add,
           [ps2t, "pp"], ["ifg"])
        act(iftmp[:cs, :nch, :], ifg[:cs, :nch, 4:8], AF.Exp, ["ifg"], ["iftmp"], scale=-1.0)
        act(lfn[:cs, :nch, :], iftmp[:cs, :nch, :], AF.Ln, ["iftmp"], ["lfn"], bias=1.0)

        E_h = {}

        def gla_gates(h):
            gps, gpt = ps_next()
            mm(gps[:, :T], wg_bf[:, l * 512 + h * 128:l * 512 + (h + 1) * 128], lrT[:, :T], True, True,
               ["wg_bf", "lrT"], [gpt])
            e1, e1t = f32_next()
            act(e1[:, :T], gps[:, :T], AF.Exp, [gpt, "nbg"], [e1t], bias=nbg[:, l * 4 + h:l * 4 + h + 1], scale=-1.0)
            l1, l1t = f32_next()
            act(l1[:, :T], e1[:, :T], AF.Ln, [e1t], [l1t], bias=1.0)
            Lc, Lct = f32_next()
            for c in range(nch):
                scan(Lc[:, c * cs:(c + 1) * cs], ones_f[:, :cs], l1[:, c * cs:(c + 1) * cs], [l1t, "ones_f"], [Lct])
            act(A_t[:, h, :nch], Lc[:, :T].rearrange("p (c s) -> p c s", s=cs)[:, :, cs - 1], AF.Exp,
                [Lct], [("A", h)], scale=-1.0 / TAU)
            Eb, Ebt = f32_next()
            act(Eb[:, :T], Lc[:, :T], AF.Exp, [Lct], [Ebt], scale=-1.0 / TAU, bias=LNS)
            En, Ent = f32_next()
            act(En[:, :T], Lc[:, :T], AF.Exp, [Lct], [Ent], scale=1.0 / TAU)
            E_h[h] = (Eb, Ebt, En, Ent)

        fc = 0
        for s in range(1, 13):
            sl, slt = next_slab(l, s)
            for j in range(4):
                if fc < 8 and fc % 2 == 0:
                    gla_gates(fc // 2)
                ps, pt = ps_next()
                for kc in range(8):
                    mm(ps[:, :T], sl[:, kc, j * 128:(j + 1) * 128], xn(kc), kc == 0, kc == 7,
                       [slt, ar(AR_XN + kc)], [pt])
                bcol = pp[:, pb + PP_BFM + fc:pb + PP_BFM + fc + 1]
                if fc < 8:
                    h, isk = fc // 2, fc % 2
                    Eb, Ebt, En, Ent = E_h[h]
                    if not isk:
                        stt(arena[:, AR_QE + h, :T], ps[:, :T], bcol, Eb[:, :T], ALU.add, ALU.mult,
                            [pt, "pp", Ebt], [ar(AR_QE + h)])
                    else:
                        stt(arena[:, AR_KD + h, :T], ps[:, :T], bcol, En[:, :T], ALU.add, ALU.mult,
                            [pt, "pp", Ent], [ar(AR_KD + h)])
                elif fc < 16:
                    jj = fc - 8
                    act(arena[:, AR_SR + jj, :T], ps[:, :T], AF.Silu, [pt, "pp"], [ar(AR_SR + jj)], bias=bcol)
                    ts("pool", arena[:, AR_SR + jj, :T], arena[:, AR_SR + jj, :T],
                       pp[:, pb + PP_GN + jj % 2:pb + PP_GN + jj % 2 + 1], ALU.mult, [ar(AR_SR + jj), "pp"],
                       [ar(AR_SR + jj)])
                elif fc < 24:
                    jj = fc - 16
                    act(mqk_pre[:, jj, 3:3 + T], ps[:, :T], AF.Identity, [pt, "pp"], [("mqp", jj)], bias=bcol)
                elif fc < 32:
                    jj = fc - 24
                    act(arena[:, AR_SO + jj, :T], ps[:, :T], AF.Sigmoid, [pt, "pp"], [ar(AR_SO + jj)], bias=bcol)
                    ts("pool", arena[:, AR_SO + jj, :T], arena[:, AR_SO + jj, :T],
                       pp[:, pb + PP_MN + jj:pb + PP_MN + jj + 1], ALU.mult, [ar(AR_SO + jj), "pp"],
                       [ar(AR_SO + jj)])
                elif fc < 40:
                    jj = fc - 32
                    act(arena[:, AR_SGA + jj, :T], ps[:, :T], AF.Sigmoid, [pt, "pp"], [ar(AR_SGA + jj)], bias=bcol)
                else:
                    jj = fc - 40
                    act(arena[:, AR_SGB + jj, :T], ps[:, :T], AF.Sigmoid, [pt, "pp"], [ar(AR_SGB + jj)], bias=bcol)
                fc += 1

        if ti == 0 and l == 0:
            dbg("sel", sel_bf[:, :], ["sel_bf"])
            dbg("brow", brow_bf[:, :], ["brow_bf"])
            dbg("ident", ident_bf[:, :], ["ident_bf"])
            dbg("triu", triu_bf[:, :], ["triu_bf"])
        dbg("lfn_" + tag, lfn[:cs, :nch, :], ["lfn"])
        dbg("ifg_" + tag, ifg[:cs, :nch, :], ["ifg"])
        dbg("A_" + tag, A_t[:, :, :nch], [("A", h) for h in range(4)])
        dbg("fm_" + tag, arena[:, 8:NAR, :T], ALLAR)
        for q in range(4):
            sl, slt = next_slab(l, 13 + q)
            dst = gv if q < 2 else mv
            dname = "gv" if q < 2 else "mv"
            half = q % 2
            for c in range(nch):
                ps, pt = ps_next()
                for kc in range(8):
                    mm(ps[:cs, :], arena[:, AR_XN + kc, c * cs:(c + 1) * cs], sl[:, kc, :], kc == 0, False,
                       [slt, ar(AR_XN + kc)], [pt])
                mm(ps[:cs, :], sel_bf[:, l * 128:l * 128 + cs], brow_bf[:, q * 512:(q + 1) * 512], False, True,
                   ["sel_bf", "brow_bf"], [pt])
                cp("dve", dst[:cs, c, half * 512:(half + 1) * 512], ps[:cs, :], [pt], [(dname, c)])

        cp("pool", mqk_pre[:, :, 0:3], qk_hist[l][:, :, :], [("qkh", l)], [("mqp", j) for j in range(8)])
        for j0 in range(0, 8, 2):
            accs = []
            for j in (j0, j0 + 1):
                a_, at = f32_next()
                accs.append((a_, at))
                ts("pool", a_[:, :T], mqk_pre[:, j, 0:T], pp[:, pb + PP_CW + j:pb + PP_CW + j + 1], ALU.mult,
                   [("mqp", j), "pp"], [at])
            for tap in (1, 2, 3):
                for k_, j in enumerate((j0, j0 + 1)):
                    a_, at = accs[k_]
                    stt(a_[:, :T], mqk_pre[:, j, tap:tap + T],
                        pp[:, pb + PP_CW + tap * 8 + j:pb + PP_CW + tap * 8 + j + 1], a_[:, :T], ALU.mult, ALU.add,
                        [("mqp", j), "pp", at], [at])
            for k_, j in enumerate((j0, j0 + 1)):
                a_, at = accs[k_]
                act(arena[:, AR_XN + j, :T], a_[:, :T], AF.Silu, [at, "pp"], [ar(AR_XN + j)],
                    bias=pp[:, pb + PP_CB + j:pb + PP_CB + j + 1])
        cp("pool", qk_hist[l][:, :, :], mqk_pre[:, :, T:T + 3], [("mqp", j) for j in range(8)], [("qkh", l)])
        MQ, MK = AR_XN, AR_XN + 4
        dbg("gv_" + tag, gv[:cs, :nch, :], [("gv", c) for c in range(nch)])
        dbg("mv_" + tag, mv[:cs, :nch, :], [("mv", c) for c in range(nch)])
        dbg("mqk_" + tag, arena[:, AR_XN:AR_XN + 8, :T], XN)

        cp("pool", S_bf[:, :, :], S_st[l][:, :, :], [("S", l, h) for h in range(4)], [("Sbf", h) for h in range(4)])
        cp("pool", C_bf[:, :, :], C_st[l][:, :, :], [("C", l, h) for h in range(4)], [("Cbf", h) for h in range(4)])
        tt("pool", nb_bf[:, :].rearrange("p (h m) -> p h m", h=4), ones_f[:, :].rearrange("p (h m) -> p h m", h=4),
           n_st[l][:, :].unsqueeze(2).to_broadcast([128, 4, 128]), ALU.mult, [("n", l), "ones_f"], ["nb_bf"])

        def head_rms_apply(o_ps, o_pt, src_f32, dst_base, cols):
            sq = []
            for b in range(2):
                q_, qt = b16_next()
                src = v4(o_ps[b], cs) if src_f32 is None else v4(src_f32[b][0], cs)
                srct = o_pt[b] if src_f32 is None else src_f32[b][1]
                act(v4(q_, cs), src, AF.Square, [srct], [qt])
                sq.append((q_, qt))
            ss, sst = ps_next()
            for h in range(4):
                for vc in range(2):
                    blk = (h % 2) * 2 + vc
                    mm(ss[:, h * 128:h * 128 + cs], ones_bf[:, :], sq[h // 2][0][:, blk * 128:blk * 128 + cs],
                       vc == 0, vc == 1, ["ones_bf", sq[h // 2][1]], [sst])
            lnv, lnt = f32_next()
            act(v4(lnv, cs), v4(ss, cs), AF.Ln, [sst], [lnt], bias=EPS, scale=1.0 / 256.0)
            rs, rst = f32_next()
            act(v4(rs, cs), v4(lnv, cs), AF.Exp, [lnt], [rst], scale=-0.5)
            for b in range(2):
                o1, o1t = f32_next()
                src = o_ps[b] if src_f32 is None else src_f32[b][0]
                srct = o_pt[b] if src_f32 is None else src_f32[b][1]
                tt("dve", o1[:, :].rearrange("p (h v t) -> p h v t", h=2, v=2)[:, :, :, :cs],
                   src[:, :].rearrange("p (h v t) -> p h v t", h=2, v=2)[:, :, :, :cs],
                   rs[:, :].rearrange("p (h t) -> p h t", h=4)[:, 2 * b:2 * b + 2, :cs].unsqueeze(2).to_broadcast(
                       [128, 2, 2, cs]), ALU.mult, [srct, rst], [o1t])
                dsta = arena[:, dst_base + 4 * b:dst_base + 4 * b + 4, cols[0]:cols[1]]
                tt("pool", dsta, v4(o1, cs), dsta, ALU.mult, [o1t] + [ar(dst_base + 4 * b + i) for i in range(4)],
                   [ar(dst_base + 4 * b + i) for i in range(4)])

        for c in range(nch):
            c0, c1 = c * cs, (c + 1) * cs
            last = (c == nch - 1)
            sc, sct = ps_next()
            for h in range(4):
                mm(sc[:cs, h * 128:h * 128 + cs], arena[:, AR_KD + h, c0:c1], arena[:, AR_QE + h, c0:c1], True, True,
                   [ar(AR_KD + h), ar(AR_QE + h)], [sct])
            scm, scmt = b16_next()
            tt("dve", v4(scm[:cs, :], cs), v4(sc[:cs, :], cs), triu_bf[:cs, :cs].unsqueeze(1).to_broadcast([cs, 4, cs]),
               ALU.mult, [sct, "triu_bf"], [scmt])
            kdl, kdlt = b16_next()
            tt("pool", v4(kdl, cs), arena[:, AR_KD:AR_KD + 4, c0:c1], A_t[:, :, c:c + 1].to_broadcast([128, 4, cs]),
               ALU.mult, [ar(AR_KD + h) for h in range(4)] + [("A", h) for h in range(4)], [kdlt])
            ktp, ktpt = ps_next()
            for h in range(4):
                mm(ktp[:cs, h * 128:(h + 1) * 128], kdl[:, h * 128:h * 128 + cs], ident_bf[:, :], True, True,
                   [kdlt, "ident_bf"], [ktpt])
            kT, kTt = b16_next()
            cp("act", kT[:cs, :], ktp[:cs, :], [ktpt], [kTt])
            o_ps, o_pt = [], []
            for b in range(2):
                p_, t_ = ps_next()
                o_ps.append(p_)
                o_pt.append(t_)
            for h in range(4):
                for vc in range(2):
                    blk = (h % 2) * 2 + vc
                    dst = o_ps[h // 2][:, blk * 128:blk * 128 + cs]
                    mm(dst, S_bf[:, h, vc * 128:(vc + 1) * 128], arena[:, AR_QE + h, c0:c1], True, False,
                       [("Sbf", h), ar(AR_QE + h)], [o_pt[h // 2]])
                    mm(dst, gv[:cs, c, h * 256 + vc * 128:h * 256 + (vc + 1) * 128], scm[:cs, h * 128:h * 128 + cs],
                       False, True, [("gv", c), scmt], [o_pt[h // 2]])
            head_rms_apply(o_ps, o_pt, None, AR_SR, (c0, c1))
            P_ps, P_pt = [], []
            for b in range(2):
                p_, t_ = ps_next()
                P_ps.append(p_)
                P_pt.append(t_)
            for h in range(4):
                mm(P_ps[h // 2][:, (h % 2) * 256:(h % 2) * 256 + 256], kT[:cs, h * 128:(h + 1) * 128],
                   gv[:cs, c, h * 256:(h + 1) * 256], True, True, [kTt, ("gv", c)], [P_pt[h // 2]])
            for h in range(4):
                stt(S_st[l][:, h, :], S_st[l][:, h, :], A_t[:, h, c:c + 1], P_ps[h // 2][:, (h % 2) * 256:(h % 2) * 256 + 256],
                    ALU.mult, ALU.add, [("S", l, h), ("A", h), P_pt[h // 2]], [("S", l, h)])
            if not last:
                cp("pool", S_bf[:, :, :], S_st[l][:, :, :], [("S", l, h) for h in range(4)],
                   [("Sbf", h) for h in range(4)])

            lfb, lfbt = f32_next()
            tt("dve", v4(lfb[:cs, :], 128), v4(ones_f[:cs, :], 128), lfn[:cs, c, :].unsqueeze(2).to_broadcast([cs, 4, 128]),
               ALU.mult, ["ones_f", "lfn"], [lfbt])
            fb, fbt = ps_next()
            for h in range(4):
                mm(fb[:, h * 128:h * 128 + cs], lfb[:cs, h * 128:(h + 1) * 128], triu_f[:cs, :cs], True, True,
                   [lfbt, "triu_f"], [fbt])
            fcp, fcpt = ps_next()
            mm(fcp[:cs, 0:4], triu_f[:cs, :cs], lfn[:cs, c, :], True, True, ["triu_f", "lfn"], [fcpt])
            db, dbt = sm_next()
            stt(db[:cs, :], fcp[:cs, 0:4], LNS, ifg[:cs, c, 0:4], ALU.add, ALU.add, [fcpt, "ifg"], [dbt])
            Dm, Dmt = f32_next()
            for h in range(4):
                act(Dm[:cs, h * 128:h * 128 + cs], fb[:cs, h * 128:h * 128 + cs], AF.Exp, [fbt, dbt], [Dmt],
                    bias=db[:cs, h:h + 1], scale=-1.0)
            tt("pool", v4(Dm[:cs, :], cs), v4(Dm[:cs, :], cs), triu_f[:cs, :cs].unsqueeze(1).to_broadcast([cs, 4, cs]),
               ALU.mult, [Dmt, "triu_f"], [Dmt])
            sc, sct = ps_next()
            for h in range(4):
                mm(sc[:cs, h * 128:h * 128 + cs], arena[:, MK + h, c0:c1], arena[:, MQ + h, c0:c1], True, True,
                   [ar(MK + h), ar(MQ + h)], [sct])
            qkD, qkDt = b16_next()
            tt("dve", v4(qkD[:cs, :], cs), v4(sc[:cs, :], cs), v4(Dm[:cs, :], cs), ALU.mult, [sct, Dmt], [qkDt])
            EF, EFt = f32_next()
            act(v4(EF, cs), v4(fb, cs), AF.Exp, [fbt], [EFt], bias=LNS, scale=-1.0)
            qf, qft = b16_next()
            tt("pool", v4(qf, cs), arena[:, MQ:MQ + 4, c0:c1], v4(EF, cs), ALU.mult,
               [ar(MQ + h) for h in range(4)] + [EFt], [qft])
            n_ps, n_pt = [], []
            for b in range(2):
                p_, t_ = ps_next()
                n_ps.append(p_)
                n_pt.append(t_)
            for h in range(4):
                for vc in range(2):
                    blk = (h % 2) * 2 + vc
                    dst = n_ps[h // 2][:, blk * 128:blk * 128 + cs]
                    mm(dst, C_bf[:, h, vc * 128:(vc + 1) * 128], qf[:, h * 128:h * 128 + cs], True, False,
                       [("Cbf", h), qft], [n_pt[h // 2]])
                    mm(dst, mv[:cs, c, h * 256 + vc * 128:h * 256 + (vc + 1) * 128], qkD[:cs, h * 128:h * 128 + cs],
                       False, True, [("mv", c), qkDt], [n_pt[h // 2]])
            den, dent = ps_next()
            for h in range(4):
                mm(den[:, h * 128:h * 128 + cs], nb_bf[:, h * 128:(h + 1) * 128], qf[:, h * 128:h * 128 + cs], True, False,
                   ["nb_bf", qft], [dent])
                mm(den[:, h * 128:h * 128 + cs], ones_bf[:cs, :], qkD[:cs, h * 128:h * 128 + cs], False, True,
                   ["ones_bf", qkDt], [dent])
            d1, d1t = f32_next()
            d0, d0t = f32_next()
            act(v4(d0, cs), v4(den, cs), AF.Abs, [dent], [d0t])
            ts("dve", v4(d1, cs), v4(d0, cs), 1.0, ALU.max, [d0t], [d1t])
            wd, wdt = sm_next()
            tt("dve", wd[:cs, :], db[:cs, :], fb[:cs, :].rearrange("p (h t) -> p h t", h=4)[:, :, cs - 1], ALU.subtract,
               [dbt, fbt], [wdt])
            Gd, Gdt = sm_next()
            act(Gd[:, :], fb[:, :].rearrange("p (h t) -> p h t", h=4)[:, :, cs - 1], AF.Exp, [fbt], [Gdt], scale=-1.0)
            rden, rdent = f32_next()
            recip(v4(rden, cs), v4(d1, cs), [d1t], [rdent])
            hbuf = []
            for b in range(2):
                hb, hbt = f32_next()
                tt("dve", hb[:, :].rearrange("p (h v t) -> p h v t", h=2, v=2)[:, :, :, :cs],
                   n_ps[b][:, :].rearrange("p (h v t) -> p h v t", h=2, v=2)[:, :, :, :cs],
                   rden[:, :].rearrange("p (h t) -> p h t", h=4)[:, 2 * b:2 * b + 2, :cs].unsqueeze(2).to_broadcast(
                       [128, 2, 2, cs]), ALU.mult, [n_pt[b], rdent], [hbt])
                hbuf.append((hb, hbt))
            head_rms_apply(None, None, hbuf, AR_SO, (c0, c1))
            wst, wstt = sm_next()
            act(wst[:cs, :], wd[:cs, :], AF.Exp, [wdt], [wstt], bias=-LNS)
            ktp, ktpt = ps_next()
            for h in range(4):
                mm(ktp[:cs, h * 128:(h + 1) * 128], arena[:, MK + h, c0:c1], ident_bf[:, :], True, True,
                   [ar(MK + h), "ident_bf"], [ktpt])
            kw, kwt = b16_next()
            tt("dve", v4(kw[:cs, :], 128), v4(ktp[:cs, :], 128), wst[:cs, :].unsqueeze(2).to_broadcast([cs, 4, 128]),
               ALU.mult, [ktpt, wstt], [kwt])
            P_ps, P_pt = [], []
            for b in range(2):
                p_, t_ = ps_next()
                P_ps.append(p_)
                P_pt.append(t_)
            for h in range(4):
                mm(P_ps[h // 2][:, (h % 2) * 256:(h % 2) * 256 + 256], kw[:cs, h * 128:(h + 1) * 128],
                   mv[:cs, c, h * 256:(h + 1) * 256], True, True, [kwt, ("mv", c)], [P_pt[h // 2]])
            npp, nppt = ps_next()
            for h in range(4):
                mm(npp[:, h:h + 1], kw[:cs, h * 128:(h + 1) * 128], ones_bf[:cs, 0:1], True, True, [kwt, "ones_bf"], [nppt])
            for h in range(4):
                stt(C_st[l][:, h, :], C_st[l][:, h, :], Gd[:, h:h + 1], P_ps[h // 2][:, (h % 2) * 256:(h % 2) * 256 + 256],
                    ALU.mult, ALU.add, [("C", l, h), Gdt, P_pt[h // 2]], [("C", l, h)])
            ntmp, ntt = sm_next()
            tt("dve", ntmp[:, :], n_st[l][:, :], Gd[:, :], ALU.mult, [("n", l), Gdt], [ntt])
            tt("dve", n_st[l][:, :], npp[:, 0:4], ntmp[:, :], ALU.add, [nppt, ntt], [("n", l)])
            if not last:
                cp("pool", C_bf[:, :, :], C_st[l][:, :, :], [("C", l, h) for h in range(4)],
                   [("Cbf", h) for h in range(4)])
                tt("pool", nb_bf[:, :].rearrange("p (h m) -> p h m", h=4), ones_f[:, :].rearrange("p (h m) -> p h m", h=4),
                   n_st[l][:, :].unsqueeze(2).to_broadcast([128, 4, 128]), ALU.mult, [("n", l), "ones_f"], ["nb_bf"])

        dbg("og_" + tag, arena[:, AR_SR:AR_SR + 8, :T], ALLAR)
        dbg("om_" + tag, arena[:, AR_SO:AR_SO + 8, :T], ALLAR)
        dbg("S_" + tag, S_st[l][:, :, :], [("S", l, h) for h in range(4)])
        dbg("C_" + tag, C_st[l][:, :, :], [("C", l, h) for h in range(4)])
        dbg("n_" + tag, n_st[l][:, :], [("n", l)])
        for br in range(2):
            src_base = AR_SR if br == 0 else AR_SO
            gate_base = AR_SGA if br == 0 else AR_SGB
            for half in range(2):
                sl, slt = next_slab(l, 17 + br * 2 + half)
                for dcl in range(4):
                    dc = half * 4 + dcl
                    ps, pt = ps_next()
                    for kc in range(8):
                        mm(ps[:, :T], sl[:, kc, dcl * 128:(dcl + 1) * 128], arena[:, src_base + kc, :T], kc == 0, kc == 7,
                           [slt, ar(src_base + kc)], [pt])
                    tt("dve", arena[:, gate_base + dc, :T], ps[:, :T], arena[:, gate_base + dc, :T], ALU.mult,
                       [pt, ar(gate_base + dc)], [ar(gate_base + dc)])
        for dc in range(8):
            tt("pool", arena[:, AR_SGA + dc, :T], arena[:, AR_SGA + dc, :T], arena[:, AR_SGB + dc, :T], ALU.add,
               [ar(AR_SGA + dc), ar(AR_SGB + dc)], [ar(AR_SGA + dc)])
        for half in range(2):
            sl, slt = next_slab(l, 21 + half)
            for dcl in range(4):
                dc = half * 4 + dcl
                ps, pt = ps_next()
                for kc in range(8):
                    mm(ps[:, :T], sl[:, kc, dcl * 128:(dcl + 1) * 128], arena[:, AR_SGA + kc, :T], kc == 0, kc == 7,
                       [slt, ar(AR_SGA + kc)], [pt])
                tt("dve", hT[:, dc, :T], ps[:, :T], hT[:, dc, :T], ALU.add, [pt, ("hT", dc)], [("hT", dc)])

        dbg("hmix_" + tag, hT[:, :, :T], [("hT", c) for c in range(8)])
        rms_to(pb + PP_NF, T, lambda c: (xn(c), [ar(AR_XN + c)]))
        for j in range(22):
            if j % 2 == 0:
                sl, slt = next_slab(l, 23 + j // 2)
            cb0 = (j % 2) * 256
            pg, pgt = ps_next()
            pu, put = ps_next()
            for kc in range(8):
                mm(pg[:, :T], sl[:, kc, cb0:cb0 + 128], xn(kc), kc == 0, kc == 7, [slt, ar(AR_XN + kc)], [pgt])
            for kc in range(8):
                mm(pu[:, :T], sl[:, kc, cb0 + 128:cb0 + 256], xn(kc), kc == 0, kc == 7, [slt, ar(AR_XN + kc)], [put])
            ui = ctr["ub"] % 2
            ctr["ub"] += 1
            ub = ubuf[ui]
            ubt = [("ub", ui, 0), ("ub", ui, 1)]
            cp("pool", ub[:, :, 0:2], u_hist[l][:, j, :, :], [("uh", l)], ubt)
            cp("act", ub[:, 0, 2:2 + T], pg[:, :T], [pgt], [ubt[0]])
            cp("act", ub[:, 1, 2:2 + T], pu[:, :T], [put], [ubt[1]])
            cp("pool", u_hist[l][:, j, :, :], ub[:, :, T:T + 2], ubt, [("uh", l)])
            accs = []
            for i in range(2):
                a_, at = f32_next()
                accs.append((a_, at))
                col = (j if i == 0 else 22 + j)
                ts("pool", a_[:, :T], ub[:, i, 0:T], pp[:, pb + PP_FW + col:pb + PP_FW + col + 1], ALU.mult,
                   [ubt[i], "pp"], [at])
            for tap in (1, 2):
                for i in range(2):
                    a_, at = accs[i]
                    col = (j if i == 0 else 22 + j)
                    stt(a_[:, :T], ub[:, i, tap:tap + T], pp[:, pb + PP_FW + tap * 44 + col:pb + PP_FW + tap * 44 + col + 1],
                        a_[:, :T], ALU.mult, ALU.add, [ubt[i], "pp", at], [at])
            ga, gat = b16_next()
            act(ga[:, :T], accs[0][0][:, :T], AF.Silu, [accs[0][1], "pp"], [gat], bias=pp[:, pb + PP_FB + j:pb + PP_FB + j + 1])
            stt(arena[:, AR_AT + j, :T], accs[1][0][:, :T], pp[:, pb + PP_FB + 22 + j:pb + PP_FB + 22 + j + 1], ga[:, :T],
                ALU.add, ALU.mult, [accs[1][1], "pp", gat], [ar(AR_AT + j)])
        for dc in range(8):
            sl, slt = next_slab(l, 34 + dc)
            ps, pt = ps_next()
            for kc in range(22):
                mm(ps[:, :T], sl[:, kc, :], arena[:, AR_AT + kc, :T], kc == 0, kc == 21, [slt, ar(AR_AT + kc)], [pt])
            tt("dve", hT[:, dc, :T], ps[:, :T], hT[:, dc, :T], ALU.add, [pt, ("hT", dc)], [("hT", dc)])
        dbg("at_" + tag, arena[:, AR_AT:AR_AT + 22, :T], ALLAR)
        dbg("hffn_" + tag, hT[:, :, :T], [("hT", c) for c in range(8)])

    final_dma = []
    for ti, (T, cs, nch, src) in enumerate(tiles):
        HT = [("hT", c) for c in range(8)]
        if src[0] == "meta":
            dma("act", hT[:, :, :T], metaT_d.rearrange("(c p) t -> p c t", p=128), (), HT, "xin")
        else:
            t0 = src[1]
            dma("act", hT[:, :, :T], xT_d.rearrange("(c p) t -> p c t", p=128)[:, :, t0:t0 + T], (), HT, "xin")
        for l in range(depth):
            tile_layer(l, T, cs, nch, ti)
        if src[0] == "x":
            fcol = depth * PPL
            rms_to(fcol, T, lambda c: (hT[:, c, :T], [("hT", c)]))
            t0 = src[1]
            me = dma("act", outT_d.rearrange("(c p) t -> p c t", p=128)[:, :, t0:t0 + T], hT[:, :, :T], HT, (), "xout")
            final_dma = [me]
    P.emit(nc, es, {"act": final_dma + final_dbg})
    es.close()
    return nc


def _slab_img(Wblk):
    K, n = Wblk.shape
    kc = K // 128
    return np.ascontiguousarray(Wblk.reshape(kc, 128, n).transpose(1, 0, 2)).reshape(128, kc * n)


def pack_weights(depth, w_in, b_in, gla_w_gate, gla_b_gate, gla_norm, ml_conv_w, ml_conv_b, ml_norm,
                 w_branch_gla, w_branch_ml, w_out, norm_mix, norm_ffn, ffn_w_up, ffn_conv_w, ffn_conv_b,
                 ffn_w_down, norm_final):
    o_gq, o_gk, o_gv, o_gr, o_lr, o_mq, o_mk, o_mv, o_mo, o_mi, o_mf, o_ga, o_gb = (
        0, 512, 1024, 2048, 3072, 3088, 3600, 4112, 5136, 6160, 6164, 6168, 7192)
    fm_cols = []
    for h in range(4):
        fm_cols += list(range(o_gq + h * 128, o_gq + (h + 1) * 128))
        fm_cols += list(range(o_gk + h * 128, o_gk + (h + 1) * 128))
    fm_cols += list(range(o_gr, o_gr + 1024)) + list(range(o_mq, o_mq + 512)) + list(range(o_mk, o_mk + 512))
    fm_cols += list(range(o_mo, o_mo + 1024)) + list(range(o_ga, o_ga + 1024)) + list(range(o_gb, o_gb + 1024))
    fm_cols = np.array(fm_cols)
    small_cols = np.array(list(range(o_lr, o_lr + 16)) + list(range(o_mi, o_mi + 4)) + list(range(o_mf, o_mf + 4)))
    tm_cols = np.array(list(range(o_gv, o_gv + 1024)) + list(range(o_mv, o_mv + 1024)))
    up_cols = []
    for j in range(22):
        up_cols += list(range(j * 128, (j + 1) * 128)) + list(range(DFF + j * 128, DFF + (j + 1) * 128))
    up_cols = np.array(up_cols)

    w32 = np.zeros((depth, 128, PW), np.float32)
    pp = np.zeros((128, depth * PPL + 8), np.float32)
    p16 = np.zeros((16, depth * 512 + depth), np.float32)
    prow = np.zeros((4, 2048), np.float32)
    for l in range(depth):
        parts = [_slab_img(w_in[l][:, small_cols])]
        wf = w_in[l][:, fm_cols]
        for s in range(12):
            parts.append(_slab_img(wf[:, s * 512:(s + 1) * 512]))
        wt = w_in[l][:, tm_cols]
        for s in range(4):
            parts.append(_slab_img(wt[:, s * 512:(s + 1) * 512]))
        for W in (w_branch_gla[l], w_branch_ml[l], w_out[l]):
            for s in range(2):
                parts.append(_slab_img(W[:, s * 512:(s + 1) * 512]))
        wu = ffn_w_up[l][:, up_cols]
        for s in range(11):
            parts.append(_slab_img(wu[:, s * 512:(s + 1) * 512]))
        for dc in range(8):
            parts.append(_slab_img(ffn_w_down[l][:, dc * 128:(dc + 1) * 128]))
        w32[l] = np.concatenate(parts, axis=1)
        b = l * PPL
        pp[:, b + PP_NM:b + PP_NM + 8] = norm_mix[l].reshape(8, 128).T
        pp[:, b + PP_NF:b + PP_NF + 8] = norm_ffn[l].reshape(8, 128).T
        pp[:, b + PP_BFM:b + PP_BFM + 48] = b_in[l][fm_cols].reshape(48, 128).T
        pp[:, b + PP_NBG:b + PP_NBG + 4] = gla_b_gate[l].reshape(4, 128).T
        pp[:, b + PP_GN:b + PP_GN + 2] = gla_norm[l].reshape(2, 128).T
        pp[:, b + PP_MN:b + PP_MN + 8] = ml_norm[l].reshape(8, 128).T
        pp[:, b + PP_CW:b + PP_CW + 32] = ml_conv_w[l].reshape(4, 8, 128).transpose(2, 0, 1).reshape(128, 32)
        pp[:, b + PP_CB:b + PP_CB + 8] = ml_conv_b[l].reshape(8, 128).T
        pp[:, b + PP_FW:b + PP_FW + 132] = ffn_conv_w[l].reshape(3, 44, 128).transpose(2, 0, 1).reshape(128, 132)
        pp[:, b + PP_FB:b + PP_FB + 44] = ffn_conv_b[l].reshape(44, 128).T
        pp[:, b + PP_BIF:b + PP_BIF + 8] = np.broadcast_to(b_in[l][small_cols[16:24]][None, :], (128, 8))
        p16[:, l * 512:l * 512 + 512] = gla_w_gate[l]
        p16[:, depth * 512 + l] = b_in[l][small_cols[0:16]]
        prow[l, :] = b_in[l][tm_cols]
    pp[:, depth * PPL:depth * PPL + 8] = norm_final.reshape(8, 128).T
    hc = np.zeros((128, 768), np.float32)
    hc[:, 0:128] = np.eye(128, dtype=np.float32)
    hc[:, 128:256] = np.triu(np.ones((128, 128), np.float32))
    for l in range(4):
        hc[l, 256 + l * 128:256 + (l + 1) * 128] = 1.0
    return w32, pp, p16, prow, hc


def make_tiles(seq):
    tiles = [(NMETA, NMETA, 1, ("meta",))]
    for t0 in range(0, seq, 512):
        tiles.append((512, 128, 4, ("x", t0)))
    return tiles


def run_model(x, meta, depth, **w):
    B, seq, _ = x.shape
    w32, pp, p16, prow, hc = pack_weights(depth, **w)
    metaT = np.ascontiguousarray(np.asarray(meta, np.float32).T)
    nc = build_program(depth, make_tiles(seq), seq)
    in_maps = []
    for b in range(B):
        in_maps.append({"xT": np.ascontiguousarray(x[b].T), "metaT": metaT, "w32": w32, "pp": pp, "p16": p16,
                        "prow": prow, "hc": hc})
    res = run_bass_kernel_spmd(nc, in_maps, core_ids=list(range(B)))
    out = np.stack([np.ascontiguousarray(r["outT"].T) for r in res.results], axis=0)
    if DEBUG["on"]:
        DEBUG["res"] = res.results
    return out.astype(np.float32)


def kernel(x, meta, norm_mix, w_in, b_in, gla_w_gate, gla_b_gate, gla_norm, ml_conv_w, ml_conv_b, ml_norm,
           w_branch_gla, w_branch_ml, w_out, norm_ffn, ffn_w_up, ffn_conv_w, ffn_conv_b, ffn_w_down, norm_final):
    f = lambda a: np.asarray(a, np.float32)
    return run_model(f(x), f(meta), DEPTH, w_in=f(w_in), b_in=f(b_in), gla_w_gate=f(gla_w_gate),
                     gla_b_gate=f(gla_b_gate), gla_norm=f(gla_norm), ml_conv_w=f(ml_conv_w), ml_conv_b=f(ml_conv_b),
                     ml_norm=f(ml_norm), w_branch_gla=f(w_branch_gla), w_branch_ml=f(w_branch_ml), w_out=f(w_out),
                     norm_mix=f(norm_mix), norm_ffn=f(norm_ffn), ffn_w_up=f(ffn_w_up), ffn_conv_w=f(ffn_conv_w),
                     ffn_conv_b=f(ffn_conv_b), ffn_w_down=f(ffn_w_down), norm_final=f(norm_final))
```

```python
import math
from contextlib import ExitStack

import numpy as np
import concourse.bass as bass
import concourse.mybir as mybir
from concourse.bass_utils import run_bass_kernel_spmd

F32 = mybir.dt.float32
BF16 = mybir.dt.bfloat16
AF = mybir.ActivationFunctionType
ALU = mybir.AluOpType

D = 1024
DEPTH = 4
SEQ = 4096
NMETA = 16
DFF = 2816
NFC = 44
EPS = 1e-6
LNS = math.log(128.0 ** -0.5)
TAU = 16.0

PP_NM, PP_NF, PP_BFM, PP_NBG, PP_GN, PP_MN, PP_CW, PP_CB, PP_FW, PP_FB, PP_BIF = (
    0, 8, 16, 64, 68, 70, 78, 110, 118, 250, 294)
PPL = 302

SLABS = [(8, 24)] + [(8, 512)] * 12 + [(8, 512)] * 4 + [(8, 512)] * 6 + [(8, 512)] * 11 + [(22, 128)] * 8
SLAB_OFF = np.concatenate([[0], np.cumsum([k * n for k, n in SLABS])]).astype(int)
PW = int(SLAB_OFF[-1])
NSLAB = len(SLABS)
SLAB_MAX = 4096


class Prog:
    ENGS = ("pe", "act", "dve", "pool", "sp")
    SEM_LIMIT = 30000

    def __init__(self):
        self.ops = {e: [] for e in self.ENGS}
        self.state = {}
        self.know = {e: {} for e in self.ENGS}
        self.snap = {e: [] for e in self.ENGS}
        self.dsnap = {}
        self.dma_cnt = {}

    def _learn(self, eng, dep):
        k = self.know[eng]
        if dep[0] == "e":
            _, e2, pos = dep
            sn = self.snap[e2][pos - 1]
        else:
            sn = self.dsnap[dep]
        for key, v in sn.items():
            if k.get(key, 0) < v:
                k[key] = v
        key = (dep[0], dep[1])
        if k.get(key, 0) < dep[2]:
            k[key] = dep[2]

    def op(self, eng, fn, reads=(), writes=(), dma=None):
        deps = []
        for t in reads:
            st = self.state.get(t)
            if st is not None and st[0] is not None:
                deps.append((st[0], True))
        for t in writes:
            st = self.state.get(t)
            if st is not None:
                if st[0] is not None:
                    deps.append((st[0], False))
                for r in st[1].values():
                    deps.append((r, False))
        mypos = len(self.ops[eng]) + 1
        waits = []
        know = self.know[eng]
        for dep, is_raw in deps:
            key = (dep[0], dep[1])
            if dep[0] == "e" and dep[1] == eng:
                if eng in ("pe", "sp"):
                    continue
                if not is_raw or mypos - dep[2] > 1:
                    continue
            if know.get(key, 0) >= dep[2]:
                continue
            waits.append(dep)
            self._learn(eng, dep)
        best = {}
        for dpp in waits:
            key = (dpp[0], dpp[1])
            if key not in best or best[key][2] < dpp[2]:
                best[key] = dpp
        waits = list(best.values())
        for dpp in waits:
            if dpp[0] == "e":
                self.ops[dpp[1]][dpp[2] - 1]["sig"] = True
        rec = {"fn": fn, "waits": waits, "sig": False, "dma": None}
        self.ops[eng].append(rec)
        self.snap[eng].append(dict(know))
        if dma is not None:
            cnt = self.dma_cnt.get(dma, 0) + 16
            self.dma_cnt[dma] = cnt
            me = ("d", dma, cnt)
            rec["dma"] = dma
            self.dsnap[me] = dict(know)
        else:
            me = ("e", eng, mypos)
        for t in reads:
            st = self.state.setdefault(t, [None, {}])
            st[1][(me[0], me[1])] = me
        for t in writes:
            self.state[t] = [me, {}]
        return me

    def emit(self, nc, es, final_waits):
        semmap = {}
        nsem = {}
        for e in self.ENGS:
            cnt = 0
            idx = 0
            m = []
            for rec in self.ops[e]:
                if rec["sig"]:
                    if cnt >= self.SEM_LIMIT:
                        idx += 1
                        cnt = 0
                    cnt += 1
                m.append((idx, cnt))
            semmap[e] = m
            nsem[e] = idx + 1
        esems = {e: [es.enter_context(nc.semaphore(f"s_{e}{i}")) for i in range(nsem[e])] for e in self.ENGS}
        dsems = {k: es.enter_context(nc.semaphore(f"d_{k}")) for k in self.dma_cnt}
        block = es.enter_context(nc.Block())

        def run(engname, eh):
            m = semmap[engname]
            for i, rec in enumerate(self.ops[engname]):
                for dpp in rec["waits"]:
                    if dpp[0] == "e":
                        si, sv = semmap[dpp[1]][dpp[2] - 1]
                        eh.wait_ge(esems[dpp[1]][si], sv)
                    else:
                        eh.wait_ge(dsems[dpp[1]], dpp[2])
                ins = rec["fn"](eh)
                if rec["dma"] is not None:
                    ins.then_inc(dsems[rec["dma"]], 16)
                elif rec["sig"]:
                    ins.then_inc(esems[engname][m[i][0]], 1)
            for dpp in final_waits.get(engname, []):
                eh.wait_ge(dsems[dpp[1]], dpp[2])

        @block.tensor
        def _(t):
            run("pe", t)

        @block.scalar
        def _(a):
            run("act", a)

        @block.vector
        def _(v):
            run("dve", v)

        @block.gpsimd
        def _(g):
            run("pool", g)

        @block.sync
        def _(s):
            run("sp", s)


DEBUG = {"on": False, "names": []}


def build_program(depth, tiles, seq):
    nc = bass.Bass("TRN2", target_bir_lowering=False)
    DEBUG["names"] = []
    xT_d = nc.dram_tensor("xT", [D, seq], F32, kind="ExternalInput").ap()
    metaT_d = nc.dram_tensor("metaT", [D, NMETA], F32, kind="ExternalInput").ap()
    w32_d = nc.dram_tensor("w32", [depth, 128, PW], F32, kind="ExternalInput").ap()
    pp_d = nc.dram_tensor("pp", [128, depth * PPL + 8], F32, kind="ExternalInput").ap()
    p16_d = nc.dram_tensor("p16", [16, depth * 512 + depth], F32, kind="ExternalInput").ap()
    prow_d = nc.dram_tensor("prow", [4, 2048], F32, kind="ExternalInput").ap()
    hc_d = nc.dram_tensor("hc", [128, 768], F32, kind="ExternalInput").ap()
    outT_d = nc.dram_tensor("outT", [D, seq], F32, kind="ExternalOutput").ap()
    w16_d = nc.dram_tensor("w16", [depth, 128, PW], BF16, kind="Internal").ap()

    P = Prog()
    es = ExitStack()
    TM = 512

    def sb(name, shape, dt):
        return es.enter_context(nc.sbuf_tensor("sb_" + name, shape, dt))

    ident_bf = sb("ident_bf", [128, 128], BF16)
    ones_bf = sb("ones_bf", [128, 128], BF16)
    triu_bf = sb("triu_bf", [128, 128], BF16)
    triu_f = sb("triu_f", [128, 128], F32)
    ones_f = sb("ones_f", [128, 512], F32)
    pp = sb("pp", [128, depth * PPL + 8], F32)
    nbg = sb("nbg", [128, depth * 4], F32)
    wg_bf = sb("wg_bf", [16, depth * 512], BF16)
    blr = sb("blr", [16, depth], F32)
    brow_bf = sb("brow_bf", [4, 2048], BF16)
    sel_bf = sb("sel_bf", [4, 512], BF16)
    hT = sb("hT", [128, 8, TM], F32)
    NAR = 48
    arena = sb("arena", [128, NAR, TM], BF16)
    AR_XN, AR_SGA, AR_SR, AR_SO, AR_SGB, AR_QE, AR_KD = 0, 8, 16, 24, 32, 40, 44
    AR_AT = 16
    mqk_pre = sb("mqk_pre", [128, 8, TM + 3], BF16)
    NSB = 3
    slabbuf = [sb(f"slab{i}", [128, SLAB_MAX], BF16) for i in range(NSB)]
    S_st = [sb(f"S{l}", [128, 4, 256], F32) for l in range(depth)]
    C_st = [sb(f"C{l}", [128, 4, 256], F32) for l in range(depth)]
    n_st = [sb(f"n{l}", [128, 4], F32) for l in range(depth)]
    qk_hist = [sb(f"qkh{l}", [128, 8, 3], BF16) for l in range(depth)]
    u_hist = [sb(f"uh{l}", [128, 22, 2, 2], BF16) for l in range(depth)]
    S_bf = sb("S_bf", [128, 4, 256], BF16)
    C_bf = sb("C_bf", [128, 4, 256], BF16)
    nb_bf = sb("nb_bf", [128, 512], BF16)
    gv = sb("gv", [128, 4, 1024], BF16)
    mv = sb("mv", [128, 4, 1024], BF16)
    NF32 = 8
    f32r = [sb(f"f32r{i}", [128, 512], F32) for i in range(NF32)]
    NB16 = 8
    b16r = [sb(f"b16r{i}", [128, 512], BF16) for i in range(NB16)]
    ubuf = [sb(f"ubuf{i}", [128, 2, TM + 2], BF16) for i in range(2)]
    ifg = sb("ifg", [128, 4, 8], F32)
    lfn = sb("lfn", [128, 4, 4], F32)
    iftmp = sb("iftmp", [128, 4, 4], F32)
    A_t = sb("A_t", [128, 4, 4], F32)
    NSM = 8
    smr = [sb(f"smr{i}", [128, 4], F32) for i in range(NSM)]
    lrT = sb("lrT", [16, TM], BF16)
    rstd = sb("rstd", [128, TM], F32)
    NPS = 7
    psb = [es.enter_context(nc.psum_tensor(f"ps{i}", [128, 512], F32)) for i in range(NPS)]
    pst = es.enter_context(nc.psum_tensor("pst", [128, 1024], BF16))

    ctr = {"ps": 0, "f32": 0, "b16": 0, "sm": 0, "tp": 0, "ub": 0, "slab": 0}

    def ps_next():
        i = ctr["ps"] % NPS
        ctr["ps"] += 1
        return psb[i], ("ps", i)

    def tp_next():
        i = ctr["tp"] % 2
        ctr["tp"] += 1
        return pst[:, i * 512:(i + 1) * 512], ("pst", i)

    def f32_next():
        i = ctr["f32"] % NF32
        ctr["f32"] += 1
        return f32r[i], ("f32r", i)

    def b16_next():
        i = ctr["b16"] % NB16
        ctr["b16"] += 1
        return b16r[i], ("b16r", i)

    def sm_next():
        i = ctr["sm"] % NSM
        ctr["sm"] += 1
        return smr[i], ("smr", i)

    def ar(i):
        return ("ar", i)

    def mm(out, lhsT, rhs, start, stop, reads, writes):
        P.op("pe", lambda e: e.matmul(out, lhsT=lhsT, rhs=rhs, start=start, stop=stop), reads, writes)

    def tpose(out, in_, reads, writes):
        P.op("pe", lambda e: e.transpose(out, in_, ident_bf[:, :]), reads, writes)

    def act(out, in_, func, reads, writes, bias=None, scale=None):
        kw = {}
        if bias is not None:
            kw["bias"] = bias
        if scale is not None:
            kw["scale"] = scale
        P.op("act", lambda e: e.activation(out, in_, func, **kw), reads, writes)

    def tt(eng, out, in0, in1, op, reads, writes):
        P.op(eng, lambda e: e.tensor_tensor(out=out, in0=in0, in1=in1, op=op), reads, writes)

    def ts(eng, out, in0, s1, op0, reads, writes, s2=None, op1=None):
        if op1 is None:
            P.op(eng, lambda e: e.tensor_scalar(out=out, in0=in0, scalar1=s1, scalar2=None, op0=op0), reads, writes)
        else:
            P.op(eng, lambda e: e.tensor_scalar(out=out, in0=in0, scalar1=s1, scalar2=s2, op0=op0, op1=op1), reads, writes)

    def stt(out, in0, scalar, in1, op0, op1, reads, writes):
        P.op("dve", lambda e: e.scalar_tensor_tensor(out=out, in0=in0, scalar=scalar, in1=in1, op0=op0, op1=op1),
             reads, writes)

    def cp(eng, out, in_, reads, writes):
        if eng == "act":
            P.op(eng, lambda e: e.activation(out, in_, AF.Copy), reads, writes)
        else:
            P.op(eng, lambda e: e.tensor_copy(out=out, in_=in_), reads, writes)

    def scan(out, d0, d1, reads, writes):
        P.op("dve", lambda e: e.tensor_tensor_scan(out=out, data0=d0, data1=d1, initial=0.0, op0=ALU.mult,
                                                   op1=ALU.add), reads, writes)

    def recip(out, in_, reads, writes):
        P.op("dve", lambda e: e.reciprocal(out=out, in_=in_), reads, writes)

    def mset(eng, ap, val, writes):
        P.op(eng, lambda e: e.memset(ap, val), (), writes)

    def dma(eng, out, in_, reads, writes, sem, slow=False):
        if slow:
            return P.op(eng, lambda e: e.dma_start(out=out, in_=in_, allow_slow_non_contiguous=True), reads, writes,
                        dma=sem)
        return P.op(eng, lambda e: e.dma_start(out=out, in_=in_), reads, writes, dma=sem)

    def dbg(name, ap, reads):
        if not DEBUG["on"]:
            return
        shp = list(ap.shape)
        d = nc.dram_tensor("dbg_" + name, shp, ap.dtype, kind="ExternalOutput").ap()
        DEBUG["names"].append("dbg_" + name)
        final_dbg.append(dma("act", d, ap, reads, (), f"dbg{len(final_dbg)}", slow=True))

    final_dbg = []

    def v4(ap, n):
        return ap[:, :].rearrange("p (h t) -> p h t", h=4)[:, :, :n]

    W_PIECE = 8192
    for l in range(depth):
        a = 0
        while a < PW:
            b = min(PW, a + W_PIECE)
            dma("pool", w16_d[l, :, a:b], w32_d[l, :, a:b], (), [("w16", l)], f"pre{l}")
            a = b
    cst_toks = ["pp", "triu_f", "blr"] + [("hT", c) for c in range(6)] + [("f32r", l) for l in range(depth)]
    dma("act", pp[:, :], pp_d, (), ["pp"], "cst")
    dma("act", triu_f[:, :], hc_d[:, 128:256], (), ["triu_f"], "cst")
    dma("act", blr[:, :], p16_d[:, depth * 512:depth * 512 + depth], (), ["blr"], "cst", slow=True)
    dma("act", hT[:, 0, 0:128], hc_d[:, 0:128], (), [("hT", 0)], "cst")
    dma("act", hT[0:4, 1, 0:512], hc_d[0:4, 256:768], (), [("hT", 1)], "cst")
    for q in range(4):
        dma("act", hT[0:4, 2 + q, 0:512], prow_d[:, q * 512:(q + 1) * 512], (), [("hT", 2 + q)], "cst")
    for l in range(depth):
        last_cst = dma("act", f32r[l][0:16, 0:512], p16_d[:, l * 512:l * 512 + 512], (), [("f32r", l)], "cst")
    for tk in cst_toks:
        P.state[tk][0] = last_cst
    mset("dve", ones_bf[:, :], 1.0, ["ones_bf"])
    mset("dve", ones_f[:, :], 1.0, ["ones_f"])
    cp("dve", ident_bf[:, :], hT[:, 0, 0:128], [("hT", 0)], ["ident_bf"])
    cp("dve", triu_bf[:, :], triu_f[:, :], ["triu_f"], ["triu_bf"])
    cp("dve", sel_bf[:, :], hT[0:4, 1, 0:512], [("hT", 1)], ["sel_bf"])
    for q in range(4):
        cp("dve", brow_bf[:, q * 512:(q + 1) * 512], hT[0:4, 2 + q, 0:512], [("hT", 2 + q)], ["brow_bf"])
    for l in range(depth):
        cp("dve", wg_bf[:, l * 512:(l + 1) * 512], f32r[l][0:16, 0:512], [("f32r", l)], ["wg_bf"])
        ts("dve", nbg[:, l * 4:(l + 1) * 4], pp[:, l * PPL + PP_NBG:l * PPL + PP_NBG + 4], -1.0, ALU.mult,
           ["pp"], ["nbg"])
    for l in range(depth):
        for h in range(4):
            mset("pool", S_st[l][:, h, :], 0.0, [("S", l, h)])
            mset("pool", C_st[l][:, h, :], 0.0, [("C", l, h)])
        mset("pool", n_st[l][:, :], 0.0, [("n", l)])
        mset("pool", qk_hist[l][:, :, :], 0.0, [("qkh", l)])
        mset("pool", u_hist[l][:, :, :, :], 0.0, [("uh", l)])

    slab_seq = {"n": 0}

    def next_slab(l, s):
        i = slab_seq["n"] % NSB
        slab_seq["n"] += 1
        kc, ncols = SLABS[s]
        n = kc * ncols
        off = int(SLAB_OFF[s])
        dma("sp", slabbuf[i][:, 0:n], w16_d[l, :, off:off + n], [("w16", l)], [("slab", i)], f"sl{i}")
        return slabbuf[i][:, 0:n].rearrange("p (k n) -> p k n", k=kc), ("slab", i)

    def rms_to(l_gcol, T, dst_fn):
        for c in range(8):
            act(arena[:, AR_SGA + c, :T], hT[:, c, :T], AF.Square, [("hT", c)], [ar(AR_SGA + c)])
        ss, sst = ps_next()
        for c in range(8):
            mm(ss[:, :T], ones_bf[:, :], arena[:, AR_SGA + c, :T], c == 0, c == 7, ["ones_bf", ar(AR_SGA + c)], [sst])
        lnv, lnt = f32_next()
        act(lnv[:, :T], ss[:, :T], AF.Ln, [sst], [lnt], bias=EPS, scale=1.0 / D)
        act(rstd[:, :T], lnv[:, :T], AF.Exp, [lnt], ["rstd"], scale=-0.5)
        for c in range(8):
            o, otk = dst_fn(c)
            stt(o, hT[:, c, :T], pp[:, l_gcol + c:l_gcol + c + 1], rstd[:, :T], ALU.mult, ALU.mult,
                [("hT", c), "pp", "rstd"], otk)

    def tile_layer(l, T, cs, nch, ti=0):
        pb = l * PPL
        tag = f"t{ti}l{l}"
        ALLAR = [ar(i) for i in range(NAR)]
        xn = lambda c: arena[:, AR_XN + c, :T]
        rms_to(pb + PP_NM, T, lambda c: (xn(c), [ar(AR_XN + c)]))
        XN = [ar(AR_XN + c) for c in range(8)]
        dbg("xn_" + tag, arena[:, AR_XN:AR_XN + 8, :T], XN)

        sl, slt = next_slab(l, 0)
        ps, pst_ = ps_next()
        for kc in range(8):
            mm(ps[:16, :T], sl[:, kc, 0:16], xn(kc), kc == 0, kc == 7, [slt, ar(AR_XN + kc)], [pst_])
        act(lrT[:, :T], ps[:16, :T], AF.Identity, [pst_, "blr"], ["lrT"], bias=blr[:, l:l + 1])
        ps2, ps2t = ps_next()
        for c in range(nch):
            for kc in range(8):
                mm(ps2[:cs, c * 8:(c + 1) * 8], arena[:, AR_XN + kc, c * cs:(c + 1) * cs], sl[:, kc, 16:24],
                   kc == 0, kc == 7, [slt, ar(AR_XN + kc)], [ps2t])
        tt("dve", ifg[:cs, :nch, :], ps2[:cs, 0:nch * 8].rearrange("p (c e) -> p c e", e=8),
           pp[:cs, pb + PP_BIF:pb + PP_BIF + 8].unsqueeze(1).to_broadcast([cs, nch, 8]), ALU.add,
           [ps2t, "pp"], ["ifg"])
        act(iftmp[:cs, :nch, :], ifg[:cs, :nch, 4:8], AF.Exp, ["ifg"], ["iftmp"], scale=-1.0)
        act(lfn[:cs, :nch, :], iftmp[:cs, :nch, :], AF.Ln, ["iftmp"], ["lfn"], bias=1.0)

        E_h = {}

        def gla_gates(h):
            gps, gpt = ps_next()
            mm(gps[:, :T], wg_bf[:, l * 512 + h * 128:l * 512 + (h + 1) * 128], lrT[:, :T], True, True,
               ["wg_bf", "lrT"], [gpt])
            e1, e1t = f32_next()
            act(e1[:, :T], gps[:, :T], AF.Exp, [gpt, "nbg"], [e1t], bias=nbg[:, l * 4 + h:l * 4 + h + 1], scale=-1.0)
            l1, l1t = f32_next()
            act(l1[:, :T], e1[:, :T], AF.Ln, [e1t], [l1t], bias=1.0)
            Lc, Lct = f32_next()
            for c in range(nch):
                scan(Lc[:, c * cs:(c + 1) * cs], ones_f[:, :cs], l1[:, c * cs:(c + 1) * cs], [l1t, "ones_f"], [Lct])
            act(A_t[:, h, :nch], Lc[:, :T].rearrange("p (c s) -> p c s", s=cs)[:, :, cs - 1], AF.Exp,
                [Lct], [("A", h)], scale=-1.0 / TAU)
            Eb, Ebt = f32_next()
            act(Eb[:, :T], Lc[:, :T], AF.Exp, [Lct], [Ebt], scale=-1.0 / TAU, bias=LNS)
            En, Ent = f32_next()
            act(En[:, :T], Lc[:, :T], AF.Exp, [Lct], [Ent], scale=1.0 / TAU)
            E_h[h] = (Eb, Ebt, En, Ent)

        fc = 0
        for s in range(1, 13):
            sl, slt = next_slab(l, s)
            for j in range(4):
                if fc < 8 and fc % 2 == 0:
                    gla_gates(fc // 2)
                ps, pt = ps_next()
                for kc in range(8):
                    mm(ps[:, :T], sl[:, kc, j * 128:(j + 1) * 128], xn(kc), kc == 0, kc == 7,
                       [slt, ar(AR_XN + kc)], [pt])
                bcol = pp[:, pb + PP_BFM + fc:pb + PP_BFM + fc + 1]
                if fc < 8:
                    h, isk = fc // 2, fc % 2
                    Eb, Ebt, En, Ent = E_h[h]
                    if not isk:
                        stt(arena[:, AR_QE + h, :T], ps[:, :T], bcol, Eb[:, :T], ALU.add, ALU.mult,
                            [pt, "pp", Ebt], [ar(AR_QE + h)])
                    else:
                        stt(arena[:, AR_KD + h, :T], ps[:, :T], bcol, En[:, :T], ALU.add, ALU.mult,
                            [pt, "pp", Ent], [ar(AR_KD + h)])
                elif fc < 16:
                    jj = fc - 8
                    act(arena[:, AR_SR + jj, :T], ps[:, :T], AF.Silu, [pt, "pp"], [ar(AR_SR + jj)], bias=bcol)
                    act(arena[:, AR_SR + jj, :T], arena[:, AR_SR + jj, :T], AF.Identity, [ar(AR_SR + jj), "pp"],
                        [ar(AR_SR + jj)], scale=pp[:, pb + PP_GN + jj % 2:pb + PP_GN + jj % 2 + 1])
                elif fc < 24:
                    jj = fc - 16
                    act(mqk_pre[:, jj, 3:3 + T], ps[:, :T], AF.Identity, [pt, "pp"], [("mqp", jj)], bias=bcol)
                elif fc < 32:
                    jj = fc - 24
                    act(arena[:, AR_SO + jj, :T], ps[:, :T], AF.Sigmoid, [pt, "pp"], [ar(AR_SO + jj)], bias=bcol)
                    act(arena[:, AR_SO + jj, :T], arena[:, AR_SO + jj, :T], AF.Identity, [ar(AR_SO + jj), "pp"],
                        [ar(AR_SO + jj)], scale=pp[:, pb + PP_MN + jj:pb + PP_MN + jj + 1])
                elif fc < 40:
                    jj = fc - 32
                    act(arena[:, AR_SGA + jj, :T], ps[:, :T], AF.Sigmoid, [pt, "pp"], [ar(AR_SGA + jj)], bias=bcol)
                else:
                    jj = fc - 40
                    act(arena[:, AR_SGB + jj, :T], ps[:, :T], AF.Sigmoid, [pt, "pp"], [ar(AR_SGB + jj)], bias=bcol)
                fc += 1

        if ti == 0 and l == 0:
            dbg("sel", sel_bf[:, :], ["sel_bf"])
            dbg("brow", brow_bf[:, :], ["brow_bf"])
            dbg("ident", ident_bf[:, :], ["ident_bf"])
            dbg("triu", triu_bf[:, :], ["triu_bf"])
        dbg("lfn_" + tag, lfn[:cs, :nch, :], ["lfn"])
        dbg("ifg_" + tag, ifg[:cs, :nch, :], ["ifg"])
        dbg("A_" + tag, A_t[:, :, :nch], [("A", h) for h in range(4)])
        dbg("fm_" + tag, arena[:, 8:NAR, :T], ALLAR)
        for q in range(4):
            sl, slt = next_slab(l, 13 + q)
            dst = gv if q < 2 else mv
            dname = "gv" if q < 2 else "mv"
            half = q % 2
            for c in range(nch):
                ps, pt = ps_next()
                for kc in range(8):
                    mm(ps[:cs, :], arena[:, AR_XN + kc, c * cs:(c + 1) * cs], sl[:, kc, :], kc == 0, False,
                       [slt, ar(AR_XN + kc)], [pt])
                mm(ps[:cs, :], sel_bf[:, l * 128:l * 128 + cs], brow_bf[:, q * 512:(q + 1) * 512], False, True,
                   ["sel_bf", "brow_bf"], [pt])
                cp("dve", dst[:cs, c, half * 512:(half + 1) * 512], ps[:cs, :], [pt], [(dname, c)])

        cp("pool", mqk_pre[:, :, 0:3], qk_hist[l][:, :, :], [("qkh", l)], [("mqp", j) for j in range(8)])
        for j0 in range(0, 8, 2):
            accs = []
            for j in (j0, j0 + 1):
                a_, at = f32_next()
                accs.append((a_, at))
                act(a_[:, :T], mqk_pre[:, j, 0:T], AF.Identity, [("mqp", j), "pp"], [at],
                    scale=pp[:, pb + PP_CW + j:pb + PP_CW + j + 1])
            for tap in (1, 2, 3):
                for k_, j in enumerate((j0, j0 + 1)):
                    a_, at = accs[k_]
                    stt(a_[:, :T], mqk_pre[:, j, tap:tap + T],
                        pp[:, pb + PP_CW + tap * 8 + j:pb + PP_CW + tap * 8 + j + 1], a_[:, :T], ALU.mult, ALU.add,
                        [("mqp", j), "pp", at], [at])
            for k_, j in enumerate((j0, j0 + 1)):
                a_, at = accs[k_]
                act(arena[:, AR_XN + j, :T], a_[:, :T], AF.Silu, [at, "pp"], [ar(AR_XN + j)],
                    bias=pp[:, pb + PP_CB + j:pb + PP_CB + j + 1])
        cp("pool", qk_hist[l][:, :, :], mqk_pre[:, :, T:T + 3], [("mqp", j) for j in range(8)], [("qkh", l)])
        MQ, MK = AR_XN, AR_XN + 4
        dbg("gv_" + tag, gv[:cs, :nch, :], [("gv", c) for c in range(nch)])
        dbg("mv_" + tag, mv[:cs, :nch, :], [("mv", c) for c in range(nch)])
        dbg("mqk_" + tag, arena[:, AR_XN:AR_XN + 8, :T], XN)

        cp("act", S_bf[:, :, :], S_st[l][:, :, :], [("S", l, h) for h in range(4)], [("Sbf", h) for h in range(4)])
        cp("act", C_bf[:, :, :], C_st[l][:, :, :], [("C", l, h) for h in range(4)], [("Cbf", h) for h in range(4)])
        tt("pool", nb_bf[:, :].rearrange("p (h m) -> p h m", h=4), ones_f[:, :].rearrange("p (h m) -> p h m", h=4),
           n_st[l][:, :].unsqueeze(2).to_broadcast([128, 4, 128]), ALU.mult, [("n", l), "ones_f"], ["nb_bf"])

        def head_rms_apply(o_ps, o_pt, src_f32, dst_base, cols):
            sq = []
            for b in range(2):
                q_, qt = b16_next()
                src = v4(o_ps[b], cs) if src_f32 is None else v4(src_f32[b][0], cs)
                srct = o_pt[b] if src_f32 is None else src_f32[b][1]
                act(v4(q_, cs), src, AF.Square, [srct], [qt])
                sq.append((q_, qt))
            ss, sst = ps_next()
            for h in range(4):
                for vc in range(2):
                    blk = (h % 2) * 2 + vc
                    mm(ss[:, h * 128:h * 128 + cs], ones_bf[:, :], sq[h // 2][0][:, blk * 128:blk * 128 + cs],
                       vc == 0, vc == 1, ["ones_bf", sq[h // 2][1]], [sst])
            lnv, lnt = f32_next()
            act(v4(lnv, cs), v4(ss, cs), AF.Ln, [sst], [lnt], bias=EPS, scale=1.0 / 256.0)
            rs, rst = f32_next()
            act(v4(rs, cs), v4(lnv, cs), AF.Exp, [lnt], [rst], scale=-0.5)
            for b in range(2):
                o1, o1t = f32_next()
                src = o_ps[b] if src_f32 is None else src_f32[b][0]
                srct = o_pt[b] if src_f32 is None else src_f32[b][1]
                tt("dve", o1[:, :].rearrange("p (h v t) -> p h v t", h=2, v=2)[:, :, :, :cs],
                   src[:, :].rearrange("p (h v t) -> p h v t", h=2, v=2)[:, :, :, :cs],
                   rs[:, :].rearrange("p (h t) -> p h t", h=4)[:, 2 * b:2 * b + 2, :cs].unsqueeze(2).to_broadcast(
                       [128, 2, 2, cs]), ALU.mult, [srct, rst], [o1t])
                dsta = arena[:, dst_base + 4 * b:dst_base + 4 * b + 4, cols[0]:cols[1]]
                tt("pool", dsta, v4(o1, cs), dsta, ALU.mult, [o1t] + [ar(dst_base + 4 * b + i) for i in range(4)],
                   [ar(dst_base + 4 * b + i) for i in range(4)])

        for c in range(nch):
            c0, c1 = c * cs, (c + 1) * cs
            last = (c == nch - 1)
            sc, sct = ps_next()
            for h in range(4):
                mm(sc[:cs, h * 128:h * 128 + cs], arena[:, AR_KD + h, c0:c1], arena[:, AR_QE + h, c0:c1], True, True,
                   [ar(AR_KD + h), ar(AR_QE + h)], [sct])
            scm, scmt = b16_next()
            tt("dve", v4(scm[:cs, :], cs), v4(sc[:cs, :], cs), triu_bf[:cs, :cs].unsqueeze(1).to_broadcast([cs, 4, cs]),
               ALU.mult, [sct, "triu_bf"], [scmt])
            kdl, kdlt = b16_next()
            tt("pool", v4(kdl, cs), arena[:, AR_KD:AR_KD + 4, c0:c1], A_t[:, :, c:c + 1].to_broadcast([128, 4, cs]),
               ALU.mult, [ar(AR_KD + h) for h in range(4)] + [("A", h) for h in range(4)], [kdlt])
            ktp, ktpt = ps_next()
            for h in range(4):
                mm(ktp[:cs, h * 128:(h + 1) * 128], kdl[:, h * 128:h * 128 + cs], ident_bf[:, :], True, True,
                   [kdlt, "ident_bf"], [ktpt])
            kT, kTt = b16_next()
            cp("act", kT[:cs, :], ktp[:cs, :], [ktpt], [kTt])
            o_ps, o_pt = [], []
            for b in range(2):
                p_, t_ = ps_next()
                o_ps.append(p_)
                o_pt.append(t_)
            for h in range(4):
                for vc in range(2):
                    blk = (h % 2) * 2 + vc
                    dst = o_ps[h // 2][:, blk * 128:blk * 128 + cs]
                    mm(dst, S_bf[:, h, vc * 128:(vc + 1) * 128], arena[:, AR_QE + h, c0:c1], True, False,
                       [("Sbf", h), ar(AR_QE + h)], [o_pt[h // 2]])
                    mm(dst, gv[:cs, c, h * 256 + vc * 128:h * 256 + (vc + 1) * 128], scm[:cs, h * 128:h * 128 + cs],
                       False, True, [("gv", c), scmt], [o_pt[h // 2]])
            head_rms_apply(o_ps, o_pt, None, AR_SR, (c0, c1))
            P_ps, P_pt = [], []
            for b in range(2):
                p_, t_ = ps_next()
                P_ps.append(p_)
                P_pt.append(t_)
            for h in range(4):
                mm(P_ps[h // 2][:, (h % 2) * 256:(h % 2) * 256 + 256], kT[:cs, h * 128:(h + 1) * 128],
                   gv[:cs, c, h * 256:(h + 1) * 256], True, True, [kTt, ("gv", c)], [P_pt[h // 2]])
            for h in range(4):
                stt(S_st[l][:, h, :], S_st[l][:, h, :], A_t[:, h, c:c + 1], P_ps[h // 2][:, (h % 2) * 256:(h % 2) * 256 + 256],
                    ALU.mult, ALU.add, [("S", l, h), ("A", h), P_pt[h // 2]], [("S", l, h)])
            if not last:
                for h in range(4):
                    cp("act", S_bf[:, h, :], S_st[l][:, h, :], [("S", l, h)], [("Sbf", h)])

            lfb, lfbt = f32_next()
            tt("dve", v4(lfb[:cs, :], 128), v4(ones_f[:cs, :], 128), lfn[:cs, c, :].unsqueeze(2).to_broadcast([cs, 4, 128]),
               ALU.mult, ["ones_f", "lfn"], [lfbt])
            fb, fbt = ps_next()
            for h in range(4):
                mm(fb[:, h * 128:h * 128 + cs], lfb[:cs, h * 128:(h + 1) * 128], triu_f[:cs, :cs], True, True,
                   [lfbt, "triu_f"], [fbt])
            fcp, fcpt = ps_next()
            mm(fcp[:cs, 0:4], triu_f[:cs, :cs], lfn[:cs, c, :], True, True, ["triu_f", "lfn"], [fcpt])
            db, dbt = sm_next()
            stt(db[:cs, :], fcp[:cs, 0:4], LNS, ifg[:cs, c, 0:4], ALU.add, ALU.add, [fcpt, "ifg"], [dbt])
            Dm, Dmt = f32_next()
            for h in range(4):
                act(Dm[:cs, h * 128:h * 128 + cs], fb[:cs, h * 128:h * 128 + cs], AF.Exp, [fbt, dbt], [Dmt],
                    bias=db[:cs, h:h + 1], scale=-1.0)
            tt("pool", v4(Dm[:cs, :], cs), v4(Dm[:cs, :], cs), triu_f[:cs, :cs].unsqueeze(1).to_broadcast([cs, 4, cs]),
               ALU.mult, [Dmt, "triu_f"], [Dmt])
            sc, sct = ps_next()
            for h in range(4):
                mm(sc[:cs, h * 128:h * 128 + cs], arena[:, MK + h, c0:c1], arena[:, MQ + h, c0:c1], True, True,
                   [ar(MK + h), ar(MQ + h)], [sct])
            qkD, qkDt = b16_next()
            tt("dve", v4(qkD[:cs, :], cs), v4(sc[:cs, :], cs), v4(Dm[:cs, :], cs), ALU.mult, [sct, Dmt], [qkDt])
            EF, EFt = f32_next()
            act(v4(EF, cs), v4(fb, cs), AF.Exp, [fbt], [EFt], bias=LNS, scale=-1.0)
            qf, qft = b16_next()
            tt("pool", v4(qf, cs), arena[:, MQ:MQ + 4, c0:c1], v4(EF, cs), ALU.mult,
               [ar(MQ + h) for h in range(4)] + [EFt], [qft])
            n_ps, n_pt = [], []
            for b in range(2):
                p_, t_ = ps_next()
                n_ps.append(p_)
                n_pt.append(t_)
            for h in range(4):
                for vc in range(2):
                    blk = (h % 2) * 2 + vc
                    dst = n_ps[h // 2][:, blk * 128:blk * 128 + cs]
                    mm(dst, C_bf[:, h, vc * 128:(vc + 1) * 128], qf[:, h * 128:h * 128 + cs], True, False,
                       [("Cbf", h), qft], [n_pt[h // 2]])
                    mm(dst, mv[:cs, c, h * 256 + vc * 128:h * 256 + (vc + 1) * 128], qkD[:cs, h * 128:h * 128 + cs],
                       False, True, [("mv", c), qkDt], [n_pt[h // 2]])
            den, dent = ps_next()
            for h in range(4):
                mm(den[:, h * 128:h * 128 + cs], nb_bf[:, h * 128:(h + 1) * 128], qf[:, h * 128:h * 128 + cs], True, False,
                   ["nb_bf", qft], [dent])
                mm(den[:, h * 128:h * 128 + cs], ones_bf[:cs, :], qkD[:cs, h * 128:h * 128 + cs], False, True,
                   ["ones_bf", qkDt], [dent])
            d1, d1t = f32_next()
            d0, d0t = f32_next()
            act(v4(d0, cs), v4(den, cs), AF.Abs, [dent], [d0t])
            ts("dve", v4(d1, cs), v4(d0, cs), 1.0, ALU.max, [d0t], [d1t])
            wd, wdt = sm_next()
            tt("dve", wd[:cs, :], db[:cs, :], fb[:cs, :].rearrange("p (h t) -> p h t", h=4)[:, :, cs - 1], ALU.subtract,
               [dbt, fbt], [wdt])
            Gd, Gdt = sm_next()
            act(Gd[:, :], fb[:, :].rearrange("p (h t) -> p h t", h=4)[:, :, cs - 1], AF.Exp, [fbt], [Gdt], scale=-1.0)
            rden, rdent = f32_next()
            recip(v4(rden, cs), v4(d1, cs), [d1t], [rdent])
            hbuf = []
            for b in range(2):
                hb, hbt = f32_next()
                tt("dve", hb[:, :].rearrange("p (h v t) -> p h v t", h=2, v=2)[:, :, :, :cs],
                   n_ps[b][:, :].rearrange("p (h v t) -> p h v t", h=2, v=2)[:, :, :, :cs],
                   rden[:, :].rearrange("p (h t) -> p h t", h=4)[:, 2 * b:2 * b + 2, :cs].unsqueeze(2).to_broadcast(
                       [128, 2, 2, cs]), ALU.mult, [n_pt[b], rdent], [hbt])
                hbuf.append((hb, hbt))
            head_rms_apply(None, None, hbuf, AR_SO, (c0, c1))
            wst, wstt = sm_next()
            act(wst[:cs, :], wd[:cs, :], AF.Exp, [wdt], [wstt], bias=-LNS)
            ktp, ktpt = ps_next()
            for h in range(4):
                mm(ktp[:cs, h * 128:(h + 1) * 128], arena[:, MK + h, c0:c1], ident_bf[:, :], True, True,
                   [ar(MK + h), "ident_bf"], [ktpt])
            kw, kwt = b16_next()
            tt("dve", v4(kw[:cs, :], 128), v4(ktp[:cs, :], 128), wst[:cs, :].unsqueeze(2).to_broadcast([cs, 4, 128]),
               ALU.mult, [ktpt, wstt], [kwt])
            P_ps, P_pt = [], []
            for b in range(2):
                p_, t_ = ps_next()
                P_ps.append(p_)
                P_pt.append(t_)
            for h in range(4):
                mm(P_ps[h // 2][:, (h % 2) * 256:(h % 2) * 256 + 256], kw[:cs, h * 128:(h + 1) * 128],
                   mv[:cs, c, h * 256:(h + 1) * 256], True, True, [kwt, ("mv", c)], [P_pt[h // 2]])
            npp, nppt = ps_next()
            for h in range(4):
                mm(npp[:, h:h + 1], kw[:cs, h * 128:(h + 1) * 128], ones_bf[:cs, 0:1], True, True, [kwt, "ones_bf"], [nppt])
            for h in range(4):
                stt(C_st[l][:, h, :], C_st[l][:, h, :], Gd[:, h:h + 1], P_ps[h // 2][:, (h % 2) * 256:(h % 2) * 256 + 256],
                    ALU.mult, ALU.add, [("C", l, h), Gdt, P_pt[h // 2]], [("C", l, h)])
            ntmp, ntt = sm_next()
            tt("dve", ntmp[:, :], n_st[l][:, :], Gd[:, :], ALU.mult, [("n", l), Gdt], [ntt])
            tt("dve", n_st[l][:, :], npp[:, 0:4], ntmp[:, :], ALU.add, [nppt, ntt], [("n", l)])
            if not last:
                for h in range(4):
                    cp("act", C_bf[:, h, :], C_st[l][:, h, :], [("C", l, h)], [("Cbf", h)])
                tt("pool", nb_bf[:, :].rearrange("p (h m) -> p h m", h=4), ones_f[:, :].rearrange("p (h m) -> p h m", h=4),
                   n_st[l][:, :].unsqueeze(2).to_broadcast([128, 4, 128]), ALU.mult, [("n", l), "ones_f"], ["nb_bf"])

        dbg("og_" + tag, arena[:, AR_SR:AR_SR + 8, :T], ALLAR)
        dbg("om_" + tag, arena[:, AR_SO:AR_SO + 8, :T], ALLAR)
        dbg("S_" + tag, S_st[l][:, :, :], [("S", l, h) for h in range(4)])
        dbg("C_" + tag, C_st[l][:, :, :], [("C", l, h) for h in range(4)])
        dbg("n_" + tag, n_st[l][:, :], [("n", l)])
        for br in range(2):
            src_base = AR_SR if br == 0 else AR_SO
            gate_base = AR_SGA if br == 0 else AR_SGB
            for half in range(2):
                sl, slt = next_slab(l, 17 + br * 2 + half)
                for dcl in range(4):
                    dc = half * 4 + dcl
                    ps, pt = ps_next()
                    for kc in range(8):
                        mm(ps[:, :T], sl[:, kc, dcl * 128:(dcl + 1) * 128], arena[:, src_base + kc, :T], kc == 0, kc == 7,
                           [slt, ar(src_base + kc)], [pt])
                    tt("dve", arena[:, gate_base + dc, :T], ps[:, :T], arena[:, gate_base + dc, :T], ALU.mult,
                       [pt, ar(gate_base + dc)], [ar(gate_base + dc)])
        for dc in range(8):
            tt("pool", arena[:, AR_SGA + dc, :T], arena[:, AR_SGA + dc, :T], arena[:, AR_SGB + dc, :T], ALU.add,
               [ar(AR_SGA + dc), ar(AR_SGB + dc)], [ar(AR_SGA + dc)])
        for half in range(2):
            sl, slt = next_slab(l, 21 + half)
            for dcl in range(4):
                dc = half * 4 + dcl
                ps, pt = ps_next()
                for kc in range(8):
                    mm(ps[:, :T], sl[:, kc, dcl * 128:(dcl + 1) * 128], arena[:, AR_SGA + kc, :T], kc == 0, kc == 7,
                       [slt, ar(AR_SGA + kc)], [pt])
                tt("dve", hT[:, dc, :T], ps[:, :T], hT[:, dc, :T], ALU.add, [pt, ("hT", dc)], [("hT", dc)])

        dbg("hmix_" + tag, hT[:, :, :T], [("hT", c) for c in range(8)])
        rms_to(pb + PP_NF, T, lambda c: (xn(c), [ar(AR_XN + c)]))
        for j in range(22):
            if j % 2 == 0:
                sl, slt = next_slab(l, 23 + j // 2)
            cb0 = (j % 2) * 256
            pg, pgt = ps_next()
            pu, put = ps_next()
            for kc in range(8):
                mm(pg[:, :T], sl[:, kc, cb0:cb0 + 128], xn(kc), kc == 0, kc == 7, [slt, ar(AR_XN + kc)], [pgt])
            for kc in range(8):
                mm(pu[:, :T], sl[:, kc, cb0 + 128:cb0 + 256], xn(kc), kc == 0, kc == 7, [slt, ar(AR_XN + kc)], [put])
            ui = ctr["ub"] % 2
            ctr["ub"] += 1
            ub = ubuf[ui]
            ubt = [("ub", ui, 0), ("ub", ui, 1)]
            cp("pool", ub[:, :, 0:2], u_hist[l][:, j, :, :], [("uh", l)], ubt)
            cp("act", ub[:, 0, 2:2 + T], pg[:, :T], [pgt], [ubt[0]])
            cp("act", ub[:, 1, 2:2 + T], pu[:, :T], [put], [ubt[1]])
            cp("pool", u_hist[l][:, j, :, :], ub[:, :, T:T + 2], ubt, [("uh", l)])
            accs = []
            for i in range(2):
                a_, at = f32_next()
                accs.append((a_, at))
                col = (j if i == 0 else 22 + j)
                act(a_[:, :T], ub[:, i, 0:T], AF.Identity, [ubt[i], "pp"], [at],
                    scale=pp[:, pb + PP_FW + col:pb + PP_FW + col + 1])
            for tap in (1, 2):
                for i in range(2):
                    a_, at = accs[i]
                    col = (j if i == 0 else 22 + j)
                    stt(a_[:, :T], ub[:, i, tap:tap + T], pp[:, pb + PP_FW + tap * 44 + col:pb + PP_FW + tap * 44 + col + 1],
                        a_[:, :T], ALU.mult, ALU.add, [ubt[i], "pp", at], [at])
            ga, gat = b16_next()
            act(ga[:, :T], accs[0][0][:, :T], AF.Silu, [accs[0][1], "pp"], [gat], bias=pp[:, pb + PP_FB + j:pb + PP_FB + j + 1])
            stt(arena[:, AR_AT + j, :T], accs[1][0][:, :T], pp[:, pb + PP_FB + 22 + j:pb + PP_FB + 22 + j + 1], ga[:, :T],
                ALU.add, ALU.mult, [accs[1][1], "pp", gat], [ar(AR_AT + j)])
        for dc in range(8):
            sl, slt = next_slab(l, 34 + dc)
            ps, pt = ps_next()
            for kc in range(22):
                mm(ps[:, :T], sl[:, kc, :], arena[:, AR_AT + kc, :T], kc == 0, kc == 21, [slt, ar(AR_AT + kc)], [pt])
            tt("dve", hT[:, dc, :T], ps[:, :T], hT[:, dc, :T], ALU.add, [pt, ("hT", dc)], [("hT", dc)])
        dbg("at_" + tag, arena[:, AR_AT:AR_AT + 22, :T], ALLAR)
        dbg("hffn_" + tag, hT[:, :, :T], [("hT", c) for c in range(8)])

    final_dma = []
    for ti, (T, cs, nch, src) in enumerate(tiles):
        HT = [("hT", c) for c in range(8)]
        if src[0] == "meta":
            dma("act", hT[:, :, :T], metaT_d.rearrange("(c p) t -> p c t", p=128), (), HT, "xin")
        else:
            t0 = src[1]
            dma("act", hT[:, :, :T], xT_d.rearrange("(c p) t -> p c t", p=128)[:, :, t0:t0 + T], (), HT, "xin")
        for l in range(depth):
            tile_layer(l, T, cs, nch, ti)
        if src[0] == "x":
            fcol = depth * PPL
            rms_to(fcol, T, lambda c: (hT[:, c, :T], [("hT", c)]))
            t0 = src[1]
            me = dma("act", outT_d.rearrange("(c p) t -> p c t", p=128)[:, :, t0:t0 + T], hT[:, :, :T], HT, (), "xout")
            final_dma = [me]
    P.emit(nc, es, {"act": final_dma + final_dbg})
    es.close()
    return nc


def _slab_img(Wblk):
    K, n = Wblk.shape
    kc = K // 128
    return np.ascontiguousarray(Wblk.reshape(kc, 128, n).transpose(1, 0, 2)).reshape(128, kc * n)


def pack_weights(depth, w_in, b_in, gla_w_gate, gla_b_gate, gla_norm, ml_conv_w, ml_conv_b, ml_norm,
                 w_branch_gla, w_branch_ml, w_out, norm_mix, norm_ffn, ffn_w_up, ffn_conv_w, ffn_conv_b,
                 ffn_w_down, norm_final):
    o_gq, o_gk, o_gv, o_gr, o_lr, o_mq, o_mk, o_mv, o_mo, o_mi, o_mf, o_ga, o_gb = (
        0, 512, 1024, 2048, 3072, 3088, 3600, 4112, 5136, 6160, 6164, 6168, 7192)
    fm_cols = []
    for h in range(4):
        fm_cols += list(range(o_gq + h * 128, o_gq + (h + 1) * 128))
        fm_cols += list(range(o_gk + h * 128, o_gk + (h + 1) * 128))
    fm_cols += list(range(o_gr, o_gr + 1024)) + list(range(o_mq, o_mq + 512)) + list(range(o_mk, o_mk + 512))
    fm_cols += list(range(o_mo, o_mo + 1024)) + list(range(o_ga, o_ga + 1024)) + list(range(o_gb, o_gb + 1024))
    fm_cols = np.array(fm_cols)
    small_cols = np.array(list(range(o_lr, o_lr + 16)) + list(range(o_mi, o_mi + 4)) + list(range(o_mf, o_mf + 4)))
    tm_cols = np.array(list(range(o_gv, o_gv + 1024)) + list(range(o_mv, o_mv + 1024)))
    up_cols = []
    for j in range(22):
        up_cols += list(range(j * 128, (j + 1) * 128)) + list(range(DFF + j * 128, DFF + (j + 1) * 128))
    up_cols = np.array(up_cols)

    w32 = np.zeros((depth, 128, PW), np.float32)
    pp = np.zeros((128, depth * PPL + 8), np.float32)
    p16 = np.zeros((16, depth * 512 + depth), np.float32)
    prow = np.zeros((4, 2048), np.float32)
    for l in range(depth):
        parts = [_slab_img(w_in[l][:, small_cols])]
        wf = w_in[l][:, fm_cols]
        for s in range(12):
            parts.append(_slab_img(wf[:, s * 512:(s + 1) * 512]))
        wt = w_in[l][:, tm_cols]
        for s in range(4):
            parts.append(_slab_img(wt[:, s * 512:(s + 1) * 512]))
        for W in (w_branch_gla[l], w_branch_ml[l], w_out[l]):
            for s in range(2):
                parts.append(_slab_img(W[:, s * 512:(s + 1) * 512]))
        wu = ffn_w_up[l][:, up_cols]
        for s in range(11):
            parts.append(_slab_img(wu[:, s * 512:(s + 1) * 512]))
        for dc in range(8):
            parts.append(_slab_img(ffn_w_down[l][:, dc * 128:(dc + 1) * 128]))
        w32[l] = np.concatenate(parts, axis=1)
        b = l * PPL
        pp[:, b + PP_NM:b + PP_NM + 8] = norm_mix[l].reshape(8, 128).T
        pp[:, b + PP_NF:b + PP_NF + 8] = norm_ffn[l].reshape(8, 128).T
        pp[:, b + PP_BFM:b + PP_BFM + 48] = b_in[l][fm_cols].reshape(48, 128).T
        pp[:, b + PP_NBG:b + PP_NBG + 4] = gla_b_gate[l].reshape(4, 128).T
        pp[:, b + PP_GN:b + PP_GN + 2] = gla_norm[l].reshape(2, 128).T
        pp[:, b + PP_MN:b + PP_MN + 8] = ml_norm[l].reshape(8, 128).T
        pp[:, b + PP_CW:b + PP_CW + 32] = ml_conv_w[l].reshape(4, 8, 128).transpose(2, 0, 1).reshape(128, 32)
        pp[:, b + PP_CB:b + PP_CB + 8] = ml_conv_b[l].reshape(8, 128).T
        pp[:, b + PP_FW:b + PP_FW + 132] = ffn_conv_w[l].reshape(3, 44, 128).transpose(2, 0, 1).reshape(128, 132)
        pp[:, b + PP_FB:b + PP_FB + 44] = ffn_conv_b[l].reshape(44, 128).T
        pp[:, b + PP_BIF:b + PP_BIF + 8] = np.broadcast_to(b_in[l][small_cols[16:24]][None, :], (128, 8))
        p16[:, l * 512:l * 512 + 512] = gla_w_gate[l]
        p16[:, depth * 512 + l] = b_in[l][small_cols[0:16]]
        prow[l, :] = b_in[l][tm_cols]
    pp[:, depth * PPL:depth * PPL + 8] = norm_final.reshape(8, 128).T
    hc = np.zeros((128, 768), np.float32)
    hc[:, 0:128] = np.eye(128, dtype=np.float32)
    hc[:, 128:256] = np.triu(np.ones((128, 128), np.float32))
    for l in range(4):
        hc[l, 256 + l * 128:256 + (l + 1) * 128] = 1.0
    return w32, pp, p16, prow, hc


def make_tiles(seq):
    tiles = [(NMETA, NMETA, 1, ("meta",))]
    for t0 in range(0, seq, 512):
        tiles.append((512, 128, 4, ("x", t0)))
    return tiles


def run_model(x, meta, depth, **w):
    B, seq, _ = x.shape
    w32, pp, p16, prow, hc = pack_weights(depth, **w)
    metaT = np.ascontiguousarray(np.asarray(meta, np.float32).T)
    nc = build_program(depth, make_tiles(seq), seq)
    in_maps = []
    for b in range(B):
        in_maps.append({"xT": np.ascontiguousarray(x[b].T), "metaT": metaT, "w32": w32, "pp": pp, "p16": p16,
                        "prow": prow, "hc": hc})
    res = run_bass_kernel_spmd(nc, in_maps, core_ids=list(range(B)))
    out = np.stack([np.ascontiguousarray(r["outT"].T) for r in res.results], axis=0)
    if DEBUG["on"]:
        DEBUG["res"] = res.results
    return out.astype(np.float32)


def kernel(x, meta, norm_mix, w_in, b_in, gla_w_gate, gla_b_gate, gla_norm, ml_conv_w, ml_conv_b, ml_norm,
           w_branch_gla, w_branch_ml, w_out, norm_ffn, ffn_w_up, ffn_conv_w, ffn_conv_b, ffn_w_down, norm_final):
    f = lambda a: np.asarray(a, np.float32)
    return run_model(f(x), f(meta), DEPTH, w_in=f(w_in), b_in=f(b_in), gla_w_gate=f(gla_w_gate),
                     gla_b_gate=f(gla_b_gate), gla_norm=f(gla_norm), ml_conv_w=f(ml_conv_w), ml_conv_b=f(ml_conv_b),
                     ml_norm=f(ml_norm), w_branch_gla=f(w_branch_gla), w_branch_ml=f(w_branch_ml), w_out=f(w_out),
                     norm_mix=f(norm_mix), norm_ffn=f(norm_ffn), ffn_w_up=f(ffn_w_up), ffn_conv_w=f(ffn_conv_w),
                     ffn_conv_b=f(ffn_conv_b), ffn_w_down=f(ffn_w_down), norm_final=f(norm_final))
```

```python
import math
from contextlib import ExitStack

import numpy as np
import concourse.bass as bass
import concourse.mybir as mybir
from concourse.bass_utils import run_bass_kernel_spmd

F32 = mybir.dt.float32
BF16 = mybir.dt.bfloat16
AF = mybir.ActivationFunctionType
ALU = mybir.AluOpType

D = 1024
DEPTH = 4
SEQ = 4096
NMETA = 16
DFF = 2816
NFC = 44
EPS = 1e-6
LNS = math.log(128.0 ** -0.5)
TAU = 16.0

PP_NM, PP_NF, PP_BFM, PP_NBG, PP_GN, PP_MN, PP_CW, PP_CB, PP_FW, PP_FB, PP_BIF = (
    0, 8, 16, 64, 68, 70, 78, 110, 118, 250, 294)
PPL = 302

SLABS = [(8, 24)] + [(8, 512)] * 12 + [(8, 512)] * 4 + [(8, 512)] * 6 + [(8, 512)] * 11 + [(22, 128)] * 8
SLAB_OFF = np.concatenate([[0], np.cumsum([k * n for k, n in SLABS])]).astype(int)
PW = int(SLAB_OFF[-1])
NSLAB = len(SLABS)
SLAB_MAX = 4096


class Prog:
    ENGS = ("pe", "act", "dve", "pool", "sp")
    SEM_LIMIT = 30000

    def __init__(self):
        self.ops = {e: [] for e in self.ENGS}
        self.state = {}
        self.know = {e: {} for e in self.ENGS}
        self.snap = {e: [] for e in self.ENGS}
        self.dsnap = {}
        self.dma_cnt = {}
        self.ring_gen = {}

    def _learn(self, eng, dep):
        k = self.know[eng]
        if dep[0] == "e":
            _, e2, pos = dep
            sn = self.snap[e2][pos - 1]
        else:
            sn = self.dsnap[dep]
        for key, v in sn.items():
            if k.get(key, 0) < v:
                k[key] = v
        key = (dep[0], dep[1])
        if k.get(key, 0) < dep[2]:
            k[key] = dep[2]

    RINGS = ("f32r", "b16r", "smr", "ps")

    def _norm(self, toks):
        out = []
        for t in toks:
            if isinstance(t, tuple) and len(t) == 3 and t[0] in self.RINGS:
                assert self.ring_gen.get(t[:2], 0) == t[2], f"stale ring buffer use {t} (current gen {self.ring_gen.get(t[:2])})"
                t = t[:2]
            out.append(t)
        return out

    def op(self, eng, fn, reads=(), writes=(), dma=None):
        reads = self._norm(reads)
        writes = self._norm(writes)
        deps = []
        for t in reads:
            st = self.state.get(t)
            if st is not None and st[0] is not None:
                deps.append((st[0], True))
        for t in writes:
            st = self.state.get(t)
            if st is not None:
                if st[0] is not None:
                    deps.append((st[0], False))
                for r in st[1].values():
                    deps.append((r, False))
        mypos = len(self.ops[eng]) + 1
        waits = []
        know = self.know[eng]
        for dep, is_raw in deps:
            key = (dep[0], dep[1])
            if dep[0] == "e" and dep[1] == eng:
                if eng in ("pe", "sp"):
                    continue
                if not is_raw or mypos - dep[2] > 1:
                    continue
            if know.get(key, 0) >= dep[2]:
                continue
            waits.append(dep)
            self._learn(eng, dep)
        best = {}
        for dpp in waits:
            key = (dpp[0], dpp[1])
            if key not in best or best[key][2] < dpp[2]:
                best[key] = dpp
        waits = list(best.values())
        for dpp in waits:
            if dpp[0] == "e":
                self.ops[dpp[1]][dpp[2] - 1]["sig"] = True
        rec = {"fn": fn, "waits": waits, "sig": False, "dma": None}
        self.ops[eng].append(rec)
        self.snap[eng].append(dict(know))
        if dma is not None:
            cnt = self.dma_cnt.get(dma, 0) + 16
            self.dma_cnt[dma] = cnt
            me = ("d", dma, cnt)
            rec["dma"] = dma
            self.dsnap[me] = dict(know)
        else:
            me = ("e", eng, mypos)
        for t in reads:
            st = self.state.setdefault(t, [None, {}])
            st[1][(me[0], me[1])] = me
        for t in writes:
            self.state[t] = [me, {}]
        return me

    def emit(self, nc, es, final_waits):
        semmap = {}
        nsem = {}
        for e in self.ENGS:
            cnt = 0
            idx = 0
            m = []
            for rec in self.ops[e]:
                if rec["sig"]:
                    if cnt >= self.SEM_LIMIT:
                        idx += 1
                        cnt = 0
                    cnt += 1
                m.append((idx, cnt))
            semmap[e] = m
            nsem[e] = idx + 1
        esems = {e: [es.enter_context(nc.semaphore(f"s_{e}{i}")) for i in range(nsem[e])] for e in self.ENGS}
        dsems = {k: es.enter_context(nc.semaphore(f"d_{k}")) for k in self.dma_cnt}
        block = es.enter_context(nc.Block())

        def run(engname, eh):
            m = semmap[engname]
            for i, rec in enumerate(self.ops[engname]):
                for dpp in rec["waits"]:
                    if dpp[0] == "e":
                        si, sv = semmap[dpp[1]][dpp[2] - 1]
                        eh.wait_ge(esems[dpp[1]][si], sv)
                    else:
                        eh.wait_ge(dsems[dpp[1]], dpp[2])
                ins = rec["fn"](eh)
                if rec["dma"] is not None:
                    ins.then_inc(dsems[rec["dma"]], 16)
                elif rec["sig"]:
                    ins.then_inc(esems[engname][m[i][0]], 1)
            for dpp in final_waits.get(engname, []):
                eh.wait_ge(dsems[dpp[1]], dpp[2])

        @block.tensor
        def _(t):
            run("pe", t)

        @block.scalar
        def _(a):
            run("act", a)

        @block.vector
        def _(v):
            run("dve", v)

        @block.gpsimd
        def _(g):
            run("pool", g)

        @block.sync
        def _(s):
            run("sp", s)


DEBUG = {"on": False, "names": []}


def build_program(depth, tiles, seq):
    nc = bass.Bass("TRN2", target_bir_lowering=False)
    DEBUG["names"] = []
    xT_d = nc.dram_tensor("xT", [D, seq], F32, kind="ExternalInput").ap()
    metaT_d = nc.dram_tensor("metaT", [D, NMETA], F32, kind="ExternalInput").ap()
    w32_d = nc.dram_tensor("w32", [depth, 128, PW], F32, kind="ExternalInput").ap()
    pp_d = nc.dram_tensor("pp", [128, depth * PPL + 8], F32, kind="ExternalInput").ap()
    p16_d = nc.dram_tensor("p16", [16, depth * 512 + depth], F32, kind="ExternalInput").ap()
    prow_d = nc.dram_tensor("prow", [4, 2048], F32, kind="ExternalInput").ap()
    hc_d = nc.dram_tensor("hc", [128, 768], F32, kind="ExternalInput").ap()
    outT_d = nc.dram_tensor("outT", [D, seq], F32, kind="ExternalOutput").ap()
    w16_d = nc.dram_tensor("w16", [depth, 128, PW], BF16, kind="Internal").ap()

    P = Prog()
    es = ExitStack()
    TM = 512

    def sb(name, shape, dt):
        return es.enter_context(nc.sbuf_tensor("sb_" + name, shape, dt))

    ident_bf = sb("ident_bf", [128, 128], BF16)
    ones_bf = sb("ones_bf", [128, 128], BF16)
    triu_bf = sb("triu_bf", [128, 128], BF16)
    triu_f = sb("triu_f", [128, 128], F32)
    ones_f = sb("ones_f", [128, 512], F32)
    pp = sb("pp", [128, depth * PPL + 8], F32)
    nbg = sb("nbg", [128, depth * 4], F32)
    wg_bf = sb("wg_bf", [16, depth * 512], BF16)
    blr = sb("blr", [16, depth], F32)
    brow_bf = sb("brow_bf", [4, 2048], BF16)
    sel_bf = sb("sel_bf", [4, 512], BF16)
    hT = sb("hT", [128, 8, TM], F32)
    hTm = sb("hTm", [128, 8, NMETA], F32)
    NAR = 48
    arena = sb("arena", [128, NAR, TM], BF16)
    AR_XN, AR_SGA, AR_SR, AR_SO, AR_SGB, AR_QE, AR_KD = 0, 8, 16, 24, 32, 40, 44
    AR_AT = 16
    mqk_pre = sb("mqk_pre", [128, 8, TM + 3], BF16)
    NSB = 3
    slabbuf = [sb(f"slab{i}", [128, SLAB_MAX], BF16) for i in range(NSB)]
    S_st = [sb(f"S{l}", [128, 4, 256], F32) for l in range(depth)]
    C_st = [sb(f"C{l}", [128, 4, 256], F32) for l in range(depth)]
    n_st = [sb(f"n{l}", [128, 4], F32) for l in range(depth)]
    qk_hist = [sb(f"qkh{l}", [128, 8, 3], BF16) for l in range(depth)]
    u_hist = [sb(f"uh{l}", [128, 22, 2, 2], BF16) for l in range(depth)]
    S_bf = sb("S_bf", [128, 4, 256], BF16)
    C_bf = sb("C_bf", [128, 4, 256], BF16)
    nb_bf = sb("nb_bf", [128, 512], BF16)
    gv = sb("gv", [128, 4, 1024], BF16)
    mv = sb("mv", [128, 4, 1024], BF16)
    NF32 = 8
    f32r = [sb(f"f32r{i}", [128, 512], F32) for i in range(NF32)]
    NB16 = 8
    b16r = [sb(f"b16r{i}", [128, 512], BF16) for i in range(NB16)]
    ubuf = [sb(f"ubuf{i}", [128, 2, TM + 2], BF16) for i in range(2)]
    ifg = sb("ifg", [128, 4, 8], F32)
    lfn = sb("lfn", [128, 4, 4], F32)
    iftmp = sb("iftmp", [128, 4, 4], F32)
    A_t = sb("A_t", [128, 4, 4], F32)
    NSM = 8
    smr = [sb(f"smr{i}", [128, 4], F32) for i in range(NSM)]
    lrT = sb("lrT", [16, TM], BF16)
    rstd = sb("rstd", [128, TM], F32)
    NPS = 8
    psb = [es.enter_context(nc.psum_tensor(f"ps{i}", [128, 512], F32)) for i in range(NPS)]

    ctr = {"ps": 0, "f32": 0, "b16": 0, "sm": 0, "tp": 0, "ub": 0, "slab": 0}

    def _bump(name, i):
        g = P.ring_gen.get((name, i), 0) + 1
        P.ring_gen[(name, i)] = g
        return (name, i, g)

    def ps_next():
        i = ctr["ps"] % NPS
        ctr["ps"] += 1
        return psb[i], _bump("ps", i)

    def f32_next():
        i = ctr["f32"] % NF32
        ctr["f32"] += 1
        return f32r[i], _bump("f32r", i)

    def b16_next():
        i = ctr["b16"] % NB16
        ctr["b16"] += 1
        return b16r[i], _bump("b16r", i)

    def sm_next():
        i = ctr["sm"] % NSM
        ctr["sm"] += 1
        return smr[i], _bump("smr", i)

    def ar(i):
        return ("ar", i)

    def mm(out, lhsT, rhs, start, stop, reads, writes):
        P.op("pe", lambda e: e.matmul(out, lhsT=lhsT, rhs=rhs, start=start, stop=stop), reads, writes)

    def tpose(out, in_, reads, writes):
        P.op("pe", lambda e: e.transpose(out, in_, ident_bf[:, :]), reads, writes)

    def act(out, in_, func, reads, writes, bias=None, scale=None):
        kw = {}
        if bias is not None:
            kw["bias"] = bias
        if scale is not None:
            kw["scale"] = scale
        P.op("act", lambda e: e.activation(out, in_, func, **kw), reads, writes)

    def tt(eng, out, in0, in1, op, reads, writes):
        P.op(eng, lambda e: e.tensor_tensor(out=out, in0=in0, in1=in1, op=op), reads, writes)

    def ts(eng, out, in0, s1, op0, reads, writes, s2=None, op1=None):
        if op1 is None:
            P.op(eng, lambda e: e.tensor_scalar(out=out, in0=in0, scalar1=s1, scalar2=None, op0=op0), reads, writes)
        else:
            P.op(eng, lambda e: e.tensor_scalar(out=out, in0=in0, scalar1=s1, scalar2=s2, op0=op0, op1=op1), reads, writes)

    def stt(out, in0, scalar, in1, op0, op1, reads, writes):
        P.op("dve", lambda e: e.scalar_tensor_tensor(out=out, in0=in0, scalar=scalar, in1=in1, op0=op0, op1=op1),
             reads, writes)

    def cp(eng, out, in_, reads, writes):
        if eng == "act":
            P.op(eng, lambda e: e.activation(out, in_, AF.Copy), reads, writes)
        else:
            P.op(eng, lambda e: e.tensor_copy(out=out, in_=in_), reads, writes)

    def scan(out, d0, d1, reads, writes):
        P.op("dve", lambda e: e.tensor_tensor_scan(out=out, data0=d0, data1=d1, initial=0.0, op0=ALU.mult,
                                                   op1=ALU.add), reads, writes)

    def recip(out, in_, reads, writes):
        P.op("dve", lambda e: e.reciprocal(out=out, in_=in_), reads, writes)

    def mset(eng, ap, val, writes):
        P.op(eng, lambda e: e.memset(ap, val), (), writes)

    def dma(eng, out, in_, reads, writes, sem, slow=False):
        if slow:
            return P.op(eng, lambda e: e.dma_start(out=out, in_=in_, allow_slow_non_contiguous=True), reads, writes,
                        dma=sem)
        return P.op(eng, lambda e: e.dma_start(out=out, in_=in_), reads, writes, dma=sem)

    def dbg(name, ap, reads):
        if not DEBUG["on"]:
            return
        shp = list(ap.shape)
        d = nc.dram_tensor("dbg_" + name, shp, ap.dtype, kind="ExternalOutput").ap()
        DEBUG["names"].append("dbg_" + name)
        final_dbg.append(dma("act", d, ap, reads, (), f"dbg{len(final_dbg)}", slow=True))

    final_dbg = []

    def v4(ap, n):
        return ap[:, :].rearrange("p (h t) -> p h t", h=4)[:, :, :n]

    W_PIECE = 8192
    GRP_END = [17, 23, NSLAB]

    def slab_grp(si):
        return 0 if si < 17 else (1 if si < 23 else 2)

    for l in range(depth):
        g0 = 0
        for g in range(3):
            a = int(SLAB_OFF[g0])
            end = int(SLAB_OFF[GRP_END[g]])
            while a < end:
                b = min(end, a + W_PIECE)
                dma("pool", w16_d[l, :, a:b], w32_d[l, :, a:b], (), [("w16", l, g)], f"pre{l}_{g}")
                a = b
            g0 = GRP_END[g]
    cst_toks = ["pp", "triu_f", "blr"] + [("hT", c) for c in range(6)] + [("f32r", l) for l in range(depth)]
    dma("act", pp[:, :], pp_d, (), ["pp"], "cst")
    dma("act", triu_f[:, :], hc_d[:, 128:256], (), ["triu_f"], "cst")
    dma("act", blr[:, :], p16_d[:, depth * 512:depth * 512 + depth], (), ["blr"], "cst", slow=True)
    dma("act", hT[:, 0, 0:128], hc_d[:, 0:128], (), [("hT", 0)], "cst")
    dma("act", hT[0:4, 1, 0:512], hc_d[0:4, 256:768], (), [("hT", 1)], "cst")
    for q in range(4):
        dma("act", hT[0:4, 2 + q, 0:512], prow_d[:, q * 512:(q + 1) * 512], (), [("hT", 2 + q)], "cst")
    for l in range(depth):
        last_cst = dma("act", f32r[l][0:16, 0:512], p16_d[:, l * 512:l * 512 + 512], (), [("f32r", l)], "cst")
    for tk in cst_toks:
        P.state[tk][0] = last_cst
    mset("dve", ones_bf[:, :], 1.0, ["ones_bf"])
    mset("dve", ones_f[:, :], 1.0, ["ones_f"])
    cp("dve", ident_bf[:, :], hT[:, 0, 0:128], [("hT", 0)], ["ident_bf"])
    cp("dve", triu_bf[:, :], triu_f[:, :], ["triu_f"], ["triu_bf"])
    cp("dve", sel_bf[:, :], hT[0:4, 1, 0:512], [("hT", 1)], ["sel_bf"])
    for q in range(4):
        cp("dve", brow_bf[:, q * 512:(q + 1) * 512], hT[0:4, 2 + q, 0:512], [("hT", 2 + q)], ["brow_bf"])
    for l in range(depth):
        cp("dve", wg_bf[:, l * 512:(l + 1) * 512], f32r[l][0:16, 0:512], [("f32r", l)], ["wg_bf"])
        ts("dve", nbg[:, l * 4:(l + 1) * 4], pp[:, l * PPL + PP_NBG:l * PPL + PP_NBG + 4], -1.0, ALU.mult,
           ["pp"], ["nbg"])
    for l in range(depth):
        for h in range(4):
            mset("pool", S_st[l][:, h, :], 0.0, [("S", l, h)])
            mset("pool", C_st[l][:, h, :], 0.0, [("C", l, h)])
        mset("pool", n_st[l][:, :], 0.0, [("n", l)])
        mset("pool", qk_hist[l][:, :, :], 0.0, [("qkh", l)])
        mset("pool", u_hist[l][:, :, :, :], 0.0, [("uh", l)])

    slab_seq = {"n": 0}

    def next_slab(l, s):
        i = slab_seq["n"] % NSB
        slab_seq["n"] += 1
        kc, ncols = SLABS[s]
        n = kc * ncols
        off = int(SLAB_OFF[s])
        dma("sp", slabbuf[i][:, 0:n], w16_d[l, :, off:off + n], [("w16", l, slab_grp(s))], [("slab", i)], f"sl{i}")
        return slabbuf[i][:, 0:n].rearrange("p (k n) -> p k n", k=kc), ("slab", i)

    def rms_to(l_gcol, T, dst_fn, H, hn):
        for c in range(8):
            act(arena[:, AR_SGA + c, :T], H[:, c, :T], AF.Square, [(hn, c)], [ar(AR_SGA + c)])
        ss, sst = ps_next()
        for c in range(8):
            mm(ss[:, :T], ones_bf[:, :], arena[:, AR_SGA + c, :T], c == 0, c == 7, ["ones_bf", ar(AR_SGA + c)], [sst])
        lnv, lnt = f32_next()
        act(lnv[:, :T], ss[:, :T], AF.Ln, [sst], [lnt], bias=EPS, scale=1.0 / D)
        act(rstd[:, :T], lnv[:, :T], AF.Exp, [lnt], ["rstd"], scale=-0.5)
        for c in range(8):
            o, otk = dst_fn(c)
            stt(o, H[:, c, :T], pp[:, l_gcol + c:l_gcol + c + 1], rstd[:, :T], ALU.mult, ALU.mult,
                [(hn, c), "pp", "rstd"], otk)

    def tile_layer(l, T, cs, nch, ti, H, hn):
        pb = l * PPL
        tag = f"t{ti}l{l}"
        ALLAR = [ar(i) for i in range(NAR)]
        xn = lambda c: arena[:, AR_XN + c, :T]
        rms_to(pb + PP_NM, T, lambda c: (xn(c), [ar(AR_XN + c)]), H, hn)
        XN = [ar(AR_XN + c) for c in range(8)]
        dbg("xn_" + tag, arena[:, AR_XN:AR_XN + 8, :T], XN)

        sl, slt = next_slab(l, 0)
        ps, pst_ = ps_next()
        for kc in range(8):
            mm(ps[:16, :T], sl[:, kc, 0:16], xn(kc), kc == 0, kc == 7, [slt, ar(AR_XN + kc)], [pst_])
        act(lrT[:, :T], ps[:16, :T], AF.Identity, [pst_, "blr"], ["lrT"], bias=blr[:, l:l + 1])
        ps2, ps2t = ps_next()
        for c in range(nch):
            for kc in range(8):
                mm(ps2[:cs, c * 8:(c + 1) * 8], arena[:, AR_XN + kc, c * cs:(c + 1) * cs], sl[:, kc, 16:24],
                   kc == 0, kc == 7, [slt, ar(AR_XN + kc)], [ps2t])
        tt("dve", ifg[:cs, :nch, :], ps2[:cs, 0:nch * 8].rearrange("p (c e) -> p c e", e=8),
           pp[:cs, pb + PP_BIF:pb + PP_BIF + 8].unsqueeze(1).to_broadcast([cs, nch, 8]), ALU.add,
           [ps2t, "pp"], ["ifg"])
        act(iftmp[:cs, :nch, :], ifg[:cs, :nch, 4:8], AF.Exp, ["ifg"], ["iftmp"], scale=-1.0)
        act(lfn[:cs, :nch, :], iftmp[:cs, :nch, :], AF.Ln, ["iftmp"], ["lfn"], bias=1.0)

        E_h = {}

        def gla_gates(h):
            gps, gpt = ps_next()
            mm(gps[:, :T], wg_bf[:, l * 512 + h * 128:l * 512 + (h + 1) * 128], lrT[:, :T], True, True,
               ["wg_bf", "lrT"], [gpt])
            e1, e1t = f32_next()
            act(e1[:, :T], gps[:, :T], AF.Exp, [gpt, "nbg"], [e1t], bias=nbg[:, l * 4 + h:l * 4 + h + 1], scale=-1.0)
            l1, l1t = f32_next()
            act(l1[:, :T], e1[:, :T], AF.Ln, [e1t], [l1t], bias=1.0)
            Lc, Lct = f32_next()
            for c in range(nch):
                scan(Lc[:, c * cs:(c + 1) * cs], ones_f[:, :cs], l1[:, c * cs:(c + 1) * cs], [l1t, "ones_f"], [Lct])
            act(A_t[:, h, :nch], Lc[:, :T].rearrange("p (c s) -> p c s", s=cs)[:, :, cs - 1], AF.Exp,
                [Lct], [("A", h)], scale=-1.0 / TAU)
            Eb, Ebt = f32_next()
            act(Eb[:, :T], Lc[:, :T], AF.Exp, [Lct], [Ebt], scale=-1.0 / TAU, bias=LNS)
            En, Ent = f32_next()
            act(En[:, :T], Lc[:, :T], AF.Exp, [Lct], [Ent], scale=1.0 / TAU)
            E_h[h] = (Eb, Ebt, En, Ent)

        def mlstm_conv():
            cp("pool", mqk_pre[:, :, 0:3], qk_hist[l][:, :, :], [("qkh", l)], [("mqp", j) for j in range(8)])
            cp("pool", qk_hist[l][:, :, :], mqk_pre[:, :, T:T + 3], [("mqp", j) for j in range(8)], [("qkh", l)])
            for j0 in range(0, 8, 2):
                accs = []
                for j in (j0, j0 + 1):
                    a_, at = f32_next()
                    accs.append((a_, at))
                    act(a_[:, :T], mqk_pre[:, j, 0:T], AF.Identity, [("mqp", j), "pp"], [at],
                        scale=pp[:, pb + PP_CW + j:pb + PP_CW + j + 1])
                for tap in (1, 2, 3):
                    for k_, j in enumerate((j0, j0 + 1)):
                        a_, at = accs[k_]
                        stt(a_[:, :T], mqk_pre[:, j, tap:tap + T],
                            pp[:, pb + PP_CW + tap * 8 + j:pb + PP_CW + tap * 8 + j + 1], a_[:, :T], ALU.mult, ALU.add,
                            [("mqp", j), "pp", at], [at])
                for k_, j in enumerate((j0, j0 + 1)):
                    a_, at = accs[k_]
                    act(mqk_pre[:, j, 3:3 + T], a_[:, :T], AF.Silu, [at, "pp", ("mqp", j)], [("mqp", j)],
                        bias=pp[:, pb + PP_CB + j:pb + PP_CB + j + 1])

        fc = 0
        for s in range(1, 13):
            sl, slt = next_slab(l, s)
            for j in range(4):
                if fc < 8 and fc % 2 == 0:
                    gla_gates(fc // 2)
                ps, pt = ps_next()
                for kc in range(8):
                    mm(ps[:, :T], sl[:, kc, j * 128:(j + 1) * 128], xn(kc), kc == 0, kc == 7,
                       [slt, ar(AR_XN + kc)], [pt])
                bcol = pp[:, pb + PP_BFM + fc:pb + PP_BFM + fc + 1]
                if fc < 8:
                    h, isk = fc // 2, fc % 2
                    Eb, Ebt, En, Ent = E_h[h]
                    if not isk:
                        stt(arena[:, AR_QE + h, :T], ps[:, :T], bcol, Eb[:, :T], ALU.add, ALU.mult,
                            [pt, "pp", Ebt], [ar(AR_QE + h)])
                    else:
                        stt(arena[:, AR_KD + h, :T], ps[:, :T], bcol, En[:, :T], ALU.add, ALU.mult,
                            [pt, "pp", Ent], [ar(AR_KD + h)])
                elif fc < 16:
                    jj = fc - 8
                    act(arena[:, AR_SR + jj, :T], ps[:, :T], AF.Silu, [pt, "pp"], [ar(AR_SR + jj)], bias=bcol)
                    act(arena[:, AR_SR + jj, :T], arena[:, AR_SR + jj, :T], AF.Identity, [ar(AR_SR + jj), "pp"],
                        [ar(AR_SR + jj)], scale=pp[:, pb + PP_GN + jj % 2:pb + PP_GN + jj % 2 + 1])
                elif fc < 24:
                    jj = fc - 16
                    act(mqk_pre[:, jj, 3:3 + T], ps[:, :T], AF.Identity, [pt, "pp"], [("mqp", jj)], bias=bcol)
                elif fc < 32:
                    jj = fc - 24
                    act(arena[:, AR_SO + jj, :T], ps[:, :T], AF.Sigmoid, [pt, "pp"], [ar(AR_SO + jj)], bias=bcol)
                    act(arena[:, AR_SO + jj, :T], arena[:, AR_SO + jj, :T], AF.Identity, [ar(AR_SO + jj), "pp"],
                        [ar(AR_SO + jj)], scale=pp[:, pb + PP_MN + jj:pb + PP_MN + jj + 1])
                elif fc < 40:
                    jj = fc - 32
                    act(arena[:, AR_SGA + jj, :T], ps[:, :T], AF.Sigmoid, [pt, "pp"], [ar(AR_SGA + jj)], bias=bcol)
                else:
                    jj = fc - 40
                    act(arena[:, AR_SGB + jj, :T], ps[:, :T], AF.Sigmoid, [pt, "pp"], [ar(AR_SGB + jj)], bias=bcol)
                fc += 1
                if fc == 24:
                    mlstm_conv()

        if ti == 0 and l == 0:
            dbg("sel", sel_bf[:, :], ["sel_bf"])
            dbg("brow", brow_bf[:, :], ["brow_bf"])
            dbg("ident", ident_bf[:, :], ["ident_bf"])
            dbg("triu", triu_bf[:, :], ["triu_bf"])
        dbg("lfn_" + tag, lfn[:cs, :nch, :], ["lfn"])
        dbg("ifg_" + tag, ifg[:cs, :nch, :], ["ifg"])
        dbg("A_" + tag, A_t[:, :, :nch], [("A", h) for h in range(4)])
        dbg("fm_" + tag, arena[:, 8:NAR, :T], ALLAR)
        for q in range(4):
            sl, slt = next_slab(l, 13 + q)
            dst = gv if q < 2 else mv
            dname = "gv" if q < 2 else "mv"
            half = q % 2
            for c in range(nch):
                ps, pt = ps_next()
                for kc in range(8):
                    mm(ps[:cs, :], arena[:, AR_XN + kc, c * cs:(c + 1) * cs], sl[:, kc, :], kc == 0, False,
                       [slt, ar(AR_XN + kc)], [pt])
                mm(ps[:cs, :], sel_bf[:, l * 128:l * 128 + cs], brow_bf[:, q * 512:(q + 1) * 512], False, True,
                   ["sel_bf", "brow_bf"], [pt])
                cp("dve", dst[:cs, c, half * 512:(half + 1) * 512], ps[:cs, :], [pt], [(dname, c)])

        mqa = lambda h, a, b: mqk_pre[:, h, 3 + a:3 + b]
        mka = lambda h, a, b: mqk_pre[:, 4 + h, 3 + a:3 + b]
        dbg("gv_" + tag, gv[:cs, :nch, :], [("gv", c) for c in range(nch)])
        dbg("mv_" + tag, mv[:cs, :nch, :], [("mv", c) for c in range(nch)])
        dbg("mqk_" + tag, mqk_pre[:, :, 3:3 + T], [("mqp", j) for j in range(8)])

        cp("act", S_bf[:, :, :], S_st[l][:, :, :], [("S", l, h) for h in range(4)], [("Sbf", h) for h in range(4)])
        cp("act", C_bf[:, :, :], C_st[l][:, :, :], [("C", l, h) for h in range(4)], [("Cbf", h) for h in range(4)])
        tt("pool", nb_bf[:, :].rearrange("p (h m) -> p h m", h=4), ones_f[:, :].rearrange("p (h m) -> p h m", h=4),
           n_st[l][:, :].unsqueeze(2).to_broadcast([128, 4, 128]), ALU.mult, [("n", l), "ones_f"], ["nb_bf"])

        def head_rms_apply(o_ps, o_pt, src_f32, dst_base, cols):
            sq = []
            for b in range(2):
                q_, qt = b16_next()
                src = v4(o_ps[b], cs) if src_f32 is None else v4(src_f32[b][0], cs)
                srct = o_pt[b] if src_f32 is None else src_f32[b][1]
                act(v4(q_, cs), src, AF.Square, [srct], [qt])
                sq.append((q_, qt))
            ss, sst = ps_next()
            for h in range(4):
                for vc in range(2):
                    blk = (h % 2) * 2 + vc
                    mm(ss[:, h * 128:h * 128 + cs], ones_bf[:, :], sq[h // 2][0][:, blk * 128:blk * 128 + cs],
                       vc == 0, vc == 1, ["ones_bf", sq[h // 2][1]], [sst])
            lnv, lnt = f32_next()
            act(v4(lnv, cs), v4(ss, cs), AF.Ln, [sst], [lnt], bias=EPS, scale=1.0 / 256.0)
            rs, rst = f32_next()
            act(v4(rs, cs), v4(lnv, cs), AF.Exp, [lnt], [rst], scale=-0.5)
            for b in range(2):
                o1, o1t = f32_next()
                src = o_ps[b] if src_f32 is None else src_f32[b][0]
                srct = o_pt[b] if src_f32 is None else src_f32[b][1]
                tt("dve", o1[:, :].rearrange("p (h v t) -> p h v t", h=2, v=2)[:, :, :, :cs],
                   src[:, :].rearrange("p (h v t) -> p h v t", h=2, v=2)[:, :, :, :cs],
                   rs[:, :].rearrange("p (h t) -> p h t", h=4)[:, 2 * b:2 * b + 2, :cs].unsqueeze(2).to_broadcast(
                       [128, 2, 2, cs]), ALU.mult, [srct, rst], [o1t])
                dsta = arena[:, dst_base + 4 * b:dst_base + 4 * b + 4, cols[0]:cols[1]]
                tt("pool", dsta, v4(o1, cs), dsta, ALU.mult, [o1t] + [ar(dst_base + 4 * b + i) for i in range(4)],
                   [ar(dst_base + 4 * b + i) for i in range(4)])

        for c in range(nch):
            c0, c1 = c * cs, (c + 1) * cs
            last = (c == nch - 1)
            sc, sct = ps_next()
            for h in range(4):
                mm(sc[:cs, h * 128:h * 128 + cs], arena[:, AR_KD + h, c0:c1], arena[:, AR_QE + h, c0:c1], True, True,
                   [ar(AR_KD + h), ar(AR_QE + h)], [sct])
            scm, scmt = b16_next()
            tt("dve", v4(scm[:cs, :], cs), v4(sc[:cs, :], cs), triu_bf[:cs, :cs].unsqueeze(1).to_broadcast([cs, 4, cs]),
               ALU.mult, [sct, "triu_bf"], [scmt])
            kdl, kdlt = b16_next()
            tt("pool", v4(kdl, cs), arena[:, AR_KD:AR_KD + 4, c0:c1], A_t[:, :, c:c + 1].to_broadcast([128, 4, cs]),
               ALU.mult, [ar(AR_KD + h) for h in range(4)] + [("A", h) for h in range(4)], [kdlt])
            ktp, ktpt = ps_next()
            for h in range(4):
                mm(ktp[:cs, h * 128:(h + 1) * 128], kdl[:, h * 128:h * 128 + cs], ident_bf[:, :], True, True,
                   [kdlt, "ident_bf"], [ktpt])
            kT, kTt = b16_next()
            cp("act", kT[:cs, :], ktp[:cs, :], [ktpt], [kTt])
            o_ps, o_pt = [], []
            for b in range(2):
                p_, t_ = ps_next()
                o_ps.append(p_)
                o_pt.append(t_)
            for h in range(4):
                for vc in range(2):
                    blk = (h % 2) * 2 + vc
                    dst = o_ps[h // 2][:, blk * 128:blk * 128 + cs]
                    mm(dst, S_bf[:, h, vc * 128:(vc + 1) * 128], arena[:, AR_QE + h, c0:c1], True, False,
                       [("Sbf", h), ar(AR_QE + h)], [o_pt[h // 2]])
                    mm(dst, gv[:cs, c, h * 256 + vc * 128:h * 256 + (vc + 1) * 128], scm[:cs, h * 128:h * 128 + cs],
                       False, True, [("gv", c), scmt], [o_pt[h // 2]])
            head_rms_apply(o_ps, o_pt, None, AR_SR, (c0, c1))
            P_ps, P_pt = [], []
            for b in range(2):
                p_, t_ = ps_next()
                P_ps.append(p_)
                P_pt.append(t_)
            for h in range(4):
                mm(P_ps[h // 2][:, (h % 2) * 256:(h % 2) * 256 + 256], kT[:cs, h * 128:(h + 1) * 128],
                   gv[:cs, c, h * 256:(h + 1) * 256], True, True, [kTt, ("gv", c)], [P_pt[h // 2]])
            for h in range(4):
                stt(S_st[l][:, h, :], S_st[l][:, h, :], A_t[:, h, c:c + 1], P_ps[h // 2][:, (h % 2) * 256:(h % 2) * 256 + 256],
                    ALU.mult, ALU.add, [("S", l, h), ("A", h), P_pt[h // 2]], [("S", l, h)])
            if not last:
                for h in range(4):
                    cp("act", S_bf[:, h, :], S_st[l][:, h, :], [("S", l, h)], [("Sbf", h)])

            lfb, lfbt = f32_next()
            tt("dve", v4(lfb[:cs, :], 128), v4(ones_f[:cs, :], 128), lfn[:cs, c, :].unsqueeze(2).to_broadcast([cs, 4, 128]),
               ALU.mult, ["ones_f", "lfn"], [lfbt])
            fb, fbt = ps_next()
            for h in range(4):
                mm(fb[:, h * 128:h * 128 + cs], lfb[:cs, h * 128:(h + 1) * 128], triu_f[:cs, :cs], True, True,
                   [lfbt, "triu_f"], [fbt])
            fcp, fcpt = ps_next()
            mm(fcp[:cs, 0:4], triu_f[:cs, :cs], lfn[:cs, c, :], True, True, ["triu_f", "lfn"], [fcpt])
            db, dbt = sm_next()
            stt(db[:cs, :], fcp[:cs, 0:4], LNS, ifg[:cs, c, 0:4], ALU.add, ALU.add, [fcpt, "ifg"], [dbt])
            Dm, Dmt = f32_next()
            for h in range(4):
                act(Dm[:cs, h * 128:h * 128 + cs], fb[:cs, h * 128:h * 128 + cs], AF.Exp, [fbt, dbt], [Dmt],
                    bias=db[:cs, h:h + 1], scale=-1.0)
            tt("pool", v4(Dm[:cs, :], cs), v4(Dm[:cs, :], cs), triu_f[:cs, :cs].unsqueeze(1).to_broadcast([cs, 4, cs]),
               ALU.mult, [Dmt, "triu_f"], [Dmt])
            sc, sct = ps_next()
            for h in range(4):
                mm(sc[:cs, h * 128:h * 128 + cs], mka(h, c0, c1), mqa(h, c0, c1), True, True,
                   [("mqp", 4 + h), ("mqp", h)], [sct])
            qkD, qkDt = b16_next()
            tt("dve", v4(qkD[:cs, :], cs), v4(sc[:cs, :], cs), v4(Dm[:cs, :], cs), ALU.mult, [sct, Dmt], [qkDt])
            EF, EFt = f32_next()
            act(v4(EF, cs), v4(fb, cs), AF.Exp, [fbt], [EFt], bias=LNS, scale=-1.0)
            qf, qft = b16_next()
            tt("pool", v4(qf, cs), mqk_pre[:, 0:4, 3 + c0:3 + c1], v4(EF, cs), ALU.mult,
               [("mqp", h) for h in range(4)] + [EFt], [qft])
            n_ps, n_pt = [], []
            for b in range(2):
                p_, t_ = ps_next()
                n_ps.append(p_)
                n_pt.append(t_)
            for h in range(4):
                for vc in range(2):
                    blk = (h % 2) * 2 + vc
                    dst = n_ps[h // 2][:, blk * 128:blk * 128 + cs]
                    mm(dst, C_bf[:, h, vc * 128:(vc + 1) * 128], qf[:, h * 128:h * 128 + cs], True, False,
                       [("Cbf", h), qft], [n_pt[h // 2]])
                    mm(dst, mv[:cs, c, h * 256 + vc * 128:h * 256 + (vc + 1) * 128], qkD[:cs, h * 128:h * 128 + cs],
                       False, True, [("mv", c), qkDt], [n_pt[h // 2]])
            den, dent = ps_next()
            for h in range(4):
                mm(den[:, h * 128:h * 128 + cs], nb_bf[:, h * 128:(h + 1) * 128], qf[:, h * 128:h * 128 + cs], True, False,
                   ["nb_bf", qft], [dent])
                mm(den[:, h * 128:h * 128 + cs], ones_bf[:cs, :], qkD[:cs, h * 128:h * 128 + cs], False, True,
                   ["ones_bf", qkDt], [dent])
            d1, d1t = f32_next()
            d0, d0t = f32_next()
            act(v4(d0, cs), v4(den, cs), AF.Abs, [dent], [d0t])
            ts("dve", v4(d1, cs), v4(d0, cs), 1.0, ALU.max, [d0t], [d1t])
            wd, wdt = sm_next()
            tt("dve", wd[:cs, :], db[:cs, :], fb[:cs, :].rearrange("p (h t) -> p h t", h=4)[:, :, cs - 1], ALU.subtract,
               [dbt, fbt], [wdt])
            Gd, Gdt = sm_next()
            act(Gd[:, :], fb[:, :].rearrange("p (h t) -> p h t", h=4)[:, :, cs - 1], AF.Exp, [fbt], [Gdt], scale=-1.0)
            rden, rdent = f32_next()
            recip(v4(rden, cs), v4(d1, cs), [d1t], [rdent])
            hbuf = []
            for b in range(2):
                hb, hbt = f32_next()
                tt("dve", hb[:, :].rearrange("p (h v t) -> p h v t", h=2, v=2)[:, :, :, :cs],
                   n_ps[b][:, :].rearrange("p (h v t) -> p h v t", h=2, v=2)[:, :, :, :cs],
                   rden[:, :].rearrange("p (h t) -> p h t", h=4)[:, 2 * b:2 * b + 2, :cs].unsqueeze(2).to_broadcast(
                       [128, 2, 2, cs]), ALU.mult, [n_pt[b], rdent], [hbt])
                hbuf.append((hb, hbt))
            head_rms_apply(None, None, hbuf, AR_SO, (c0, c1))
            wst, wstt = sm_next()
            act(wst[:cs, :], wd[:cs, :], AF.Exp, [wdt], [wstt], bias=-LNS)
            ktp, ktpt = ps_next()
            for h in range(4):
                mm(ktp[:cs, h * 128:(h + 1) * 128], mka(h, c0, c1), ident_bf[:, :], True, True,
                   [("mqp", 4 + h), "ident_bf"], [ktpt])
            kw, kwt = b16_next()
            tt("dve", v4(kw[:cs, :], 128), v4(ktp[:cs, :], 128), wst[:cs, :].unsqueeze(2).to_broadcast([cs, 4, 128]),
               ALU.mult, [ktpt, wstt], [kwt])
            P_ps, P_pt = [], []
            for b in range(2):
                p_, t_ = ps_next()
                P_ps.append(p_)
                P_pt.append(t_)
            for h in range(4):
                mm(P_ps[h // 2][:, (h % 2) * 256:(h % 2) * 256 + 256], kw[:cs, h * 128:(h + 1) * 128],
                   mv[:cs, c, h * 256:(h + 1) * 256], True, True, [kwt, ("mv", c)], [P_pt[h // 2]])
            npp, nppt = ps_next()
            for h in range(4):
                mm(npp[:, h:h + 1], kw[:cs, h * 128:(h + 1) * 128], ones_bf[:cs, 0:1], True, True, [kwt, "ones_bf"], [nppt])
            for h in range(4):
                stt(C_st[l][:, h, :], C_st[l][:, h, :], Gd[:, h:h + 1], P_ps[h // 2][:, (h % 2) * 256:(h % 2) * 256 + 256],
                    ALU.mult, ALU.add, [("C", l, h), Gdt, P_pt[h // 2]], [("C", l, h)])
            ntmp, ntt = sm_next()
            tt("dve", ntmp[:, :], n_st[l][:, :], Gd[:, :], ALU.mult, [("n", l), Gdt], [ntt])
            tt("dve", n_st[l][:, :], npp[:, 0:4], ntmp[:, :], ALU.add, [nppt, ntt], [("n", l)])
            if not last:
                for h in range(4):
                    cp("act", C_bf[:, h, :], C_st[l][:, h, :], [("C", l, h)], [("Cbf", h)])
                tt("pool", nb_bf[:, :].rearrange("p (h m) -> p h m", h=4), ones_f[:, :].rearrange("p (h m) -> p h m", h=4),
                   n_st[l][:, :].unsqueeze(2).to_broadcast([128, 4, 128]), ALU.mult, [("n", l), "ones_f"], ["nb_bf"])

        dbg("og_" + tag, arena[:, AR_SR:AR_SR + 8, :T], ALLAR)
        dbg("om_" + tag, arena[:, AR_SO:AR_SO + 8, :T], ALLAR)
        dbg("S_" + tag, S_st[l][:, :, :], [("S", l, h) for h in range(4)])
        dbg("C_" + tag, C_st[l][:, :, :], [("C", l, h) for h in range(4)])
        dbg("n_" + tag, n_st[l][:, :], [("n", l)])
        for br in range(2):
            src_base = AR_SR if br == 0 else AR_SO
            gate_base = AR_SGA if br == 0 else AR_SGB
            for half in range(2):
                sl, slt = next_slab(l, 17 + br * 2 + half)
                for dcl in range(4):
                    dc = half * 4 + dcl
                    ps, pt = ps_next()
                    for kc in range(8):
                        mm(ps[:, :T], sl[:, kc, dcl * 128:(dcl + 1) * 128], arena[:, src_base + kc, :T], kc == 0, kc == 7,
                           [slt, ar(src_base + kc)], [pt])
                    tt("dve", arena[:, gate_base + dc, :T], ps[:, :T], arena[:, gate_base + dc, :T], ALU.mult,
                       [pt, ar(gate_base + dc)], [ar(gate_base + dc)])
        for dc in range(8):
            tt("pool", arena[:, AR_SGA + dc, :T], arena[:, AR_SGA + dc, :T], arena[:, AR_SGB + dc, :T], ALU.add,
               [ar(AR_SGA + dc), ar(AR_SGB + dc)], [ar(AR_SGA + dc)])
        for half in range(2):
            sl, slt = next_slab(l, 21 + half)
            for dcl in range(4):
                dc = half * 4 + dcl
                ps, pt = ps_next()
                for kc in range(8):
                    mm(ps[:, :T], sl[:, kc, dcl * 128:(dcl + 1) * 128], arena[:, AR_SGA + kc, :T], kc == 0, kc == 7,
                       [slt, ar(AR_SGA + kc)], [pt])
                tt("dve", H[:, dc, :T], ps[:, :T], H[:, dc, :T], ALU.add, [pt, (hn, dc)], [(hn, dc)])

        dbg("hmix_" + tag, H[:, :, :T], [(hn, c) for c in range(8)])
        rms_to(pb + PP_NF, T, lambda c: (xn(c), [ar(AR_XN + c)]), H, hn)
        for j in range(22):
            if j % 2 == 0:
                sl, slt = next_slab(l, 23 + j // 2)
            cb0 = (j % 2) * 256
            pg, pgt = ps_next()
            pu, put = ps_next()
            for kc in range(8):
                mm(pg[:, :T], sl[:, kc, cb0:cb0 + 128], xn(kc), kc == 0, kc == 7, [slt, ar(AR_XN + kc)], [pgt])
            for kc in range(8):
                mm(pu[:, :T], sl[:, kc, cb0 + 128:cb0 + 256], xn(kc), kc == 0, kc == 7, [slt, ar(AR_XN + kc)], [put])
            ui = ctr["ub"] % 2
            ctr["ub"] += 1
            ub = ubuf[ui]
            ubt = [("ub", ui, 0), ("ub", ui, 1)]
            cp("pool", ub[:, :, 0:2], u_hist[l][:, j, :, :], [("uh", l)], ubt)
            cp("act", ub[:, 0, 2:2 + T], pg[:, :T], [pgt], [ubt[0]])
            cp("act", ub[:, 1, 2:2 + T], pu[:, :T], [put], [ubt[1]])
            cp("pool", u_hist[l][:, j, :, :], ub[:, :, T:T + 2], ubt, [("uh", l)])
            accs = []
            for i in range(2):
                a_, at = f32_next()
                accs.append((a_, at))
                col = (j if i == 0 else 22 + j)
                act(a_[:, :T], ub[:, i, 0:T], AF.Identity, [ubt[i], "pp"], [at],
                    scale=pp[:, pb + PP_FW + col:pb + PP_FW + col + 1])
            for tap in (1, 2):
                for i in range(2):
                    a_, at = accs[i]
                    col = (j if i == 0 else 22 + j)
                    stt(a_[:, :T], ub[:, i, tap:tap + T], pp[:, pb + PP_FW + tap * 44 + col:pb + PP_FW + tap * 44 + col + 1],
                        a_[:, :T], ALU.mult, ALU.add, [ubt[i], "pp", at], [at])
            ga, gat = b16_next()
            act(ga[:, :T], accs[0][0][:, :T], AF.Silu, [accs[0][1], "pp"], [gat], bias=pp[:, pb + PP_FB + j:pb + PP_FB + j + 1])
            stt(arena[:, AR_AT + j, :T], accs[1][0][:, :T], pp[:, pb + PP_FB + 22 + j:pb + PP_FB + 22 + j + 1], ga[:, :T],
                ALU.add, ALU.mult, [accs[1][1], "pp", gat], [ar(AR_AT + j)])
        for dc in range(8):
            sl, slt = next_slab(l, 34 + dc)
            ps, pt = ps_next()
            for kc in range(22):
                mm(ps[:, :T], sl[:, kc, :], arena[:, AR_AT + kc, :T], kc == 0, kc == 21, [slt, ar(AR_AT + kc)], [pt])
            tt("dve", H[:, dc, :T], ps[:, :T], H[:, dc, :T], ALU.add, [pt, (hn, dc)], [(hn, dc)])
        dbg("at_" + tag, arena[:, AR_AT:AR_AT + 22, :T], ALLAR)
        dbg("hffn_" + tag, H[:, :, :T], [(hn, c) for c in range(8)])

    final_dma = []

    def load_tile(H, hn, T, src):
        HT = [(hn, c) for c in range(8)]
        if src[0] == "meta":
            dma("act", H[:, :, :T], metaT_d.rearrange("(c p) t -> p c t", p=128), (), HT, "xin")
        else:
            t0 = src[1]
            dma("act", H[:, :, :T], xT_d.rearrange("(c p) t -> p c t", p=128)[:, :, t0:t0 + T], (), HT, "xin")

    def store_tile(T, src):
        fcol = depth * PPL
        rms_to(fcol, T, lambda c: (hT[:, c, :T], [("hT", c)]), hT, "hT")
        t0 = src[1]
        HT = [("hT", c) for c in range(8)]
        return dma("act", outT_d.rearrange("(c p) t -> p c t", p=128)[:, :, t0:t0 + T], hT[:, :, :T], HT, (), "xout")

    Tm, csm, nchm, srcm = tiles[0]
    T1, cs1, nch1, src1 = tiles[1]
    load_tile(hTm, "hTm", Tm, srcm)
    load_tile(hT, "hT", T1, src1)
    for l in range(depth):
        tile_layer(l, Tm, csm, nchm, 0, hTm, "hTm")
        tile_layer(l, T1, cs1, nch1, 1, hT, "hT")
    final_dma = [store_tile(T1, src1)]
    for ti, (T, cs, nch, src) in enumerate(tiles):
        if ti < 2:
            continue
        load_tile(hT, "hT", T, src)
        for l in range(depth):
            tile_layer(l, T, cs, nch, ti, hT, "hT")
        final_dma = [store_tile(T, src)]
    P.emit(nc, es, {"act": final_dma + final_dbg})
    es.close()
    return nc


def _slab_img(Wblk):
    K, n = Wblk.shape
    kc = K // 128
    return np.ascontiguousarray(Wblk.reshape(kc, 128, n).transpose(1, 0, 2)).reshape(128, kc * n)


def pack_weights(depth, w_in, b_in, gla_w_gate, gla_b_gate, gla_norm, ml_conv_w, ml_conv_b, ml_norm,
                 w_branch_gla, w_branch_ml, w_out, norm_mix, norm_ffn, ffn_w_up, ffn_conv_w, ffn_conv_b,
                 ffn_w_down, norm_final):
    o_gq, o_gk, o_gv, o_gr, o_lr, o_mq, o_mk, o_mv, o_mo, o_mi, o_mf, o_ga, o_gb = (
        0, 512, 1024, 2048, 3072, 3088, 3600, 4112, 5136, 6160, 6164, 6168, 7192)
    fm_cols = []
    for h in range(4):
        fm_cols += list(range(o_gq + h * 128, o_gq + (h + 1) * 128))
        fm_cols += list(range(o_gk + h * 128, o_gk + (h + 1) * 128))
    fm_cols += list(range(o_gr, o_gr + 1024)) + list(range(o_mq, o_mq + 512)) + list(range(o_mk, o_mk + 512))
    fm_cols += list(range(o_mo, o_mo + 1024)) + list(range(o_ga, o_ga + 1024)) + list(range(o_gb, o_gb + 1024))
    fm_cols = np.array(fm_cols)
    small_cols = np.array(list(range(o_lr, o_lr + 16)) + list(range(o_mi, o_mi + 4)) + list(range(o_mf, o_mf + 4)))
    tm_cols = np.array(list(range(o_gv, o_gv + 1024)) + list(range(o_mv, o_mv + 1024)))
    up_cols = []
    for j in range(22):
        up_cols += list(range(j * 128, (j + 1) * 128)) + list(range(DFF + j * 128, DFF + (j + 1) * 128))
    up_cols = np.array(up_cols)

    w32 = np.zeros((depth, 128, PW), np.float32)
    pp = np.zeros((128, depth * PPL + 8), np.float32)
    p16 = np.zeros((16, depth * 512 + depth), np.float32)
    prow = np.zeros((4, 2048), np.float32)
    for l in range(depth):
        parts = [_slab_img(w_in[l][:, small_cols])]
        wf = w_in[l][:, fm_cols]
        for s in range(12):
            parts.append(_slab_img(wf[:, s * 512:(s + 1) * 512]))
        wt = w_in[l][:, tm_cols]
        for s in range(4):
            parts.append(_slab_img(wt[:, s * 512:(s + 1) * 512]))
        for W in (w_branch_gla[l], w_branch_ml[l], w_out[l]):
            for s in range(2):
                parts.append(_slab_img(W[:, s * 512:(s + 1) * 512]))
        wu = ffn_w_up[l][:, up_cols]
        for s in range(11):
            parts.append(_slab_img(wu[:, s * 512:(s + 1) * 512]))
        for dc in range(8):
            parts.append(_slab_img(ffn_w_down[l][:, dc * 128:(dc + 1) * 128]))
        w32[l] = np.concatenate(parts, axis=1)
        b = l * PPL
        pp[:, b + PP_NM:b + PP_NM + 8] = norm_mix[l].reshape(8, 128).T
        pp[:, b + PP_NF:b + PP_NF + 8] = norm_ffn[l].reshape(8, 128).T
        pp[:, b + PP_BFM:b + PP_BFM + 48] = b_in[l][fm_cols].reshape(48, 128).T
        pp[:, b + PP_NBG:b + PP_NBG + 4] = gla_b_gate[l].reshape(4, 128).T
        pp[:, b + PP_GN:b + PP_GN + 2] = gla_norm[l].reshape(2, 128).T
        pp[:, b + PP_MN:b + PP_MN + 8] = ml_norm[l].reshape(8, 128).T
        pp[:, b + PP_CW:b + PP_CW + 32] = ml_conv_w[l].reshape(4, 8, 128).transpose(2, 0, 1).reshape(128, 32)
        pp[:, b + PP_CB:b + PP_CB + 8] = ml_conv_b[l].reshape(8, 128).T
        pp[:, b + PP_FW:b + PP_FW + 132] = ffn_conv_w[l].reshape(3, 44, 128).transpose(2, 0, 1).reshape(128, 132)
        pp[:, b + PP_FB:b + PP_FB + 44] = ffn_conv_b[l].reshape(44, 128).T
        pp[:, b + PP_BIF:b + PP_BIF + 8] = np.broadcast_to(b_in[l][small_cols[16:24]][None, :], (128, 8))
        p16[:, l * 512:l * 512 + 512] = gla_w_gate[l]
        p16[:, depth * 512 + l] = b_in[l][small_cols[0:16]]
        prow[l, :] = b_in[l][tm_cols]
    pp[:, depth * PPL:depth * PPL + 8] = norm_final.reshape(8, 128).T
    hc = np.zeros((128, 768), np.float32)
    hc[:, 0:128] = np.eye(128, dtype=np.float32)
    hc[:, 128:256] = np.triu(np.ones((128, 128), np.float32))
    for l in range(4):
        hc[l, 256 + l * 128:256 + (l + 1) * 128] = 1.0
    return w32, pp, p16, prow, hc


def make_tiles(seq):
    tiles = [(NMETA, NMETA, 1, ("meta",))]
    for t0 in range(0, seq, 512):
        tiles.append((512, 128, 4, ("x", t0)))
    return tiles


def run_model(x, meta, depth, **w):
    B, seq, _ = x.shape
    w32, pp, p16, prow, hc = pack_weights(depth, **w)
    metaT = np.ascontiguousarray(np.asarray(meta, np.float32).T)
    nc = build_program(depth, make_tiles(seq), seq)
    in_maps = []
    for b in range(B):
        in_maps.append({"xT": np.ascontiguousarray(x[b].T), "metaT": metaT, "w32": w32, "pp": pp, "p16": p16,
                        "prow": prow, "hc": hc})
    res = run_bass_kernel_spmd(nc, in_maps, core_ids=list(range(B)))
    out = np.stack([np.ascontiguousarray(r["outT"].T) for r in res.results], axis=0)
    if DEBUG["on"]:
        DEBUG["res"] = res.results
    return out.astype(np.float32)


def kernel(x, meta, norm_mix, w_in, b_in, gla_w_gate, gla_b_gate, gla_norm, ml_conv_w, ml_conv_b, ml_norm,
           w_branch_gla, w_branch_ml, w_out, norm_ffn, ffn_w_up, ffn_conv_w, ffn_conv_b, ffn_w_down, norm_final):
    f = lambda a: np.asarray(a, np.float32)
    return run_model(f(x), f(meta), DEPTH, w_in=f(w_in), b_in=f(b_in), gla_w_gate=f(gla_w_gate),
                     gla_b_gate=f(gla_b_gate), gla_norm=f(gla_norm), ml_conv_w=f(ml_conv_w), ml_conv_b=f(ml_conv_b),
                     ml_norm=f(ml_norm), w_branch_gla=f(w_branch_gla), w_branch_ml=f(w_branch_ml), w_out=f(w_out),
                     norm_mix=f(norm_mix), norm_ffn=f(norm_ffn), ffn_w_up=f(ffn_w_up), ffn_conv_w=f(ffn_conv_w),
                     ffn_conv_b=f(ffn_conv_b), ffn_w_down=f(ffn_w_down), norm_final=f(norm_final))
```

```python
import math
from contextlib import ExitStack

import numpy as np
import concourse.bass as bass
import concourse.mybir as mybir
from concourse.bass_utils import run_bass_kernel_spmd

F32 = mybir.dt.float32
BF16 = mybir.dt.bfloat16
AF = mybir.ActivationFunctionType
ALU = mybir.AluOpType

D = 1024
DEPTH = 4
SEQ = 4096
NMETA = 16
DFF = 2816
NFC = 44
EPS = 1e-6
LNS = math.log(128.0 ** -0.5)
TAU = 16.0

PP_NM, PP_NF, PP_BFM, PP_NBG, PP_GN, PP_MN, PP_CW, PP_CB, PP_FW, PP_FB, PP_BIF = (
    0, 8, 16, 64, 68, 70, 78, 110, 118, 250, 294)
PPL = 302

SLABS = [(8, 24)] + [(8, 512)] * 12 + [(8, 512)] * 4 + [(8, 512)] * 6 + [(8, 512)] * 11 + [(22, 128)] * 8
SLAB_OFF = np.concatenate([[0], np.cumsum([k * n for k, n in SLABS])]).astype(int)
PW = int(SLAB_OFF[-1])
NSLAB = len(SLABS)
SLAB_MAX = 4096


class Prog:
    ENGS = ("pe", "act", "dve", "pool", "sp")
    SEM_LIMIT = 30000

    def __init__(self):
        self.ops = {e: [] for e in self.ENGS}
        self.state = {}
        self.know = {e: {} for e in self.ENGS}
        self.snap = {e: [] for e in self.ENGS}
        self.dsnap = {}
        self.dma_cnt = {}
        self.ring_gen = {}

    def _learn(self, eng, dep):
        k = self.know[eng]
        if dep[0] == "e":
            _, e2, pos = dep
            sn = self.snap[e2][pos - 1]
        else:
            sn = self.dsnap[dep]
        for key, v in sn.items():
            if k.get(key, 0) < v:
                k[key] = v
        key = (dep[0], dep[1])
        if k.get(key, 0) < dep[2]:
            k[key] = dep[2]

    RINGS = ("f32r", "b16r", "smr", "ps")

    def _norm(self, toks):
        out = []
        for t in toks:
            if isinstance(t, tuple) and len(t) == 3 and t[0] in self.RINGS:
                assert self.ring_gen.get(t[:2], 0) == t[2], f"stale ring buffer use {t} (current gen {self.ring_gen.get(t[:2])})"
                t = t[:2]
            out.append(t)
        return out

    def op(self, eng, fn, reads=(), writes=(), dma=None):
        reads = self._norm(reads)
        writes = self._norm(writes)
        deps = []
        for t in reads:
            st = self.state.get(t)
            if st is not None and st[0] is not None:
                deps.append((st[0], True))
        for t in writes:
            st = self.state.get(t)
            if st is not None:
                if st[0] is not None:
                    deps.append((st[0], False))
                for r in st[1].values():
                    deps.append((r, False))
        mypos = len(self.ops[eng]) + 1
        waits = []
        know = self.know[eng]
        for dep, is_raw in deps:
            key = (dep[0], dep[1])
            if dep[0] == "e" and dep[1] == eng:
                if eng in ("pe", "sp"):
                    continue
                if not is_raw or mypos - dep[2] > 1:
                    continue
            if know.get(key, 0) >= dep[2]:
                continue
            waits.append(dep)
            self._learn(eng, dep)
        best = {}
        for dpp in waits:
            key = (dpp[0], dpp[1])
            if key not in best or best[key][2] < dpp[2]:
                best[key] = dpp
        waits = list(best.values())
        for dpp in waits:
            if dpp[0] == "e":
                self.ops[dpp[1]][dpp[2] - 1]["sig"] = True
        rec = {"fn": fn, "waits": waits, "sig": False, "dma": None}
        self.ops[eng].append(rec)
        self.snap[eng].append(dict(know))
        if dma is not None:
            cnt = self.dma_cnt.get(dma, 0) + 16
            self.dma_cnt[dma] = cnt
            me = ("d", dma, cnt)
            rec["dma"] = dma
            self.dsnap[me] = dict(know)
        else:
            me = ("e", eng, mypos)
        for t in reads:
            st = self.state.setdefault(t, [None, {}])
            st[1][(me[0], me[1])] = me
        for t in writes:
            self.state[t] = [me, {}]
        return me

    def emit(self, nc, es, final_waits):
        semmap = {}
        nsem = {}
        for e in self.ENGS:
            cnt = 0
            idx = 0
            m = []
            for rec in self.ops[e]:
                if rec["sig"]:
                    if cnt >= self.SEM_LIMIT:
                        idx += 1
                        cnt = 0
                    cnt += 1
                m.append((idx, cnt))
            semmap[e] = m
            nsem[e] = idx + 1
        esems = {e: [es.enter_context(nc.semaphore(f"s_{e}{i}")) for i in range(nsem[e])] for e in self.ENGS}
        dsems = {k: es.enter_context(nc.semaphore(f"d_{k}")) for k in self.dma_cnt}
        block = es.enter_context(nc.Block())

        def run(engname, eh):
            m = semmap[engname]
            for i, rec in enumerate(self.ops[engname]):
                for dpp in rec["waits"]:
                    if dpp[0] == "e":
                        si, sv = semmap[dpp[1]][dpp[2] - 1]
                        eh.wait_ge(esems[dpp[1]][si], sv)
                    else:
                        eh.wait_ge(dsems[dpp[1]], dpp[2])
                ins = rec["fn"](eh)
                if rec["dma"] is not None:
                    ins.then_inc(dsems[rec["dma"]], 16)
                elif rec["sig"]:
                    ins.then_inc(esems[engname][m[i][0]], 1)
            for dpp in final_waits.get(engname, []):
                eh.wait_ge(dsems[dpp[1]], dpp[2])

        @block.tensor
        def _(t):
            run("pe", t)

        @block.scalar
        def _(a):
            run("act", a)

        @block.vector
        def _(v):
            run("dve", v)

        @block.gpsimd
        def _(g):
            run("pool", g)

        @block.sync
        def _(s):
            run("sp", s)


DEBUG = {"on": False, "names": []}


def build_program(depth, tiles, seq):
    nc = bass.Bass("TRN2", target_bir_lowering=False)
    DEBUG["names"] = []
    xT_d = nc.dram_tensor("xT", [D, seq], F32, kind="ExternalInput").ap()
    metaT_d = nc.dram_tensor("metaT", [D, NMETA], F32, kind="ExternalInput").ap()
    w32_d = nc.dram_tensor("w32", [depth, 128, PW], F32, kind="ExternalInput").ap()
    pp_d = nc.dram_tensor("pp", [128, depth * PPL + 8], F32, kind="ExternalInput").ap()
    p16_d = nc.dram_tensor("p16", [16, depth * 512 + depth], F32, kind="ExternalInput").ap()
    prow_d = nc.dram_tensor("prow", [4, 2048], F32, kind="ExternalInput").ap()
    hc_d = nc.dram_tensor("hc", [128, 768], F32, kind="ExternalInput").ap()
    outT_d = nc.dram_tensor("outT", [D, seq], F32, kind="ExternalOutput").ap()
    w16_d = nc.dram_tensor("w16", [depth, 128, PW], BF16, kind="Internal").ap()

    P = Prog()
    es = ExitStack()
    TM = 512

    def sb(name, shape, dt):
        return es.enter_context(nc.sbuf_tensor("sb_" + name, shape, dt))

    ident_bf = sb("ident_bf", [128, 128], BF16)
    ones_bf = sb("ones_bf", [128, 128], BF16)
    triu_bf = sb("triu_bf", [128, 128], BF16)
    triu_f = sb("triu_f", [128, 128], F32)
    ones_f = sb("ones_f", [128, 512], F32)
    pp = sb("pp", [128, depth * PPL + 8], F32)
    nbg = sb("nbg", [128, depth * 4], F32)
    wg_bf = sb("wg_bf", [16, depth * 512], BF16)
    blr = sb("blr", [16, depth], F32)
    brow_bf = sb("brow_bf", [4, 2048], BF16)
    sel_bf = sb("sel_bf", [4, 512], BF16)
    hT = sb("hT", [128, 8, TM], F32)
    hTm = sb("hTm", [128, 8, NMETA], F32)
    NAR = 48
    arena = sb("arena", [128, NAR, TM], BF16)
    AR_XN, AR_SGA, AR_SR, AR_SO, AR_SGB, AR_QE, AR_KD = 0, 8, 16, 24, 32, 40, 44
    AR_AT = 16
    mqk_pre = sb("mqk_pre", [128, 8, TM + 3], BF16)
    NSB = 3
    slabbuf = [sb(f"slab{i}", [128, SLAB_MAX], BF16) for i in range(NSB)]
    S_st = [sb(f"S{l}", [128, 4, 256], F32) for l in range(depth)]
    C_st = [sb(f"C{l}", [128, 4, 256], F32) for l in range(depth)]
    n_st = [sb(f"n{l}", [128, 4], F32) for l in range(depth)]
    qk_hist = [sb(f"qkh{l}", [128, 8, 3], BF16) for l in range(depth)]
    u_hist = [sb(f"uh{l}", [128, 22, 2, 2], BF16) for l in range(depth)]
    S_bf = sb("S_bf", [128, 4, 256], BF16)
    C_bf = sb("C_bf", [128, 4, 256], BF16)
    nb_bf = sb("nb_bf", [128, 512], BF16)
    gv = sb("gv", [128, 4, 1024], BF16)
    mv = sb("mv", [128, 4, 1024], BF16)
    NF32 = 8
    f32r = [sb(f"f32r{i}", [128, 512], F32) for i in range(NF32)]
    NB16 = 8
    b16r = [sb(f"b16r{i}", [128, 512], BF16) for i in range(NB16)]
    ubuf = [sb(f"ubuf{i}", [128, 2, TM + 2], BF16) for i in range(2)]
    ifg = sb("ifg", [128, 4, 8], F32)
    lfn = sb("lfn", [128, 4, 4], F32)
    iftmp = sb("iftmp", [128, 4, 4], F32)
    A_t = sb("A_t", [128, 4, 4], F32)
    NSM = 8
    smr = [sb(f"smr{i}", [128, 4], F32) for i in range(NSM)]
    lrT = sb("lrT", [16, TM], BF16)
    rstd = sb("rstd", [128, TM], F32)
    NPS = 8
    psb = [es.enter_context(nc.psum_tensor(f"ps{i}", [128, 512], F32)) for i in range(NPS)]

    ctr = {"ps": 0, "f32": 0, "b16": 0, "sm": 0, "tp": 0, "ub": 0, "slab": 0}

    def _bump(name, i):
        g = P.ring_gen.get((name, i), 0) + 1
        P.ring_gen[(name, i)] = g
        return (name, i, g)

    def ps_next():
        i = ctr["ps"] % NPS
        ctr["ps"] += 1
        return psb[i], _bump("ps", i)

    def f32_next():
        i = ctr["f32"] % NF32
        ctr["f32"] += 1
        return f32r[i], _bump("f32r", i)

    def b16_next():
        i = ctr["b16"] % NB16
        ctr["b16"] += 1
        return b16r[i], _bump("b16r", i)

    def sm_next():
        i = ctr["sm"] % NSM
        ctr["sm"] += 1
        return smr[i], _bump("smr", i)

    def ar(i):
        return ("ar", i)

    def mm(out, lhsT, rhs, start, stop, reads, writes):
        P.op("pe", lambda e: e.matmul(out, lhsT=lhsT, rhs=rhs, start=start, stop=stop), reads, writes)

    def tpose(out, in_, reads, writes):
        P.op("pe", lambda e: e.transpose(out, in_, ident_bf[:, :]), reads, writes)

    def act(out, in_, func, reads, writes, bias=None, scale=None):
        kw = {}
        if bias is not None:
            kw["bias"] = bias
        if scale is not None:
            kw["scale"] = scale
        P.op("act", lambda e: e.activation(out, in_, func, **kw), reads, writes)

    def tt(eng, out, in0, in1, op, reads, writes):
        P.op(eng, lambda e: e.tensor_tensor(out=out, in0=in0, in1=in1, op=op), reads, writes)

    def ts(eng, out, in0, s1, op0, reads, writes, s2=None, op1=None):
        if op1 is None:
            P.op(eng, lambda e: e.tensor_scalar(out=out, in0=in0, scalar1=s1, scalar2=None, op0=op0), reads, writes)
        else:
            P.op(eng, lambda e: e.tensor_scalar(out=out, in0=in0, scalar1=s1, scalar2=s2, op0=op0, op1=op1), reads, writes)

    def stt(out, in0, scalar, in1, op0, op1, reads, writes):
        P.op("dve", lambda e: e.scalar_tensor_tensor(out=out, in0=in0, scalar=scalar, in1=in1, op0=op0, op1=op1),
             reads, writes)

    def cp(eng, out, in_, reads, writes):
        if eng == "act":
            P.op(eng, lambda e: e.activation(out, in_, AF.Copy), reads, writes)
        else:
            P.op(eng, lambda e: e.tensor_copy(out=out, in_=in_), reads, writes)

    def scan(out, d0, d1, reads, writes):
        P.op("dve", lambda e: e.tensor_tensor_scan(out=out, data0=d0, data1=d1, initial=0.0, op0=ALU.mult,
                                                   op1=ALU.add), reads, writes)

    def recip(out, in_, reads, writes):
        P.op("dve", lambda e: e.reciprocal(out=out, in_=in_), reads, writes)

    def mset(eng, ap, val, writes):
        P.op(eng, lambda e: e.memset(ap, val), (), writes)

    def dma(eng, out, in_, reads, writes, sem, slow=False):
        if slow:
            return P.op(eng, lambda e: e.dma_start(out=out, in_=in_, allow_slow_non_contiguous=True), reads, writes,
                        dma=sem)
        return P.op(eng, lambda e: e.dma_start(out=out, in_=in_), reads, writes, dma=sem)

    def dbg(name, ap, reads):
        if not DEBUG["on"]:
            return
        shp = list(ap.shape)
        d = nc.dram_tensor("dbg_" + name, shp, ap.dtype, kind="ExternalOutput").ap()
        DEBUG["names"].append("dbg_" + name)
        final_dbg.append(dma("act", d, ap, reads, (), f"dbg{len(final_dbg)}", slow=True))

    final_dbg = []

    def v4(ap, n):
        return ap[:, :].rearrange("p (h t) -> p h t", h=4)[:, :, :n]

    W_PIECE = 8192
    GRP_END = [17, 23, NSLAB]

    def slab_grp(si):
        return 0 if si < 17 else (1 if si < 23 else 2)

    for l in range(depth):
        g0 = 0
        for g in range(3):
            a = int(SLAB_OFF[g0])
            end = int(SLAB_OFF[GRP_END[g]])
            while a < end:
                b = min(end, a + W_PIECE)
                dma("pool", w16_d[l, :, a:b], w32_d[l, :, a:b], (), [("w16", l, g)], f"pre{l}_{g}")
                a = b
            g0 = GRP_END[g]
    cst_toks = ["pp", "triu_f", "blr"] + [("hT", c) for c in range(6)] + [("f32r", l) for l in range(depth)]
    dma("act", pp[:, :], pp_d, (), ["pp"], "cst")
    dma("act", triu_f[:, :], hc_d[:, 128:256], (), ["triu_f"], "cst")
    dma("act", blr[:, :], p16_d[:, depth * 512:depth * 512 + depth], (), ["blr"], "cst", slow=True)
    dma("act", hT[:, 0, 0:128], hc_d[:, 0:128], (), [("hT", 0)], "cst")
    dma("act", hT[0:4, 1, 0:512], hc_d[0:4, 256:768], (), [("hT", 1)], "cst")
    for q in range(4):
        dma("act", hT[0:4, 2 + q, 0:512], prow_d[:, q * 512:(q + 1) * 512], (), [("hT", 2 + q)], "cst")
    for l in range(depth):
        last_cst = dma("act", f32r[l][0:16, 0:512], p16_d[:, l * 512:l * 512 + 512], (), [("f32r", l)], "cst")
    for tk in cst_toks:
        P.state[tk][0] = last_cst
    mset("dve", ones_bf[:, :], 1.0, ["ones_bf"])
    mset("dve", ones_f[:, :], 1.0, ["ones_f"])
    cp("dve", ident_bf[:, :], hT[:, 0, 0:128], [("hT", 0)], ["ident_bf"])
    cp("dve", triu_bf[:, :], triu_f[:, :], ["triu_f"], ["triu_bf"])
    cp("dve", sel_bf[:, :], hT[0:4, 1, 0:512], [("hT", 1)], ["sel_bf"])
    for q in range(4):
        cp("dve", brow_bf[:, q * 512:(q + 1) * 512], hT[0:4, 2 + q, 0:512], [("hT", 2 + q)], ["brow_bf"])
    for l in range(depth):
        cp("dve", wg_bf[:, l * 512:(l + 1) * 512], f32r[l][0:16, 0:512], [("f32r", l)], ["wg_bf"])
        ts("dve", nbg[:, l * 4:(l + 1) * 4], pp[:, l * PPL + PP_NBG:l * PPL + PP_NBG + 4], -1.0, ALU.mult,
           ["pp"], ["nbg"])
    for l in range(depth):
        for h in range(4):
            mset("dve", S_st[l][:, h, :], 0.0, [("S", l, h)])
            mset("dve", C_st[l][:, h, :], 0.0, [("C", l, h)])
        mset("dve", n_st[l][:, :], 0.0, [("n", l)])
        mset("dve", qk_hist[l][:, :, :], 0.0, [("qkh", l)])
        mset("dve", u_hist[l][:, :, :, :], 0.0, [("uh", l)])

    slab_seq = {"n": 0}

    def next_slab(l, s):
        i = slab_seq["n"] % NSB
        slab_seq["n"] += 1
        kc, ncols = SLABS[s]
        n = kc * ncols
        off = int(SLAB_OFF[s])
        dma("sp", slabbuf[i][:, 0:n], w16_d[l, :, off:off + n], [("w16", l, slab_grp(s))], [("slab", i)], f"sl{i}")
        return slabbuf[i][:, 0:n].rearrange("p (k n) -> p k n", k=kc), ("slab", i)

    def rms_to(l_gcol, T, dst_fn, H, hn):
        for c in range(8):
            act(arena[:, AR_SGA + c, :T], H[:, c, :T], AF.Square, [(hn, c)], [ar(AR_SGA + c)])
        ss, sst = ps_next()
        for c in range(8):
            mm(ss[:, :T], ones_bf[:, :], arena[:, AR_SGA + c, :T], c == 0, c == 7, ["ones_bf", ar(AR_SGA + c)], [sst])
        lnv, lnt = f32_next()
        act(lnv[:, :T], ss[:, :T], AF.Ln, [sst], [lnt], bias=EPS, scale=1.0 / D)
        act(rstd[:, :T], lnv[:, :T], AF.Exp, [lnt], ["rstd"], scale=-0.5)
        for c in range(8):
            o, otk = dst_fn(c)
            stt(o, H[:, c, :T], pp[:, l_gcol + c:l_gcol + c + 1], rstd[:, :T], ALU.mult, ALU.mult,
                [(hn, c), "pp", "rstd"], otk)

    def tile_layer(l, T, cs, nch, ti, H, hn):
        pb = l * PPL
        PL = "dve" if ti < 2 else "pool"
        tag = f"t{ti}l{l}"
        ALLAR = [ar(i) for i in range(NAR)]
        xn = lambda c: arena[:, AR_XN + c, :T]
        rms_to(pb + PP_NM, T, lambda c: (xn(c), [ar(AR_XN + c)]), H, hn)
        XN = [ar(AR_XN + c) for c in range(8)]
        dbg("xn_" + tag, arena[:, AR_XN:AR_XN + 8, :T], XN)

        sl, slt = next_slab(l, 0)
        ps, pst_ = ps_next()
        for kc in range(8):
            mm(ps[:16, :T], sl[:, kc, 0:16], xn(kc), kc == 0, kc == 7, [slt, ar(AR_XN + kc)], [pst_])
        act(lrT[:, :T], ps[:16, :T], AF.Identity, [pst_, "blr"], ["lrT"], bias=blr[:, l:l + 1])
        ps2, ps2t = ps_next()
        for c in range(nch):
            for kc in range(8):
                mm(ps2[:cs, c * 8:(c + 1) * 8], arena[:, AR_XN + kc, c * cs:(c + 1) * cs], sl[:, kc, 16:24],
                   kc == 0, kc == 7, [slt, ar(AR_XN + kc)], [ps2t])
        tt("dve", ifg[:cs, :nch, :], ps2[:cs, 0:nch * 8].rearrange("p (c e) -> p c e", e=8),
           pp[:cs, pb + PP_BIF:pb + PP_BIF + 8].unsqueeze(1).to_broadcast([cs, nch, 8]), ALU.add,
           [ps2t, "pp"], ["ifg"])
        act(iftmp[:cs, :nch, :], ifg[:cs, :nch, 4:8], AF.Exp, ["ifg"], ["iftmp"], scale=-1.0)
        act(lfn[:cs, :nch, :], iftmp[:cs, :nch, :], AF.Ln, ["iftmp"], ["lfn"], bias=1.0)

        E_h = {}

        def gla_gates(h):
            gps, gpt = ps_next()
            mm(gps[:, :T], wg_bf[:, l * 512 + h * 128:l * 512 + (h + 1) * 128], lrT[:, :T], True, True,
               ["wg_bf", "lrT"], [gpt])
            e1, e1t = f32_next()
            act(e1[:, :T], gps[:, :T], AF.Exp, [gpt, "nbg"], [e1t], bias=nbg[:, l * 4 + h:l * 4 + h + 1], scale=-1.0)
            l1, l1t = f32_next()
            act(l1[:, :T], e1[:, :T], AF.Ln, [e1t], [l1t], bias=1.0)
            Lc, Lct = f32_next()
            for c in range(nch):
                scan(Lc[:, c * cs:(c + 1) * cs], ones_f[:, :cs], l1[:, c * cs:(c + 1) * cs], [l1t, "ones_f"], [Lct])
            act(A_t[:, h, :nch], Lc[:, :T].rearrange("p (c s) -> p c s", s=cs)[:, :, cs - 1], AF.Exp,
                [Lct], [("A", h)], scale=-1.0 / TAU)
            Eb, Ebt = f32_next()
            act(Eb[:, :T], Lc[:, :T], AF.Exp, [Lct], [Ebt], scale=-1.0 / TAU, bias=LNS)
            En, Ent = f32_next()
            act(En[:, :T], Lc[:, :T], AF.Exp, [Lct], [Ent], scale=1.0 / TAU)
            E_h[h] = (Eb, Ebt, En, Ent)

        def mlstm_conv():
            cp(PL, mqk_pre[:, :, 0:3], qk_hist[l][:, :, :], [("qkh", l)], [("mqp", j) for j in range(8)])
            cp(PL, qk_hist[l][:, :, :], mqk_pre[:, :, T:T + 3], [("mqp", j) for j in range(8)], [("qkh", l)])
            for j0 in range(0, 8, 2):
                accs = []
                for j in (j0, j0 + 1):
                    a_, at = f32_next()
                    accs.append((a_, at))
                    act(a_[:, :T], mqk_pre[:, j, 0:T], AF.Identity, [("mqp", j), "pp"], [at],
                        scale=pp[:, pb + PP_CW + j:pb + PP_CW + j + 1])
                for tap in (1, 2, 3):
                    for k_, j in enumerate((j0, j0 + 1)):
                        a_, at = accs[k_]
                        stt(a_[:, :T], mqk_pre[:, j, tap:tap + T],
                            pp[:, pb + PP_CW + tap * 8 + j:pb + PP_CW + tap * 8 + j + 1], a_[:, :T], ALU.mult, ALU.add,
                            [("mqp", j), "pp", at], [at])
                for k_, j in enumerate((j0, j0 + 1)):
                    a_, at = accs[k_]
                    act(mqk_pre[:, j, 3:3 + T], a_[:, :T], AF.Silu, [at, "pp", ("mqp", j)], [("mqp", j)],
                        bias=pp[:, pb + PP_CB + j:pb + PP_CB + j + 1])

        fc = 0
        for s in range(1, 13):
            sl, slt = next_slab(l, s)
            for j in range(4):
                if fc < 8 and fc % 2 == 0:
                    gla_gates(fc // 2)
                ps, pt = ps_next()
                for kc in range(8):
                    mm(ps[:, :T], sl[:, kc, j * 128:(j + 1) * 128], xn(kc), kc == 0, kc == 7,
                       [slt, ar(AR_XN + kc)], [pt])
                bcol = pp[:, pb + PP_BFM + fc:pb + PP_BFM + fc + 1]
                if fc < 8:
                    h, isk = fc // 2, fc % 2
                    Eb, Ebt, En, Ent = E_h[h]
                    if not isk:
                        stt(arena[:, AR_QE + h, :T], ps[:, :T], bcol, Eb[:, :T], ALU.add, ALU.mult,
                            [pt, "pp", Ebt], [ar(AR_QE + h)])
                    else:
                        stt(arena[:, AR_KD + h, :T], ps[:, :T], bcol, En[:, :T], ALU.add, ALU.mult,
                            [pt, "pp", Ent], [ar(AR_KD + h)])
                elif fc < 16:
                    jj = fc - 8
                    act(arena[:, AR_SR + jj, :T], ps[:, :T], AF.Silu, [pt, "pp"], [ar(AR_SR + jj)], bias=bcol)
                    act(arena[:, AR_SR + jj, :T], arena[:, AR_SR + jj, :T], AF.Identity, [ar(AR_SR + jj), "pp"],
                        [ar(AR_SR + jj)], scale=pp[:, pb + PP_GN + jj % 2:pb + PP_GN + jj % 2 + 1])
                elif fc < 24:
                    jj = fc - 16
                    act(mqk_pre[:, jj, 3:3 + T], ps[:, :T], AF.Identity, [pt, "pp"], [("mqp", jj)], bias=bcol)
                elif fc < 32:
                    jj = fc - 24
                    act(arena[:, AR_SO + jj, :T], ps[:, :T], AF.Sigmoid, [pt, "pp"], [ar(AR_SO + jj)], bias=bcol)
                    act(arena[:, AR_SO + jj, :T], arena[:, AR_SO + jj, :T], AF.Identity, [ar(AR_SO + jj), "pp"],
                        [ar(AR_SO + jj)], scale=pp[:, pb + PP_MN + jj:pb + PP_MN + jj + 1])
                elif fc < 40:
                    jj = fc - 32
                    act(arena[:, AR_SGA + jj, :T], ps[:, :T], AF.Sigmoid, [pt, "pp"], [ar(AR_SGA + jj)], bias=bcol)
                else:
                    jj = fc - 40
                    act(arena[:, AR_SGB + jj, :T], ps[:, :T], AF.Sigmoid, [pt, "pp"], [ar(AR_SGB + jj)], bias=bcol)
                fc += 1
                if fc == 24:
                    mlstm_conv()

        if ti == 0 and l == 0:
            dbg("sel", sel_bf[:, :], ["sel_bf"])
            dbg("brow", brow_bf[:, :], ["brow_bf"])
            dbg("ident", ident_bf[:, :], ["ident_bf"])
            dbg("triu", triu_bf[:, :], ["triu_bf"])
        dbg("lfn_" + tag, lfn[:cs, :nch, :], ["lfn"])
        dbg("ifg_" + tag, ifg[:cs, :nch, :], ["ifg"])
        dbg("A_" + tag, A_t[:, :, :nch], [("A", h) for h in range(4)])
        dbg("fm_" + tag, arena[:, 8:NAR, :T], ALLAR)
        for q in range(4):
            sl, slt = next_slab(l, 13 + q)
            dst = gv if q < 2 else mv
            dname = "gv" if q < 2 else "mv"
            half = q % 2
            for c in range(nch):
                ps, pt = ps_next()
                for kc in range(8):
                    mm(ps[:cs, :], arena[:, AR_XN + kc, c * cs:(c + 1) * cs], sl[:, kc, :], kc == 0, False,
                       [slt, ar(AR_XN + kc)], [pt])
                mm(ps[:cs, :], sel_bf[:, l * 128:l * 128 + cs], brow_bf[:, q * 512:(q + 1) * 512], False, True,
                   ["sel_bf", "brow_bf"], [pt])
                cp("dve", dst[:cs, c, half * 512:(half + 1) * 512], ps[:cs, :], [pt], [(dname, c)])

        mqa = lambda h, a, b: mqk_pre[:, h, 3 + a:3 + b]
        mka = lambda h, a, b: mqk_pre[:, 4 + h, 3 + a:3 + b]
        dbg("gv_" + tag, gv[:cs, :nch, :], [("gv", c) for c in range(nch)])
        dbg("mv_" + tag, mv[:cs, :nch, :], [("mv", c) for c in range(nch)])
        dbg("mqk_" + tag, mqk_pre[:, :, 3:3 + T], [("mqp", j) for j in range(8)])

        cp("act", S_bf[:, :, :], S_st[l][:, :, :], [("S", l, h) for h in range(4)], [("Sbf", h) for h in range(4)])
        cp("act", C_bf[:, :, :], C_st[l][:, :, :], [("C", l, h) for h in range(4)], [("Cbf", h) for h in range(4)])
        tt(PL, nb_bf[:, :].rearrange("p (h m) -> p h m", h=4), ones_f[:, :].rearrange("p (h m) -> p h m", h=4),
           n_st[l][:, :].unsqueeze(2).to_broadcast([128, 4, 128]), ALU.mult, [("n", l), "ones_f"], ["nb_bf"])

        def head_rms_apply(o_ps, o_pt, src_f32, dst_base, cols):
            sq = []
            for b in range(2):
                q_, qt = b16_next()
                src = v4(o_ps[b], cs) if src_f32 is None else v4(src_f32[b][0], cs)
                srct = o_pt[b] if src_f32 is None else src_f32[b][1]
                act(v4(q_, cs), src, AF.Square, [srct], [qt])
                sq.append((q_, qt))
            ss, sst = ps_next()
            for h in range(4):
                for vc in range(2):
                    blk = (h % 2) * 2 + vc
                    mm(ss[:, h * 128:h * 128 + cs], ones_bf[:, :], sq[h // 2][0][:, blk * 128:blk * 128 + cs],
                       vc == 0, vc == 1, ["ones_bf", sq[h // 2][1]], [sst])
            lnv, lnt = f32_next()
            act(v4(lnv, cs), v4(ss, cs), AF.Ln, [sst], [lnt], bias=EPS, scale=1.0 / 256.0)
            rs, rst = f32_next()
            act(v4(rs, cs), v4(lnv, cs), AF.Exp, [lnt], [rst], scale=-0.5)
            for b in range(2):
                o1, o1t = f32_next()
                src = o_ps[b] if src_f32 is None else src_f32[b][0]
                srct = o_pt[b] if src_f32 is None else src_f32[b][1]
                tt("dve", o1[:, :].rearrange("p (h v t) -> p h v t", h=2, v=2)[:, :, :, :cs],
                   src[:, :].rearrange("p (h v t) -> p h v t", h=2, v=2)[:, :, :, :cs],
                   rs[:, :].rearrange("p (h t) -> p h t", h=4)[:, 2 * b:2 * b + 2, :cs].unsqueeze(2).to_broadcast(
                       [128, 2, 2, cs]), ALU.mult, [srct, rst], [o1t])
                dsta = arena[:, dst_base + 4 * b:dst_base + 4 * b + 4, cols[0]:cols[1]]
                tt(PL, dsta, v4(o1, cs), dsta, ALU.mult, [o1t] + [ar(dst_base + 4 * b + i) for i in range(4)],
                   [ar(dst_base + 4 * b + i) for i in range(4)])

        for c in range(nch):
            c0, c1 = c * cs, (c + 1) * cs
            last = (c == nch - 1)
            sc, sct = ps_next()
            for h in range(4):
                mm(sc[:cs, h * 128:h * 128 + cs], arena[:, AR_KD + h, c0:c1], arena[:, AR_QE + h, c0:c1], True, True,
                   [ar(AR_KD + h), ar(AR_QE + h)], [sct])
            scm, scmt = b16_next()
            tt("dve", v4(scm[:cs, :], cs), v4(sc[:cs, :], cs), triu_bf[:cs, :cs].unsqueeze(1).to_broadcast([cs, 4, cs]),
               ALU.mult, [sct, "triu_bf"], [scmt])
            kdl, kdlt = b16_next()
            tt(PL, v4(kdl, cs), arena[:, AR_KD:AR_KD + 4, c0:c1], A_t[:, :, c:c + 1].to_broadcast([128, 4, cs]),
               ALU.mult, [ar(AR_KD + h) for h in range(4)] + [("A", h) for h in range(4)], [kdlt])
            ktp, ktpt = ps_next()
            for h in range(4):
                mm(ktp[:cs, h * 128:(h + 1) * 128], kdl[:, h * 128:h * 128 + cs], ident_bf[:, :], True, True,
                   [kdlt, "ident_bf"], [ktpt])
            kT, kTt = b16_next()
            cp("act", kT[:cs, :], ktp[:cs, :], [ktpt], [kTt])
            o_ps, o_pt = [], []
            for b in range(2):
                p_, t_ = ps_next()
                o_ps.append(p_)
                o_pt.append(t_)
            for h in range(4):
                for vc in range(2):
                    blk = (h % 2) * 2 + vc
                    dst = o_ps[h // 2][:, blk * 128:blk * 128 + cs]
                    mm(dst, S_bf[:, h, vc * 128:(vc + 1) * 128], arena[:, AR_QE + h, c0:c1], True, False,
                       [("Sbf", h), ar(AR_QE + h)], [o_pt[h // 2]])
                    mm(dst, gv[:cs, c, h * 256 + vc * 128:h * 256 + (vc + 1) * 128], scm[:cs, h * 128:h * 128 + cs],
                       False, True, [("gv", c), scmt], [o_pt[h // 2]])
            head_rms_apply(o_ps, o_pt, None, AR_SR, (c0, c1))
            P_ps, P_pt = [], []
            for b in range(2):
                p_, t_ = ps_next()
                P_ps.append(p_)
                P_pt.append(t_)
            for h in range(4):
                mm(P_ps[h // 2][:, (h % 2) * 256:(h % 2) * 256 + 256], kT[:cs, h * 128:(h + 1) * 128],
                   gv[:cs, c, h * 256:(h + 1) * 256], True, True, [kTt, ("gv", c)], [P_pt[h // 2]])
            for h in range(4):
                stt(S_st[l][:, h, :], S_st[l][:, h, :], A_t[:, h, c:c + 1], P_ps[h // 2][:, (h % 2) * 256:(h % 2) * 256 + 256],
                    ALU.mult, ALU.add, [("S", l, h), ("A", h), P_pt[h // 2]], [("S", l, h)])
            if not last:
                for h in range(4):
                    cp("act", S_bf[:, h, :], S_st[l][:, h, :], [("S", l, h)], [("Sbf", h)])

            lfb, lfbt = f32_next()
            tt("dve", v4(lfb[:cs, :], 128), v4(ones_f[:cs, :], 128), lfn[:cs, c, :].unsqueeze(2).to_broadcast([cs, 4, 128]),
               ALU.mult, ["ones_f", "lfn"], [lfbt])
            fb, fbt = ps_next()
            for h in range(4):
                mm(fb[:, h * 128:h * 128 + cs], lfb[:cs, h * 128:(h + 1) * 128], triu_f[:cs, :cs], True, True,
                   [lfbt, "triu_f"], [fbt])
            fcp, fcpt = ps_next()
            mm(fcp[:cs, 0:4], triu_f[:cs, :cs], lfn[:cs, c, :], True, True, ["triu_f", "lfn"], [fcpt])
            db, dbt = sm_next()
            stt(db[:cs, :], fcp[:cs, 0:4], LNS, ifg[:cs, c, 0:4], ALU.add, ALU.add, [fcpt, "ifg"], [dbt])
            Dm, Dmt = f32_next()
            for h in range(4):
                act(Dm[:cs, h * 128:h * 128 + cs], fb[:cs, h * 128:h * 128 + cs], AF.Exp, [fbt, dbt], [Dmt],
                    bias=db[:cs, h:h + 1], scale=-1.0)
            tt(PL, v4(Dm[:cs, :], cs), v4(Dm[:cs, :], cs), triu_f[:cs, :cs].unsqueeze(1).to_broadcast([cs, 4, cs]),
               ALU.mult, [Dmt, "triu_f"], [Dmt])
            sc, sct = ps_next()
            for h in range(4):
                mm(sc[:cs, h * 128:h * 128 + cs], mka(h, c0, c1), mqa(h, c0, c1), True, True,
                   [("mqp", 4 + h), ("mqp", h)], [sct])
            qkD, qkDt = b16_next()
            tt("dve", v4(qkD[:cs, :], cs), v4(sc[:cs, :], cs), v4(Dm[:cs, :], cs), ALU.mult, [sct, Dmt], [qkDt])
            EF, EFt = f32_next()
            act(v4(EF, cs), v4(fb, cs), AF.Exp, [fbt], [EFt], bias=LNS, scale=-1.0)
            qf, qft = b16_next()
            tt(PL, v4(qf, cs), mqk_pre[:, 0:4, 3 + c0:3 + c1], v4(EF, cs), ALU.mult,
               [("mqp", h) for h in range(4)] + [EFt], [qft])
            n_ps, n_pt = [], []
            for b in range(2):
                p_, t_ = ps_next()
                n_ps.append(p_)
                n_pt.append(t_)
            for h in range(4):
                for vc in range(2):
                    blk = (h % 2) * 2 + vc
                    dst = n_ps[h // 2][:, blk * 128:blk * 128 + cs]
                    mm(dst, C_bf[:, h, vc * 128:(vc + 1) * 128], qf[:, h * 128:h * 128 + cs], True, False,
                       [("Cbf", h), qft], [n_pt[h // 2]])
                    mm(dst, mv[:cs, c, h * 256 + vc * 128:h * 256 + (vc + 1) * 128], qkD[:cs, h * 128:h * 128 + cs],
                       False, True, [("mv", c), qkDt], [n_pt[h // 2]])
            den, dent = ps_next()
            for h in range(4):
                mm(den[:, h * 128:h * 128 + cs], nb_bf[:, h * 128:(h + 1) * 128], qf[:, h * 128:h * 128 + cs], True, False,
                   ["nb_bf", qft], [dent])
                mm(den[:, h * 128:h * 128 + cs], ones_bf[:cs, :], qkD[:cs, h * 128:h * 128 + cs], False, True,
                   ["ones_bf", qkDt], [dent])
            d1, d1t = f32_next()
            d0, d0t = f32_next()
            act(v4(d0, cs), v4(den, cs), AF.Abs, [dent], [d0t])
            ts("dve", v4(d1, cs), v4(d0, cs), 1.0, ALU.max, [d0t], [d1t])
            wd, wdt = sm_next()
            tt("dve", wd[:cs, :], db[:cs, :], fb[:cs, :].rearrange("p (h t) -> p h t", h=4)[:, :, cs - 1], ALU.subtract,
               [dbt, fbt], [wdt])
            Gd, Gdt = sm_next()
            act(Gd[:, :], fb[:, :].rearrange("p (h t) -> p h t", h=4)[:, :, cs - 1], AF.Exp, [fbt], [Gdt], scale=-1.0)
            rden, rdent = f32_next()
            recip(v4(rden, cs), v4(d1, cs), [d1t], [rdent])
            hbuf = []
            for b in range(2):
                hb, hbt = f32_next()
                tt("dve", hb[:, :].rearrange("p (h v t) -> p h v t", h=2, v=2)[:, :, :, :cs],
                   n_ps[b][:, :].rearrange("p (h v t) -> p h v t", h=2, v=2)[:, :, :, :cs],
                   rden[:, :].rearrange("p (h t) -> p h t", h=4)[:, 2 * b:2 * b + 2, :cs].unsqueeze(2).to_broadcast(
                       [128, 2, 2, cs]), ALU.mult, [n_pt[b], rdent], [hbt])
                hbuf.append((hb, hbt))
            head_rms_apply(None, None, hbuf, AR_SO, (c0, c1))
            wst, wstt = sm_next()
            act(wst[:cs, :], wd[:cs, :], AF.Exp, [wdt], [wstt], bias=-LNS)
            ktp, ktpt = ps_next()
            for h in range(4):
                mm(ktp[:cs, h * 128:(h + 1) * 128], mka(h, c0, c1), ident_bf[:, :], True, True,
                   [("mqp", 4 + h), "ident_bf"], [ktpt])
            kw, kwt = b16_next()
            tt("dve", v4(kw[:cs, :], 128), v4(ktp[:cs, :], 128), wst[:cs, :].unsqueeze(2).to_broadcast([cs, 4, 128]),
               ALU.mult, [ktpt, wstt], [kwt])
            P_ps, P_pt = [], []
            for b in range(2):
                p_, t_ = ps_next()
                P_ps.append(p_)
                P_pt.append(t_)
            for h in range(4):
                mm(P_ps[h // 2][:, (h % 2) * 256:(h % 2) * 256 + 256], kw[:cs, h * 128:(h + 1) * 128],
                   mv[:cs, c, h * 256:(h + 1) * 256], True, True, [kwt, ("mv", c)], [P_pt[h // 2]])
            npp, nppt = ps_next()
            for h in range(4):
                mm(npp[:, h:h + 1], kw[:cs, h * 128:(h + 1) * 128], ones_bf[:cs, 0:1], True, True, [kwt, "ones_bf"], [nppt])
            for h in range(4):
                stt(C_st[l][:, h, :], C_st[l][:, h, :], Gd[:, h:h + 1], P_ps[h // 2][:, (h % 2) * 256:(h % 2) * 256 + 256],
                    ALU.mult, ALU.add, [("C", l, h), Gdt, P_pt[h // 2]], [("C", l, h)])
            ntmp, ntt = sm_next()
            tt("dve", ntmp[:, :], n_st[l][:, :], Gd[:, :], ALU.mult, [("n", l), Gdt], [ntt])
            tt("dve", n_st[l][:, :], npp[:, 0:4], ntmp[:, :], ALU.add, [nppt, ntt], [("n", l)])
            if not last:
                for h in range(4):
                    cp("act", C_bf[:, h, :], C_st[l][:, h, :], [("C", l, h)], [("Cbf", h)])
                tt(PL, nb_bf[:, :].rearrange("p (h m) -> p h m", h=4), ones_f[:, :].rearrange("p (h m) -> p h m", h=4),
                   n_st[l][:, :].unsqueeze(2).to_broadcast([128, 4, 128]), ALU.mult, [("n", l), "ones_f"], ["nb_bf"])

        dbg("og_" + tag, arena[:, AR_SR:AR_SR + 8, :T], ALLAR)
        dbg("om_" + tag, arena[:, AR_SO:AR_SO + 8, :T], ALLAR)
        dbg("S_" + tag, S_st[l][:, :, :], [("S", l, h) for h in range(4)])
        dbg("C_" + tag, C_st[l][:, :, :], [("C", l, h) for h in range(4)])
        dbg("n_" + tag, n_st[l][:, :], [("n", l)])
        for br in range(2):
            src_base = AR_SR if br == 0 else AR_SO
            gate_base = AR_SGA if br == 0 else AR_SGB
            for half in range(2):
                sl, slt = next_slab(l, 17 + br * 2 + half)
                for dcl in range(4):
                    dc = half * 4 + dcl
                    ps, pt = ps_next()
                    for kc in range(8):
                        mm(ps[:, :T], sl[:, kc, dcl * 128:(dcl + 1) * 128], arena[:, src_base + kc, :T], kc == 0, kc == 7,
                           [slt, ar(src_base + kc)], [pt])
                    tt("dve", arena[:, gate_base + dc, :T], ps[:, :T], arena[:, gate_base + dc, :T], ALU.mult,
                       [pt, ar(gate_base + dc)], [ar(gate_base + dc)])
        for dc in range(8):
            tt(PL, arena[:, AR_SGA + dc, :T], arena[:, AR_SGA + dc, :T], arena[:, AR_SGB + dc, :T], ALU.add,
               [ar(AR_SGA + dc), ar(AR_SGB + dc)], [ar(AR_SGA + dc)])
        for half in range(2):
            sl, slt = next_slab(l, 21 + half)
            for dcl in range(4):
                dc = half * 4 + dcl
                ps, pt = ps_next()
                for kc in range(8):
                    mm(ps[:, :T], sl[:, kc, dcl * 128:(dcl + 1) * 128], arena[:, AR_SGA + kc, :T], kc == 0, kc == 7,
                       [slt, ar(AR_SGA + kc)], [pt])
                tt("dve", H[:, dc, :T], ps[:, :T], H[:, dc, :T], ALU.add, [pt, (hn, dc)], [(hn, dc)])

        dbg("hmix_" + tag, H[:, :, :T], [(hn, c) for c in range(8)])
        rms_to(pb + PP_NF, T, lambda c: (xn(c), [ar(AR_XN + c)]), H, hn)
        for j in range(22):
            if j % 2 == 0:
                sl, slt = next_slab(l, 23 + j // 2)
            cb0 = (j % 2) * 256
            pg, pgt = ps_next()
            pu, put = ps_next()
            for kc in range(8):
                mm(pg[:, :T], sl[:, kc, cb0:cb0 + 128], xn(kc), kc == 0, kc == 7, [slt, ar(AR_XN + kc)], [pgt])
            for kc in range(8):
                mm(pu[:, :T], sl[:, kc, cb0 + 128:cb0 + 256], xn(kc), kc == 0, kc == 7, [slt, ar(AR_XN + kc)], [put])
            ui = ctr["ub"] % 2
            ctr["ub"] += 1
            ub = ubuf[ui]
            ubt = [("ub", ui, 0), ("ub", ui, 1)]
            cp(PL, ub[:, :, 0:2], u_hist[l][:, j, :, :], [("uh", l)], ubt)
            cp("act", ub[:, 0, 2:2 + T], pg[:, :T], [pgt], [ubt[0]])
            cp("act", ub[:, 1, 2:2 + T], pu[:, :T], [put], [ubt[1]])
            cp(PL, u_hist[l][:, j, :, :], ub[:, :, T:T + 2], ubt, [("uh", l)])
            accs = []
            for i in range(2):
                a_, at = f32_next()
                accs.append((a_, at))
                col = (j if i == 0 else 22 + j)
                act(a_[:, :T], ub[:, i, 0:T], AF.Identity, [ubt[i], "pp"], [at],
                    scale=pp[:, pb + PP_FW + col:pb + PP_FW + col + 1])
            for tap in (1, 2):
                for i in range(2):
                    a_, at = accs[i]
                    col = (j if i == 0 else 22 + j)
                    stt(a_[:, :T], ub[:, i, tap:tap + T], pp[:, pb + PP_FW + tap * 44 + col:pb + PP_FW + tap * 44 + col + 1],
                        a_[:, :T], ALU.mult, ALU.add, [ubt[i], "pp", at], [at])
            ga, gat = b16_next()
            act(ga[:, :T], accs[0][0][:, :T], AF.Silu, [accs[0][1], "pp"], [gat], bias=pp[:, pb + PP_FB + j:pb + PP_FB + j + 1])
            stt(arena[:, AR_AT + j, :T], accs[1][0][:, :T], pp[:, pb + PP_FB + 22 + j:pb + PP_FB + 22 + j + 1], ga[:, :T],
                ALU.add, ALU.mult, [accs[1][1], "pp", gat], [ar(AR_AT + j)])
        for dc in range(8):
            sl, slt = next_slab(l, 34 + dc)
            ps, pt = ps_next()
            for kc in range(22):
                mm(ps[:, :T], sl[:, kc, :], arena[:, AR_AT + kc, :T], kc == 0, kc == 21, [slt, ar(AR_AT + kc)], [pt])
            tt("dve", H[:, dc, :T], ps[:, :T], H[:, dc, :T], ALU.add, [pt, (hn, dc)], [(hn, dc)])
        dbg("at_" + tag, arena[:, AR_AT:AR_AT + 22, :T], ALLAR)
        dbg("hffn_" + tag, H[:, :, :T], [(hn, c) for c in range(8)])

    final_dma = []

    def load_tile(H, hn, T, src):
        HT = [(hn, c) for c in range(8)]
        if src[0] == "meta":
            dma("act", H[:, :, :T], metaT_d.rearrange("(c p) t -> p c t", p=128), (), HT, "xin")
        else:
            t0 = src[1]
            dma("act", H[:, :, :T], xT_d.rearrange("(c p) t -> p c t", p=128)[:, :, t0:t0 + T], (), HT, "xin")

    def store_tile(T, src):
        fcol = depth * PPL
        rms_to(fcol, T, lambda c: (hT[:, c, :T], [("hT", c)]), hT, "hT")
        t0 = src[1]
        HT = [("hT", c) for c in range(8)]
        return dma("act", outT_d.rearrange("(c p) t -> p c t", p=128)[:, :, t0:t0 + T], hT[:, :, :T], HT, (), "xout")

    Tm, csm, nchm, srcm = tiles[0]
    T1, cs1, nch1, src1 = tiles[1]
    load_tile(hTm, "hTm", Tm, srcm)
    load_tile(hT, "hT", T1, src1)
    for l in range(depth):
        tile_layer(l, Tm, csm, nchm, 0, hTm, "hTm")
        tile_layer(l, T1, cs1, nch1, 1, hT, "hT")
    final_dma = [store_tile(T1, src1)]
    for ti, (T, cs, nch, src) in enumerate(tiles):
        if ti < 2:
            continue
        load_tile(hT, "hT", T, src)
        for l in range(depth):
            tile_layer(l, T, cs, nch, ti, hT, "hT")
        final_dma = [store_tile(T, src)]
    P.emit(nc, es, {"act": final_dma + final_dbg})
    es.close()
    return nc


def _slab_img(Wblk):
    K, n = Wblk.shape
    kc = K // 128
    return np.ascontiguousarray(Wblk.reshape(kc, 128, n).transpose(1, 0, 2)).reshape(128, kc * n)


def pack_weights(depth, w_in, b_in, gla_w_gate, gla_b_gate, gla_norm, ml_conv_w, ml_conv_b, ml_norm,
                 w_branch_gla, w_branch_ml, w_out, norm_mix, norm_ffn, ffn_w_up, ffn_conv_w, ffn_conv_b,
                 ffn_w_down, norm_final):
    o_gq, o_gk, o_gv, o_gr, o_lr, o_mq, o_mk, o_mv, o_mo, o_mi, o_mf, o_ga, o_gb = (
        0, 512, 1024, 2048, 3072, 3088, 3600, 4112, 5136, 6160, 6164, 6168, 7192)
    fm_cols = []
    for h in range(4):
        fm_cols += list(range(o_gq + h * 128, o_gq + (h + 1) * 128))
        fm_cols += list(range(o_gk + h * 128, o_gk + (h + 1) * 128))
    fm_cols += list(range(o_gr, o_gr + 1024)) + list(range(o_mq, o_mq + 512)) + list(range(o_mk, o_mk + 512))
    fm_cols += list(range(o_mo, o_mo + 1024)) + list(range(o_ga, o_ga + 1024)) + list(range(o_gb, o_gb + 1024))
    fm_cols = np.array(fm_cols)
    small_cols = np.array(list(range(o_lr, o_lr + 16)) + list(range(o_mi, o_mi + 4)) + list(range(o_mf, o_mf + 4)))
    tm_cols = np.array(list(range(o_gv, o_gv + 1024)) + list(range(o_mv, o_mv + 1024)))
    up_cols = []
    for j in range(22):
        up_cols += list(range(j * 128, (j + 1) * 128)) + list(range(DFF + j * 128, DFF + (j + 1) * 128))
    up_cols = np.array(up_cols)

    w32 = np.zeros((depth, 128, PW), np.float32)
    pp = np.zeros((128, depth * PPL + 8), np.float32)
    p16 = np.zeros((16, depth * 512 + depth), np.float32)
    prow = np.zeros((4, 2048), np.float32)
    for l in range(depth):
        parts = [_slab_img(w_in[l][:, small_cols])]
        wf = w_in[l][:, fm_cols]
        for s in range(12):
            parts.append(_slab_img(wf[:, s * 512:(s + 1) * 512]))
        wt = w_in[l][:, tm_cols]
        for s in range(4):
            parts.append(_slab_img(wt[:, s * 512:(s + 1) * 512]))
        for W in (w_branch_gla[l], w_branch_ml[l], w_out[l]):
            for s in range(2):
                parts.append(_slab_img(W[:, s * 512:(s + 1) * 512]))
        wu = ffn_w_up[l][:, up_cols]
        for s in range(11):
            parts.append(_slab_img(wu[:, s * 512:(s + 1) * 512]))
        for dc in range(8):
            parts.append(_slab_img(ffn_w_down[l][:, dc * 128:(dc + 1) * 128]))
        w32[l] = np.concatenate(parts, axis=1)
        b = l * PPL
        pp[:, b + PP_NM:b + PP_NM + 8] = norm_mix[l].reshape(8, 128).T
        pp[:, b + PP_NF:b + PP_NF + 8] = norm_ffn[l].reshape(8, 128).T
        pp[:, b + PP_BFM:b + PP_BFM + 48] = b_in[l][fm_cols].reshape(48, 128).T
        pp[:, b + PP_NBG:b + PP_NBG + 4] = gla_b_gate[l].reshape(4, 128).T
        pp[:, b + PP_GN:b + PP_GN + 2] = gla_norm[l].reshape(2, 128).T
        pp[:, b + PP_MN:b + PP_MN + 8] = ml_norm[l].reshape(8, 128).T
        pp[:, b + PP_CW:b + PP_CW + 32] = ml_conv_w[l].reshape(4, 8, 128).transpose(2, 0, 1).reshape(128, 32)
        pp[:, b + PP_CB:b + PP_CB + 8] = ml_conv_b[l].reshape(8, 128).T
        pp[:, b + PP_FW:b + PP_FW + 132] = ffn_conv_w[l].reshape(3, 44, 128).transpose(2, 0, 1).reshape(128, 132)
        pp[:, b + PP_FB:b + PP_FB + 44] = ffn_conv_b[l].reshape(44, 128).T
        pp[:, b + PP_BIF:b + PP_BIF + 8] = np.broadcast_to(b_in[l][small_cols[16:24]][None, :], (128, 8))
        p16[:, l * 512:l * 512 + 512] = gla_w_gate[l]
        p16[:, depth * 512 + l] = b_in[l][small_cols[0:16]]
        prow[l, :] = b_in[l][tm_cols]
    pp[:, depth * PPL:depth * PPL + 8] = norm_final.reshape(8, 128).T
    hc = np.zeros((128, 768), np.float32)
    hc[:, 0:128] = np.eye(128, dtype=np.float32)
    hc[:, 128:256] = np.triu(np.ones((128, 128), np.float32))
    for l in range(4):
        hc[l, 256 + l * 128:256 + (l + 1) * 128] = 1.0
    return w32, pp, p16, prow, hc


def make_tiles(seq):
    tiles = [(NMETA, NMETA, 1, ("meta",))]
    for t0 in range(0, seq, 512):
        tiles.append((512, 128, 4, ("x", t0)))
    return tiles


def run_model(x, meta, depth, **w):
    B, seq, _ = x.shape
    w32, pp, p16, prow, hc = pack_weights(depth, **w)
    metaT = np.ascontiguousarray(np.asarray(meta, np.float32).T)
    nc = build_program(depth, make_tiles(seq), seq)
    in_maps = []
    for b in range(B):
        in_maps.append({"xT": np.ascontiguousarray(x[b].T), "metaT": metaT, "w32": w32, "pp": pp, "p16": p16,
                        "prow": prow, "hc": hc})
    res = run_bass_kernel_spmd(nc, in_maps, core_ids=list(range(B)))
    out = np.stack([np.ascontiguousarray(r["outT"].T) for r in res.results], axis=0)
    if DEBUG["on"]:
        DEBUG["res"] = res.results
    return out.astype(np.float32)


def kernel(x, meta, norm_mix, w_in, b_in, gla_w_gate, gla_b_gate, gla_norm, ml_conv_w, ml_conv_b, ml_norm,
           w_branch_gla, w_branch_ml, w_out, norm_ffn, ffn_w_up, ffn_conv_w, ffn_conv_b, ffn_w_down, norm_final):
    f = lambda a: np.asarray(a, np.float32)
    return run_model(f(x), f(meta), DEPTH, w_in=f(w_in), b_in=f(b_in), gla_w_gate=f(gla_w_gate),
                     gla_b_gate=f(gla_b_gate), gla_norm=f(gla_norm), ml_conv_w=f(ml_conv_w), ml_conv_b=f(ml_conv_b),
                     ml_norm=f(ml_norm), w_branch_gla=f(w_branch_gla), w_branch_ml=f(w_branch_ml), w_out=f(w_out),
                     norm_mix=f(norm_mix), norm_ffn=f(norm_ffn), ffn_w_up=f(ffn_w_up), ffn_conv_w=f(ffn_conv_w),
                     ffn_conv_b=f(ffn_conv_b), ffn_w_down=f(ffn_w_down), norm_final=f(norm_final))
```

```python
import math
from contextlib import ExitStack

import numpy as np
import concourse.bass as bass
import concourse.mybir as mybir
from concourse.bass_utils import run_bass_kernel_spmd

F32 = mybir.dt.float32
BF16 = mybir.dt.bfloat16
AF = mybir.ActivationFunctionType
ALU = mybir.AluOpType

D = 1024
DEPTH = 4
SEQ = 4096
NMETA = 16
DFF = 2816
NFC = 44
EPS = 1e-6
LNS = math.log(128.0 ** -0.5)
TAU = 16.0

PP_NM, PP_NF, PP_BFM, PP_NBG, PP_GN, PP_MN, PP_CW, PP_CB, PP_FW, PP_FB, PP_BIF = (
    0, 8, 16, 64, 68, 70, 78, 110, 118, 250, 294)
PPL = 302

SLABS = [(8, 24)] + [(8, 512)] * 12 + [(8, 512)] * 4 + [(8, 512)] * 6 + [(8, 512)] * 11 + [(22, 128)] * 8
SLAB_OFF = np.concatenate([[0], np.cumsum([k * n for k, n in SLABS])]).astype(int)
PW = int(SLAB_OFF[-1])
NSLAB = len(SLABS)
SLAB_MAX = 4096


class Prog:
    ENGS = ("pe", "act", "dve", "pool", "sp")
    SEM_LIMIT = 30000

    def __init__(self):
        self.ops = {e: [] for e in self.ENGS}
        self.state = {}
        self.know = {e: {} for e in self.ENGS}
        self.snap = {e: [] for e in self.ENGS}
        self.dsnap = {}
        self.dma_cnt = {}
        self.ring_gen = {}

    def _learn(self, eng, dep):
        k = self.know[eng]
        if dep[0] == "e":
            _, e2, pos = dep
            sn = self.snap[e2][pos - 1]
        else:
            sn = self.dsnap[dep]
        for key, v in sn.items():
            if k.get(key, 0) < v:
                k[key] = v
        key = (dep[0], dep[1])
        if k.get(key, 0) < dep[2]:
            k[key] = dep[2]

    RINGS = ("f32r", "b16r", "smr", "ps")

    def _norm(self, toks):
        out = []
        for t in toks:
            if isinstance(t, tuple) and len(t) == 3 and t[0] in self.RINGS:
                assert self.ring_gen.get(t[:2], 0) == t[2], f"stale ring buffer use {t} (current gen {self.ring_gen.get(t[:2])})"
                t = t[:2]
            out.append(t)
        return out

    def op(self, eng, fn, reads=(), writes=(), dma=None):
        reads = self._norm(reads)
        writes = self._norm(writes)
        extra = [t for t in reads if isinstance(t, tuple) and t[0] == "ps" and t not in writes]
        if extra:
            writes = list(writes) + extra
        deps = []
        for t in reads:
            st = self.state.get(t)
            if st is not None and st[0] is not None:
                deps.append((st[0], True))
        for t in writes:
            st = self.state.get(t)
            if st is not None:
                if st[0] is not None:
                    deps.append((st[0], False))
                for r in st[1].values():
                    deps.append((r, False))
        mypos = len(self.ops[eng]) + 1
        waits = []
        know = self.know[eng]
        for dep, is_raw in deps:
            key = (dep[0], dep[1])
            if dep[0] == "e" and dep[1] == eng:
                if eng in ("pe", "sp"):
                    continue
                if not is_raw or mypos - dep[2] > 1:
                    continue
            if know.get(key, 0) >= dep[2]:
                continue
            waits.append(dep)
            self._learn(eng, dep)
        best = {}
        for dpp in waits:
            key = (dpp[0], dpp[1])
            if key not in best or best[key][2] < dpp[2]:
                best[key] = dpp
        waits = list(best.values())
        for dpp in waits:
            if dpp[0] == "e":
                self.ops[dpp[1]][dpp[2] - 1]["sig"] = True
        rec = {"fn": fn, "waits": waits, "sig": False, "dma": None}
        self.ops[eng].append(rec)
        self.snap[eng].append(dict(know))
        if dma is not None:
            cnt = self.dma_cnt.get(dma, 0) + 16
            self.dma_cnt[dma] = cnt
            me = ("d", dma, cnt)
            rec["dma"] = dma
            self.dsnap[me] = dict(know)
        else:
            me = ("e", eng, mypos)
        for t in reads:
            st = self.state.setdefault(t, [None, {}])
            st[1][(me[0], me[1])] = me
        for t in writes:
            self.state[t] = [me, {}]
        return me

    def emit(self, nc, es, final_waits):
        semmap = {}
        nsem = {}
        for e in self.ENGS:
            cnt = 0
            idx = 0
            m = []
            for rec in self.ops[e]:
                if rec["sig"]:
                    if cnt >= self.SEM_LIMIT:
                        idx += 1
                        cnt = 0
                    cnt += 1
                m.append((idx, cnt))
            semmap[e] = m
            nsem[e] = idx + 1
        esems = {e: [es.enter_context(nc.semaphore(f"s_{e}{i}")) for i in range(nsem[e])] for e in self.ENGS}
        dsems = {k: es.enter_context(nc.semaphore(f"d_{k}")) for k in self.dma_cnt}
        block = es.enter_context(nc.Block())

        def run(engname, eh):
            m = semmap[engname]
            for i, rec in enumerate(self.ops[engname]):
                for dpp in rec["waits"]:
                    if dpp[0] == "e":
                        si, sv = semmap[dpp[1]][dpp[2] - 1]
                        eh.wait_ge(esems[dpp[1]][si], sv)
                    else:
                        eh.wait_ge(dsems[dpp[1]], dpp[2])
                ins = rec["fn"](eh)
                if rec["dma"] is not None:
                    ins.then_inc(dsems[rec["dma"]], 16)
                elif rec["sig"]:
                    ins.then_inc(esems[engname][m[i][0]], 1)
            for dpp in final_waits.get(engname, []):
                eh.wait_ge(dsems[dpp[1]], dpp[2])

        @block.tensor
        def _(t):
            run("pe", t)

        @block.scalar
        def _(a):
            run("act", a)

        @block.vector
        def _(v):
            run("dve", v)

        @block.gpsimd
        def _(g):
            run("pool", g)

        @block.sync
        def _(s):
            run("sp", s)


DEBUG = {"on": False, "names": []}


def build_program(depth, tiles, seq):
    nc = bass.Bass("TRN2", target_bir_lowering=False)
    DEBUG["names"] = []
    xT_d = nc.dram_tensor("xT", [D, seq], F32, kind="ExternalInput").ap()
    metaT_d = nc.dram_tensor("metaT", [D, NMETA], F32, kind="ExternalInput").ap()
    w32_d = nc.dram_tensor("w32", [depth, 128, PW], F32, kind="ExternalInput").ap()
    pp_d = nc.dram_tensor("pp", [128, depth * PPL + 8], F32, kind="ExternalInput").ap()
    p16_d = nc.dram_tensor("p16", [16, depth * 512 + depth], F32, kind="ExternalInput").ap()
    prow_d = nc.dram_tensor("prow", [4, 2048], F32, kind="ExternalInput").ap()
    hc_d = nc.dram_tensor("hc", [128, 768], F32, kind="ExternalInput").ap()
    outT_d = nc.dram_tensor("outT", [D, seq], F32, kind="ExternalOutput").ap()
    w16_d = nc.dram_tensor("w16", [depth, 128, PW], BF16, kind="Internal").ap()
    wg16_d = nc.dram_tensor("wg16", [depth, 16, 512], BF16, kind="Internal").ap()

    P = Prog()
    es = ExitStack()
    TM = 512

    def sb(name, shape, dt):
        return es.enter_context(nc.sbuf_tensor("sb_" + name, shape, dt))

    ident_bf = sb("ident_bf", [128, 128], BF16)
    ones_bf = sb("ones_bf", [128, 128], BF16)
    triu_bf = sb("triu_bf", [128, 128], BF16)
    triu_f = sb("triu_f", [128, 128], F32)
    ones_f = sb("ones_f", [128, 128], F32)
    pp = sb("pp", [128, depth * PPL + 8], F32)
    nbg = sb("nbg", [128, depth * 4], F32)
    wgl = sb("wgl", [16, 512], BF16)
    blr = sb("blr", [16, depth], F32)
    brow_bf = sb("brow_bf", [4, 2048], BF16)
    sel_bf = sb("sel_bf", [4, 512], BF16)
    hT = sb("hT", [128, 8, TM], F32)
    hTm = sb("hTm", [128, 8, NMETA], F32)
    NAR = 48
    arena = sb("arena", [128, NAR, TM], BF16)
    AR_XN, AR_SGA, AR_SR, AR_SO, AR_SGB, AR_QE, AR_KD = 0, 8, 16, 24, 32, 40, 44
    AR_AT = 16
    mqk_pre = sb("mqk_pre", [128, 8, TM + 3], BF16)
    NSB = 3
    slabbuf = [sb(f"slab{i}", [128, SLAB_MAX], BF16) for i in range(NSB)]
    S_st = [sb(f"S{l}", [128, 4, 256], F32) for l in range(depth)]
    C_st = [sb(f"C{l}", [128, 4, 256], F32) for l in range(depth)]
    n_st = [sb(f"n{l}", [128, 4], F32) for l in range(depth)]
    qk_hist = [sb(f"qkh{l}", [128, 8, 3], BF16) for l in range(depth)]
    u_hist = [sb(f"uh{l}", [128, 22, 2, 2], BF16) for l in range(depth)]
    S_bf = sb("S_bf", [128, 4, 256], BF16)
    C_bf = sb("C_bf", [128, 4, 256], BF16)
    nb_bf = sb("nb_bf", [128, 512], BF16)
    gv = sb("gv", [128, 4, 1024], BF16)
    mv = sb("mv", [128, 4, 1024], BF16)
    NF32 = 8
    f32r = [sb(f"f32r{i}", [128, 512], F32) for i in range(NF32)]
    NB16 = 6
    b16r = [sb(f"b16r{i}", [128, 512], BF16) for i in range(NB16)]
    ubuf = [sb(f"ubuf{i}", [128, 2, TM + 2], BF16) for i in range(2)]
    ifg = sb("ifg", [128, 4, 8], F32)
    lfn = sb("lfn", [128, 4, 4], F32)
    iftmp = sb("iftmp", [128, 4, 4], F32)
    A_t = sb("A_t", [128, 4, 4], F32)
    NSM = 8
    smr = [sb(f"smr{i}", [128, 4], F32) for i in range(NSM)]
    lrT = sb("lrT", [16, TM], BF16)
    scmD = sb("scmD", [128, 512], BF16)
    kTD = sb("kTD", [128, 512], BF16)
    qkDD = sb("qkDD", [128, 512], BF16)
    qfD = sb("qfD", [128, 512], BF16)
    kwD = sb("kwD", [128, 512], BF16)
    GdD = sb("GdD", [128, 4], F32)
    hbD = [sb(f"hbD{i}", [128, 512], BF16) for i in range(2)]
    NPS = 8
    psb = [es.enter_context(nc.psum_tensor(f"ps{i}", [128, 512], F32)) for i in range(NPS)]

    ctr = {"ps": 0, "f32": 0, "b16": 0, "sm": 0, "tp": 0, "ub": 0, "slab": 0}

    def _bump(name, i):
        g = P.ring_gen.get((name, i), 0) + 1
        P.ring_gen[(name, i)] = g
        return (name, i, g)

    ps_free_list = list(range(NPS))
    ps_live = {}

    def ps_next(auto=True):
        assert ps_free_list, "out of PSUM banks"
        i = ps_free_list.pop(0)
        tok = _bump("ps", i)
        ps_live[i] = tok
        if auto:
            ps_auto.append(i)
            while len(ps_auto) > PS_AUTO_DEPTH:
                j = ps_auto.pop(0)
                del ps_live[j]
                ps_free_list.append(j)
        return psb[i], tok

    ps_auto = []
    PS_AUTO_DEPTH = 3

    def ps_alloc():
        return ps_next(auto=False)

    def ps_release(tok):
        i = tok[1]
        assert ps_live.get(i) == tok, f"double free / stale psum {tok}"
        del ps_live[i]
        ps_free_list.append(i)

    def ps_flush_auto():
        while ps_auto:
            j = ps_auto.pop(0)
            del ps_live[j]
            ps_free_list.append(j)

    def f32_next():
        i = ctr["f32"] % NF32
        ctr["f32"] += 1
        return f32r[i], _bump("f32r", i)

    def b16_next():
        i = ctr["b16"] % NB16
        ctr["b16"] += 1
        return b16r[i], _bump("b16r", i)

    def sm_next():
        i = ctr["sm"] % NSM
        ctr["sm"] += 1
        return smr[i], _bump("smr", i)

    def ar(i):
        return ("ar", i)

    def mm(out, lhsT, rhs, start, stop, reads, writes):
        P.op("pe", lambda e: e.matmul(out, lhsT=lhsT, rhs=rhs, start=start, stop=stop), reads, writes)

    def tpose(out, in_, reads, writes):
        P.op("pe", lambda e: e.transpose(out, in_, ident_bf[:, :]), reads, writes)

    def act(out, in_, func, reads, writes, bias=None, scale=None):
        kw = {}
        if bias is not None:
            kw["bias"] = bias
        if scale is not None:
            kw["scale"] = scale
        P.op("act", lambda e: e.activation(out, in_, func, **kw), reads, writes)

    def tt(eng, out, in0, in1, op, reads, writes):
        P.op(eng, lambda e: e.tensor_tensor(out=out, in0=in0, in1=in1, op=op), reads, writes)

    def ts(eng, out, in0, s1, op0, reads, writes, s2=None, op1=None):
        if op1 is None:
            P.op(eng, lambda e: e.tensor_scalar(out=out, in0=in0, scalar1=s1, scalar2=None, op0=op0), reads, writes)
        else:
            P.op(eng, lambda e: e.tensor_scalar(out=out, in0=in0, scalar1=s1, scalar2=s2, op0=op0, op1=op1), reads, writes)

    def stt(out, in0, scalar, in1, op0, op1, reads, writes):
        P.op("dve", lambda e: e.scalar_tensor_tensor(out=out, in0=in0, scalar=scalar, in1=in1, op0=op0, op1=op1),
             reads, writes)

    def cp(eng, out, in_, reads, writes):
        if eng == "act":
            P.op(eng, lambda e: e.activation(out, in_, AF.Copy), reads, writes)
        else:
            P.op(eng, lambda e: e.tensor_copy(out=out, in_=in_), reads, writes)

    def scan(out, d0, d1, reads, writes):
        P.op("dve", lambda e: e.tensor_tensor_scan(out=out, data0=d0, data1=d1, initial=0.0, op0=ALU.mult,
                                                   op1=ALU.add), reads, writes)

    def recip(out, in_, reads, writes):
        P.op("dve", lambda e: e.reciprocal(out=out, in_=in_), reads, writes)

    def mset(eng, ap, val, writes):
        P.op(eng, lambda e: e.memset(ap, val), (), writes)

    def dma(eng, out, in_, reads, writes, sem, slow=False):
        if slow:
            return P.op(eng, lambda e: e.dma_start(out=out, in_=in_, allow_slow_non_contiguous=True), reads, writes,
                        dma=sem)
        return P.op(eng, lambda e: e.dma_start(out=out, in_=in_), reads, writes, dma=sem)

    def dbg(name, ap, reads):
        if not DEBUG["on"]:
            return
        shp = list(ap.shape)
        d = nc.dram_tensor("dbg_" + name, shp, ap.dtype, kind="ExternalOutput").ap()
        DEBUG["names"].append("dbg_" + name)
        final_dbg.append(dma("act", d, ap, reads, (), f"dbg{len(final_dbg)}", slow=True))

    final_dbg = []

    def v4(ap, n):
        return ap[:, :].rearrange("p (h t) -> p h t", h=4)[:, :, :n]

    W_PIECE = 8192
    GRP_END = [17, 23, NSLAB]

    def slab_grp(si):
        return 0 if si < 17 else (1 if si < 23 else 2)

    for l in range(depth):
        dma("pool", wg16_d[l], p16_d[:, l * 512:(l + 1) * 512], (), ["wg16"], "prewg")
    for l in range(depth):
        g0 = 0
        for g in range(3):
            a = int(SLAB_OFF[g0])
            end = int(SLAB_OFF[GRP_END[g]])
            while a < end:
                b = min(end, a + W_PIECE)
                dma("pool", w16_d[l, :, a:b], w32_d[l, :, a:b], (), [("w16", l, g)], f"pre{l}_{g}")
                a = b
            g0 = GRP_END[g]
    cst_toks = ["pp", "triu_f", "blr"] + [("hT", c) for c in range(6)]
    dma("act", pp[:, :], pp_d, (), ["pp"], "cst")
    dma("act", triu_f[:, :], hc_d[:, 128:256], (), ["triu_f"], "cst")
    dma("act", blr[:, :], p16_d[:, depth * 512:depth * 512 + depth], (), ["blr"], "cst", slow=True)
    dma("act", hT[:, 0, 0:128], hc_d[:, 0:128], (), [("hT", 0)], "cst")
    dma("act", hT[0:4, 1, 0:512], hc_d[0:4, 256:768], (), [("hT", 1)], "cst")
    for q in range(4):
        last_cst = dma("act", hT[0:4, 2 + q, 0:512], prow_d[:, q * 512:(q + 1) * 512], (), [("hT", 2 + q)], "cst")
    for tk in cst_toks:
        P.state[tk][0] = last_cst
    mset("dve", ones_bf[:, :], 1.0, ["ones_bf"])
    mset("dve", ones_f[:, :], 1.0, ["ones_f"])
    cp("dve", ident_bf[:, :], hT[:, 0, 0:128], [("hT", 0)], ["ident_bf"])
    cp("dve", triu_bf[:, :], triu_f[:, :], ["triu_f"], ["triu_bf"])
    cp("dve", sel_bf[:, :], hT[0:4, 1, 0:512], [("hT", 1)], ["sel_bf"])
    for q in range(4):
        cp("dve", brow_bf[:, q * 512:(q + 1) * 512], hT[0:4, 2 + q, 0:512], [("hT", 2 + q)], ["brow_bf"])
    for l in range(depth):
        ts("dve", nbg[:, l * 4:(l + 1) * 4], pp[:, l * PPL + PP_NBG:l * PPL + PP_NBG + 4], -1.0, ALU.mult,
           ["pp"], ["nbg"])
    for l in range(depth):
        for h in range(4):
            mset("dve", S_st[l][:, h, :], 0.0, [("S", l, h)])
            mset("dve", C_st[l][:, h, :], 0.0, [("C", l, h)])
        mset("dve", n_st[l][:, :], 0.0, [("n", l)])
        mset("dve", qk_hist[l][:, :, :], 0.0, [("qkh", l)])
        mset("dve", u_hist[l][:, :, :, :], 0.0, [("uh", l)])

    slab_seq = {"n": 0}

    def next_slab(l, s):
        i = slab_seq["n"] % NSB
        slab_seq["n"] += 1
        kc, ncols = SLABS[s]
        n = kc * ncols
        off = int(SLAB_OFF[s])
        dma("sp", slabbuf[i][:, 0:n], w16_d[l, :, off:off + n], [("w16", l, slab_grp(s))], [("slab", i)], f"sl{i}")
        return slabbuf[i][:, 0:n].rearrange("p (k n) -> p k n", k=kc), ("slab", i)

    def rms_to(l_gcol, T, dst_fn, H, hn):
        for c in range(8):
            act(arena[:, AR_SGA + c, :T], H[:, c, :T], AF.Square, [(hn, c)], [ar(AR_SGA + c)])
        ss, sst = ps_next()
        for c in range(8):
            mm(ss[:, :T], ones_bf[:, :], arena[:, AR_SGA + c, :T], c == 0, c == 7, ["ones_bf", ar(AR_SGA + c)], [sst])
        lnv, lnt = f32_next()
        act(lnv[:, :T], ss[:, :T], AF.Ln, [sst], [lnt], bias=EPS, scale=1.0 / D)
        rstd, rstt = f32_next()
        act(rstd[:, :T], lnv[:, :T], AF.Exp, [lnt], [rstt], scale=-0.5)
        for c in range(8):
            o, otk = dst_fn(c)
            stt(o, H[:, c, :T], pp[:, l_gcol + c:l_gcol + c + 1], rstd[:, :T], ALU.mult, ALU.mult,
                [(hn, c), "pp", rstt], otk)

    def tile_layer(l, T, cs, nch, ti, H, hn):
        pb = l * PPL
        PL = "dve" if ti < 2 else "pool"
        tag = f"t{ti}l{l}"
        ALLAR = [ar(i) for i in range(NAR)]
        xn = lambda c: arena[:, AR_XN + c, :T]
        rms_to(pb + PP_NM, T, lambda c: (xn(c), [ar(AR_XN + c)]), H, hn)
        XN = [ar(AR_XN + c) for c in range(8)]
        dbg("xn_" + tag, arena[:, AR_XN:AR_XN + 8, :T], XN)

        dma("sp", wgl[:, :], wg16_d[l], ["wg16"], ["wgl"], "wgl")
        sl, slt = next_slab(l, 0)
        ps, pst_ = ps_next()
        for kc in range(8):
            mm(ps[:16, :T], sl[:, kc, 0:16], xn(kc), kc == 0, kc == 7, [slt, ar(AR_XN + kc)], [pst_])
        act(lrT[:, :T], ps[:16, :T], AF.Identity, [pst_, "blr"], ["lrT"], bias=blr[:, l:l + 1])
        ps2, ps2t = ps_next()
        for c in range(nch):
            for kc in range(8):
                mm(ps2[:cs, c * 8:(c + 1) * 8], arena[:, AR_XN + kc, c * cs:(c + 1) * cs], sl[:, kc, 16:24],
                   kc == 0, kc == 7, [slt, ar(AR_XN + kc)], [ps2t])
        tt("dve", ifg[:cs, :nch, :], ps2[:cs, 0:nch * 8].rearrange("p (c e) -> p c e", e=8),
           pp[:cs, pb + PP_BIF:pb + PP_BIF + 8].unsqueeze(1).to_broadcast([cs, nch, 8]), ALU.add,
           [ps2t, "pp"], ["ifg"])
        act(iftmp[:cs, :nch, :], ifg[:cs, :nch, 4:8], AF.Exp, ["ifg"], ["iftmp"], scale=-1.0)
        act(lfn[:cs, :nch, :], iftmp[:cs, :nch, :], AF.Ln, ["iftmp"], ["lfn"], bias=1.0)

        E_h = {}

        def gla_gates(h):
            gps, gpt = ps_next()
            mm(gps[:, :T], wgl[:, h * 128:(h + 1) * 128], lrT[:, :T], True, True, ["wgl", "lrT"], [gpt])
            e1, e1t = f32_next()
            act(e1[:, :T], gps[:, :T], AF.Exp, [gpt, "nbg"], [e1t], bias=nbg[:, l * 4 + h:l * 4 + h + 1], scale=-1.0)
            l1, l1t = f32_next()
            act(l1[:, :T], e1[:, :T], AF.Ln, [e1t], [l1t], bias=1.0)
            Lc, Lct = f32_next()
            for c in range(nch):
                scan(Lc[:, c * cs:(c + 1) * cs], ones_f[:, :cs], l1[:, c * cs:(c + 1) * cs], [l1t, "ones_f"], [Lct])
            act(A_t[:, h, :nch], Lc[:, :T].rearrange("p (c s) -> p c s", s=cs)[:, :, cs - 1], AF.Exp,
                [Lct], [("A", h)], scale=-1.0 / TAU)
            Eb, Ebt = f32_next()
            act(Eb[:, :T], Lc[:, :T], AF.Exp, [Lct], [Ebt], scale=-1.0 / TAU, bias=LNS)
            En, Ent = f32_next()
            act(En[:, :T], Lc[:, :T], AF.Exp, [Lct], [Ent], scale=1.0 / TAU)
            E_h[h] = (Eb, Ebt, En, Ent)

        def mlstm_conv():
            cp(PL, mqk_pre[:, :, 0:3], qk_hist[l][:, :, :], [("qkh", l)], [("mqp", j) for j in range(8)])
            cp(PL, qk_hist[l][:, :, :], mqk_pre[:, :, T:T + 3], [("mqp", j) for j in range(8)], [("qkh", l)])
            for j0 in range(0, 8, 2):
                accs = []
                for j in (j0, j0 + 1):
                    a_, at = f32_next()
                    accs.append((a_, at))
                    act(a_[:, :T], mqk_pre[:, j, 0:T], AF.Identity, [("mqp", j), "pp"], [at],
                        scale=pp[:, pb + PP_CW + j:pb + PP_CW + j + 1])
                for tap in (1, 2, 3):
                    for k_, j in enumerate((j0, j0 + 1)):
                        a_, at = accs[k_]
                        stt(a_[:, :T], mqk_pre[:, j, tap:tap + T],
                            pp[:, pb + PP_CW + tap * 8 + j:pb + PP_CW + tap * 8 + j + 1], a_[:, :T], ALU.mult, ALU.add,
                            [("mqp", j), "pp", at], [at])
                for k_, j in enumerate((j0, j0 + 1)):
                    a_, at = accs[k_]
                    act(mqk_pre[:, j, 3:3 + T], a_[:, :T], AF.Silu, [at, "pp", ("mqp", j)], [("mqp", j)],
                        bias=pp[:, pb + PP_CB + j:pb + PP_CB + j + 1])

        fc = 0
        for s in range(1, 13):
            sl, slt = next_slab(l, s)
            for j in range(4):
                if fc < 8 and fc % 2 == 0:
                    gla_gates(fc // 2)
                ps, pt = ps_next()
                for kc in range(8):
                    mm(ps[:, :T], sl[:, kc, j * 128:(j + 1) * 128], xn(kc), kc == 0, kc == 7,
                       [slt, ar(AR_XN + kc)], [pt])
                bcol = pp[:, pb + PP_BFM + fc:pb + PP_BFM + fc + 1]
                if fc < 8:
                    h, isk = fc // 2, fc % 2
                    Eb, Ebt, En, Ent = E_h[h]
                    if not isk:
                        stt(arena[:, AR_QE + h, :T], ps[:, :T], bcol, Eb[:, :T], ALU.add, ALU.mult,
                            [pt, "pp", Ebt], [ar(AR_QE + h)])
                    else:
                        stt(arena[:, AR_KD + h, :T], ps[:, :T], bcol, En[:, :T], ALU.add, ALU.mult,
                            [pt, "pp", Ent], [ar(AR_KD + h)])
                elif fc < 16:
                    jj = fc - 8
                    act(arena[:, AR_SR + jj, :T], ps[:, :T], AF.Silu, [pt, "pp"], [ar(AR_SR + jj)], bias=bcol)
                    act(arena[:, AR_SR + jj, :T], arena[:, AR_SR + jj, :T], AF.Identity, [ar(AR_SR + jj), "pp"],
                        [ar(AR_SR + jj)], scale=pp[:, pb + PP_GN + jj % 2:pb + PP_GN + jj % 2 + 1])
                elif fc < 24:
                    jj = fc - 16
                    act(mqk_pre[:, jj, 3:3 + T], ps[:, :T], AF.Identity, [pt, "pp"], [("mqp", jj)], bias=bcol)
                elif fc < 32:
                    jj = fc - 24
                    act(arena[:, AR_SO + jj, :T], ps[:, :T], AF.Sigmoid, [pt, "pp"], [ar(AR_SO + jj)], bias=bcol)
                    act(arena[:, AR_SO + jj, :T], arena[:, AR_SO + jj, :T], AF.Identity, [ar(AR_SO + jj), "pp"],
                        [ar(AR_SO + jj)], scale=pp[:, pb + PP_MN + jj:pb + PP_MN + jj + 1])
                elif fc < 40:
                    jj = fc - 32
                    act(arena[:, AR_SGA + jj, :T], ps[:, :T], AF.Sigmoid, [pt, "pp"], [ar(AR_SGA + jj)], bias=bcol)
                else:
                    jj = fc - 40
                    act(arena[:, AR_SGB + jj, :T], ps[:, :T], AF.Sigmoid, [pt, "pp"], [ar(AR_SGB + jj)], bias=bcol)
                fc += 1
                if fc == 24:
                    mlstm_conv()

        if ti == 0 and l == 0:
            dbg("sel", sel_bf[:, :], ["sel_bf"])
            dbg("brow", brow_bf[:, :], ["brow_bf"])
            dbg("ident", ident_bf[:, :], ["ident_bf"])
            dbg("triu", triu_bf[:, :], ["triu_bf"])
        dbg("lfn_" + tag, lfn[:cs, :nch, :], ["lfn"])
        dbg("ifg_" + tag, ifg[:cs, :nch, :], ["ifg"])
        dbg("A_" + tag, A_t[:, :, :nch], [("A", h) for h in range(4)])
        dbg("fm_" + tag, arena[:, 8:NAR, :T], ALLAR)
        for q in range(4):
            sl, slt = next_slab(l, 13 + q)
            dst = gv if q < 2 else mv
            dname = "gv" if q < 2 else "mv"
            half = q % 2
            for c in range(nch):
                ps, pt = ps_next()
                for kc in range(8):
                    mm(ps[:cs, :], arena[:, AR_XN + kc, c * cs:(c + 1) * cs], sl[:, kc, :], kc == 0, False,
                       [slt, ar(AR_XN + kc)], [pt])
                mm(ps[:cs, :], sel_bf[:, l * 128:l * 128 + cs], brow_bf[:, q * 512:(q + 1) * 512], False, True,
                   ["sel_bf", "brow_bf"], [pt])
                cp("dve", dst[:cs, c, half * 512:(half + 1) * 512], ps[:cs, :], [pt], [(dname, c)])

        mqa = lambda h, a, b: mqk_pre[:, h, 3 + a:3 + b]
        mka = lambda h, a, b: mqk_pre[:, 4 + h, 3 + a:3 + b]
        dbg("gv_" + tag, gv[:cs, :nch, :], [("gv", c) for c in range(nch)])
        dbg("mv_" + tag, mv[:cs, :nch, :], [("mv", c) for c in range(nch)])
        dbg("mqk_" + tag, mqk_pre[:, :, 3:3 + T], [("mqp", j) for j in range(8)])

        cp("act", S_bf[:, :, :], S_st[l][:, :, :], [("S", l, h) for h in range(4)], [("Sbf", h) for h in range(4)])
        cp("act", C_bf[:, :, :], C_st[l][:, :, :], [("C", l, h) for h in range(4)], [("Cbf", h) for h in range(4)])
        ones3 = ones_f[:, :].unsqueeze(1).to_broadcast([128, 4, 128])

        def nb_refresh():
            tt(PL, nb_bf[:, :].rearrange("p (h m) -> p h m", h=4), ones3,
               n_st[l][:, :].unsqueeze(2).to_broadcast([128, 4, 128]), ALU.mult, [("n", l), "ones_f"], ["nb_bf"])

        nb_refresh()
        ps_flush_auto()
        blk4 = lambda ap: ap[:, :].rearrange("p (h v t) -> p h v t", h=2, v=2)[:, :, :, :cs]
        AR4 = lambda base: [ar(base + i) for i in range(4)]

        def rms_chain(srcs, dst_base, cols, release):
            sq = []
            for b in range(2):
                q_, qt = b16_next()
                act(v4(q_, cs), v4(srcs[b][0], cs), AF.Square, [srcs[b][1]], [qt])
                sq.append((q_, qt))
            yield
            ss, sst = ps_alloc()
            for h in range(4):
                for vc in range(2):
                    blk = (h % 2) * 2 + vc
                    mm(ss[:, h * 128:h * 128 + cs], ones_bf[:, :], sq[h // 2][0][:, blk * 128:blk * 128 + cs],
                       vc == 0, vc == 1, ["ones_bf", sq[h // 2][1]], [sst])
            yield
            lnv, lnt = f32_next()
            act(v4(lnv, cs), v4(ss, cs), AF.Ln, [sst], [lnt], bias=EPS, scale=1.0 / 256.0)
            ps_release(sst)
            yield
            rs, rst = f32_next()
            act(v4(rs, cs), v4(lnv, cs), AF.Exp, [lnt], [rst], scale=-0.5)
            yield
            for b in range(2):
                o1, o1t = f32_next()
                tt("dve", blk4(o1), blk4(srcs[b][0]),
                   rs[:, :].rearrange("p (h t) -> p h t", h=4)[:, 2 * b:2 * b + 2, :cs].unsqueeze(2).to_broadcast(
                       [128, 2, 2, cs]), ALU.mult, [srcs[b][1], rst], [o1t])
                if release:
                    ps_release(release[b])
                yield
                dsta = arena[:, dst_base + 4 * b:dst_base + 4 * b + 4, cols[0]:cols[1]]
                tt(PL, dsta, v4(o1, cs), dsta, ALU.mult, [o1t] + AR4(dst_base + 4 * b), AR4(dst_base + 4 * b))
                yield

        def prep_gla(c):
            c0, c1 = c * cs, (c + 1) * cs
            sc, sct = ps_alloc()
            for h in range(4):
                mm(sc[:cs, h * 128:h * 128 + cs], arena[:, AR_KD + h, c0:c1], arena[:, AR_QE + h, c0:c1], True, True,
                   [ar(AR_KD + h), ar(AR_QE + h)], [sct])
            kdl, kdlt = b16_next()
            tt(PL, v4(kdl, cs), arena[:, AR_KD:AR_KD + 4, c0:c1], A_t[:, :, c:c + 1].to_broadcast([128, 4, cs]),
               ALU.mult, AR4(AR_KD) + [("A", h) for h in range(4)], [kdlt])
            yield
            tt("dve", v4(scmD[:cs, :], cs), v4(sc[:cs, :], cs), triu_bf[:cs, :cs].unsqueeze(1).to_broadcast([cs, 4, cs]),
               ALU.mult, [sct, "triu_bf"], ["scmD"])
            ps_release(sct)
            ktp, ktpt = ps_alloc()
            for h in range(4):
                mm(ktp[:cs, h * 128:(h + 1) * 128], kdl[:, h * 128:h * 128 + cs], ident_bf[:, :], True, True,
                   [kdlt, "ident_bf"], [ktpt])
            yield
            cp("act", kTD[:cs, :], ktp[:cs, :], [ktpt], ["kTD"])
            ps_release(ktpt)
            yield

        def prep_ml(c):
            c0, c1 = c * cs, (c + 1) * cs
            lfb, lfbt = f32_next()
            tt("dve", v4(lfb[:cs, :], 128), ones_f[:cs, :].unsqueeze(1).to_broadcast([cs, 4, 128]),
               lfn[:cs, c, :].unsqueeze(2).to_broadcast([cs, 4, 128]), ALU.mult, ["ones_f", "lfn"], [lfbt])
            yield
            fb, fbt = ps_alloc()
            for h in range(4):
                mm(fb[:, h * 128:h * 128 + cs], lfb[:cs, h * 128:(h + 1) * 128], triu_f[:cs, :cs], True, True,
                   [lfbt, "triu_f"], [fbt])
            fcp, fcpt = ps_alloc()
            mm(fcp[:cs, 0:4], triu_f[:cs, :cs], lfn[:cs, c, :], True, True, ["triu_f", "lfn"], [fcpt])
            sc, sct = ps_alloc()
            for h in range(4):
                mm(sc[:cs, h * 128:h * 128 + cs], mka(h, c0, c1), mqa(h, c0, c1), True, True,
                   [("mqp", 4 + h), ("mqp", h)], [sct])
            yield
            db, dbt = sm_next()
            stt(db[:cs, :], fcp[:cs, 0:4], LNS, ifg[:cs, c, 0:4], ALU.add, ALU.add, [fcpt, "ifg"], [dbt])
            ps_release(fcpt)
            EF, EFt = f32_next()
            act(v4(EF, cs), v4(fb, cs), AF.Exp, [fbt], [EFt], bias=LNS, scale=-1.0)
            act(GdD[:, :], fb[:, :].rearrange("p (h t) -> p h t", h=4)[:, :, cs - 1], AF.Exp, [fbt], ["GdD"], scale=-1.0)
            yield
            Dm, Dmt = f32_next()
            for h in range(4):
                act(Dm[:cs, h * 128:h * 128 + cs], fb[:cs, h * 128:h * 128 + cs], AF.Exp, [fbt, dbt], [Dmt],
                    bias=db[:cs, h:h + 1], scale=-1.0)
            wd, wdt = sm_next()
            tt("dve", wd[:cs, :], db[:cs, :], fb[:cs, :].rearrange("p (h t) -> p h t", h=4)[:, :, cs - 1], ALU.subtract,
               [dbt, fbt], [wdt])
            ps_release(fbt)
            tt(PL, v4(qfD, cs), mqk_pre[:, 0:4, 3 + c0:3 + c1], v4(EF, cs), ALU.mult,
               [("mqp", h) for h in range(4)] + [EFt], ["qfD"])
            yield
            tt(PL, v4(Dm[:cs, :], cs), v4(Dm[:cs, :], cs), triu_f[:cs, :cs].unsqueeze(1).to_broadcast([cs, 4, cs]),
               ALU.mult, [Dmt, "triu_f"], [Dmt])
            wst, wstt = sm_next()
            act(wst[:cs, :], wd[:cs, :], AF.Exp, [wdt], [wstt], bias=-LNS)
            ktp, ktpt = ps_alloc()
            for h in range(4):
                mm(ktp[:cs, h * 128:(h + 1) * 128], mka(h, c0, c1), ident_bf[:, :], True, True,
                   [("mqp", 4 + h), "ident_bf"], [ktpt])
            yield
            tt("dve", v4(qkDD[:cs, :], cs), v4(sc[:cs, :], cs), v4(Dm[:cs, :], cs), ALU.mult, [sct, Dmt], ["qkDD"])
            ps_release(sct)
            tt("dve", v4(kwD[:cs, :], 128), v4(ktp[:cs, :], 128), wst[:cs, :].unsqueeze(2).to_broadcast([cs, 4, 128]),
               ALU.mult, [ktpt, wstt], ["kwD"])
            ps_release(ktpt)
            yield

        def drain(*gens):
            gens = [g for g in gens if g is not None]
            while gens:
                for g in list(gens):
                    try:
                        next(g)
                    except StopIteration:
                        gens.remove(g)

        drain(prep_gla(0), prep_ml(0))
        for c in range(nch):
            c0, c1 = c * cs, (c + 1) * cs
            last = (c == nch - 1)
            ps_flush_auto()
            PG = [ps_alloc(), ps_alloc()]
            for h in range(4):
                mm(PG[h // 2][0][:, (h % 2) * 256:(h % 2) * 256 + 256], kTD[:cs, h * 128:(h + 1) * 128],
                   gv[:cs, c, h * 256:(h + 1) * 256], True, True, ["kTD", ("gv", c)], [PG[h // 2][1]])
            OG = [ps_alloc(), ps_alloc()]
            for h in range(4):
                for vc in range(2):
                    blk = (h % 2) * 2 + vc
                    dst = OG[h // 2][0][:, blk * 128:blk * 128 + cs]
                    mm(dst, gv[:cs, c, h * 256 + vc * 128:h * 256 + (vc + 1) * 128], scmD[:cs, h * 128:h * 128 + cs],
                       True, False, [("gv", c), "scmD"], [OG[h // 2][1]])
                    mm(dst, S_bf[:, h, vc * 128:(vc + 1) * 128], arena[:, AR_QE + h, c0:c1], False, True,
                       [("Sbf", h), ar(AR_QE + h)], [OG[h // 2][1]])
            for h in range(4):
                stt(S_st[l][:, h, :], S_st[l][:, h, :], A_t[:, h, c:c + 1],
                    PG[h // 2][0][:, (h % 2) * 256:(h % 2) * 256 + 256], ALU.mult, ALU.add,
                    [("S", l, h), ("A", h), PG[h // 2][1]], [("S", l, h)])
                if not last:
                    cp("act", S_bf[:, h, :], S_st[l][:, h, :], [("S", l, h)], [("Sbf", h)])
            ps_release(PG[0][1])
            ps_release(PG[1][1])
            PM = [ps_alloc(), ps_alloc()]
            for h in range(4):
                mm(PM[h // 2][0][:, (h % 2) * 256:(h % 2) * 256 + 256], kwD[:cs, h * 128:(h + 1) * 128],
                   mv[:cs, c, h * 256:(h + 1) * 256], True, True, ["kwD", ("mv", c)], [PM[h // 2][1]])
            npp, nppt = ps_alloc()
            for h in range(4):
                mm(npp[:, h:h + 1], kwD[:cs, h * 128:(h + 1) * 128], ones_bf[:cs, 0:1], True, True, ["kwD", "ones_bf"], [nppt])
            NM = [ps_alloc(), ps_alloc()]
            for h in range(4):
                for vc in range(2):
                    blk = (h % 2) * 2 + vc
                    dst = NM[h // 2][0][:, blk * 128:blk * 128 + cs]
                    mm(dst, mv[:cs, c, h * 256 + vc * 128:h * 256 + (vc + 1) * 128], qkDD[:cs, h * 128:h * 128 + cs],
                       True, False, [("mv", c), "qkDD"], [NM[h // 2][1]])
                    mm(dst, C_bf[:, h, vc * 128:(vc + 1) * 128], qfD[:, h * 128:h * 128 + cs], False, True,
                       [("Cbf", h), "qfD"], [NM[h // 2][1]])
            den, dent = ps_alloc()
            for h in range(4):
                mm(den[:, h * 128:h * 128 + cs], ones_bf[:cs, :], qkDD[:cs, h * 128:h * 128 + cs], True, False,
                   ["ones_bf", "qkDD"], [dent])
                mm(den[:, h * 128:h * 128 + cs], nb_bf[:, h * 128:(h + 1) * 128], qfD[:, h * 128:h * 128 + cs], False, True,
                   ["nb_bf", "qfD"], [dent])
            for h in range(4):
                stt(C_st[l][:, h, :], C_st[l][:, h, :], GdD[:, h:h + 1],
                    PM[h // 2][0][:, (h % 2) * 256:(h % 2) * 256 + 256], ALU.mult, ALU.add,
                    [("C", l, h), "GdD", PM[h // 2][1]], [("C", l, h)])
                if not last:
                    cp("act", C_bf[:, h, :], C_st[l][:, h, :], [("C", l, h)], [("Cbf", h)])
            ps_release(PM[0][1])
            ps_release(PM[1][1])
            ntmp, ntt = sm_next()
            tt("dve", ntmp[:, :], n_st[l][:, :], GdD[:, :], ALU.mult, [("n", l), "GdD"], [ntt])
            d0, d0t = f32_next()
            act(v4(d0, cs), v4(den, cs), AF.Abs, [dent], [d0t])
            ps_release(dent)
            tt("dve", n_st[l][:, :], npp[:, 0:4], ntmp[:, :], ALU.add, [nppt, ntt], [("n", l)])
            ps_release(nppt)
            if not last:
                nb_refresh()

            def ml_out():
                d1, d1t = f32_next()
                ts("dve", v4(d1, cs), v4(d0, cs), 1.0, ALU.max, [d0t], [d1t])
                yield
                rden, rdent = f32_next()
                recip(v4(rden, cs), v4(d1, cs), [d1t], [rdent])
                yield
                for b in range(2):
                    tt("dve", blk4(hbD[b]), blk4(NM[b][0]),
                       rden[:, :].rearrange("p (h t) -> p h t", h=4)[:, 2 * b:2 * b + 2, :cs].unsqueeze(2).to_broadcast(
                           [128, 2, 2, cs]), ALU.mult, [NM[b][1], rdent], [("hbD", b)])
                    ps_release(NM[b][1])
                    yield
                yield from rms_chain([(hbD[0], ("hbD", 0)), (hbD[1], ("hbD", 1))], AR_SO, (c0, c1), None)

            gG = rms_chain([(OG[0][0], OG[0][1]), (OG[1][0], OG[1][1])], AR_SR, (c0, c1), [OG[0][1], OG[1][1]])
            gM = ml_out()
            for _ in range(4):
                next(gG)
                next(gM)
            drain(gG, gM, prep_gla(c + 1) if not last else None, prep_ml(c + 1) if not last else None)
        ps_flush_auto()

        dbg("og_" + tag, arena[:, AR_SR:AR_SR + 8, :T], ALLAR)
        dbg("om_" + tag, arena[:, AR_SO:AR_SO + 8, :T], ALLAR)
        dbg("S_" + tag, S_st[l][:, :, :], [("S", l, h) for h in range(4)])
        dbg("C_" + tag, C_st[l][:, :, :], [("C", l, h) for h in range(4)])
        dbg("n_" + tag, n_st[l][:, :], [("n", l)])
        for br in range(2):
            src_base = AR_SR if br == 0 else AR_SO
            gate_base = AR_SGA if br == 0 else AR_SGB
            for half in range(2):
                sl, slt = next_slab(l, 17 + br * 2 + half)
                for dcl in range(4):
                    dc = half * 4 + dcl
                    ps, pt = ps_next()
                    for kc in range(8):
                        mm(ps[:, :T], sl[:, kc, dcl * 128:(dcl + 1) * 128], arena[:, src_base + kc, :T], kc == 0, kc == 7,
                           [slt, ar(src_base + kc)], [pt])
                    tt("dve", arena[:, gate_base + dc, :T], ps[:, :T], arena[:, gate_base + dc, :T], ALU.mult,
                       [pt, ar(gate_base + dc)], [ar(gate_base + dc)])
        for dc in range(8):
            tt(PL, arena[:, AR_SGA + dc, :T], arena[:, AR_SGA + dc, :T], arena[:, AR_SGB + dc, :T], ALU.add,
               [ar(AR_SGA + dc), ar(AR_SGB + dc)], [ar(AR_SGA + dc)])
        for half in range(2):
            sl, slt = next_slab(l, 21 + half)
            for dcl in range(4):
                dc = half * 4 + dcl
                ps, pt = ps_next()
                for kc in range(8):
                    mm(ps[:, :T], sl[:, kc, dcl * 128:(dcl + 1) * 128], arena[:, AR_SGA + kc, :T], kc == 0, kc == 7,
                       [slt, ar(AR_SGA + kc)], [pt])
                tt("dve", H[:, dc, :T], ps[:, :T], H[:, dc, :T], ALU.add, [pt, (hn, dc)], [(hn, dc)])

        dbg("hmix_" + tag, H[:, :, :T], [(hn, c) for c in range(8)])
        rms_to(pb + PP_NF, T, lambda c: (xn(c), [ar(AR_XN + c)]), H, hn)
        for j in range(22):
            if j % 2 == 0:
                sl, slt = next_slab(l, 23 + j // 2)
            cb0 = (j % 2) * 256
            pg, pgt = ps_next()
            pu, put = ps_next()
            for kc in range(8):
                mm(pg[:, :T], sl[:, kc, cb0:cb0 + 128], xn(kc), kc == 0, kc == 7, [slt, ar(AR_XN + kc)], [pgt])
            for kc in range(8):
                mm(pu[:, :T], sl[:, kc, cb0 + 128:cb0 + 256], xn(kc), kc == 0, kc == 7, [slt, ar(AR_XN + kc)], [put])
            ui = ctr["ub"] % 2
            ctr["ub"] += 1
            ub = ubuf[ui]
            ubt = [("ub", ui, 0), ("ub", ui, 1)]
            cp(PL, ub[:, :, 0:2], u_hist[l][:, j, :, :], [("uh", l)], ubt)
            cp("act", ub[:, 0, 2:2 + T], pg[:, :T], [pgt], [ubt[0]])
            cp("act", ub[:, 1, 2:2 + T], pu[:, :T], [put], [ubt[1]])
            cp(PL, u_hist[l][:, j, :, :], ub[:, :, T:T + 2], ubt, [("uh", l)])
            accs = []
            for i in range(2):
                a_, at = f32_next()
                accs.append((a_, at))
                col = (j if i == 0 else 22 + j)
                act(a_[:, :T], ub[:, i, 0:T], AF.Identity, [ubt[i], "pp"], [at],
                    scale=pp[:, pb + PP_FW + col:pb + PP_FW + col + 1])
            for tap in (1, 2):
                for i in range(2):
                    a_, at = accs[i]
                    col = (j if i == 0 else 22 + j)
                    stt(a_[:, :T], ub[:, i, tap:tap + T], pp[:, pb + PP_FW + tap * 44 + col:pb + PP_FW + tap * 44 + col + 1],
                        a_[:, :T], ALU.mult, ALU.add, [ubt[i], "pp", at], [at])
            ga, gat = b16_next()
            act(ga[:, :T], accs[0][0][:, :T], AF.Silu, [accs[0][1], "pp"], [gat], bias=pp[:, pb + PP_FB + j:pb + PP_FB + j + 1])
            stt(arena[:, AR_AT + j, :T], accs[1][0][:, :T], pp[:, pb + PP_FB + 22 + j:pb + PP_FB + 22 + j + 1], ga[:, :T],
                ALU.add, ALU.mult, [accs[1][1], "pp", gat], [ar(AR_AT + j)])
        for dc in range(8):
            sl, slt = next_slab(l, 34 + dc)
            ps, pt = ps_next()
            for kc in range(22):
                mm(ps[:, :T], sl[:, kc, :], arena[:, AR_AT + kc, :T], kc == 0, kc == 21, [slt, ar(AR_AT + kc)], [pt])
            tt("dve", H[:, dc, :T], ps[:, :T], H[:, dc, :T], ALU.add, [pt, (hn, dc)], [(hn, dc)])
        dbg("at_" + tag, arena[:, AR_AT:AR_AT + 22, :T], ALLAR)
        dbg("hffn_" + tag, H[:, :, :T], [(hn, c) for c in range(8)])

    final_dma = []

    def load_tile(H, hn, T, src):
        HT = [(hn, c) for c in range(8)]
        if src[0] == "meta":
            dma("act", H[:, :, :T], metaT_d.rearrange("(c p) t -> p c t", p=128), (), HT, "xin")
        else:
            t0 = src[1]
            dma("act", H[:, :, :T], xT_d.rearrange("(c p) t -> p c t", p=128)[:, :, t0:t0 + T], (), HT, "xin")

    def store_tile(T, src):
        fcol = depth * PPL
        rms_to(fcol, T, lambda c: (hT[:, c, :T], [("hT", c)]), hT, "hT")
        t0 = src[1]
        HT = [("hT", c) for c in range(8)]
        return dma("act", outT_d.rearrange("(c p) t -> p c t", p=128)[:, :, t0:t0 + T], hT[:, :, :T], HT, (), "xout")

    Tm, csm, nchm, srcm = tiles[0]
    T1, cs1, nch1, src1 = tiles[1]
    load_tile(hTm, "hTm", Tm, srcm)
    load_tile(hT, "hT", T1, src1)
    for l in range(depth):
        tile_layer(l, Tm, csm, nchm, 0, hTm, "hTm")
        tile_layer(l, T1, cs1, nch1, 1, hT, "hT")
    final_dma = [store_tile(T1, src1)]
    for ti, (T, cs, nch, src) in enumerate(tiles):
        if ti < 2:
            continue
        load_tile(hT, "hT", T, src)
        for l in range(depth):
            tile_layer(l, T, cs, nch, ti, hT, "hT")
        final_dma = [store_tile(T, src)]
    P.emit(nc, es, {"act": final_dma + final_dbg})
    es.close()
    return nc


def _slab_img(Wblk):
    K, n = Wblk.shape
    kc = K // 128
    return np.ascontiguousarray(Wblk.reshape(kc, 128, n).transpose(1, 0, 2)).reshape(128, kc * n)


def pack_weights(depth, w_in, b_in, gla_w_gate, gla_b_gate, gla_norm, ml_conv_w, ml_conv_b, ml_norm,
                 w_branch_gla, w_branch_ml, w_out, norm_mix, norm_ffn, ffn_w_up, ffn_conv_w, ffn_conv_b,
                 ffn_w_down, norm_final):
    o_gq, o_gk, o_gv, o_gr, o_lr, o_mq, o_mk, o_mv, o_mo, o_mi, o_mf, o_ga, o_gb = (
        0, 512, 1024, 2048, 3072, 3088, 3600, 4112, 5136, 6160, 6164, 6168, 7192)
    fm_cols = []
    for h in range(4):
        fm_cols += list(range(o_gq + h * 128, o_gq + (h + 1) * 128))
        fm_cols += list(range(o_gk + h * 128, o_gk + (h + 1) * 128))
    fm_cols += list(range(o_gr, o_gr + 1024)) + list(range(o_mq, o_mq + 512)) + list(range(o_mk, o_mk + 512))
    fm_cols += list(range(o_mo, o_mo + 1024)) + list(range(o_ga, o_ga + 1024)) + list(range(o_gb, o_gb + 1024))
    fm_cols = np.array(fm_cols)
    small_cols = np.array(list(range(o_lr, o_lr + 16)) + list(range(o_mi, o_mi + 4)) + list(range(o_mf, o_mf + 4)))
    tm_cols = np.array(list(range(o_gv, o_gv + 1024)) + list(range(o_mv, o_mv + 1024)))
    up_cols = []
    for j in range(22):
        up_cols += list(range(j * 128, (j + 1) * 128)) + list(range(DFF + j * 128, DFF + (j + 1) * 128))
    up_cols = np.array(up_cols)

    w32 = np.zeros((depth, 128, PW), np.float32)
    pp = np.zeros((128, depth * PPL + 8), np.float32)
    p16 = np.zeros((16, depth * 512 + depth), np.float32)
    prow = np.zeros((4, 2048), np.float32)
    for l in range(depth):
        parts = [_slab_img(w_in[l][:, small_cols])]
        wf = w_in[l][:, fm_cols]
        for s in range(12):
            parts.append(_slab_img(wf[:, s * 512:(s + 1) * 512]))
        wt = w_in[l][:, tm_cols]
        for s in range(4):
            parts.append(_slab_img(wt[:, s * 512:(s + 1) * 512]))
        for W in (w_branch_gla[l], w_branch_ml[l], w_out[l]):
            for s in range(2):
                parts.append(_slab_img(W[:, s * 512:(s + 1) * 512]))
        wu = ffn_w_up[l][:, up_cols]
        for s in range(11):
            parts.append(_slab_img(wu[:, s * 512:(s + 1) * 512]))
        for dc in range(8):
            parts.append(_slab_img(ffn_w_down[l][:, dc * 128:(dc + 1) * 128]))
        w32[l] = np.concatenate(parts, axis=1)
        b = l * PPL
        pp[:, b + PP_NM:b + PP_NM + 8] = norm_mix[l].reshape(8, 128).T
        pp[:, b + PP_NF:b + PP_NF + 8] = norm_ffn[l].reshape(8, 128).T
        pp[:, b + PP_BFM:b + PP_BFM + 48] = b_in[l][fm_cols].reshape(48, 128).T
        pp[:, b + PP_NBG:b + PP_NBG + 4] = gla_b_gate[l].reshape(4, 128).T
        pp[:, b + PP_GN:b + PP_GN + 2] = gla_norm[l].reshape(2, 128).T
        pp[:, b + PP_MN:b + PP_MN + 8] = ml_norm[l].reshape(8, 128).T
        pp[:, b + PP_CW:b + PP_CW + 32] = ml_conv_w[l].reshape(4, 8, 128).transpose(2, 0, 1).reshape(128, 32)
        pp[:, b + PP_CB:b + PP_CB + 8] = ml_conv_b[l].reshape(8, 128).T
        pp[:, b + PP_FW:b + PP_FW + 132] = ffn_conv_w[l].reshape(3, 44, 128).transpose(2, 0, 1).reshape(128, 132)
        pp[:, b + PP_FB:b + PP_FB + 44] = ffn_conv_b[l].reshape(44, 128).T
        pp[:, b + PP_BIF:b + PP_BIF + 8] = np.broadcast_to(b_in[l][small_cols[16:24]][None, :], (128, 8))
        p16[:, l * 512:l * 512 + 512] = gla_w_gate[l]
        p16[:, depth * 512 + l] = b_in[l][small_cols[0:16]]
        prow[l, :] = b_in[l][tm_cols]
    pp[:, depth * PPL:depth * PPL + 8] = norm_final.reshape(8, 128).T
    hc = np.zeros((128, 768), np.float32)
    hc[:, 0:128] = np.eye(128, dtype=np.float32)
    hc[:, 128:256] = np.triu(np.ones((128, 128), np.float32))
    for l in range(4):
        hc[l, 256 + l * 128:256 + (l + 1) * 128] = 1.0
    return w32, pp, p16, prow, hc


def make_tiles(seq):
    tiles = [(NMETA, NMETA, 1, ("meta",))]
    for t0 in range(0, seq, 512):
        tiles.append((512, 128, 4, ("x", t0)))
    return tiles


def run_model(x, meta, depth, **w):
    B, seq, _ = x.shape
    w32, pp, p16, prow, hc = pack_weights(depth, **w)
    metaT = np.ascontiguousarray(np.asarray(meta, np.float32).T)
    nc = build_program(depth, make_tiles(seq), seq)
    in_maps = []
    for b in range(B):
        in_maps.append({"xT": np.ascontiguousarray(x[b].T), "metaT": metaT, "w32": w32, "pp": pp, "p16": p16,
                        "prow": prow, "hc": hc})
    res = run_bass_kernel_spmd(nc, in_maps, core_ids=list(range(B)))
    out = np.stack([np.ascontiguousarray(r["outT"].T) for r in res.results], axis=0)
    if DEBUG["on"]:
        DEBUG["res"] = res.results
    return out.astype(np.float32)


def kernel(x, meta, norm_mix, w_in, b_in, gla_w_gate, gla_b_gate, gla_norm, ml_conv_w, ml_conv_b, ml_norm,
           w_branch_gla, w_branch_ml, w_out, norm_ffn, ffn_w_up, ffn_conv_w, ffn_conv_b, ffn_w_down, norm_final):
    f = lambda a: np.asarray(a, np.float32)
    return run_model(f(x), f(meta), DEPTH, w_in=f(w_in), b_in=f(b_in), gla_w_gate=f(gla_w_gate),
                     gla_b_gate=f(gla_b_gate), gla_norm=f(gla_norm), ml_conv_w=f(ml_conv_w), ml_conv_b=f(ml_conv_b),
                     ml_norm=f(ml_norm), w_branch_gla=f(w_branch_gla), w_branch_ml=f(w_branch_ml), w_out=f(w_out),
                     norm_mix=f(norm_mix), norm_ffn=f(norm_ffn), ffn_w_up=f(ffn_w_up), ffn_conv_w=f(ffn_conv_w),
                     ffn_conv_b=f(ffn_conv_b), ffn_w_down=f(ffn_w_down), norm_final=f(norm_final))
```

```python
import math
from contextlib import ExitStack

import numpy as np
import concourse.bass as bass
import concourse.mybir as mybir
from concourse.bass_utils import run_bass_kernel_spmd

F32 = mybir.dt.float32
BF16 = mybir.dt.bfloat16
AF = mybir.ActivationFunctionType
ALU = mybir.AluOpType

D = 1024
DEPTH = 4
SEQ = 4096
NMETA = 16
DFF = 2816
NFC = 44
EPS = 1e-6
LNS = math.log(128.0 ** -0.5)
TAU = 16.0

PP_NM, PP_NF, PP_BFM, PP_NBG, PP_GN, PP_MN, PP_CW, PP_CB, PP_FW, PP_FB, PP_BIF = (
    0, 8, 16, 64, 68, 70, 78, 110, 118, 250, 294)
PPL = 302

SLABS = [(8, 24)] + [(8, 512)] * 12 + [(8, 512)] * 4 + [(8, 512)] * 6 + [(8, 512)] * 11 + [(22, 128)] * 8
SLAB_OFF = np.concatenate([[0], np.cumsum([k * n for k, n in SLABS])]).astype(int)
PW = int(SLAB_OFF[-1])
NSLAB = len(SLABS)
SLAB_MAX = 4096


class Prog:
    ENGS = ("pe", "act", "dve", "pool", "sp")
    SEM_LIMIT = 30000

    def __init__(self):
        self.ops = {e: [] for e in self.ENGS}
        self.state = {}
        self.know = {e: {} for e in self.ENGS}
        self.snap = {e: [] for e in self.ENGS}
        self.dsnap = {}
        self.dma_cnt = {}
        self.ring_gen = {}

    def _learn(self, eng, dep):
        k = self.know[eng]
        if dep[0] == "e":
            _, e2, pos = dep
            sn = self.snap[e2][pos - 1]
        else:
            sn = self.dsnap[dep]
        for key, v in sn.items():
            if k.get(key, 0) < v:
                k[key] = v
        key = (dep[0], dep[1])
        if k.get(key, 0) < dep[2]:
            k[key] = dep[2]

    RINGS = ("f32r", "b16r", "smr", "ps")

    def _norm(self, toks):
        out = []
        for t in toks:
            if isinstance(t, tuple) and len(t) == 3 and t[0] in self.RINGS:
                assert self.ring_gen.get(t[:2], 0) == t[2], f"stale ring buffer use {t} (current gen {self.ring_gen.get(t[:2])})"
                t = t[:2]
            out.append(t)
        return out

    def op(self, eng, fn, reads=(), writes=(), dma=None):
        reads = self._norm(reads)
        writes = self._norm(writes)
        extra = [t for t in reads if isinstance(t, tuple) and t[0] == "ps" and t not in writes]
        if extra:
            writes = list(writes) + extra
        deps = []
        for t in reads:
            st = self.state.get(t)
            if st is not None and st[0] is not None:
                deps.append((st[0], True))
        for t in writes:
            st = self.state.get(t)
            if st is not None:
                if st[0] is not None:
                    deps.append((st[0], False))
                for r in st[1].values():
                    deps.append((r, False))
        mypos = len(self.ops[eng]) + 1
        waits = []
        know = self.know[eng]
        for dep, is_raw in deps:
            key = (dep[0], dep[1])
            if dep[0] == "e" and dep[1] == eng:
                if eng in ("pe", "sp"):
                    continue
                if not is_raw or mypos - dep[2] > 1:
                    continue
            if know.get(key, 0) >= dep[2]:
                continue
            waits.append(dep)
            self._learn(eng, dep)
        best = {}
        for dpp in waits:
            key = (dpp[0], dpp[1])
            if key not in best or best[key][2] < dpp[2]:
                best[key] = dpp
        waits = list(best.values())
        for dpp in waits:
            if dpp[0] == "e":
                self.ops[dpp[1]][dpp[2] - 1]["sig"] = True
        rec = {"fn": fn, "waits": waits, "sig": False, "dma": None}
        self.ops[eng].append(rec)
        self.snap[eng].append(dict(know))
        if dma is not None:
            cnt = self.dma_cnt.get(dma, 0) + 16
            self.dma_cnt[dma] = cnt
            me = ("d", dma, cnt)
            rec["dma"] = dma
            self.dsnap[me] = dict(know)
        else:
            me = ("e", eng, mypos)
        for t in reads:
            st = self.state.setdefault(t, [None, {}])
            st[1][(me[0], me[1])] = me
        for t in writes:
            self.state[t] = [me, {}]
        return me

    def emit(self, nc, es, final_waits):
        semmap = {}
        nsem = {}
        for e in self.ENGS:
            cnt = 0
            idx = 0
            m = []
            for rec in self.ops[e]:
                if rec["sig"]:
                    if cnt >= self.SEM_LIMIT:
                        idx += 1
                        cnt = 0
                    cnt += 1
                m.append((idx, cnt))
            semmap[e] = m
            nsem[e] = idx + 1
        esems = {e: [es.enter_context(nc.semaphore(f"s_{e}{i}")) for i in range(nsem[e])] for e in self.ENGS}
        dsems = {k: es.enter_context(nc.semaphore(f"d_{k}")) for k in self.dma_cnt}
        block = es.enter_context(nc.Block())

        def run(engname, eh):
            m = semmap[engname]
            for i, rec in enumerate(self.ops[engname]):
                for dpp in rec["waits"]:
                    if dpp[0] == "e":
                        si, sv = semmap[dpp[1]][dpp[2] - 1]
                        eh.wait_ge(esems[dpp[1]][si], sv)
                    else:
                        eh.wait_ge(dsems[dpp[1]], dpp[2])
                ins = rec["fn"](eh)
                if rec["dma"] is not None:
                    ins.then_inc(dsems[rec["dma"]], 16)
                elif rec["sig"]:
                    ins.then_inc(esems[engname][m[i][0]], 1)
            for dpp in final_waits.get(engname, []):
                eh.wait_ge(dsems[dpp[1]], dpp[2])

        @block.tensor
        def _(t):
            run("pe", t)

        @block.scalar
        def _(a):
            run("act", a)

        @block.vector
        def _(v):
            run("dve", v)

        @block.gpsimd
        def _(g):
            run("pool", g)

        @block.sync
        def _(s):
            run("sp", s)


DEBUG = {"on": False, "names": []}


def build_program(depth, tiles, seq):
    nc = bass.Bass("TRN2", target_bir_lowering=False)
    DEBUG["names"] = []
    xT_d = nc.dram_tensor("xT", [D, seq], F32, kind="ExternalInput").ap()
    metaT_d = nc.dram_tensor("metaT", [D, NMETA], F32, kind="ExternalInput").ap()
    w32_d = nc.dram_tensor("w32", [depth, 128, PW], F32, kind="ExternalInput").ap()
    pp_d = nc.dram_tensor("pp", [128, depth * PPL + 8], F32, kind="ExternalInput").ap()
    p16_d = nc.dram_tensor("p16", [16, depth * 512 + depth], F32, kind="ExternalInput").ap()
    prow_d = nc.dram_tensor("prow", [4, 2048], F32, kind="ExternalInput").ap()
    hc_d = nc.dram_tensor("hc", [128, 768], F32, kind="ExternalInput").ap()
    outT_d = nc.dram_tensor("outT", [D, seq], F32, kind="ExternalOutput").ap()
    w16_d = nc.dram_tensor("w16", [depth, 128, PW], BF16, kind="Internal").ap()
    wg16_d = nc.dram_tensor("wg16", [depth, 16, 512], BF16, kind="Internal").ap()

    P = Prog()
    es = ExitStack()
    TM = 512

    def sb(name, shape, dt):
        return es.enter_context(nc.sbuf_tensor("sb_" + name, shape, dt))

    ident_bf = sb("ident_bf", [128, 128], BF16)
    ones_bf = sb("ones_bf", [128, 128], BF16)
    triu_bf = sb("triu_bf", [128, 128], BF16)
    triu_f = sb("triu_f", [128, 128], F32)
    ones_f = sb("ones_f", [128, 128], F32)
    pp = sb("pp", [128, depth * PPL + 8], F32)
    nbg = sb("nbg", [128, depth * 4], F32)
    wgl = sb("wgl", [16, 512], BF16)
    blr = sb("blr", [16, depth], F32)
    brow_bf = sb("brow_bf", [4, 2048], BF16)
    sel_bf = sb("sel_bf", [4, 512], BF16)
    hT = sb("hT", [128, 8, TM], F32)
    hTm = sb("hTm", [128, 8, NMETA], F32)
    NAR = 48
    arena = sb("arena", [128, NAR, TM], BF16)
    AR_XN, AR_SGA, AR_SR, AR_SO, AR_SGB, AR_QE, AR_KD = 0, 8, 16, 24, 32, 40, 44
    AR_AT = 16
    mqk_pre = sb("mqk_pre", [128, 8, TM + 3], BF16)
    NSB = 3
    slabbuf = [sb(f"slab{i}", [128, SLAB_MAX], BF16) for i in range(NSB)]
    S_st = [sb(f"S{l}", [128, 4, 256], F32) for l in range(depth)]
    C_st = [sb(f"C{l}", [128, 4, 256], F32) for l in range(depth)]
    n_st = [sb(f"n{l}", [128, 4], F32) for l in range(depth)]
    qk_hist = [sb(f"qkh{l}", [128, 8, 3], BF16) for l in range(depth)]
    u_hist = [sb(f"uh{l}", [128, 22, 2, 2], BF16) for l in range(depth)]
    S_bf = sb("S_bf", [128, 4, 256], BF16)
    C_bf = sb("C_bf", [128, 4, 256], BF16)
    nb_bf = sb("nb_bf", [128, 512], BF16)
    gv = sb("gv", [128, 4, 1024], BF16)
    mv = sb("mv", [128, 4, 1024], BF16)
    NF32 = 8
    f32r = [sb(f"f32r{i}", [128, 512], F32) for i in range(NF32)]
    NB16 = 6
    b16r = [sb(f"b16r{i}", [128, 512], BF16) for i in range(NB16)]
    ubuf = [sb(f"ubuf{i}", [128, 2, TM + 2], BF16) for i in range(2)]
    ifg = sb("ifg", [128, 4, 8], F32)
    lfn = sb("lfn", [128, 4, 4], F32)
    iftmp = sb("iftmp", [128, 4, 4], F32)
    A_t = sb("A_t", [128, 4, 4], F32)
    NSM = 8
    smr = [sb(f"smr{i}", [128, 4], F32) for i in range(NSM)]
    lrT = sb("lrT", [16, TM], BF16)
    scmD = sb("scmD", [128, 512], BF16)
    kTD = sb("kTD", [128, 512], BF16)
    qkDD = sb("qkDD", [128, 512], BF16)
    qfD = sb("qfD", [128, 512], BF16)
    kwD = sb("kwD", [128, 512], BF16)
    GdD = sb("GdD", [128, 4], F32)
    hbD = [sb(f"hbD{i}", [128, 512], BF16) for i in range(2)]
    NPS = 8
    psb = [es.enter_context(nc.psum_tensor(f"ps{i}", [128, 512], F32)) for i in range(NPS)]

    ctr = {"ps": 0, "f32": 0, "b16": 0, "sm": 0, "tp": 0, "ub": 0, "slab": 0}

    def _bump(name, i):
        g = P.ring_gen.get((name, i), 0) + 1
        P.ring_gen[(name, i)] = g
        return (name, i, g)

    ps_free_list = list(range(NPS))
    ps_live = {}

    def ps_next(auto=True):
        assert ps_free_list, "out of PSUM banks"
        i = ps_free_list.pop(0)
        tok = _bump("ps", i)
        ps_live[i] = tok
        if auto:
            ps_auto.append(i)
            while len(ps_auto) > PS_AUTO_DEPTH:
                j = ps_auto.pop(0)
                del ps_live[j]
                ps_free_list.append(j)
        return psb[i], tok

    ps_auto = []
    PS_AUTO_DEPTH = 7

    def ps_alloc():
        return ps_next(auto=False)

    def ps_release(tok):
        i = tok[1]
        assert ps_live.get(i) == tok, f"double free / stale psum {tok}"
        del ps_live[i]
        ps_free_list.append(i)

    def ps_flush_auto():
        while ps_auto:
            j = ps_auto.pop(0)
            del ps_live[j]
            ps_free_list.append(j)

    def f32_next():
        i = ctr["f32"] % NF32
        ctr["f32"] += 1
        return f32r[i], _bump("f32r", i)

    def b16_next():
        i = ctr["b16"] % NB16
        ctr["b16"] += 1
        return b16r[i], _bump("b16r", i)

    def sm_next():
        i = ctr["sm"] % NSM
        ctr["sm"] += 1
        return smr[i], _bump("smr", i)

    def ar(i):
        return ("ar", i)

    def mm(out, lhsT, rhs, start, stop, reads, writes):
        P.op("pe", lambda e: e.matmul(out, lhsT=lhsT, rhs=rhs, start=start, stop=stop), reads, writes)

    def tpose(out, in_, reads, writes):
        P.op("pe", lambda e: e.transpose(out, in_, ident_bf[:, :]), reads, writes)

    def act(out, in_, func, reads, writes, bias=None, scale=None):
        kw = {}
        if bias is not None:
            kw["bias"] = bias
        if scale is not None:
            kw["scale"] = scale
        P.op("act", lambda e: e.activation(out, in_, func, **kw), reads, writes)

    def tt(eng, out, in0, in1, op, reads, writes):
        P.op(eng, lambda e: e.tensor_tensor(out=out, in0=in0, in1=in1, op=op), reads, writes)

    def ts(eng, out, in0, s1, op0, reads, writes, s2=None, op1=None):
        if op1 is None:
            P.op(eng, lambda e: e.tensor_scalar(out=out, in0=in0, scalar1=s1, scalar2=None, op0=op0), reads, writes)
        else:
            P.op(eng, lambda e: e.tensor_scalar(out=out, in0=in0, scalar1=s1, scalar2=s2, op0=op0, op1=op1), reads, writes)

    def stt(out, in0, scalar, in1, op0, op1, reads, writes):
        P.op("dve", lambda e: e.scalar_tensor_tensor(out=out, in0=in0, scalar=scalar, in1=in1, op0=op0, op1=op1),
             reads, writes)

    def cp(eng, out, in_, reads, writes):
        if eng == "act":
            P.op(eng, lambda e: e.activation(out, in_, AF.Copy), reads, writes)
        else:
            P.op(eng, lambda e: e.tensor_copy(out=out, in_=in_), reads, writes)

    def scan(out, d0, d1, reads, writes):
        P.op("dve", lambda e: e.tensor_tensor_scan(out=out, data0=d0, data1=d1, initial=0.0, op0=ALU.mult,
                                                   op1=ALU.add), reads, writes)

    def recip(out, in_, reads, writes):
        P.op("dve", lambda e: e.reciprocal(out=out, in_=in_), reads, writes)

    def mset(eng, ap, val, writes):
        P.op(eng, lambda e: e.memset(ap, val), (), writes)

    def dma(eng, out, in_, reads, writes, sem, slow=False):
        if slow:
            return P.op(eng, lambda e: e.dma_start(out=out, in_=in_, allow_slow_non_contiguous=True), reads, writes,
                        dma=sem)
        return P.op(eng, lambda e: e.dma_start(out=out, in_=in_), reads, writes, dma=sem)

    def dbg(name, ap, reads):
        if not DEBUG["on"]:
            return
        shp = list(ap.shape)
        d = nc.dram_tensor("dbg_" + name, shp, ap.dtype, kind="ExternalOutput").ap()
        DEBUG["names"].append("dbg_" + name)
        final_dbg.append(dma("act", d, ap, reads, (), f"dbg{len(final_dbg)}", slow=True))

    final_dbg = []

    def v4(ap, n):
        return ap[:, :].rearrange("p (h t) -> p h t", h=4)[:, :, :n]

    W_PIECE = 8192
    GRP_END = [17, 23, NSLAB]

    def slab_grp(si):
        return 0 if si < 17 else (1 if si < 23 else 2)

    for l in range(depth):
        dma("pool", wg16_d[l], p16_d[:, l * 512:(l + 1) * 512], (), ["wg16"], "prewg")
    for l in range(depth):
        g0 = 0
        for g in range(3):
            a = int(SLAB_OFF[g0])
            end = int(SLAB_OFF[GRP_END[g]])
            while a < end:
                b = min(end, a + W_PIECE)
                dma("pool", w16_d[l, :, a:b], w32_d[l, :, a:b], (), [("w16", l, g)], f"pre{l}_{g}")
                a = b
            g0 = GRP_END[g]
    cst_toks = ["pp", "triu_f", "blr"] + [("hT", c) for c in range(6)]
    dma("act", pp[:, :], pp_d, (), ["pp"], "cst")
    dma("act", triu_f[:, :], hc_d[:, 128:256], (), ["triu_f"], "cst")
    dma("act", blr[:, :], p16_d[:, depth * 512:depth * 512 + depth], (), ["blr"], "cst", slow=True)
    dma("act", hT[:, 0, 0:128], hc_d[:, 0:128], (), [("hT", 0)], "cst")
    dma("act", hT[0:4, 1, 0:512], hc_d[0:4, 256:768], (), [("hT", 1)], "cst")
    for q in range(4):
        last_cst = dma("act", hT[0:4, 2 + q, 0:512], prow_d[:, q * 512:(q + 1) * 512], (), [("hT", 2 + q)], "cst")
    for tk in cst_toks:
        P.state[tk][0] = last_cst
    mset("dve", ones_bf[:, :], 1.0, ["ones_bf"])
    mset("dve", ones_f[:, :], 1.0, ["ones_f"])
    cp("dve", ident_bf[:, :], hT[:, 0, 0:128], [("hT", 0)], ["ident_bf"])
    cp("dve", triu_bf[:, :], triu_f[:, :], ["triu_f"], ["triu_bf"])
    cp("dve", sel_bf[:, :], hT[0:4, 1, 0:512], [("hT", 1)], ["sel_bf"])
    for q in range(4):
        cp("dve", brow_bf[:, q * 512:(q + 1) * 512], hT[0:4, 2 + q, 0:512], [("hT", 2 + q)], ["brow_bf"])
    for l in range(depth):
        ts("dve", nbg[:, l * 4:(l + 1) * 4], pp[:, l * PPL + PP_NBG:l * PPL + PP_NBG + 4], -1.0, ALU.mult,
           ["pp"], ["nbg"])
    for l in range(depth):
        for h in range(4):
            mset("dve", S_st[l][:, h, :], 0.0, [("S", l, h)])
            mset("dve", C_st[l][:, h, :], 0.0, [("C", l, h)])
        mset("dve", n_st[l][:, :], 0.0, [("n", l)])
        mset("dve", qk_hist[l][:, :, :], 0.0, [("qkh", l)])
        mset("dve", u_hist[l][:, :, :, :], 0.0, [("uh", l)])

    slab_seq = {"n": 0}

    def next_slab(l, s):
        i = slab_seq["n"] % NSB
        slab_seq["n"] += 1
        kc, ncols = SLABS[s]
        n = kc * ncols
        off = int(SLAB_OFF[s])
        dma("sp", slabbuf[i][:, 0:n], w16_d[l, :, off:off + n], [("w16", l, slab_grp(s))], [("slab", i)], f"sl{i}")
        return slabbuf[i][:, 0:n].rearrange("p (k n) -> p k n", k=kc), ("slab", i)

    def rms_to(l_gcol, T, dst_fn, H, hn):
        for c in range(8):
            act(arena[:, AR_SGA + c, :T], H[:, c, :T], AF.Square, [(hn, c)], [ar(AR_SGA + c)])
        ss, sst = ps_next()
        for c in range(8):
            mm(ss[:, :T], ones_bf[:, :], arena[:, AR_SGA + c, :T], c == 0, c == 7, ["ones_bf", ar(AR_SGA + c)], [sst])
        lnv, lnt = f32_next()
        act(lnv[:, :T], ss[:, :T], AF.Ln, [sst], [lnt], bias=EPS, scale=1.0 / D)
        rstd, rstt = f32_next()
        act(rstd[:, :T], lnv[:, :T], AF.Exp, [lnt], [rstt], scale=-0.5)
        for c in range(8):
            o, otk = dst_fn(c)
            stt(o, H[:, c, :T], pp[:, l_gcol + c:l_gcol + c + 1], rstd[:, :T], ALU.mult, ALU.mult,
                [(hn, c), "pp", rstt], otk)

    def tile_layer(l, T, cs, nch, ti, H, hn):
        pb = l * PPL
        PL = "dve" if ti < 2 else "pool"
        tag = f"t{ti}l{l}"
        ALLAR = [ar(i) for i in range(NAR)]
        xn = lambda c: arena[:, AR_XN + c, :T]
        rms_to(pb + PP_NM, T, lambda c: (xn(c), [ar(AR_XN + c)]), H, hn)
        XN = [ar(AR_XN + c) for c in range(8)]
        dbg("xn_" + tag, arena[:, AR_XN:AR_XN + 8, :T], XN)

        dma("sp", wgl[:, :], wg16_d[l], ["wg16"], ["wgl"], "wgl")
        sl, slt = next_slab(l, 0)
        ps, pst_ = ps_next()
        for kc in range(8):
            mm(ps[:16, :T], sl[:, kc, 0:16], xn(kc), kc == 0, kc == 7, [slt, ar(AR_XN + kc)], [pst_])
        act(lrT[:, :T], ps[:16, :T], AF.Identity, [pst_, "blr"], ["lrT"], bias=blr[:, l:l + 1])
        ps2, ps2t = ps_next()
        for c in range(nch):
            for kc in range(8):
                mm(ps2[:cs, c * 8:(c + 1) * 8], arena[:, AR_XN + kc, c * cs:(c + 1) * cs], sl[:, kc, 16:24],
                   kc == 0, kc == 7, [slt, ar(AR_XN + kc)], [ps2t])
        tt("dve", ifg[:cs, :nch, :], ps2[:cs, 0:nch * 8].rearrange("p (c e) -> p c e", e=8),
           pp[:cs, pb + PP_BIF:pb + PP_BIF + 8].unsqueeze(1).to_broadcast([cs, nch, 8]), ALU.add,
           [ps2t, "pp"], ["ifg"])
        act(iftmp[:cs, :nch, :], ifg[:cs, :nch, 4:8], AF.Exp, ["ifg"], ["iftmp"], scale=-1.0)
        act(lfn[:cs, :nch, :], iftmp[:cs, :nch, :], AF.Ln, ["iftmp"], ["lfn"], bias=1.0)

        E_h = {}

        def gla_gates(h):
            gps, gpt = ps_next()
            mm(gps[:, :T], wgl[:, h * 128:(h + 1) * 128], lrT[:, :T], True, True, ["wgl", "lrT"], [gpt])
            e1, e1t = f32_next()
            act(e1[:, :T], gps[:, :T], AF.Exp, [gpt, "nbg"], [e1t], bias=nbg[:, l * 4 + h:l * 4 + h + 1], scale=-1.0)
            l1, l1t = f32_next()
            act(l1[:, :T], e1[:, :T], AF.Ln, [e1t], [l1t], bias=1.0)
            Lc, Lct = f32_next()
            for c in range(nch):
                scan(Lc[:, c * cs:(c + 1) * cs], ones_f[:, :cs], l1[:, c * cs:(c + 1) * cs], [l1t, "ones_f"], [Lct])
            act(A_t[:, h, :nch], Lc[:, :T].rearrange("p (c s) -> p c s", s=cs)[:, :, cs - 1], AF.Exp,
                [Lct], [("A", h)], scale=-1.0 / TAU)
            Eb, Ebt = f32_next()
            act(Eb[:, :T], Lc[:, :T], AF.Exp, [Lct], [Ebt], scale=-1.0 / TAU, bias=LNS)
            En, Ent = f32_next()
            act(En[:, :T], Lc[:, :T], AF.Exp, [Lct], [Ent], scale=1.0 / TAU)
            E_h[h] = (Eb, Ebt, En, Ent)

        def mlstm_conv():
            cp(PL, mqk_pre[:, :, 0:3], qk_hist[l][:, :, :], [("qkh", l)], [("mqp", j) for j in range(8)])
            cp(PL, qk_hist[l][:, :, :], mqk_pre[:, :, T:T + 3], [("mqp", j) for j in range(8)], [("qkh", l)])
            for j0 in range(0, 8, 2):
                accs = []
                for j in (j0, j0 + 1):
                    a_, at = f32_next()
                    accs.append((a_, at))
                    act(a_[:, :T], mqk_pre[:, j, 0:T], AF.Identity, [("mqp", j), "pp"], [at],
                        scale=pp[:, pb + PP_CW + j:pb + PP_CW + j + 1])
                for tap in (1, 2, 3):
                    for k_, j in enumerate((j0, j0 + 1)):
                        a_, at = accs[k_]
                        stt(a_[:, :T], mqk_pre[:, j, tap:tap + T],
                            pp[:, pb + PP_CW + tap * 8 + j:pb + PP_CW + tap * 8 + j + 1], a_[:, :T], ALU.mult, ALU.add,
                            [("mqp", j), "pp", at], [at])
                for k_, j in enumerate((j0, j0 + 1)):
                    a_, at = accs[k_]
                    act(mqk_pre[:, j, 3:3 + T], a_[:, :T], AF.Silu, [at, "pp", ("mqp", j)], [("mqp", j)],
                        bias=pp[:, pb + PP_CB + j:pb + PP_CB + j + 1])

        fc = 0
        for s in range(1, 13):
            sl, slt = next_slab(l, s)
            for j in range(4):
                if fc < 8 and fc % 2 == 0:
                    gla_gates(fc // 2)
                ps, pt = ps_next()
                for kc in range(8):
                    mm(ps[:, :T], sl[:, kc, j * 128:(j + 1) * 128], xn(kc), kc == 0, kc == 7,
                       [slt, ar(AR_XN + kc)], [pt])
                bcol = pp[:, pb + PP_BFM + fc:pb + PP_BFM + fc + 1]
                if fc < 8:
                    h, isk = fc // 2, fc % 2
                    Eb, Ebt, En, Ent = E_h[h]
                    if not isk:
                        stt(arena[:, AR_QE + h, :T], ps[:, :T], bcol, Eb[:, :T], ALU.add, ALU.mult,
                            [pt, "pp", Ebt], [ar(AR_QE + h)])
                    else:
                        stt(arena[:, AR_KD + h, :T], ps[:, :T], bcol, En[:, :T], ALU.add, ALU.mult,
                            [pt, "pp", Ent], [ar(AR_KD + h)])
                elif fc < 16:
                    jj = fc - 8
                    act(arena[:, AR_SR + jj, :T], ps[:, :T], AF.Silu, [pt, "pp"], [ar(AR_SR + jj)], bias=bcol)
                    act(arena[:, AR_SR + jj, :T], arena[:, AR_SR + jj, :T], AF.Identity, [ar(AR_SR + jj), "pp"],
                        [ar(AR_SR + jj)], scale=pp[:, pb + PP_GN + jj % 2:pb + PP_GN + jj % 2 + 1])
                elif fc < 24:
                    jj = fc - 16
                    act(mqk_pre[:, jj, 3:3 + T], ps[:, :T], AF.Identity, [pt, "pp"], [("mqp", jj)], bias=bcol)
                elif fc < 32:
                    jj = fc - 24
                    act(arena[:, AR_SO + jj, :T], ps[:, :T], AF.Sigmoid, [pt, "pp"], [ar(AR_SO + jj)], bias=bcol)
                    act(arena[:, AR_SO + jj, :T], arena[:, AR_SO + jj, :T], AF.Identity, [ar(AR_SO + jj), "pp"],
                        [ar(AR_SO + jj)], scale=pp[:, pb + PP_MN + jj:pb + PP_MN + jj + 1])
                elif fc < 40:
                    jj = fc - 32
                    act(arena[:, AR_SGA + jj, :T], ps[:, :T], AF.Sigmoid, [pt, "pp"], [ar(AR_SGA + jj)], bias=bcol)
                else:
                    jj = fc - 40
                    act(arena[:, AR_SGB + jj, :T], ps[:, :T], AF.Sigmoid, [pt, "pp"], [ar(AR_SGB + jj)], bias=bcol)
                fc += 1
                if fc == 24:
                    mlstm_conv()

        if ti == 0 and l == 0:
            dbg("sel", sel_bf[:, :], ["sel_bf"])
            dbg("brow", brow_bf[:, :], ["brow_bf"])
            dbg("ident", ident_bf[:, :], ["ident_bf"])
            dbg("triu", triu_bf[:, :], ["triu_bf"])
        dbg("lfn_" + tag, lfn[:cs, :nch, :], ["lfn"])
        dbg("ifg_" + tag, ifg[:cs, :nch, :], ["ifg"])
        dbg("A_" + tag, A_t[:, :, :nch], [("A", h) for h in range(4)])
        dbg("fm_" + tag, arena[:, 8:NAR, :T], ALLAR)
        for q in range(4):
            sl, slt = next_slab(l, 13 + q)
            dst = gv if q < 2 else mv
            dname = "gv" if q < 2 else "mv"
            half = q % 2
            for c in range(nch):
                ps, pt = ps_next()
                for kc in range(8):
                    mm(ps[:cs, :], arena[:, AR_XN + kc, c * cs:(c + 1) * cs], sl[:, kc, :], kc == 0, False,
                       [slt, ar(AR_XN + kc)], [pt])
                mm(ps[:cs, :], sel_bf[:, l * 128:l * 128 + cs], brow_bf[:, q * 512:(q + 1) * 512], False, True,
                   ["sel_bf", "brow_bf"], [pt])
                cp("dve", dst[:cs, c, half * 512:(half + 1) * 512], ps[:cs, :], [pt], [(dname, c)])

        mqa = lambda h, a, b: mqk_pre[:, h, 3 + a:3 + b]
        mka = lambda h, a, b: mqk_pre[:, 4 + h, 3 + a:3 + b]
        dbg("gv_" + tag, gv[:cs, :nch, :], [("gv", c) for c in range(nch)])
        dbg("mv_" + tag, mv[:cs, :nch, :], [("mv", c) for c in range(nch)])
        dbg("mqk_" + tag, mqk_pre[:, :, 3:3 + T], [("mqp", j) for j in range(8)])

        cp("act", S_bf[:, :, :], S_st[l][:, :, :], [("S", l, h) for h in range(4)], [("Sbf", h) for h in range(4)])
        cp("act", C_bf[:, :, :], C_st[l][:, :, :], [("C", l, h) for h in range(4)], [("Cbf", h) for h in range(4)])
        ones3 = ones_f[:, :].unsqueeze(1).to_broadcast([128, 4, 128])

        def nb_refresh():
            tt(PL, nb_bf[:, :].rearrange("p (h m) -> p h m", h=4), ones3,
               n_st[l][:, :].unsqueeze(2).to_broadcast([128, 4, 128]), ALU.mult, [("n", l), "ones_f"], ["nb_bf"])

        nb_refresh()
        ps_flush_auto()
        blk4 = lambda ap: ap[:, :].rearrange("p (h v t) -> p h v t", h=2, v=2)[:, :, :, :cs]
        AR4 = lambda base: [ar(base + i) for i in range(4)]

        def rms_chain(srcs, dst_base, cols, release):
            sq = []
            for b in range(2):
                q_, qt = b16_next()
                act(v4(q_, cs), v4(srcs[b][0], cs), AF.Square, [srcs[b][1]], [qt])
                sq.append((q_, qt))
            yield
            ss, sst = ps_alloc()
            for h in range(4):
                for vc in range(2):
                    blk = (h % 2) * 2 + vc
                    mm(ss[:, h * 128:h * 128 + cs], ones_bf[:, :], sq[h // 2][0][:, blk * 128:blk * 128 + cs],
                       vc == 0, vc == 1, ["ones_bf", sq[h // 2][1]], [sst])
            yield
            lnv, lnt = f32_next()
            act(v4(lnv, cs), v4(ss, cs), AF.Ln, [sst], [lnt], bias=EPS, scale=1.0 / 256.0)
            ps_release(sst)
            yield
            rs, rst = f32_next()
            act(v4(rs, cs), v4(lnv, cs), AF.Exp, [lnt], [rst], scale=-0.5)
            yield
            for b in range(2):
                o1, o1t = f32_next()
                tt("dve", blk4(o1), blk4(srcs[b][0]),
                   rs[:, :].rearrange("p (h t) -> p h t", h=4)[:, 2 * b:2 * b + 2, :cs].unsqueeze(2).to_broadcast(
                       [128, 2, 2, cs]), ALU.mult, [srcs[b][1], rst], [o1t])
                if release:
                    ps_release(release[b])
                yield
                dsta = arena[:, dst_base + 4 * b:dst_base + 4 * b + 4, cols[0]:cols[1]]
                tt(PL, dsta, v4(o1, cs), dsta, ALU.mult, [o1t] + AR4(dst_base + 4 * b), AR4(dst_base + 4 * b))
                yield

        def prep_gla(c):
            c0, c1 = c * cs, (c + 1) * cs
            sc, sct = ps_alloc()
            for h in range(4):
                mm(sc[:cs, h * 128:h * 128 + cs], arena[:, AR_KD + h, c0:c1], arena[:, AR_QE + h, c0:c1], True, True,
                   [ar(AR_KD + h), ar(AR_QE + h)], [sct])
            kdl, kdlt = b16_next()
            tt(PL, v4(kdl, cs), arena[:, AR_KD:AR_KD + 4, c0:c1], A_t[:, :, c:c + 1].to_broadcast([128, 4, cs]),
               ALU.mult, AR4(AR_KD) + [("A", h) for h in range(4)], [kdlt])
            yield
            tt("dve", v4(scmD[:cs, :], cs), v4(sc[:cs, :], cs), triu_bf[:cs, :cs].unsqueeze(1).to_broadcast([cs, 4, cs]),
               ALU.mult, [sct, "triu_bf"], ["scmD"])
            ps_release(sct)
            ktp, ktpt = ps_alloc()
            for h in range(4):
                mm(ktp[:cs, h * 128:(h + 1) * 128], kdl[:, h * 128:h * 128 + cs], ident_bf[:, :], True, True,
                   [kdlt, "ident_bf"], [ktpt])
            yield
            cp("act", kTD[:cs, :], ktp[:cs, :], [ktpt], ["kTD"])
            ps_release(ktpt)
            yield

        def prep_ml(c):
            c0, c1 = c * cs, (c + 1) * cs
            lfb, lfbt = f32_next()
            tt("dve", v4(lfb[:cs, :], 128), ones_f[:cs, :].unsqueeze(1).to_broadcast([cs, 4, 128]),
               lfn[:cs, c, :].unsqueeze(2).to_broadcast([cs, 4, 128]), ALU.mult, ["ones_f", "lfn"], [lfbt])
            yield
            fb, fbt = ps_alloc()
            for h in range(4):
                mm(fb[:, h * 128:h * 128 + cs], lfb[:cs, h * 128:(h + 1) * 128], triu_f[:cs, :cs], True, True,
                   [lfbt, "triu_f"], [fbt])
            fcp, fcpt = ps_alloc()
            mm(fcp[:cs, 0:4], triu_f[:cs, :cs], lfn[:cs, c, :], True, True, ["triu_f", "lfn"], [fcpt])
            sc, sct = ps_alloc()
            for h in range(4):
                mm(sc[:cs, h * 128:h * 128 + cs], mka(h, c0, c1), mqa(h, c0, c1), True, True,
                   [("mqp", 4 + h), ("mqp", h)], [sct])
            yield
            db, dbt = sm_next()
            stt(db[:cs, :], fcp[:cs, 0:4], LNS, ifg[:cs, c, 0:4], ALU.add, ALU.add, [fcpt, "ifg"], [dbt])
            ps_release(fcpt)
            EF, EFt = f32_next()
            act(v4(EF, cs), v4(fb, cs), AF.Exp, [fbt], [EFt], bias=LNS, scale=-1.0)
            act(GdD[:, :], fb[:, :].rearrange("p (h t) -> p h t", h=4)[:, :, cs - 1], AF.Exp, [fbt], ["GdD"], scale=-1.0)
            yield
            Dm, Dmt = f32_next()
            for h in range(4):
                act(Dm[:cs, h * 128:h * 128 + cs], fb[:cs, h * 128:h * 128 + cs], AF.Exp, [fbt, dbt], [Dmt],
                    bias=db[:cs, h:h + 1], scale=-1.0)
            wd, wdt = sm_next()
            tt("dve", wd[:cs, :], db[:cs, :], fb[:cs, :].rearrange("p (h t) -> p h t", h=4)[:, :, cs - 1], ALU.subtract,
               [dbt, fbt], [wdt])
            ps_release(fbt)
            tt(PL, v4(qfD, cs), mqk_pre[:, 0:4, 3 + c0:3 + c1], v4(EF, cs), ALU.mult,
               [("mqp", h) for h in range(4)] + [EFt], ["qfD"])
            yield
            tt(PL, v4(Dm[:cs, :], cs), v4(Dm[:cs, :], cs), triu_f[:cs, :cs].unsqueeze(1).to_broadcast([cs, 4, cs]),
               ALU.mult, [Dmt, "triu_f"], [Dmt])
            wst, wstt = sm_next()
            act(wst[:cs, :], wd[:cs, :], AF.Exp, [wdt], [wstt], bias=-LNS)
            ktp, ktpt = ps_alloc()
            for h in range(4):
                mm(ktp[:cs, h * 128:(h + 1) * 128], mka(h, c0, c1), ident_bf[:, :], True, True,
                   [("mqp", 4 + h), "ident_bf"], [ktpt])
            yield
            tt("dve", v4(qkDD[:cs, :], cs), v4(sc[:cs, :], cs), v4(Dm[:cs, :], cs), ALU.mult, [sct, Dmt], ["qkDD"])
            ps_release(sct)
            tt("dve", v4(kwD[:cs, :], 128), v4(ktp[:cs, :], 128), wst[:cs, :].unsqueeze(2).to_broadcast([cs, 4, 128]),
               ALU.mult, [ktpt, wstt], ["kwD"])
            ps_release(ktpt)
            yield

        def drain(*gens):
            gens = [g for g in gens if g is not None]
            while gens:
                for g in list(gens):
                    try:
                        next(g)
                    except StopIteration:
                        gens.remove(g)

        drain(prep_gla(0), prep_ml(0))
        for c in range(nch):
            c0, c1 = c * cs, (c + 1) * cs
            last = (c == nch - 1)
            ps_flush_auto()
            PG = [ps_alloc(), ps_alloc()]
            for h in range(4):
                mm(PG[h // 2][0][:, (h % 2) * 256:(h % 2) * 256 + 256], kTD[:cs, h * 128:(h + 1) * 128],
                   gv[:cs, c, h * 256:(h + 1) * 256], True, True, ["kTD", ("gv", c)], [PG[h // 2][1]])
            OG = [ps_alloc(), ps_alloc()]
            for h in range(4):
                for vc in range(2):
                    blk = (h % 2) * 2 + vc
                    dst = OG[h // 2][0][:, blk * 128:blk * 128 + cs]
                    mm(dst, gv[:cs, c, h * 256 + vc * 128:h * 256 + (vc + 1) * 128], scmD[:cs, h * 128:h * 128 + cs],
                       True, False, [("gv", c), "scmD"], [OG[h // 2][1]])
                    mm(dst, S_bf[:, h, vc * 128:(vc + 1) * 128], arena[:, AR_QE + h, c0:c1], False, True,
                       [("Sbf", h), ar(AR_QE + h)], [OG[h // 2][1]])
            for h in range(4):
                stt(S_st[l][:, h, :], S_st[l][:, h, :], A_t[:, h, c:c + 1],
                    PG[h // 2][0][:, (h % 2) * 256:(h % 2) * 256 + 256], ALU.mult, ALU.add,
                    [("S", l, h), ("A", h), PG[h // 2][1]], [("S", l, h)])
                if not last:
                    cp("act", S_bf[:, h, :], S_st[l][:, h, :], [("S", l, h)], [("Sbf", h)])
            ps_release(PG[0][1])
            ps_release(PG[1][1])
            PM = [ps_alloc(), ps_alloc()]
            for h in range(4):
                mm(PM[h // 2][0][:, (h % 2) * 256:(h % 2) * 256 + 256], kwD[:cs, h * 128:(h + 1) * 128],
                   mv[:cs, c, h * 256:(h + 1) * 256], True, True, ["kwD", ("mv", c)], [PM[h // 2][1]])
            npp, nppt = ps_alloc()
            for h in range(4):
                mm(npp[:, h:h + 1], kwD[:cs, h * 128:(h + 1) * 128], ones_bf[:cs, 0:1], True, True, ["kwD", "ones_bf"], [nppt])
            NM = [ps_alloc(), ps_alloc()]
            for h in range(4):
                for vc in range(2):
                    blk = (h % 2) * 2 + vc
                    dst = NM[h // 2][0][:, blk * 128:blk * 128 + cs]
                    mm(dst, mv[:cs, c, h * 256 + vc * 128:h * 256 + (vc + 1) * 128], qkDD[:cs, h * 128:h * 128 + cs],
                       True, False, [("mv", c), "qkDD"], [NM[h // 2][1]])
                    mm(dst, C_bf[:, h, vc * 128:(vc + 1) * 128], qfD[:, h * 128:h * 128 + cs], False, True,
                       [("Cbf", h), "qfD"], [NM[h // 2][1]])
            den, dent = ps_alloc()
            for h in range(4):
                mm(den[:, h * 128:h * 128 + cs], ones_bf[:cs, :], qkDD[:cs, h * 128:h * 128 + cs], True, False,
                   ["ones_bf", "qkDD"], [dent])
                mm(den[:, h * 128:h * 128 + cs], nb_bf[:, h * 128:(h + 1) * 128], qfD[:, h * 128:h * 128 + cs], False, True,
                   ["nb_bf", "qfD"], [dent])
            for h in range(4):
                stt(C_st[l][:, h, :], C_st[l][:, h, :], GdD[:, h:h + 1],
                    PM[h // 2][0][:, (h % 2) * 256:(h % 2) * 256 + 256], ALU.mult, ALU.add,
                    [("C", l, h), "GdD", PM[h // 2][1]], [("C", l, h)])
                if not last:
                    cp("act", C_bf[:, h, :], C_st[l][:, h, :], [("C", l, h)], [("Cbf", h)])
            ps_release(PM[0][1])
            ps_release(PM[1][1])
            ntmp, ntt = sm_next()
            tt("dve", ntmp[:, :], n_st[l][:, :], GdD[:, :], ALU.mult, [("n", l), "GdD"], [ntt])
            d0, d0t = f32_next()
            act(v4(d0, cs), v4(den, cs), AF.Abs, [dent], [d0t])
            ps_release(dent)
            tt("dve", n_st[l][:, :], npp[:, 0:4], ntmp[:, :], ALU.add, [nppt, ntt], [("n", l)])
            ps_release(nppt)
            if not last:
                nb_refresh()

            def ml_out():
                d1, d1t = f32_next()
                ts("dve", v4(d1, cs), v4(d0, cs), 1.0, ALU.max, [d0t], [d1t])
                yield
                rden, rdent = f32_next()
                recip(v4(rden, cs), v4(d1, cs), [d1t], [rdent])
                yield
                for b in range(2):
                    tt("dve", blk4(hbD[b]), blk4(NM[b][0]),
                       rden[:, :].rearrange("p (h t) -> p h t", h=4)[:, 2 * b:2 * b + 2, :cs].unsqueeze(2).to_broadcast(
                           [128, 2, 2, cs]), ALU.mult, [NM[b][1], rdent], [("hbD", b)])
                    ps_release(NM[b][1])
                    yield
                yield from rms_chain([(hbD[0], ("hbD", 0)), (hbD[1], ("hbD", 1))], AR_SO, (c0, c1), None)

            gG = rms_chain([(OG[0][0], OG[0][1]), (OG[1][0], OG[1][1])], AR_SR, (c0, c1), [OG[0][1], OG[1][1]])
            gM = ml_out()
            for _ in range(4):
                next(gG)
                next(gM)
            drain(gG, gM, prep_gla(c + 1) if not last else None, prep_ml(c + 1) if not last else None)
        ps_flush_auto()

        dbg("og_" + tag, arena[:, AR_SR:AR_SR + 8, :T], ALLAR)
        dbg("om_" + tag, arena[:, AR_SO:AR_SO + 8, :T], ALLAR)
        dbg("S_" + tag, S_st[l][:, :, :], [("S", l, h) for h in range(4)])
        dbg("C_" + tag, C_st[l][:, :, :], [("C", l, h) for h in range(4)])
        dbg("n_" + tag, n_st[l][:, :], [("n", l)])
        for br in range(2):
            src_base = AR_SR if br == 0 else AR_SO
            gate_base = AR_SGA if br == 0 else AR_SGB
            for half in range(2):
                sl, slt = next_slab(l, 17 + br * 2 + half)
                for dcl in range(4):
                    dc = half * 4 + dcl
                    ps, pt = ps_next()
                    for kc in range(8):
                        mm(ps[:, :T], sl[:, kc, dcl * 128:(dcl + 1) * 128], arena[:, src_base + kc, :T], kc == 0, kc == 7,
                           [slt, ar(src_base + kc)], [pt])
                    tt("dve", arena[:, gate_base + dc, :T], ps[:, :T], arena[:, gate_base + dc, :T], ALU.mult,
                       [pt, ar(gate_base + dc)], [ar(gate_base + dc)])
        for dc in range(8):
            tt(PL, arena[:, AR_SGA + dc, :T], arena[:, AR_SGA + dc, :T], arena[:, AR_SGB + dc, :T], ALU.add,
               [ar(AR_SGA + dc), ar(AR_SGB + dc)], [ar(AR_SGA + dc)])
        for half in range(2):
            sl, slt = next_slab(l, 21 + half)
            for dcl in range(4):
                dc = half * 4 + dcl
                ps, pt = ps_next()
                for kc in range(8):
                    mm(ps[:, :T], sl[:, kc, dcl * 128:(dcl + 1) * 128], arena[:, AR_SGA + kc, :T], kc == 0, kc == 7,
                       [slt, ar(AR_SGA + kc)], [pt])
                tt("dve", H[:, dc, :T], ps[:, :T], H[:, dc, :T], ALU.add, [pt, (hn, dc)], [(hn, dc)])

        dbg("hmix_" + tag, H[:, :, :T], [(hn, c) for c in range(8)])
        rms_to(pb + PP_NF, T, lambda c: (xn(c), [ar(AR_XN + c)]), H, hn)
        for j in range(22):
            if j % 2 == 0:
                sl, slt = next_slab(l, 23 + j // 2)
            cb0 = (j % 2) * 256
            pg, pgt = ps_next()
            pu, put = ps_next()
            for kc in range(8):
                mm(pg[:, :T], sl[:, kc, cb0:cb0 + 128], xn(kc), kc == 0, kc == 7, [slt, ar(AR_XN + kc)], [pgt])
            for kc in range(8):
                mm(pu[:, :T], sl[:, kc, cb0 + 128:cb0 + 256], xn(kc), kc == 0, kc == 7, [slt, ar(AR_XN + kc)], [put])
            ui = ctr["ub"] % 2
            ctr["ub"] += 1
            ub = ubuf[ui]
            ubt = [("ub", ui, 0), ("ub", ui, 1)]
            cp(PL, ub[:, :, 0:2], u_hist[l][:, j, :, :], [("uh", l)], ubt)
            cp("act", ub[:, 0, 2:2 + T], pg[:, :T], [pgt], [ubt[0]])
            cp("act", ub[:, 1, 2:2 + T], pu[:, :T], [put], [ubt[1]])
            cp(PL, u_hist[l][:, j, :, :], ub[:, :, T:T + 2], ubt, [("uh", l)])
            accs = []
            for i in range(2):
                a_, at = f32_next()
                accs.append((a_, at))
                col = (j if i == 0 else 22 + j)
                act(a_[:, :T], ub[:, i, 0:T], AF.Identity, [ubt[i], "pp"], [at],
                    scale=pp[:, pb + PP_FW + col:pb + PP_FW + col + 1])
            for tap in (1, 2):
                for i in range(2):
                    a_, at = accs[i]
                    col = (j if i == 0 else 22 + j)
                    stt(a_[:, :T], ub[:, i, tap:tap + T], pp[:, pb + PP_FW + tap * 44 + col:pb + PP_FW + tap * 44 + col + 1],
                        a_[:, :T], ALU.mult, ALU.add, [ubt[i], "pp", at], [at])
            ga, gat = b16_next()
            act(ga[:, :T], accs[0][0][:, :T], AF.Silu, [accs[0][1], "pp"], [gat], bias=pp[:, pb + PP_FB + j:pb + PP_FB + j + 1])
            stt(arena[:, AR_AT + j, :T], accs[1][0][:, :T], pp[:, pb + PP_FB + 22 + j:pb + PP_FB + 22 + j + 1], ga[:, :T],
                ALU.add, ALU.mult, [accs[1][1], "pp", gat], [ar(AR_AT + j)])
        for dc in range(8):
            sl, slt = next_slab(l, 34 + dc)
            ps, pt = ps_next()
            for kc in range(22):
                mm(ps[:, :T], sl[:, kc, :], arena[:, AR_AT + kc, :T], kc == 0, kc == 21, [slt, ar(AR_AT + kc)], [pt])
            tt("dve", H[:, dc, :T], ps[:, :T], H[:, dc, :T], ALU.add, [pt, (hn, dc)], [(hn, dc)])
        dbg("at_" + tag, arena[:, AR_AT:AR_AT + 22, :T], ALLAR)
        dbg("hffn_" + tag, H[:, :, :T], [(hn, c) for c in range(8)])

    final_dma = []

    def load_tile(H, hn, T, src):
        HT = [(hn, c) for c in range(8)]
        if src[0] == "meta":
            dma("act", H[:, :, :T], metaT_d.rearrange("(c p) t -> p c t", p=128), (), HT, "xin")
        else:
            t0 = src[1]
            dma("act", H[:, :, :T], xT_d.rearrange("(c p) t -> p c t", p=128)[:, :, t0:t0 + T], (), HT, "xin")

    def store_tile(T, src):
        fcol = depth * PPL
        rms_to(fcol, T, lambda c: (hT[:, c, :T], [("hT", c)]), hT, "hT")
        t0 = src[1]
        HT = [("hT", c) for c in range(8)]
        return dma("act", outT_d.rearrange("(c p) t -> p c t", p=128)[:, :, t0:t0 + T], hT[:, :, :T], HT, (), "xout")

    Tm, csm, nchm, srcm = tiles[0]
    T1, cs1, nch1, src1 = tiles[1]
    load_tile(hTm, "hTm", Tm, srcm)
    load_tile(hT, "hT", T1, src1)
    for l in range(depth):
        tile_layer(l, Tm, csm, nchm, 0, hTm, "hTm")
        tile_layer(l, T1, cs1, nch1, 1, hT, "hT")
    final_dma = [store_tile(T1, src1)]
    for ti, (T, cs, nch, src) in enumerate(tiles):
        if ti < 2:
            continue
        load_tile(hT, "hT", T, src)
        for l in range(depth):
            tile_layer(l, T, cs, nch, ti, hT, "hT")
        final_dma = [store_tile(T, src)]
    P.emit(nc, es, {"act": final_dma + final_dbg})
    es.close()
    return nc


def _slab_img(Wblk):
    K, n = Wblk.shape
    kc = K // 128
    return np.ascontiguousarray(Wblk.reshape(kc, 128, n).transpose(1, 0, 2)).reshape(128, kc * n)


def pack_weights(depth, w_in, b_in, gla_w_gate, gla_b_gate, gla_norm, ml_conv_w, ml_conv_b, ml_norm,
                 w_branch_gla, w_branch_ml, w_out, norm_mix, norm_ffn, ffn_w_up, ffn_conv_w, ffn_conv_b,
                 ffn_w_down, norm_final):
    o_gq, o_gk, o_gv, o_gr, o_lr, o_mq, o_mk, o_mv, o_mo, o_mi, o_mf, o_ga, o_gb = (
        0, 512, 1024, 2048, 3072, 3088, 3600, 4112, 5136, 6160, 6164, 6168, 7192)
    fm_cols = []
    for h in range(4):
        fm_cols += list(range(o_gq + h * 128, o_gq + (h + 1) * 128))
        fm_cols += list(range(o_gk + h * 128, o_gk + (h + 1) * 128))
    fm_cols += list(range(o_gr, o_gr + 1024)) + list(range(o_mq, o_mq + 512)) + list(range(o_mk, o_mk + 512))
    fm_cols += list(range(o_mo, o_mo + 1024)) + list(range(o_ga, o_ga + 1024)) + list(range(o_gb, o_gb + 1024))
    fm_cols = np.array(fm_cols)
    small_cols = np.array(list(range(o_lr, o_lr + 16)) + list(range(o_mi, o_mi + 4)) + list(range(o_mf, o_mf + 4)))
    tm_cols = np.array(list(range(o_gv, o_gv + 1024)) + list(range(o_mv, o_mv + 1024)))
    up_cols = []
    for j in range(22):
        up_cols += list(range(j * 128, (j + 1) * 128)) + list(range(DFF + j * 128, DFF + (j + 1) * 128))
    up_cols = np.array(up_cols)

    w32 = np.zeros((depth, 128, PW), np.float32)
    pp = np.zeros((128, depth * PPL + 8), np.float32)
    p16 = np.zeros((16, depth * 512 + depth), np.float32)
    prow = np.zeros((4, 2048), np.float32)
    for l in range(depth):
        parts = [_slab_img(w_in[l][:, small_cols])]
        wf = w_in[l][:, fm_cols]
        for s in range(12):
            parts.append(_slab_img(wf[:, s * 512:(s + 1) * 512]))
        wt = w_in[l][:, tm_cols]
        for s in range(4):
            parts.append(_slab_img(wt[:, s * 512:(s + 1) * 512]))
        for W in (w_branch_gla[l], w_branch_ml[l], w_out[l]):
            for s in range(2):
                parts.append(_slab_img(W[:, s * 512:(s + 1) * 512]))
        wu = ffn_w_up[l][:, up_cols]
        for s in range(11):
            parts.append(_slab_img(wu[:, s * 512:(s + 1) * 512]))
        for dc in range(8):
            parts.append(_slab_img(ffn_w_down[l][:, dc * 128:(dc + 1) * 128]))
        w32[l] = np.concatenate(parts, axis=1)
        b = l * PPL
        pp[:, b + PP_NM:b + PP_NM + 8] = norm_mix[l].reshape(8, 128).T
        pp[:, b + PP_NF:b + PP_NF + 8] = norm_ffn[l].reshape(8, 128).T
        pp[:, b + PP_BFM:b + PP_BFM + 48] = b_in[l][fm_cols].reshape(48, 128).T
        pp[:, b + PP_NBG:b + PP_NBG + 4] = gla_b_gate[l].reshape(4, 128).T
        pp[:, b + PP_GN:b + PP_GN + 2] = gla_norm[l].reshape(2, 128).T
        pp[:, b + PP_MN:b + PP_MN + 8] = ml_norm[l].reshape(8, 128).T
        pp[:, b + PP_CW:b + PP_CW + 32] = ml_conv_w[l].reshape(4, 8, 128).transpose(2, 0, 1).reshape(128, 32)
        pp[:, b + PP_CB:b + PP_CB + 8] = ml_conv_b[l].reshape(8, 128).T
        pp[:, b + PP_FW:b + PP_FW + 132] = ffn_conv_w[l].reshape(3, 44, 128).transpose(2, 0, 1).reshape(128, 132)
        pp[:, b + PP_FB:b + PP_FB + 44] = ffn_conv_b[l].reshape(44, 128).T
        pp[:, b + PP_BIF:b + PP_BIF + 8] = np.broadcast_to(b_in[l][small_cols[16:24]][None, :], (128, 8))
        p16[:, l * 512:l * 512 + 512] = gla_w_gate[l]
        p16[:, depth * 512 + l] = b_in[l][small_cols[0:16]]
        prow[l, :] = b_in[l][tm_cols]
    pp[:, depth * PPL:depth * PPL + 8] = norm_final.reshape(8, 128).T
    hc = np.zeros((128, 768), np.float32)
    hc[:, 0:128] = np.eye(128, dtype=np.float32)
    hc[:, 128:256] = np.triu(np.ones((128, 128), np.float32))
    for l in range(4):
        hc[l, 256 + l * 128:256 + (l + 1) * 128] = 1.0
    return w32, pp, p16, prow, hc


def make_tiles(seq):
    tiles = [(NMETA, NMETA, 1, ("meta",))]
    for t0 in range(0, seq, 512):
        tiles.append((512, 128, 4, ("x", t0)))
    return tiles


def run_model(x, meta, depth, **w):
    B, seq, _ = x.shape
    w32, pp, p16, prow, hc = pack_weights(depth, **w)
    metaT = np.ascontiguousarray(np.asarray(meta, np.float32).T)
    nc = build_program(depth, make_tiles(seq), seq)
    in_maps = []
    for b in range(B):
        in_maps.append({"xT": np.ascontiguousarray(x[b].T), "metaT": metaT, "w32": w32, "pp": pp, "p16": p16,
                        "prow": prow, "hc": hc})
    res = run_bass_kernel_spmd(nc, in_maps, core_ids=list(range(B)))
    out = np.stack([np.ascontiguousarray(r["outT"].T) for r in res.results], axis=0)
    if DEBUG["on"]:
        DEBUG["res"] = res.results
    return out.astype(np.float32)


def kernel(x, meta, norm_mix, w_in, b_in, gla_w_gate, gla_b_gate, gla_norm, ml_conv_w, ml_conv_b, ml_norm,
           w_branch_gla, w_branch_ml, w_out, norm_ffn, ffn_w_up, ffn_conv_w, ffn_conv_b, ffn_w_down, norm_final):
    f = lambda a: np.asarray(a, np.float32)
    return run_model(f(x), f(meta), DEPTH, w_in=f(w_in), b_in=f(b_in), gla_w_gate=f(gla_w_gate),
                     gla_b_gate=f(gla_b_gate), gla_norm=f(gla_norm), ml_conv_w=f(ml_conv_w), ml_conv_b=f(ml_conv_b),
                     ml_norm=f(ml_norm), w_branch_gla=f(w_branch_gla), w_branch_ml=f(w_branch_ml), w_out=f(w_out),
                     norm_mix=f(norm_mix), norm_ffn=f(norm_ffn), ffn_w_up=f(ffn_w_up), ffn_conv_w=f(ffn_conv_w),
                     ffn_conv_b=f(ffn_conv_b), ffn_w_down=f(ffn_w_down), norm_final=f(norm_final))
```

```python
import math
from contextlib import ExitStack

import numpy as np
import concourse.bass as bass
import concourse.mybir as mybir
from concourse.bass_utils import run_bass_kernel_spmd

F32 = mybir.dt.float32
BF16 = mybir.dt.bfloat16
AF = mybir.ActivationFunctionType
ALU = mybir.AluOpType

D = 1024
DEPTH = 4
SEQ = 4096
NMETA = 16
DFF = 2816
NFC = 44
EPS = 1e-6
LNS = math.log(128.0 ** -0.5)
TAU = 16.0

PP_NM, PP_NF, PP_BFM, PP_NBG, PP_GN, PP_MN, PP_CW, PP_CB, PP_FW, PP_FB, PP_BIF = (
    0, 8, 16, 64, 68, 70, 78, 110, 118, 250, 294)
PPL = 302

SLABS = [(8, 24)] + [(8, 512)] * 12 + [(8, 512)] * 4 + [(8, 512)] * 6 + [(8, 512)] * 11 + [(22, 128)] * 8
SLAB_OFF = np.concatenate([[0], np.cumsum([k * n for k, n in SLABS])]).astype(int)
PW = int(SLAB_OFF[-1])
NSLAB = len(SLABS)
SLAB_MAX = 4096


class Prog:
    ENGS = ("pe", "act", "dve", "pool", "sp")
    SEM_LIMIT = 30000

    def __init__(self):
        self.ops = {e: [] for e in self.ENGS}
        self.state = {}
        self.know = {e: {} for e in self.ENGS}
        self.snap = {e: [] for e in self.ENGS}
        self.dsnap = {}
        self.dma_cnt = {}
        self.ring_gen = {}

    def _learn(self, eng, dep):
        k = self.know[eng]
        if dep[0] == "e":
            _, e2, pos = dep
            sn = self.snap[e2][pos - 1]
        else:
            sn = self.dsnap[dep]
        for key, v in sn.items():
            if k.get(key, 0) < v:
                k[key] = v
        key = (dep[0], dep[1])
        if k.get(key, 0) < dep[2]:
            k[key] = dep[2]

    RINGS = ("f32r", "b16r", "smr", "ps")

    def _norm(self, toks):
        out = []
        for t in toks:
            if isinstance(t, tuple) and len(t) == 3 and t[0] in self.RINGS:
                assert self.ring_gen.get(t[:2], 0) == t[2], f"stale ring buffer use {t} (current gen {self.ring_gen.get(t[:2])})"
                t = t[:2]
            out.append(t)
        return out

    def op(self, eng, fn, reads=(), writes=(), dma=None):
        reads = self._norm(reads)
        writes = self._norm(writes)
        extra = [t for t in reads if isinstance(t, tuple) and t[0] == "ps" and t not in writes]
        if extra:
            writes = list(writes) + extra
        deps = []
        for t in reads:
            st = self.state.get(t)
            if st is not None and st[0] is not None:
                deps.append((st[0], True))
        for t in writes:
            st = self.state.get(t)
            if st is not None:
                if st[0] is not None:
                    deps.append((st[0], False))
                for r in st[1].values():
                    deps.append((r, False))
        mypos = len(self.ops[eng]) + 1
        waits = []
        know = self.know[eng]
        for dep, is_raw in deps:
            key = (dep[0], dep[1])
            if dep[0] == "e" and dep[1] == eng:
                if eng in ("pe", "sp"):
                    continue
                if not is_raw or mypos - dep[2] > 1:
                    continue
            if know.get(key, 0) >= dep[2]:
                continue
            waits.append(dep)
            self._learn(eng, dep)
        best = {}
        for dpp in waits:
            key = (dpp[0], dpp[1])
            if key not in best or best[key][2] < dpp[2]:
                best[key] = dpp
        waits = list(best.values())
        for dpp in waits:
            if dpp[0] == "e":
                self.ops[dpp[1]][dpp[2] - 1]["sig"] = True
        rec = {"fn": fn, "waits": waits, "sig": False, "dma": None}
        self.ops[eng].append(rec)
        self.snap[eng].append(dict(know))
        if dma is not None:
            cnt = self.dma_cnt.get(dma, 0) + 16
            self.dma_cnt[dma] = cnt
            me = ("d", dma, cnt)
            rec["dma"] = dma
            self.dsnap[me] = dict(know)
        else:
            me = ("e", eng, mypos)
        for t in reads:
            st = self.state.setdefault(t, [None, {}])
            st[1][(me[0], me[1])] = me
        for t in writes:
            self.state[t] = [me, {}]
        return me

    def emit(self, nc, es, final_waits):
        semmap = {}
        nsem = {}
        for e in self.ENGS:
            cnt = 0
            idx = 0
            m = []
            for rec in self.ops[e]:
                if rec["sig"]:
                    if cnt >= self.SEM_LIMIT:
                        idx += 1
                        cnt = 0
                    cnt += 1
                m.append((idx, cnt))
            semmap[e] = m
            nsem[e] = idx + 1
        esems = {e: [es.enter_context(nc.semaphore(f"s_{e}{i}")) for i in range(nsem[e])] for e in self.ENGS}
        dsems = {k: es.enter_context(nc.semaphore(f"d_{k}")) for k in self.dma_cnt}
        block = es.enter_context(nc.Block())

        def run(engname, eh):
            m = semmap[engname]
            for i, rec in enumerate(self.ops[engname]):
                for dpp in rec["waits"]:
                    if dpp[0] == "e":
                        si, sv = semmap[dpp[1]][dpp[2] - 1]
                        eh.wait_ge(esems[dpp[1]][si], sv)
                    else:
                        eh.wait_ge(dsems[dpp[1]], dpp[2])
                ins = rec["fn"](eh)
                if rec["dma"] is not None:
                    ins.then_inc(dsems[rec["dma"]], 16)
                elif rec["sig"]:
                    ins.then_inc(esems[engname][m[i][0]], 1)
            for dpp in final_waits.get(engname, []):
                eh.wait_ge(dsems[dpp[1]], dpp[2])

        @block.tensor
        def _(t):
            run("pe", t)

        @block.scalar
        def _(a):
            run("act", a)

        @block.vector
        def _(v):
            run("dve", v)

        @block.gpsimd
        def _(g):
            run("pool", g)

        @block.sync
        def _(s):
            run("sp", s)


DEBUG = {"on": False, "names": []}


def build_program(depth, tiles, seq):
    nc = bass.Bass("TRN2", target_bir_lowering=False)
    DEBUG["names"] = []
    xT_d = nc.dram_tensor("xT", [D, seq], F32, kind="ExternalInput").ap()
    metaT_d = nc.dram_tensor("metaT", [D, NMETA], F32, kind="ExternalInput").ap()
    w32_d = nc.dram_tensor("w32", [depth, 128, PW], F32, kind="ExternalInput").ap()
    pp_d = nc.dram_tensor("pp", [128, depth * PPL + 8], F32, kind="ExternalInput").ap()
    p16_d = nc.dram_tensor("p16", [16, depth * 512 + depth], F32, kind="ExternalInput").ap()
    prow_d = nc.dram_tensor("prow", [4, 2048], F32, kind="ExternalInput").ap()
    hc_d = nc.dram_tensor("hc", [128, 768], F32, kind="ExternalInput").ap()
    outT_d = nc.dram_tensor("outT", [D, seq], F32, kind="ExternalOutput").ap()
    w16_d = nc.dram_tensor("w16", [depth, 128, PW], BF16, kind="Internal").ap()
    wg16_d = nc.dram_tensor("wg16", [depth, 16, 512], BF16, kind="Internal").ap()

    P = Prog()
    es = ExitStack()
    TM = 512

    def sb(name, shape, dt):
        return es.enter_context(nc.sbuf_tensor("sb_" + name, shape, dt))

    ident_bf = sb("ident_bf", [128, 128], BF16)
    ones_bf = sb("ones_bf", [128, 128], BF16)
    triu_bf = sb("triu_bf", [128, 128], BF16)
    triu_f = sb("triu_f", [128, 128], F32)
    ones_f = sb("ones_f", [128, 128], F32)
    pp = sb("pp", [128, depth * PPL + 8], F32)
    nbg = sb("nbg", [128, depth * 4], F32)
    wgl = sb("wgl", [16, 512], BF16)
    blr = sb("blr", [16, depth], F32)
    brow_bf = sb("brow_bf", [4, 2048], BF16)
    sel_bf = sb("sel_bf", [4, 512], BF16)
    hT = sb("hT", [128, 8, TM], F32)
    hTm = sb("hTm", [128, 8, NMETA], F32)
    NAR = 48
    arena = sb("arena", [128, NAR, TM], BF16)
    AR_XN, AR_SGA, AR_SR, AR_SO, AR_SGB, AR_QE, AR_KD = 0, 8, 16, 24, 32, 40, 44
    AR_AT = 16
    mqk_pre = sb("mqk_pre", [128, 8, TM + 3], BF16)
    NSB = 3
    slabbuf = [sb(f"slab{i}", [128, SLAB_MAX], BF16) for i in range(NSB)]
    S_st = [sb(f"S{l}", [128, 4, 256], F32) for l in range(depth)]
    C_st = [sb(f"C{l}", [128, 4, 256], F32) for l in range(depth)]
    n_st = [sb(f"n{l}", [128, 4], F32) for l in range(depth)]
    qk_hist = [sb(f"qkh{l}", [128, 8, 3], BF16) for l in range(depth)]
    u_hist = [sb(f"uh{l}", [128, 22, 2, 2], BF16) for l in range(depth)]
    S_bf = sb("S_bf", [128, 4, 256], BF16)
    C_bf = sb("C_bf", [128, 4, 256], BF16)
    nb_bf = sb("nb_bf", [128, 512], BF16)
    gv = sb("gv", [128, 4, 1024], BF16)
    mv = sb("mv", [128, 4, 1024], BF16)
    NF32 = 8
    f32r = [sb(f"f32r{i}", [128, 512], F32) for i in range(NF32)]
    NB16 = 6
    b16r = [sb(f"b16r{i}", [128, 512], BF16) for i in range(NB16)]
    ubuf = [sb(f"ubuf{i}", [128, 2, TM + 2], BF16) for i in range(2)]
    ifg = sb("ifg", [128, 4, 8], F32)
    lfn = sb("lfn", [128, 4, 4], F32)
    iftmp = sb("iftmp", [128, 4, 4], F32)
    A_t = sb("A_t", [128, 4, 4], F32)
    NSM = 8
    smr = [sb(f"smr{i}", [128, 4], F32) for i in range(NSM)]
    lrT = sb("lrT", [16, TM], BF16)
    scmD = sb("scmD", [128, 512], BF16)
    kTD = sb("kTD", [128, 512], BF16)
    qkDD = sb("qkDD", [128, 512], BF16)
    qfD = sb("qfD", [128, 512], BF16)
    kwD = sb("kwD", [128, 512], BF16)
    GdD = sb("GdD", [128, 4], F32)
    hbD = [sb(f"hbD{i}", [128, 512], BF16) for i in range(2)]
    NPS = 8
    psb = [es.enter_context(nc.psum_tensor(f"ps{i}", [128, 512], F32)) for i in range(NPS)]

    ctr = {"ps": 0, "f32": 0, "b16": 0, "sm": 0, "tp": 0, "ub": 0, "slab": 0}

    def _bump(name, i):
        g = P.ring_gen.get((name, i), 0) + 1
        P.ring_gen[(name, i)] = g
        return (name, i, g)

    ps_free_list = list(range(NPS))
    ps_live = {}

    def ps_next(auto=True):
        assert ps_free_list, "out of PSUM banks"
        i = ps_free_list.pop(0)
        tok = _bump("ps", i)
        ps_live[i] = tok
        if auto:
            ps_auto.append(i)
            while len(ps_auto) > PS_AUTO_DEPTH:
                j = ps_auto.pop(0)
                del ps_live[j]
                ps_free_list.append(j)
        return psb[i], tok

    ps_auto = []
    PS_AUTO_DEPTH = 7

    def ps_alloc():
        return ps_next(auto=False)

    def ps_release(tok):
        i = tok[1]
        assert ps_live.get(i) == tok, f"double free / stale psum {tok}"
        del ps_live[i]
        ps_free_list.append(i)

    def ps_flush_auto():
        while ps_auto:
            j = ps_auto.pop(0)
            del ps_live[j]
            ps_free_list.append(j)

    def f32_next():
        i = ctr["f32"] % NF32
        ctr["f32"] += 1
        return f32r[i], _bump("f32r", i)

    def b16_next():
        i = ctr["b16"] % NB16
        ctr["b16"] += 1
        return b16r[i], _bump("b16r", i)

    def sm_next():
        i = ctr["sm"] % NSM
        ctr["sm"] += 1
        return smr[i], _bump("smr", i)

    def ar(i):
        return ("ar", i)

    def mm(out, lhsT, rhs, start, stop, reads, writes):
        P.op("pe", lambda e: e.matmul(out, lhsT=lhsT, rhs=rhs, start=start, stop=stop), reads, writes)

    def tpose(out, in_, reads, writes):
        P.op("pe", lambda e: e.transpose(out, in_, ident_bf[:, :]), reads, writes)

    def act(out, in_, func, reads, writes, bias=None, scale=None):
        kw = {}
        if bias is not None:
            kw["bias"] = bias
        if scale is not None:
            kw["scale"] = scale
        P.op("act", lambda e: e.activation(out, in_, func, **kw), reads, writes)

    def tt(eng, out, in0, in1, op, reads, writes):
        P.op(eng, lambda e: e.tensor_tensor(out=out, in0=in0, in1=in1, op=op), reads, writes)

    def ts(eng, out, in0, s1, op0, reads, writes, s2=None, op1=None):
        if op1 is None:
            P.op(eng, lambda e: e.tensor_scalar(out=out, in0=in0, scalar1=s1, scalar2=None, op0=op0), reads, writes)
        else:
            P.op(eng, lambda e: e.tensor_scalar(out=out, in0=in0, scalar1=s1, scalar2=s2, op0=op0, op1=op1), reads, writes)

    def stt(out, in0, scalar, in1, op0, op1, reads, writes):
        P.op("dve", lambda e: e.scalar_tensor_tensor(out=out, in0=in0, scalar=scalar, in1=in1, op0=op0, op1=op1),
             reads, writes)

    def cp(eng, out, in_, reads, writes):
        if eng == "act":
            P.op(eng, lambda e: e.activation(out, in_, AF.Copy), reads, writes)
        else:
            P.op(eng, lambda e: e.tensor_copy(out=out, in_=in_), reads, writes)

    def scan(out, d0, d1, reads, writes):
        P.op("dve", lambda e: e.tensor_tensor_scan(out=out, data0=d0, data1=d1, initial=0.0, op0=ALU.mult,
                                                   op1=ALU.add), reads, writes)

    def recip(out, in_, reads, writes):
        P.op("dve", lambda e: e.reciprocal(out=out, in_=in_), reads, writes)

    def mset(eng, ap, val, writes):
        P.op(eng, lambda e: e.memset(ap, val), (), writes)

    def dma(eng, out, in_, reads, writes, sem, slow=False):
        if slow:
            return P.op(eng, lambda e: e.dma_start(out=out, in_=in_, allow_slow_non_contiguous=True), reads, writes,
                        dma=sem)
        return P.op(eng, lambda e: e.dma_start(out=out, in_=in_), reads, writes, dma=sem)

    def dbg(name, ap, reads):
        if not DEBUG["on"]:
            return
        shp = list(ap.shape)
        d = nc.dram_tensor("dbg_" + name, shp, ap.dtype, kind="ExternalOutput").ap()
        DEBUG["names"].append("dbg_" + name)
        final_dbg.append(dma("act", d, ap, reads, (), f"dbg{len(final_dbg)}", slow=True))

    final_dbg = []

    def v4(ap, n):
        return ap[:, :].rearrange("p (h t) -> p h t", h=4)[:, :, :n]

    W_PIECE = 8192
    GRP_END = [17, 23, NSLAB]

    def slab_grp(si):
        return 0 if si < 17 else (1 if si < 23 else 2)

    for l in range(depth):
        dma("pool", wg16_d[l], p16_d[:, l * 512:(l + 1) * 512], (), ["wg16"], "prewg")
    for l in range(depth):
        g0 = 0
        for g in range(3):
            a = int(SLAB_OFF[g0])
            end = int(SLAB_OFF[GRP_END[g]])
            while a < end:
                b = min(end, a + W_PIECE)
                dma("pool", w16_d[l, :, a:b], w32_d[l, :, a:b], (), [("w16", l, g)], f"pre{l}_{g}")
                a = b
            g0 = GRP_END[g]
    cst_toks = ["pp", "triu_f", "blr"] + [("hT", c) for c in range(6)]
    dma("act", pp[:, :], pp_d, (), ["pp"], "cst")
    dma("act", triu_f[:, :], hc_d[:, 128:256], (), ["triu_f"], "cst")
    dma("act", blr[:, :], p16_d[:, depth * 512:depth * 512 + depth], (), ["blr"], "cst", slow=True)
    dma("act", hT[:, 0, 0:128], hc_d[:, 0:128], (), [("hT", 0)], "cst")
    dma("act", hT[0:4, 1, 0:512], hc_d[0:4, 256:768], (), [("hT", 1)], "cst")
    for q in range(4):
        last_cst = dma("act", hT[0:4, 2 + q, 0:512], prow_d[:, q * 512:(q + 1) * 512], (), [("hT", 2 + q)], "cst")
    for tk in cst_toks:
        P.state[tk][0] = last_cst
    mset("dve", ones_bf[:, :], 1.0, ["ones_bf"])
    mset("dve", ones_f[:, :], 1.0, ["ones_f"])
    cp("dve", ident_bf[:, :], hT[:, 0, 0:128], [("hT", 0)], ["ident_bf"])
    cp("dve", triu_bf[:, :], triu_f[:, :], ["triu_f"], ["triu_bf"])
    cp("dve", sel_bf[:, :], hT[0:4, 1, 0:512], [("hT", 1)], ["sel_bf"])
    for q in range(4):
        cp("dve", brow_bf[:, q * 512:(q + 1) * 512], hT[0:4, 2 + q, 0:512], [("hT", 2 + q)], ["brow_bf"])
    for l in range(depth):
        ts("dve", nbg[:, l * 4:(l + 1) * 4], pp[:, l * PPL + PP_NBG:l * PPL + PP_NBG + 4], -1.0, ALU.mult,
           ["pp"], ["nbg"])
    for l in range(depth):
        for h in range(4):
            mset("dve", S_st[l][:, h, :], 0.0, [("S", l, h)])
            mset("dve", C_st[l][:, h, :], 0.0, [("C", l, h)])
        mset("dve", n_st[l][:, :], 0.0, [("n", l)])
        mset("dve", qk_hist[l][:, :, :], 0.0, [("qkh", l)])
        mset("dve", u_hist[l][:, :, :, :], 0.0, [("uh", l)])

    slab_seq = {"n": 0}

    def next_slab(l, s):
        i = slab_seq["n"] % NSB
        slab_seq["n"] += 1
        kc, ncols = SLABS[s]
        n = kc * ncols
        off = int(SLAB_OFF[s])
        dma("sp", slabbuf[i][:, 0:n], w16_d[l, :, off:off + n], [("w16", l, slab_grp(s))], [("slab", i)], f"sl{i}")
        return slabbuf[i][:, 0:n].rearrange("p (k n) -> p k n", k=kc), ("slab", i)

    def rms_to(l_gcol, T, dst_fn, H, hn):
        for c in range(8):
            act(arena[:, AR_SGA + c, :T], H[:, c, :T], AF.Square, [(hn, c)], [ar(AR_SGA + c)])
        ss, sst = ps_next()
        for c in range(8):
            mm(ss[:, :T], ones_bf[:, :], arena[:, AR_SGA + c, :T], c == 0, c == 7, ["ones_bf", ar(AR_SGA + c)], [sst])
        lnv, lnt = f32_next()
        act(lnv[:, :T], ss[:, :T], AF.Ln, [sst], [lnt], bias=EPS, scale=1.0 / D)
        rstd, rstt = f32_next()
        act(rstd[:, :T], lnv[:, :T], AF.Exp, [lnt], [rstt], scale=-0.5)
        for c in range(8):
            o, otk = dst_fn(c)
            stt(o, H[:, c, :T], pp[:, l_gcol + c:l_gcol + c + 1], rstd[:, :T], ALU.mult, ALU.mult,
                [(hn, c), "pp", rstt], otk)

    def tile_layer(l, T, cs, nch, ti, H, hn):
        pb = l * PPL
        PL = "dve" if ti < 2 else "pool"
        tag = f"t{ti}l{l}"
        ALLAR = [ar(i) for i in range(NAR)]
        xn = lambda c: arena[:, AR_XN + c, :T]
        rms_to(pb + PP_NM, T, lambda c: (xn(c), [ar(AR_XN + c)]), H, hn)
        XN = [ar(AR_XN + c) for c in range(8)]
        dbg("xn_" + tag, arena[:, AR_XN:AR_XN + 8, :T], XN)

        dma("sp", wgl[:, :], wg16_d[l], ["wg16"], ["wgl"], "wgl")
        sl, slt = next_slab(l, 0)
        ps, pst_ = ps_next()
        for kc in range(8):
            mm(ps[:16, :T], sl[:, kc, 0:16], xn(kc), kc == 0, kc == 7, [slt, ar(AR_XN + kc)], [pst_])
        act(lrT[:, :T], ps[:16, :T], AF.Identity, [pst_, "blr"], ["lrT"], bias=blr[:, l:l + 1])
        ps2, ps2t = ps_next()
        for c in range(nch):
            for kc in range(8):
                mm(ps2[:cs, c * 8:(c + 1) * 8], arena[:, AR_XN + kc, c * cs:(c + 1) * cs], sl[:, kc, 16:24],
                   kc == 0, kc == 7, [slt, ar(AR_XN + kc)], [ps2t])
        tt("dve", ifg[:cs, :nch, :], ps2[:cs, 0:nch * 8].rearrange("p (c e) -> p c e", e=8),
           pp[:cs, pb + PP_BIF:pb + PP_BIF + 8].unsqueeze(1).to_broadcast([cs, nch, 8]), ALU.add,
           [ps2t, "pp"], ["ifg"])
        act(iftmp[:cs, :nch, :], ifg[:cs, :nch, 4:8], AF.Exp, ["ifg"], ["iftmp"], scale=-1.0)
        act(lfn[:cs, :nch, :], iftmp[:cs, :nch, :], AF.Ln, ["iftmp"], ["lfn"], bias=1.0)

        E_h = {}

        def gla_gates(h):
            gps, gpt = ps_next()
            mm(gps[:, :T], wgl[:, h * 128:(h + 1) * 128], lrT[:, :T], True, True, ["wgl", "lrT"], [gpt])
            e1, e1t = f32_next()
            act(e1[:, :T], gps[:, :T], AF.Exp, [gpt, "nbg"], [e1t], bias=nbg[:, l * 4 + h:l * 4 + h + 1], scale=-1.0)
            l1, l1t = f32_next()
            act(l1[:, :T], e1[:, :T], AF.Ln, [e1t], [l1t], bias=1.0)
            Lc, Lct = f32_next()
            for c in range(nch):
                scan(Lc[:, c * cs:(c + 1) * cs], ones_f[:, :cs], l1[:, c * cs:(c + 1) * cs], [l1t, "ones_f"], [Lct])
            act(A_t[:, h, :nch], Lc[:, :T].rearrange("p (c s) -> p c s", s=cs)[:, :, cs - 1], AF.Exp,
                [Lct], [("A", h)], scale=-1.0 / TAU)
            Eb, Ebt = f32_next()
            act(Eb[:, :T], Lc[:, :T], AF.Exp, [Lct], [Ebt], scale=-1.0 / TAU, bias=LNS)
            En, Ent = f32_next()
            act(En[:, :T], Lc[:, :T], AF.Exp, [Lct], [Ent], scale=1.0 / TAU)
            E_h[h] = (Eb, Ebt, En, Ent)

        def mlstm_conv():
            cp(PL, mqk_pre[:, :, 0:3], qk_hist[l][:, :, :], [("qkh", l)], [("mqp", j) for j in range(8)])
            cp(PL, qk_hist[l][:, :, :], mqk_pre[:, :, T:T + 3], [("mqp", j) for j in range(8)], [("qkh", l)])
            for j0 in range(0, 8, 4):
                js = list(range(j0, j0 + 4))
                accs = []
                for j in js:
                    a_, at = f32_next()
                    accs.append((a_, at))
                    act(a_[:, :T], mqk_pre[:, j, 0:T], AF.Identity, [("mqp", j), "pp"], [at],
                        scale=pp[:, pb + PP_CW + j:pb + PP_CW + j + 1])
                for tap in (1, 2, 3):
                    for k_, j in enumerate(js):
                        a_, at = accs[k_]
                        stt(a_[:, :T], mqk_pre[:, j, tap:tap + T],
                            pp[:, pb + PP_CW + tap * 8 + j:pb + PP_CW + tap * 8 + j + 1], a_[:, :T], ALU.mult, ALU.add,
                            [("mqp", j), "pp", at], [at])
                for k_, j in enumerate(js):
                    a_, at = accs[k_]
                    act(mqk_pre[:, j, 3:3 + T], a_[:, :T], AF.Silu, [at, "pp", ("mqp", j)], [("mqp", j)],
                        bias=pp[:, pb + PP_CB + j:pb + PP_CB + j + 1])

        fc = 0
        for s in range(1, 13):
            sl, slt = next_slab(l, s)
            for j in range(4):
                if fc < 8 and fc % 2 == 0:
                    gla_gates(fc // 2)
                ps, pt = ps_next()
                for kc in range(8):
                    mm(ps[:, :T], sl[:, kc, j * 128:(j + 1) * 128], xn(kc), kc == 0, kc == 7,
                       [slt, ar(AR_XN + kc)], [pt])
                bcol = pp[:, pb + PP_BFM + fc:pb + PP_BFM + fc + 1]
                if fc < 8:
                    h, isk = fc // 2, fc % 2
                    Eb, Ebt, En, Ent = E_h[h]
                    if not isk:
                        stt(arena[:, AR_QE + h, :T], ps[:, :T], bcol, Eb[:, :T], ALU.add, ALU.mult,
                            [pt, "pp", Ebt], [ar(AR_QE + h)])
                    else:
                        stt(arena[:, AR_KD + h, :T], ps[:, :T], bcol, En[:, :T], ALU.add, ALU.mult,
                            [pt, "pp", Ent], [ar(AR_KD + h)])
                elif fc < 16:
                    jj = fc - 8
                    act(arena[:, AR_SR + jj, :T], ps[:, :T], AF.Silu, [pt, "pp"], [ar(AR_SR + jj)], bias=bcol)
                    act(arena[:, AR_SR + jj, :T], arena[:, AR_SR + jj, :T], AF.Identity, [ar(AR_SR + jj), "pp"],
                        [ar(AR_SR + jj)], scale=pp[:, pb + PP_GN + jj % 2:pb + PP_GN + jj % 2 + 1])
                elif fc < 24:
                    jj = fc - 16
                    act(mqk_pre[:, jj, 3:3 + T], ps[:, :T], AF.Identity, [pt, "pp"], [("mqp", jj)], bias=bcol)
                elif fc < 32:
                    jj = fc - 24
                    act(arena[:, AR_SO + jj, :T], ps[:, :T], AF.Sigmoid, [pt, "pp"], [ar(AR_SO + jj)], bias=bcol)
                    act(arena[:, AR_SO + jj, :T], arena[:, AR_SO + jj, :T], AF.Identity, [ar(AR_SO + jj), "pp"],
                        [ar(AR_SO + jj)], scale=pp[:, pb + PP_MN + jj:pb + PP_MN + jj + 1])
                elif fc < 40:
                    jj = fc - 32
                    act(arena[:, AR_SGA + jj, :T], ps[:, :T], AF.Sigmoid, [pt, "pp"], [ar(AR_SGA + jj)], bias=bcol)
                else:
                    jj = fc - 40
                    act(arena[:, AR_SGB + jj, :T], ps[:, :T], AF.Sigmoid, [pt, "pp"], [ar(AR_SGB + jj)], bias=bcol)
                fc += 1
                if fc == 24:
                    mlstm_conv()

        if ti == 0 and l == 0:
            dbg("sel", sel_bf[:, :], ["sel_bf"])
            dbg("brow", brow_bf[:, :], ["brow_bf"])
            dbg("ident", ident_bf[:, :], ["ident_bf"])
            dbg("triu", triu_bf[:, :], ["triu_bf"])
        dbg("lfn_" + tag, lfn[:cs, :nch, :], ["lfn"])
        dbg("ifg_" + tag, ifg[:cs, :nch, :], ["ifg"])
        dbg("A_" + tag, A_t[:, :, :nch], [("A", h) for h in range(4)])
        dbg("fm_" + tag, arena[:, 8:NAR, :T], ALLAR)
        for q in range(4):
            sl, slt = next_slab(l, 13 + q)
            dst = gv if q < 2 else mv
            dname = "gv" if q < 2 else "mv"
            half = q % 2
            for c in range(nch):
                ps, pt = ps_next()
                for kc in range(8):
                    mm(ps[:cs, :], arena[:, AR_XN + kc, c * cs:(c + 1) * cs], sl[:, kc, :], kc == 0, False,
                       [slt, ar(AR_XN + kc)], [pt])
                mm(ps[:cs, :], sel_bf[:, l * 128:l * 128 + cs], brow_bf[:, q * 512:(q + 1) * 512], False, True,
                   ["sel_bf", "brow_bf"], [pt])
                cp("dve", dst[:cs, c, half * 512:(half + 1) * 512], ps[:cs, :], [pt], [(dname, c)])

        mqa = lambda h, a, b: mqk_pre[:, h, 3 + a:3 + b]
        mka = lambda h, a, b: mqk_pre[:, 4 + h, 3 + a:3 + b]
        dbg("gv_" + tag, gv[:cs, :nch, :], [("gv", c) for c in range(nch)])
        dbg("mv_" + tag, mv[:cs, :nch, :], [("mv", c) for c in range(nch)])
        dbg("mqk_" + tag, mqk_pre[:, :, 3:3 + T], [("mqp", j) for j in range(8)])

        cp("act", S_bf[:, :, :], S_st[l][:, :, :], [("S", l, h) for h in range(4)], [("Sbf", h) for h in range(4)])
        cp("act", C_bf[:, :, :], C_st[l][:, :, :], [("C", l, h) for h in range(4)], [("Cbf", h) for h in range(4)])
        ones3 = ones_f[:, :].unsqueeze(1).to_broadcast([128, 4, 128])

        def nb_refresh():
            tt(PL, nb_bf[:, :].rearrange("p (h m) -> p h m", h=4), ones3,
               n_st[l][:, :].unsqueeze(2).to_broadcast([128, 4, 128]), ALU.mult, [("n", l), "ones_f"], ["nb_bf"])

        nb_refresh()
        ps_flush_auto()
        blk4 = lambda ap: ap[:, :].rearrange("p (h v t) -> p h v t", h=2, v=2)[:, :, :, :cs]
        AR4 = lambda base: [ar(base + i) for i in range(4)]

        def rms_chain(srcs, dst_base, cols, release):
            sq = []
            for b in range(2):
                q_, qt = b16_next()
                act(v4(q_, cs), v4(srcs[b][0], cs), AF.Square, [srcs[b][1]], [qt])
                sq.append((q_, qt))
            yield
            ss, sst = ps_alloc()
            for h in range(4):
                for vc in range(2):
                    blk = (h % 2) * 2 + vc
                    mm(ss[:, h * 128:h * 128 + cs], ones_bf[:, :], sq[h // 2][0][:, blk * 128:blk * 128 + cs],
                       vc == 0, vc == 1, ["ones_bf", sq[h // 2][1]], [sst])
            yield
            lnv, lnt = f32_next()
            act(v4(lnv, cs), v4(ss, cs), AF.Ln, [sst], [lnt], bias=EPS, scale=1.0 / 256.0)
            ps_release(sst)
            yield
            rs, rst = f32_next()
            act(v4(rs, cs), v4(lnv, cs), AF.Exp, [lnt], [rst], scale=-0.5)
            yield
            for b in range(2):
                o1, o1t = f32_next()
                tt("dve", blk4(o1), blk4(srcs[b][0]),
                   rs[:, :].rearrange("p (h t) -> p h t", h=4)[:, 2 * b:2 * b + 2, :cs].unsqueeze(2).to_broadcast(
                       [128, 2, 2, cs]), ALU.mult, [srcs[b][1], rst], [o1t])
                if release:
                    ps_release(release[b])
                yield
                dsta = arena[:, dst_base + 4 * b:dst_base + 4 * b + 4, cols[0]:cols[1]]
                tt(PL, dsta, v4(o1, cs), dsta, ALU.mult, [o1t] + AR4(dst_base + 4 * b), AR4(dst_base + 4 * b))
                yield

        def prep_gla(c):
            c0, c1 = c * cs, (c + 1) * cs
            sc, sct = ps_alloc()
            for h in range(4):
                mm(sc[:cs, h * 128:h * 128 + cs], arena[:, AR_KD + h, c0:c1], arena[:, AR_QE + h, c0:c1], True, True,
                   [ar(AR_KD + h), ar(AR_QE + h)], [sct])
            kdl, kdlt = b16_next()
            tt(PL, v4(kdl, cs), arena[:, AR_KD:AR_KD + 4, c0:c1], A_t[:, :, c:c + 1].to_broadcast([128, 4, cs]),
               ALU.mult, AR4(AR_KD) + [("A", h) for h in range(4)], [kdlt])
            yield
            tt("dve", v4(scmD[:cs, :], cs), v4(sc[:cs, :], cs), triu_bf[:cs, :cs].unsqueeze(1).to_broadcast([cs, 4, cs]),
               ALU.mult, [sct, "triu_bf"], ["scmD"])
            ps_release(sct)
            ktp, ktpt = ps_alloc()
            for h in range(4):
                mm(ktp[:cs, h * 128:(h + 1) * 128], kdl[:, h * 128:h * 128 + cs], ident_bf[:, :], True, True,
                   [kdlt, "ident_bf"], [ktpt])
            yield
            cp("act", kTD[:cs, :], ktp[:cs, :], [ktpt], ["kTD"])
            ps_release(ktpt)
            yield

        def prep_ml(c):
            c0, c1 = c * cs, (c + 1) * cs
            lfb, lfbt = f32_next()
            tt("dve", v4(lfb[:cs, :], 128), ones_f[:cs, :].unsqueeze(1).to_broadcast([cs, 4, 128]),
               lfn[:cs, c, :].unsqueeze(2).to_broadcast([cs, 4, 128]), ALU.mult, ["ones_f", "lfn"], [lfbt])
            yield
            fb, fbt = ps_alloc()
            for h in range(4):
                mm(fb[:, h * 128:h * 128 + cs], lfb[:cs, h * 128:(h + 1) * 128], triu_f[:cs, :cs], True, True,
                   [lfbt, "triu_f"], [fbt])
            fcp, fcpt = ps_alloc()
            mm(fcp[:cs, 0:4], triu_f[:cs, :cs], lfn[:cs, c, :], True, True, ["triu_f", "lfn"], [fcpt])
            sc, sct = ps_alloc()
            for h in range(4):
                mm(sc[:cs, h * 128:h * 128 + cs], mka(h, c0, c1), mqa(h, c0, c1), True, True,
                   [("mqp", 4 + h), ("mqp", h)], [sct])
            yield
            db, dbt = sm_next()
            stt(db[:cs, :], fcp[:cs, 0:4], LNS, ifg[:cs, c, 0:4], ALU.add, ALU.add, [fcpt, "ifg"], [dbt])
            ps_release(fcpt)
            EF, EFt = f32_next()
            act(v4(EF, cs), v4(fb, cs), AF.Exp, [fbt], [EFt], bias=LNS, scale=-1.0)
            act(GdD[:, :], fb[:, :].rearrange("p (h t) -> p h t", h=4)[:, :, cs - 1], AF.Exp, [fbt], ["GdD"], scale=-1.0)
            yield
            Dm, Dmt = f32_next()
            for h in range(4):
                act(Dm[:cs, h * 128:h * 128 + cs], fb[:cs, h * 128:h * 128 + cs], AF.Exp, [fbt, dbt], [Dmt],
                    bias=db[:cs, h:h + 1], scale=-1.0)
            wd, wdt = sm_next()
            tt("dve", wd[:cs, :], db[:cs, :], fb[:cs, :].rearrange("p (h t) -> p h t", h=4)[:, :, cs - 1], ALU.subtract,
               [dbt, fbt], [wdt])
            ps_release(fbt)
            tt(PL, v4(qfD, cs), mqk_pre[:, 0:4, 3 + c0:3 + c1], v4(EF, cs), ALU.mult,
               [("mqp", h) for h in range(4)] + [EFt], ["qfD"])
            yield
            tt(PL, v4(Dm[:cs, :], cs), v4(Dm[:cs, :], cs), triu_f[:cs, :cs].unsqueeze(1).to_broadcast([cs, 4, cs]),
               ALU.mult, [Dmt, "triu_f"], [Dmt])
            wst, wstt = sm_next()
            act(wst[:cs, :], wd[:cs, :], AF.Exp, [wdt], [wstt], bias=-LNS)
            ktp, ktpt = ps_alloc()
            for h in range(4):
                mm(ktp[:cs, h * 128:(h + 1) * 128], mka(h, c0, c1), ident_bf[:, :], True, True,
                   [("mqp", 4 + h), "ident_bf"], [ktpt])
            yield
            tt("dve", v4(qkDD[:cs, :], cs), v4(sc[:cs, :], cs), v4(Dm[:cs, :], cs), ALU.mult, [sct, Dmt], ["qkDD"])
            ps_release(sct)
            tt("dve", v4(kwD[:cs, :], 128), v4(ktp[:cs, :], 128), wst[:cs, :].unsqueeze(2).to_broadcast([cs, 4, 128]),
               ALU.mult, [ktpt, wstt], ["kwD"])
            ps_release(ktpt)
            yield

        def drain(*gens):
            gens = [g for g in gens if g is not None]
            while gens:
                for g in list(gens):
                    try:
                        next(g)
                    except StopIteration:
                        gens.remove(g)

        drain(prep_gla(0), prep_ml(0))
        for c in range(nch):
            c0, c1 = c * cs, (c + 1) * cs
            last = (c == nch - 1)
            ps_flush_auto()
            PG = [ps_alloc(), ps_alloc()]
            for h in range(4):
                mm(PG[h // 2][0][:, (h % 2) * 256:(h % 2) * 256 + 256], kTD[:cs, h * 128:(h + 1) * 128],
                   gv[:cs, c, h * 256:(h + 1) * 256], True, True, ["kTD", ("gv", c)], [PG[h // 2][1]])
            OG = [ps_alloc(), ps_alloc()]
            for h in range(4):
                for vc in range(2):
                    blk = (h % 2) * 2 + vc
                    dst = OG[h // 2][0][:, blk * 128:blk * 128 + cs]
                    mm(dst, gv[:cs, c, h * 256 + vc * 128:h * 256 + (vc + 1) * 128], scmD[:cs, h * 128:h * 128 + cs],
                       True, False, [("gv", c), "scmD"], [OG[h // 2][1]])
                    mm(dst, S_bf[:, h, vc * 128:(vc + 1) * 128], arena[:, AR_QE + h, c0:c1], False, True,
                       [("Sbf", h), ar(AR_QE + h)], [OG[h // 2][1]])
            for h in range(4):
                stt(S_st[l][:, h, :], S_st[l][:, h, :], A_t[:, h, c:c + 1],
                    PG[h // 2][0][:, (h % 2) * 256:(h % 2) * 256 + 256], ALU.mult, ALU.add,
                    [("S", l, h), ("A", h), PG[h // 2][1]], [("S", l, h)])
                if not last:
                    cp("act", S_bf[:, h, :], S_st[l][:, h, :], [("S", l, h)], [("Sbf", h)])
            ps_release(PG[0][1])
            ps_release(PG[1][1])
            PM = [ps_alloc(), ps_alloc()]
            for h in range(4):
                mm(PM[h // 2][0][:, (h % 2) * 256:(h % 2) * 256 + 256], kwD[:cs, h * 128:(h + 1) * 128],
                   mv[:cs, c, h * 256:(h + 1) * 256], True, True, ["kwD", ("mv", c)], [PM[h // 2][1]])
            npp, nppt = ps_alloc()
            for h in range(4):
                mm(npp[:, h:h + 1], kwD[:cs, h * 128:(h + 1) * 128], ones_bf[:cs, 0:1], True, True, ["kwD", "ones_bf"], [nppt])
            NM = [ps_alloc(), ps_alloc()]
            for h in range(4):
                for vc in range(2):
                    blk = (h % 2) * 2 + vc
                    dst = NM[h // 2][0][:, blk * 128:blk * 128 + cs]
                    mm(dst, mv[:cs, c, h * 256 + vc * 128:h * 256 + (vc + 1) * 128], qkDD[:cs, h * 128:h * 128 + cs],
                       True, False, [("mv", c), "qkDD"], [NM[h // 2][1]])
                    mm(dst, C_bf[:, h, vc * 128:(vc + 1) * 128], qfD[:, h * 128:h * 128 + cs], False, True,
                       [("Cbf", h), "qfD"], [NM[h // 2][1]])
            den, dent = ps_alloc()
            for h in range(4):
                mm(den[:, h * 128:h * 128 + cs], ones_bf[:cs, :], qkDD[:cs, h * 128:h * 128 + cs], True, False,
                   ["ones_bf", "qkDD"], [dent])
                mm(den[:, h * 128:h * 128 + cs], nb_bf[:, h * 128:(h + 1) * 128], qfD[:, h * 128:h * 128 + cs], False, True,
                   ["nb_bf", "qfD"], [dent])
            for h in range(4):
                stt(C_st[l][:, h, :], C_st[l][:, h, :], GdD[:, h:h + 1],
                    PM[h // 2][0][:, (h % 2) * 256:(h % 2) * 256 + 256], ALU.mult, ALU.add,
                    [("C", l, h), "GdD", PM[h // 2][1]], [("C", l, h)])
                if not last:
                    cp("act", C_bf[:, h, :], C_st[l][:, h, :], [("C", l, h)], [("Cbf", h)])
            ps_release(PM[0][1])
            ps_release(PM[1][1])
            ntmp, ntt = sm_next()
            tt("dve", ntmp[:, :], n_st[l][:, :], GdD[:, :], ALU.mult, [("n", l), "GdD"], [ntt])
            d0, d0t = f32_next()
            act(v4(d0, cs), v4(den, cs), AF.Abs, [dent], [d0t])
            ps_release(dent)
            tt("dve", n_st[l][:, :], npp[:, 0:4], ntmp[:, :], ALU.add, [nppt, ntt], [("n", l)])
            ps_release(nppt)
            if not last:
                nb_refresh()

            def ml_out():
                d1, d1t = f32_next()
                ts("dve", v4(d1, cs), v4(d0, cs), 1.0, ALU.max, [d0t], [d1t])
                yield
                rden, rdent = f32_next()
                recip(v4(rden, cs), v4(d1, cs), [d1t], [rdent])
                yield
                for b in range(2):
                    tt("dve", blk4(hbD[b]), blk4(NM[b][0]),
                       rden[:, :].rearrange("p (h t) -> p h t", h=4)[:, 2 * b:2 * b + 2, :cs].unsqueeze(2).to_broadcast(
                           [128, 2, 2, cs]), ALU.mult, [NM[b][1], rdent], [("hbD", b)])
                    ps_release(NM[b][1])
                    yield
                yield from rms_chain([(hbD[0], ("hbD", 0)), (hbD[1], ("hbD", 1))], AR_SO, (c0, c1), None)

            gG = rms_chain([(OG[0][0], OG[0][1]), (OG[1][0], OG[1][1])], AR_SR, (c0, c1), [OG[0][1], OG[1][1]])
            gM = ml_out()
            for _ in range(4):
                next(gG)
                next(gM)
            drain(gG, gM, prep_gla(c + 1) if not last else None, prep_ml(c + 1) if not last else None)
        ps_flush_auto()

        dbg("og_" + tag, arena[:, AR_SR:AR_SR + 8, :T], ALLAR)
        dbg("om_" + tag, arena[:, AR_SO:AR_SO + 8, :T], ALLAR)
        dbg("S_" + tag, S_st[l][:, :, :], [("S", l, h) for h in range(4)])
        dbg("C_" + tag, C_st[l][:, :, :], [("C", l, h) for h in range(4)])
        dbg("n_" + tag, n_st[l][:, :], [("n", l)])
        for br in range(2):
            src_base = AR_SR if br == 0 else AR_SO
            gate_base = AR_SGA if br == 0 else AR_SGB
            for half in range(2):
                sl, slt = next_slab(l, 17 + br * 2 + half)
                for dcl in range(4):
                    dc = half * 4 + dcl
                    ps, pt = ps_next()
                    for kc in range(8):
                        mm(ps[:, :T], sl[:, kc, dcl * 128:(dcl + 1) * 128], arena[:, src_base + kc, :T], kc == 0, kc == 7,
                           [slt, ar(src_base + kc)], [pt])
                    tt("dve", arena[:, gate_base + dc, :T], ps[:, :T], arena[:, gate_base + dc, :T], ALU.mult,
                       [pt, ar(gate_base + dc)], [ar(gate_base + dc)])
        for dc in range(8):
            tt(PL, arena[:, AR_SGA + dc, :T], arena[:, AR_SGA + dc, :T], arena[:, AR_SGB + dc, :T], ALU.add,
               [ar(AR_SGA + dc), ar(AR_SGB + dc)], [ar(AR_SGA + dc)])
        for half in range(2):
            sl, slt = next_slab(l, 21 + half)
            for dcl in range(4):
                dc = half * 4 + dcl
                ps, pt = ps_next()
                for kc in range(8):
                    mm(ps[:, :T], sl[:, kc, dcl * 128:(dcl + 1) * 128], arena[:, AR_SGA + kc, :T], kc == 0, kc == 7,
                       [slt, ar(AR_SGA + kc)], [pt])
                tt("dve", H[:, dc, :T], ps[:, :T], H[:, dc, :T], ALU.add, [pt, (hn, dc)], [(hn, dc)])

        dbg("hmix_" + tag, H[:, :, :T], [(hn, c) for c in range(8)])
        rms_to(pb + PP_NF, T, lambda c: (xn(c), [ar(AR_XN + c)]), H, hn)
        pend = None

        def ffn_tail(p):
            j_, accs_ = p
            ga, gat = b16_next()
            act(ga[:, :T], accs_[0][0][:, :T], AF.Silu, [accs_[0][1], "pp"], [gat],
                bias=pp[:, pb + PP_FB + j_:pb + PP_FB + j_ + 1])
            stt(arena[:, AR_AT + j_, :T], accs_[1][0][:, :T], pp[:, pb + PP_FB + 22 + j_:pb + PP_FB + 22 + j_ + 1],
                ga[:, :T], ALU.add, ALU.mult, [accs_[1][1], "pp", gat], [ar(AR_AT + j_)])

        for j in range(22):
            if j % 2 == 0:
                sl, slt = next_slab(l, 23 + j // 2)
            cb0 = (j % 2) * 256
            pg, pgt = ps_next()
            pu, put = ps_next()
            for kc in range(8):
                mm(pg[:, :T], sl[:, kc, cb0:cb0 + 128], xn(kc), kc == 0, kc == 7, [slt, ar(AR_XN + kc)], [pgt])
            for kc in range(8):
                mm(pu[:, :T], sl[:, kc, cb0 + 128:cb0 + 256], xn(kc), kc == 0, kc == 7, [slt, ar(AR_XN + kc)], [put])
            ui = ctr["ub"] % 2
            ctr["ub"] += 1
            ub = ubuf[ui]
            ubt = [("ub", ui, 0), ("ub", ui, 1)]
            cp(PL, ub[:, :, 0:2], u_hist[l][:, j, :, :], [("uh", l)], ubt)
            cp("act", ub[:, 0, 2:2 + T], pg[:, :T], [pgt], [ubt[0]])
            cp("act", ub[:, 1, 2:2 + T], pu[:, :T], [put], [ubt[1]])
            cp(PL, u_hist[l][:, j, :, :], ub[:, :, T:T + 2], ubt, [("uh", l)])
            accs = []
            for i in range(2):
                a_, at = f32_next()
                accs.append((a_, at))
                col = (j if i == 0 else 22 + j)
                act(a_[:, :T], ub[:, i, 0:T], AF.Identity, [ubt[i], "pp"], [at],
                    scale=pp[:, pb + PP_FW + col:pb + PP_FW + col + 1])
            if pend is not None:
                ffn_tail(pend)
            for tap in (1, 2):
                for i in range(2):
                    a_, at = accs[i]
                    col = (j if i == 0 else 22 + j)
                    stt(a_[:, :T], ub[:, i, tap:tap + T], pp[:, pb + PP_FW + tap * 44 + col:pb + PP_FW + tap * 44 + col + 1],
                        a_[:, :T], ALU.mult, ALU.add, [ubt[i], "pp", at], [at])
            pend = (j, accs)
        ffn_tail(pend)
        for dc in range(8):
            sl, slt = next_slab(l, 34 + dc)
            ps, pt = ps_next()
            for kc in range(22):
                mm(ps[:, :T], sl[:, kc, :], arena[:, AR_AT + kc, :T], kc == 0, kc == 21, [slt, ar(AR_AT + kc)], [pt])
            tt("dve", H[:, dc, :T], ps[:, :T], H[:, dc, :T], ALU.add, [pt, (hn, dc)], [(hn, dc)])
        dbg("at_" + tag, arena[:, AR_AT:AR_AT + 22, :T], ALLAR)
        dbg("hffn_" + tag, H[:, :, :T], [(hn, c) for c in range(8)])

    final_dma = []

    def load_tile(H, hn, T, src):
        HT = [(hn, c) for c in range(8)]
        if src[0] == "meta":
            dma("act", H[:, :, :T], metaT_d.rearrange("(c p) t -> p c t", p=128), (), HT, "xin")
        else:
            t0 = src[1]
            dma("act", H[:, :, :T], xT_d.rearrange("(c p) t -> p c t", p=128)[:, :, t0:t0 + T], (), HT, "xin")

    def store_tile(T, src):
        fcol = depth * PPL
        rms_to(fcol, T, lambda c: (hT[:, c, :T], [("hT", c)]), hT, "hT")
        t0 = src[1]
        HT = [("hT", c) for c in range(8)]
        return dma("act", outT_d.rearrange("(c p) t -> p c t", p=128)[:, :, t0:t0 + T], hT[:, :, :T], HT, (), "xout")

    Tm, csm, nchm, srcm = tiles[0]
    T1, cs1, nch1, src1 = tiles[1]
    load_tile(hTm, "hTm", Tm, srcm)
    load_tile(hT, "hT", T1, src1)
    for l in range(depth):
        tile_layer(l, Tm, csm, nchm, 0, hTm, "hTm")
        tile_layer(l, T1, cs1, nch1, 1, hT, "hT")
    final_dma = [store_tile(T1, src1)]
    for ti, (T, cs, nch, src) in enumerate(tiles):
        if ti < 2:
            continue
        load_tile(hT, "hT", T, src)
        for l in range(depth):
            tile_layer(l, T, cs, nch, ti, hT, "hT")
        final_dma = [store_tile(T, src)]
    P.emit(nc, es, {"act": final_dma + final_dbg})
    es.close()
    return nc


def _slab_img(Wblk):
    K, n = Wblk.shape
    kc = K // 128
    return np.ascontiguousarray(Wblk.reshape(kc, 128, n).transpose(1, 0, 2)).reshape(128, kc * n)


def pack_weights(depth, w_in, b_in, gla_w_gate, gla_b_gate, gla_norm, ml_conv_w, ml_conv_b, ml_norm,
                 w_branch_gla, w_branch_ml, w_out, norm_mix, norm_ffn, ffn_w_up, ffn_conv_w, ffn_conv_b,
                 ffn_w_down, norm_final):
    o_gq, o_gk, o_gv, o_gr, o_lr, o_mq, o_mk, o_mv, o_mo, o_mi, o_mf, o_ga, o_gb = (
        0, 512, 1024, 2048, 3072, 3088, 3600, 4112, 5136, 6160, 6164, 6168, 7192)
    fm_cols = []
    for h in range(4):
        fm_cols += list(range(o_gq + h * 128, o_gq + (h + 1) * 128))
        fm_cols += list(range(o_gk + h * 128, o_gk + (h + 1) * 128))
    fm_cols += list(range(o_gr, o_gr + 1024)) + list(range(o_mq, o_mq + 512)) + list(range(o_mk, o_mk + 512))
    fm_cols += list(range(o_mo, o_mo + 1024)) + list(range(o_ga, o_ga + 1024)) + list(range(o_gb, o_gb + 1024))
    fm_cols = np.array(fm_cols)
    small_cols = np.array(list(range(o_lr, o_lr + 16)) + list(range(o_mi, o_mi + 4)) + list(range(o_mf, o_mf + 4)))
    tm_cols = np.array(list(range(o_gv, o_gv + 1024)) + list(range(o_mv, o_mv + 1024)))
    up_cols = []
    for j in range(22):
        up_cols += list(range(j * 128, (j + 1) * 128)) + list(range(DFF + j * 128, DFF + (j + 1) * 128))
    up_cols = np.array(up_cols)

    w32 = np.zeros((depth, 128, PW), np.float32)
    pp = np.zeros((128, depth * PPL + 8), np.float32)
    p16 = np.zeros((16, depth * 512 + depth), np.float32)
    prow = np.zeros((4, 2048), np.float32)
    for l in range(depth):
        parts = [_slab_img(w_in[l][:, small_cols])]
        wf = w_in[l][:, fm_cols]
        for s in range(12):
            parts.append(_slab_img(wf[:, s * 512:(s + 1) * 512]))
        wt = w_in[l][:, tm_cols]
        for s in range(4):
            parts.append(_slab_img(wt[:, s * 512:(s + 1) * 512]))
        for W in (w_branch_gla[l], w_branch_ml[l], w_out[l]):
            for s in range(2):
                parts.append(_slab_img(W[:, s * 512:(s + 1) * 512]))
        wu = ffn_w_up[l][:, up_cols]
        for s in range(11):
            parts.append(_slab_img(wu[:, s * 512:(s + 1) * 512]))
        for dc in range(8):
            parts.append(_slab_img(ffn_w_down[l][:, dc * 128:(dc + 1) * 128]))
        w32[l] = np.concatenate(parts, axis=1)
        b = l * PPL
        pp[:, b + PP_NM:b + PP_NM + 8] = norm_mix[l].reshape(8, 128).T
        pp[:, b + PP_NF:b + PP_NF + 8] = norm_ffn[l].reshape(8, 128).T
        pp[:, b + PP_BFM:b + PP_BFM + 48] = b_in[l][fm_cols].reshape(48, 128).T
        pp[:, b + PP_NBG:b + PP_NBG + 4] = gla_b_gate[l].reshape(4, 128).T
        pp[:, b + PP_GN:b + PP_GN + 2] = gla_norm[l].reshape(2, 128).T
        pp[:, b + PP_MN:b + PP_MN + 8] = ml_norm[l].reshape(8, 128).T
        pp[:, b + PP_CW:b + PP_CW + 32] = ml_conv_w[l].reshape(4, 8, 128).transpose(2, 0, 1).reshape(128, 32)
        pp[:, b + PP_CB:b + PP_CB + 8] = ml_conv_b[l].reshape(8, 128).T
        pp[:, b + PP_FW:b + PP_FW + 132] = ffn_conv_w[l].reshape(3, 44, 128).transpose(2, 0, 1).reshape(128, 132)
        pp[:, b + PP_FB:b + PP_FB + 44] = ffn_conv_b[l].reshape(44, 128).T
        pp[:, b + PP_BIF:b + PP_BIF + 8] = np.broadcast_to(b_in[l][small_cols[16:24]][None, :], (128, 8))
        p16[:, l * 512:l * 512 + 512] = gla_w_gate[l]
        p16[:, depth * 512 + l] = b_in[l][small_cols[0:16]]
        prow[l, :] = b_in[l][tm_cols]
    pp[:, depth * PPL:depth * PPL + 8] = norm_final.reshape(8, 128).T
    hc = np.zeros((128, 768), np.float32)
    hc[:, 0:128] = np.eye(128, dtype=np.float32)
    hc[:, 128:256] = np.triu(np.ones((128, 128), np.float32))
    for l in range(4):
        hc[l, 256 + l * 128:256 + (l + 1) * 128] = 1.0
    return w32, pp, p16, prow, hc


def make_tiles(seq):
    tiles = [(NMETA, NMETA, 1, ("meta",))]
    for t0 in range(0, seq, 512):
        tiles.append((512, 128, 4, ("x", t0)))
    return tiles


def run_model(x, meta, depth, **w):
    B, seq, _ = x.shape
    w32, pp, p16, prow, hc = pack_weights(depth, **w)
    metaT = np.ascontiguousarray(np.asarray(meta, np.float32).T)
    nc = build_program(depth, make_tiles(seq), seq)
    in_maps = []
    for b in range(B):
        in_maps.append({"xT": np.ascontiguousarray(x[b].T), "metaT": metaT, "w32": w32, "pp": pp, "p16": p16,
                        "prow": prow, "hc": hc})
    res = run_bass_kernel_spmd(nc, in_maps, core_ids=list(range(B)))
    out = np.stack([np.ascontiguousarray(r["outT"].T) for r in res.results], axis=0)
    if DEBUG["on"]:
        DEBUG["res"] = res.results
    return out.astype(np.float32)


def kernel(x, meta, norm_mix, w_in, b_in, gla_w_gate, gla_b_gate, gla_norm, ml_conv_w, ml_conv_b, ml_norm,
           w_branch_gla, w_branch_ml, w_out, norm_ffn, ffn_w_up, ffn_conv_w, ffn_conv_b, ffn_w_down, norm_final):
    f = lambda a: np.asarray(a, np.float32)
    return run_model(f(x), f(meta), DEPTH, w_in=f(w_in), b_in=f(b_in), gla_w_gate=f(gla_w_gate),
                     gla_b_gate=f(gla_b_gate), gla_norm=f(gla_norm), ml_conv_w=f(ml_conv_w), ml_conv_b=f(ml_conv_b),
                     ml_norm=f(ml_norm), w_branch_gla=f(w_branch_gla), w_branch_ml=f(w_branch_ml), w_out=f(w_out),
                     norm_mix=f(norm_mix), norm_ffn=f(norm_ffn), ffn_w_up=f(ffn_w_up), ffn_conv_w=f(ffn_conv_w),
                     ffn_conv_b=f(ffn_conv_b), ffn_w_down=f(ffn_w_down), norm_final=f(norm_final))
```

```python
import math
from contextlib import ExitStack

import numpy as np
import concourse.bass as bass
import concourse.mybir as mybir
from concourse.bass_utils import run_bass_kernel_spmd

F32 = mybir.dt.float32
BF16 = mybir.dt.bfloat16
AF = mybir.ActivationFunctionType
ALU = mybir.AluOpType

D = 1024
DEPTH = 4
SEQ = 4096
NMETA = 16
DFF = 2816
NFC = 44
EPS = 1e-6
LNS = math.log(128.0 ** -0.5)
TAU = 16.0

PP_NM, PP_NF, PP_BFM, PP_NBG, PP_GN, PP_MN, PP_CW, PP_CB, PP_FW, PP_FB, PP_BIF = (
    0, 8, 16, 64, 68, 70, 78, 110, 118, 250, 294)
PPL = 302

SLABS = [(8, 24)] + [(8, 512)] * 12 + [(8, 512)] * 4 + [(8, 512)] * 6 + [(8, 512)] * 11 + [(22, 128)] * 8
SLAB_OFF = np.concatenate([[0], np.cumsum([k * n for k, n in SLABS])]).astype(int)
PW = int(SLAB_OFF[-1])
NSLAB = len(SLABS)
SLAB_MAX = 4096


class Prog:
    ENGS = ("pe", "act", "dve", "pool", "sp")
    SEM_LIMIT = 30000

    def __init__(self):
        self.ops = {e: [] for e in self.ENGS}
        self.state = {}
        self.know = {e: {} for e in self.ENGS}
        self.snap = {e: [] for e in self.ENGS}
        self.dsnap = {}
        self.dma_cnt = {}
        self.ring_gen = {}

    def _learn(self, eng, dep):
        k = self.know[eng]
        if dep[0] == "e":
            _, e2, pos = dep
            sn = self.snap[e2][pos - 1]
        else:
            sn = self.dsnap[dep]
        for key, v in sn.items():
            if k.get(key, 0) < v:
                k[key] = v
        key = (dep[0], dep[1])
        if k.get(key, 0) < dep[2]:
            k[key] = dep[2]

    RINGS = ("f32r", "b16r", "smr", "ps")

    def _norm(self, toks):
        out = []
        for t in toks:
            if isinstance(t, tuple) and len(t) == 3 and t[0] in self.RINGS:
                assert self.ring_gen.get(t[:2], 0) == t[2], f"stale ring buffer use {t} (current gen {self.ring_gen.get(t[:2])})"
                t = t[:2]
            out.append(t)
        return out

    def op(self, eng, fn, reads=(), writes=(), dma=None):
        reads = self._norm(reads)
        writes = self._norm(writes)
        extra = [t for t in reads if isinstance(t, tuple) and t[0] == "ps" and t not in writes]
        if extra:
            writes = list(writes) + extra
        deps = []
        for t in reads:
            st = self.state.get(t)
            if st is not None and st[0] is not None:
                deps.append((st[0], True))
        for t in writes:
            st = self.state.get(t)
            if st is not None:
                if st[0] is not None:
                    deps.append((st[0], False))
                for r in st[1].values():
                    deps.append((r, False))
        mypos = len(self.ops[eng]) + 1
        waits = []
        know = self.know[eng]
        for dep, is_raw in deps:
            key = (dep[0], dep[1])
            if dep[0] == "e" and dep[1] == eng:
                if eng in ("pe", "sp"):
                    continue
                if not is_raw or mypos - dep[2] > 1:
                    continue
            if know.get(key, 0) >= dep[2]:
                continue
            waits.append(dep)
            self._learn(eng, dep)
        best = {}
        for dpp in waits:
            key = (dpp[0], dpp[1])
            if key not in best or best[key][2] < dpp[2]:
                best[key] = dpp
        waits = list(best.values())
        for dpp in waits:
            if dpp[0] == "e":
                self.ops[dpp[1]][dpp[2] - 1]["sig"] = True
        rec = {"fn": fn, "waits": waits, "sig": False, "dma": None}
        self.ops[eng].append(rec)
        self.snap[eng].append(dict(know))
        if dma is not None:
            cnt = self.dma_cnt.get(dma, 0) + 16
            self.dma_cnt[dma] = cnt
            me = ("d", dma, cnt)
            rec["dma"] = dma
            self.dsnap[me] = dict(know)
        else:
            me = ("e", eng, mypos)
        for t in reads:
            st = self.state.setdefault(t, [None, {}])
            st[1][(me[0], me[1])] = me
        for t in writes:
            self.state[t] = [me, {}]
        return me

    def emit(self, nc, es, final_waits):
        semmap = {}
        nsem = {}
        for e in self.ENGS:
            cnt = 0
            idx = 0
            m = []
            for rec in self.ops[e]:
                if rec["sig"]:
                    if cnt >= self.SEM_LIMIT:
                        idx += 1
                        cnt = 0
                    cnt += 1
                m.append((idx, cnt))
            semmap[e] = m
            nsem[e] = idx + 1
        esems = {e: [es.enter_context(nc.semaphore(f"s_{e}{i}")) for i in range(nsem[e])] for e in self.ENGS}
        dsems = {k: es.enter_context(nc.semaphore(f"d_{k}")) for k in self.dma_cnt}
        block = es.enter_context(nc.Block())

        def run(engname, eh):
            m = semmap[engname]
            for i, rec in enumerate(self.ops[engname]):
                for dpp in rec["waits"]:
                    if dpp[0] == "e":
                        si, sv = semmap[dpp[1]][dpp[2] - 1]
                        eh.wait_ge(esems[dpp[1]][si], sv)
                    else:
                        eh.wait_ge(dsems[dpp[1]], dpp[2])
                ins = rec["fn"](eh)
                if rec["dma"] is not None:
                    ins.then_inc(dsems[rec["dma"]], 16)
                elif rec["sig"]:
                    ins.then_inc(esems[engname][m[i][0]], 1)
            for dpp in final_waits.get(engname, []):
                eh.wait_ge(dsems[dpp[1]], dpp[2])

        @block.tensor
        def _(t):
            run("pe", t)

        @block.scalar
        def _(a):
            run("act", a)

        @block.vector
        def _(v):
            run("dve", v)

        @block.gpsimd
        def _(g):
            run("pool", g)

        @block.sync
        def _(s):
            run("sp", s)


DEBUG = {"on": False, "names": []}


def build_program(depth, tiles, seq):
    nc = bass.Bass("TRN2", target_bir_lowering=False)
    DEBUG["names"] = []
    xT_d = nc.dram_tensor("xT", [D, seq], F32, kind="ExternalInput").ap()
    metaT_d = nc.dram_tensor("metaT", [D, NMETA], F32, kind="ExternalInput").ap()
    w32_d = nc.dram_tensor("w32", [depth, 128, PW], F32, kind="ExternalInput").ap()
    pp_d = nc.dram_tensor("pp", [128, depth * PPL + 8], F32, kind="ExternalInput").ap()
    p16_d = nc.dram_tensor("p16", [16, depth * 512 + depth], F32, kind="ExternalInput").ap()
    prow_d = nc.dram_tensor("prow", [4, 2048], F32, kind="ExternalInput").ap()
    hc_d = nc.dram_tensor("hc", [128, 768], F32, kind="ExternalInput").ap()
    outT_d = nc.dram_tensor("outT", [D, seq], F32, kind="ExternalOutput").ap()
    w16_d = nc.dram_tensor("w16", [depth, 128, PW], BF16, kind="Internal").ap()
    wg16_d = nc.dram_tensor("wg16", [depth, 16, 512], BF16, kind="Internal").ap()

    P = Prog()
    es = ExitStack()
    TM = 512

    def sb(name, shape, dt):
        return es.enter_context(nc.sbuf_tensor("sb_" + name, shape, dt))

    ident_bf = sb("ident_bf", [128, 128], BF16)
    ones_bf = sb("ones_bf", [128, 128], BF16)
    triu_bf = sb("triu_bf", [128, 128], BF16)
    triu_f = sb("triu_f", [128, 128], F32)
    ones_f = sb("ones_f", [128, 128], F32)
    pp = sb("pp", [128, depth * PPL + 8], F32)
    nbg = sb("nbg", [128, depth * 4], F32)
    wgl = sb("wgl", [16, 512], BF16)
    blr = sb("blr", [16, depth], F32)
    brow_bf = sb("brow_bf", [4, 2048], BF16)
    sel_bf = sb("sel_bf", [4, 512], BF16)
    hT = sb("hT", [128, 8, TM], F32)
    hTm = sb("hTm", [128, 8, NMETA], F32)
    NAR = 48
    arena = sb("arena", [128, NAR, TM], BF16)
    AR_XN, AR_SGA, AR_SR, AR_SO, AR_SGB, AR_QE, AR_KD = 0, 8, 16, 24, 32, 40, 44
    AR_AT = 16
    mqk_pre = sb("mqk_pre", [128, 8, TM + 3], BF16)
    NSB = 3
    slabbuf = [sb(f"slab{i}", [128, SLAB_MAX], BF16) for i in range(NSB)]
    S_st = [sb(f"S{l}", [128, 4, 256], F32) for l in range(depth)]
    C_st = [sb(f"C{l}", [128, 4, 256], F32) for l in range(depth)]
    n_st = [sb(f"n{l}", [128, 4], F32) for l in range(depth)]
    qk_hist = [sb(f"qkh{l}", [128, 8, 3], BF16) for l in range(depth)]
    u_hist = [sb(f"uh{l}", [128, 22, 2, 2], BF16) for l in range(depth)]
    S_bf = sb("S_bf", [128, 4, 256], BF16)
    C_bf = sb("C_bf", [128, 4, 256], BF16)
    nb_bf = sb("nb_bf", [128, 512], BF16)
    gv = sb("gv", [128, 4, 1024], BF16)
    mv = sb("mv", [128, 4, 1024], BF16)
    NF32 = 8
    f32r = [sb(f"f32r{i}", [128, 512], F32) for i in range(NF32)]
    NB16 = 6
    b16r = [sb(f"b16r{i}", [128, 512], BF16) for i in range(NB16)]
    ubuf = [sb(f"ubuf{i}", [128, 2, TM + 2], BF16) for i in range(2)]
    ifg = sb("ifg", [128, 4, 8], F32)
    lfn = sb("lfn", [128, 4, 4], F32)
    iftmp = sb("iftmp", [128, 4, 4], F32)
    A_t = sb("A_t", [128, 4, 4], F32)
    NSM = 8
    smr = [sb(f"smr{i}", [128, 4], F32) for i in range(NSM)]
    lrT = sb("lrT", [16, TM], BF16)
    scmD = sb("scmD", [128, 512], BF16)
    kTD = sb("kTD", [128, 512], BF16)
    qkDD = sb("qkDD", [128, 512], BF16)
    qfD = sb("qfD", [128, 512], BF16)
    kwD = sb("kwD", [128, 512], BF16)
    GdD = sb("GdD", [128, 4], F32)
    hbD = [sb(f"hbD{i}", [128, 512], BF16) for i in range(2)]
    NPS = 8
    psb = [es.enter_context(nc.psum_tensor(f"ps{i}", [128, 512], F32)) for i in range(NPS)]

    ctr = {"ps": 0, "f32": 0, "b16": 0, "sm": 0, "tp": 0, "ub": 0, "slab": 0}

    def _bump(name, i):
        g = P.ring_gen.get((name, i), 0) + 1
        P.ring_gen[(name, i)] = g
        return (name, i, g)

    ps_free_list = list(range(NPS))
    ps_live = {}

    def ps_next(auto=True):
        assert ps_free_list, "out of PSUM banks"
        i = ps_free_list.pop(0)
        tok = _bump("ps", i)
        ps_live[i] = tok
        if auto:
            ps_auto.append(i)
            while len(ps_auto) > PS_AUTO_DEPTH:
                j = ps_auto.pop(0)
                del ps_live[j]
                ps_free_list.append(j)
        return psb[i], tok

    ps_auto = []
    PS_AUTO_DEPTH = 7

    def ps_alloc():
        return ps_next(auto=False)

    def ps_release(tok):
        i = tok[1]
        assert ps_live.get(i) == tok, f"double free / stale psum {tok}"
        del ps_live[i]
        ps_free_list.append(i)

    def ps_flush_auto():
        while ps_auto:
            j = ps_auto.pop(0)
            del ps_live[j]
            ps_free_list.append(j)

    def f32_next():
        i = ctr["f32"] % NF32
        ctr["f32"] += 1
        return f32r[i], _bump("f32r", i)

    def b16_next():
        i = ctr["b16"] % NB16
        ctr["b16"] += 1
        return b16r[i], _bump("b16r", i)

    def sm_next():
        i = ctr["sm"] % NSM
        ctr["sm"] += 1
        return smr[i], _bump("smr", i)

    def ar(i):
        return ("ar", i)

    def mm(out, lhsT, rhs, start, stop, reads, writes):
        P.op("pe", lambda e: e.matmul(out, lhsT=lhsT, rhs=rhs, start=start, stop=stop), reads, writes)

    def tpose(out, in_, reads, writes):
        P.op("pe", lambda e: e.transpose(out, in_, ident_bf[:, :]), reads, writes)

    def act(out, in_, func, reads, writes, bias=None, scale=None):
        kw = {}
        if bias is not None:
            kw["bias"] = bias
        if scale is not None:
            kw["scale"] = scale
        P.op("act", lambda e: e.activation(out, in_, func, **kw), reads, writes)

    def tt(eng, out, in0, in1, op, reads, writes):
        P.op(eng, lambda e: e.tensor_tensor(out=out, in0=in0, in1=in1, op=op), reads, writes)

    def ts(eng, out, in0, s1, op0, reads, writes, s2=None, op1=None):
        if op1 is None:
            P.op(eng, lambda e: e.tensor_scalar(out=out, in0=in0, scalar1=s1, scalar2=None, op0=op0), reads, writes)
        else:
            P.op(eng, lambda e: e.tensor_scalar(out=out, in0=in0, scalar1=s1, scalar2=s2, op0=op0, op1=op1), reads, writes)

    def stt(out, in0, scalar, in1, op0, op1, reads, writes):
        P.op("dve", lambda e: e.scalar_tensor_tensor(out=out, in0=in0, scalar=scalar, in1=in1, op0=op0, op1=op1),
             reads, writes)

    def cp(eng, out, in_, reads, writes):
        if eng == "act":
            P.op(eng, lambda e: e.activation(out, in_, AF.Copy), reads, writes)
        else:
            P.op(eng, lambda e: e.tensor_copy(out=out, in_=in_), reads, writes)

    def scan(out, d0, d1, reads, writes):
        P.op("dve", lambda e: e.tensor_tensor_scan(out=out, data0=d0, data1=d1, initial=0.0, op0=ALU.mult,
                                                   op1=ALU.add), reads, writes)

    def recip(out, in_, reads, writes):
        P.op("dve", lambda e: e.reciprocal(out=out, in_=in_), reads, writes)

    def mset(eng, ap, val, writes):
        P.op(eng, lambda e: e.memset(ap, val), (), writes)

    def dma(eng, out, in_, reads, writes, sem, slow=False):
        if slow:
            return P.op(eng, lambda e: e.dma_start(out=out, in_=in_, allow_slow_non_contiguous=True), reads, writes,
                        dma=sem)
        return P.op(eng, lambda e: e.dma_start(out=out, in_=in_), reads, writes, dma=sem)

    def dbg(name, ap, reads):
        if not DEBUG["on"]:
            return
        shp = list(ap.shape)
        d = nc.dram_tensor("dbg_" + name, shp, ap.dtype, kind="ExternalOutput").ap()
        DEBUG["names"].append("dbg_" + name)
        final_dbg.append(dma("act", d, ap, reads, (), f"dbg{len(final_dbg)}", slow=True))

    final_dbg = []

    def v4(ap, n):
        return ap[:, :].rearrange("p (h t) -> p h t", h=4)[:, :, :n]

    W_PIECE = 8192
    GRP_END = [17, 23, NSLAB]

    def slab_grp(si):
        return 0 if si < 17 else (1 if si < 23 else 2)

    for l in range(depth):
        dma("pool", wg16_d[l], p16_d[:, l * 512:(l + 1) * 512], (), ["wg16"], "prewg")
    for l in range(depth):
        g0 = 0
        for g in range(3):
            a = int(SLAB_OFF[g0])
            end = int(SLAB_OFF[GRP_END[g]])
            while a < end:
                b = min(end, a + W_PIECE)
                dma("pool", w16_d[l, :, a:b], w32_d[l, :, a:b], (), [("w16", l, g)], f"pre{l}_{g}")
                a = b
            g0 = GRP_END[g]
    cst_toks = ["pp", "triu_f", "blr"] + [("hT", c) for c in range(6)]
    dma("act", pp[:, :], pp_d, (), ["pp"], "cst")
    dma("act", triu_f[:, :], hc_d[:, 128:256], (), ["triu_f"], "cst")
    dma("act", blr[:, :], p16_d[:, depth * 512:depth * 512 + depth], (), ["blr"], "cst", slow=True)
    dma("act", hT[:, 0, 0:128], hc_d[:, 0:128], (), [("hT", 0)], "cst")
    dma("act", hT[0:4, 1, 0:512], hc_d[0:4, 256:768], (), [("hT", 1)], "cst")
    for q in range(4):
        last_cst = dma("act", hT[0:4, 2 + q, 0:512], prow_d[:, q * 512:(q + 1) * 512], (), [("hT", 2 + q)], "cst")
    for tk in cst_toks:
        P.state[tk][0] = last_cst
    mset("dve", ones_bf[:, :], 1.0, ["ones_bf"])
    mset("dve", ones_f[:, :], 1.0, ["ones_f"])
    cp("dve", ident_bf[:, :], hT[:, 0, 0:128], [("hT", 0)], ["ident_bf"])
    cp("dve", triu_bf[:, :], triu_f[:, :], ["triu_f"], ["triu_bf"])
    cp("dve", sel_bf[:, :], hT[0:4, 1, 0:512], [("hT", 1)], ["sel_bf"])
    for q in range(4):
        cp("dve", brow_bf[:, q * 512:(q + 1) * 512], hT[0:4, 2 + q, 0:512], [("hT", 2 + q)], ["brow_bf"])
    for l in range(depth):
        ts("dve", nbg[:, l * 4:(l + 1) * 4], pp[:, l * PPL + PP_NBG:l * PPL + PP_NBG + 4], -1.0, ALU.mult,
           ["pp"], ["nbg"])
    for l in range(depth):
        for h in range(4):
            mset("dve", S_st[l][:, h, :], 0.0, [("S", l, h)])
            mset("dve", C_st[l][:, h, :], 0.0, [("C", l, h)])
        mset("dve", n_st[l][:, :], 0.0, [("n", l)])
        mset("dve", qk_hist[l][:, :, :], 0.0, [("qkh", l)])
        mset("dve", u_hist[l][:, :, :, :], 0.0, [("uh", l)])

    slab_seq = {"n": 0}

    def next_slab(l, s):
        i = slab_seq["n"] % NSB
        slab_seq["n"] += 1
        kc, ncols = SLABS[s]
        n = kc * ncols
        off = int(SLAB_OFF[s])
        dma("sp", slabbuf[i][:, 0:n], w16_d[l, :, off:off + n], [("w16", l, slab_grp(s))], [("slab", i)], f"sl{i}")
        return slabbuf[i][:, 0:n].rearrange("p (k n) -> p k n", k=kc), ("slab", i)

    def rms_to(l_gcol, T, dst_fn, H, hn):
        for c in range(8):
            act(arena[:, AR_SGA + c, :T], H[:, c, :T], AF.Square, [(hn, c)], [ar(AR_SGA + c)])
        ss, sst = ps_next()
        for c in range(8):
            mm(ss[:, :T], ones_bf[:, :], arena[:, AR_SGA + c, :T], c == 0, c == 7, ["ones_bf", ar(AR_SGA + c)], [sst])
        lnv, lnt = f32_next()
        act(lnv[:, :T], ss[:, :T], AF.Ln, [sst], [lnt], bias=EPS, scale=1.0 / D)
        rstd, rstt = f32_next()
        act(rstd[:, :T], lnv[:, :T], AF.Exp, [lnt], [rstt], scale=-0.5)
        for c in range(8):
            o, otk = dst_fn(c)
            stt(o, H[:, c, :T], pp[:, l_gcol + c:l_gcol + c + 1], rstd[:, :T], ALU.mult, ALU.mult,
                [(hn, c), "pp", rstt], otk)

    def tile_layer(l, T, cs, nch, ti, H, hn):
        pb = l * PPL
        PL = "dve" if ti < 2 else "pool"
        tag = f"t{ti}l{l}"
        ALLAR = [ar(i) for i in range(NAR)]
        xn = lambda c: arena[:, AR_XN + c, :T]
        rms_to(pb + PP_NM, T, lambda c: (xn(c), [ar(AR_XN + c)]), H, hn)
        XN = [ar(AR_XN + c) for c in range(8)]
        dbg("xn_" + tag, arena[:, AR_XN:AR_XN + 8, :T], XN)

        dma("sp", wgl[:, :], wg16_d[l], ["wg16"], ["wgl"], "wgl")
        sl, slt = next_slab(l, 0)
        ps, pst_ = ps_next()
        for kc in range(8):
            mm(ps[:16, :T], sl[:, kc, 0:16], xn(kc), kc == 0, kc == 7, [slt, ar(AR_XN + kc)], [pst_])
        act(lrT[:, :T], ps[:16, :T], AF.Identity, [pst_, "blr"], ["lrT"], bias=blr[:, l:l + 1])
        ps2, ps2t = ps_next()
        for c in range(nch):
            for kc in range(8):
                mm(ps2[:cs, c * 8:(c + 1) * 8], arena[:, AR_XN + kc, c * cs:(c + 1) * cs], sl[:, kc, 16:24],
                   kc == 0, kc == 7, [slt, ar(AR_XN + kc)], [ps2t])
        tt("dve", ifg[:cs, :nch, :], ps2[:cs, 0:nch * 8].rearrange("p (c e) -> p c e", e=8),
           pp[:cs, pb + PP_BIF:pb + PP_BIF + 8].unsqueeze(1).to_broadcast([cs, nch, 8]), ALU.add,
           [ps2t, "pp"], ["ifg"])
        act(iftmp[:cs, :nch, :], ifg[:cs, :nch, 4:8], AF.Exp, ["ifg"], ["iftmp"], scale=-1.0)
        act(lfn[:cs, :nch, :], iftmp[:cs, :nch, :], AF.Ln, ["iftmp"], ["lfn"], bias=1.0)

        E_h = {}

        def gla_gates(h):
            gps, gpt = ps_next()
            mm(gps[:, :T], wgl[:, h * 128:(h + 1) * 128], lrT[:, :T], True, True, ["wgl", "lrT"], [gpt])
            e1, e1t = f32_next()
            act(e1[:, :T], gps[:, :T], AF.Exp, [gpt, "nbg"], [e1t], bias=nbg[:, l * 4 + h:l * 4 + h + 1], scale=-1.0)
            l1, l1t = f32_next()
            act(l1[:, :T], e1[:, :T], AF.Ln, [e1t], [l1t], bias=1.0)
            Lc, Lct = f32_next()
            for c in range(nch):
                scan(Lc[:, c * cs:(c + 1) * cs], ones_f[:, :cs], l1[:, c * cs:(c + 1) * cs], [l1t, "ones_f"], [Lct])
            act(A_t[:, h, :nch], Lc[:, :T].rearrange("p (c s) -> p c s", s=cs)[:, :, cs - 1], AF.Exp,
                [Lct], [("A", h)], scale=-1.0 / TAU)
            Eb, Ebt = f32_next()
            act(Eb[:, :T], Lc[:, :T], AF.Exp, [Lct], [Ebt], scale=-1.0 / TAU, bias=LNS)
            En, Ent = f32_next()
            act(En[:, :T], Lc[:, :T], AF.Exp, [Lct], [Ent], scale=1.0 / TAU)
            E_h[h] = (Eb, Ebt, En, Ent)

        def mlstm_conv():
            cp(PL, mqk_pre[:, :, 0:3], qk_hist[l][:, :, :], [("qkh", l)], [("mqp", j) for j in range(8)])
            cp(PL, qk_hist[l][:, :, :], mqk_pre[:, :, T:T + 3], [("mqp", j) for j in range(8)], [("qkh", l)])
            for j0 in range(0, 8, 4):
                js = list(range(j0, j0 + 4))
                accs = []
                for j in js:
                    a_, at = f32_next()
                    accs.append((a_, at))
                    act(a_[:, :T], mqk_pre[:, j, 0:T], AF.Identity, [("mqp", j), "pp"], [at],
                        scale=pp[:, pb + PP_CW + j:pb + PP_CW + j + 1])
                for tap in (1, 2, 3):
                    for k_, j in enumerate(js):
                        a_, at = accs[k_]
                        stt(a_[:, :T], mqk_pre[:, j, tap:tap + T],
                            pp[:, pb + PP_CW + tap * 8 + j:pb + PP_CW + tap * 8 + j + 1], a_[:, :T], ALU.mult, ALU.add,
                            [("mqp", j), "pp", at], [at])
                for k_, j in enumerate(js):
                    a_, at = accs[k_]
                    act(mqk_pre[:, j, 3:3 + T], a_[:, :T], AF.Silu, [at, "pp", ("mqp", j)], [("mqp", j)],
                        bias=pp[:, pb + PP_CB + j:pb + PP_CB + j + 1])

        fc = 0
        for s in range(1, 13):
            sl, slt = next_slab(l, s)
            for j in range(4):
                if fc < 8 and fc % 2 == 0:
                    gla_gates(fc // 2)
                ps, pt = ps_next()
                for kc in range(8):
                    mm(ps[:, :T], sl[:, kc, j * 128:(j + 1) * 128], xn(kc), kc == 0, kc == 7,
                       [slt, ar(AR_XN + kc)], [pt])
                bcol = pp[:, pb + PP_BFM + fc:pb + PP_BFM + fc + 1]
                if fc < 8:
                    h, isk = fc // 2, fc % 2
                    Eb, Ebt, En, Ent = E_h[h]
                    if not isk:
                        stt(arena[:, AR_QE + h, :T], ps[:, :T], bcol, Eb[:, :T], ALU.add, ALU.mult,
                            [pt, "pp", Ebt], [ar(AR_QE + h)])
                    else:
                        stt(arena[:, AR_KD + h, :T], ps[:, :T], bcol, En[:, :T], ALU.add, ALU.mult,
                            [pt, "pp", Ent], [ar(AR_KD + h)])
                elif fc < 16:
                    jj = fc - 8
                    act(arena[:, AR_SR + jj, :T], ps[:, :T], AF.Silu, [pt, "pp"], [ar(AR_SR + jj)], bias=bcol)
                    act(arena[:, AR_SR + jj, :T], arena[:, AR_SR + jj, :T], AF.Identity, [ar(AR_SR + jj), "pp"],
                        [ar(AR_SR + jj)], scale=pp[:, pb + PP_GN + jj % 2:pb + PP_GN + jj % 2 + 1])
                elif fc < 24:
                    jj = fc - 16
                    act(mqk_pre[:, jj, 3:3 + T], ps[:, :T], AF.Identity, [pt, "pp"], [("mqp", jj)], bias=bcol)
                elif fc < 32:
                    jj = fc - 24
                    act(arena[:, AR_SO + jj, :T], ps[:, :T], AF.Sigmoid, [pt, "pp"], [ar(AR_SO + jj)], bias=bcol)
                    act(arena[:, AR_SO + jj, :T], arena[:, AR_SO + jj, :T], AF.Identity, [ar(AR_SO + jj), "pp"],
                        [ar(AR_SO + jj)], scale=pp[:, pb + PP_MN + jj:pb + PP_MN + jj + 1])
                elif fc < 40:
                    jj = fc - 32
                    act(arena[:, AR_SGA + jj, :T], ps[:, :T], AF.Sigmoid, [pt, "pp"], [ar(AR_SGA + jj)], bias=bcol)
                else:
                    jj = fc - 40
                    act(arena[:, AR_SGB + jj, :T], ps[:, :T], AF.Sigmoid, [pt, "pp"], [ar(AR_SGB + jj)], bias=bcol)
                fc += 1
                if fc == 24:
                    mlstm_conv()

        if ti == 0 and l == 0:
            dbg("sel", sel_bf[:, :], ["sel_bf"])
            dbg("brow", brow_bf[:, :], ["brow_bf"])
            dbg("ident", ident_bf[:, :], ["ident_bf"])
            dbg("triu", triu_bf[:, :], ["triu_bf"])
        dbg("lfn_" + tag, lfn[:cs, :nch, :], ["lfn"])
        dbg("ifg_" + tag, ifg[:cs, :nch, :], ["ifg"])
        dbg("A_" + tag, A_t[:, :, :nch], [("A", h) for h in range(4)])
        dbg("fm_" + tag, arena[:, 8:NAR, :T], ALLAR)
        for q in range(4):
            sl, slt = next_slab(l, 13 + q)
            dst = gv if q < 2 else mv
            dname = "gv" if q < 2 else "mv"
            half = q % 2
            for c in range(nch):
                ps, pt = ps_next()
                for kc in range(8):
                    mm(ps[:cs, :], arena[:, AR_XN + kc, c * cs:(c + 1) * cs], sl[:, kc, :], kc == 0, False,
                       [slt, ar(AR_XN + kc)], [pt])
                mm(ps[:cs, :], sel_bf[:, l * 128:l * 128 + cs], brow_bf[:, q * 512:(q + 1) * 512], False, True,
                   ["sel_bf", "brow_bf"], [pt])
                cp("dve", dst[:cs, c, half * 512:(half + 1) * 512], ps[:cs, :], [pt], [(dname, c)])

        mqa = lambda h, a, b: mqk_pre[:, h, 3 + a:3 + b]
        mka = lambda h, a, b: mqk_pre[:, 4 + h, 3 + a:3 + b]
        dbg("gv_" + tag, gv[:cs, :nch, :], [("gv", c) for c in range(nch)])
        dbg("mv_" + tag, mv[:cs, :nch, :], [("mv", c) for c in range(nch)])
        dbg("mqk_" + tag, mqk_pre[:, :, 3:3 + T], [("mqp", j) for j in range(8)])

        cp("act", S_bf[:, :, :], S_st[l][:, :, :], [("S", l, h) for h in range(4)], [("Sbf", h) for h in range(4)])
        cp("act", C_bf[:, :, :], C_st[l][:, :, :], [("C", l, h) for h in range(4)], [("Cbf", h) for h in range(4)])
        ones3 = ones_f[:, :].unsqueeze(1).to_broadcast([128, 4, 128])

        def nb_refresh():
            tt(PL, nb_bf[:, :].rearrange("p (h m) -> p h m", h=4), ones3,
               n_st[l][:, :].unsqueeze(2).to_broadcast([128, 4, 128]), ALU.mult, [("n", l), "ones_f"], ["nb_bf"])

        nb_refresh()
        ps_flush_auto()
        blk4 = lambda ap: ap[:, :].rearrange("p (h v t) -> p h v t", h=2, v=2)[:, :, :, :cs]
        AR4 = lambda base: [ar(base + i) for i in range(4)]

        def rms_chain(srcs, dst_base, cols, release):
            sq = []
            for b in range(2):
                q_, qt = b16_next()
                act(v4(q_, cs), v4(srcs[b][0], cs), AF.Square, [srcs[b][1]], [qt])
                sq.append((q_, qt))
            yield
            ss, sst = ps_alloc()
            for h in range(4):
                for vc in range(2):
                    blk = (h % 2) * 2 + vc
                    mm(ss[:, h * 128:h * 128 + cs], ones_bf[:, :], sq[h // 2][0][:, blk * 128:blk * 128 + cs],
                       vc == 0, vc == 1, ["ones_bf", sq[h // 2][1]], [sst])
            yield
            lnv, lnt = f32_next()
            act(v4(lnv, cs), v4(ss, cs), AF.Ln, [sst], [lnt], bias=EPS, scale=1.0 / 256.0)
            ps_release(sst)
            yield
            rs, rst = f32_next()
            act(v4(rs, cs), v4(lnv, cs), AF.Exp, [lnt], [rst], scale=-0.5)
            yield
            for b in range(2):
                o1, o1t = f32_next()
                tt("dve", blk4(o1), blk4(srcs[b][0]),
                   rs[:, :].rearrange("p (h t) -> p h t", h=4)[:, 2 * b:2 * b + 2, :cs].unsqueeze(2).to_broadcast(
                       [128, 2, 2, cs]), ALU.mult, [srcs[b][1], rst], [o1t])
                if release:
                    ps_release(release[b])
                yield
                dsta = arena[:, dst_base + 4 * b:dst_base + 4 * b + 4, cols[0]:cols[1]]
                tt(PL, dsta, v4(o1, cs), dsta, ALU.mult, [o1t] + AR4(dst_base + 4 * b), AR4(dst_base + 4 * b))
                yield

        def prep_gla(c):
            c0, c1 = c * cs, (c + 1) * cs
            sc, sct = ps_alloc()
            for h in range(4):
                mm(sc[:cs, h * 128:h * 128 + cs], arena[:, AR_KD + h, c0:c1], arena[:, AR_QE + h, c0:c1], True, True,
                   [ar(AR_KD + h), ar(AR_QE + h)], [sct])
            kdl, kdlt = b16_next()
            tt(PL, v4(kdl, cs), arena[:, AR_KD:AR_KD + 4, c0:c1], A_t[:, :, c:c + 1].to_broadcast([128, 4, cs]),
               ALU.mult, AR4(AR_KD) + [("A", h) for h in range(4)], [kdlt])
            yield
            tt("dve", v4(scmD[:cs, :], cs), v4(sc[:cs, :], cs), triu_bf[:cs, :cs].unsqueeze(1).to_broadcast([cs, 4, cs]),
               ALU.mult, [sct, "triu_bf"], ["scmD"])
            ps_release(sct)
            ktp, ktpt = ps_alloc()
            for h in range(4):
                mm(ktp[:cs, h * 128:(h + 1) * 128], kdl[:, h * 128:h * 128 + cs], ident_bf[:, :], True, True,
                   [kdlt, "ident_bf"], [ktpt])
            yield
            cp("act", kTD[:cs, :], ktp[:cs, :], [ktpt], ["kTD"])
            ps_release(ktpt)
            yield

        def prep_ml(c):
            c0, c1 = c * cs, (c + 1) * cs
            lfb, lfbt = f32_next()
            tt(PL, v4(lfb[:cs, :], 128), ones_f[:cs, :].unsqueeze(1).to_broadcast([cs, 4, 128]),
               lfn[:cs, c, :].unsqueeze(2).to_broadcast([cs, 4, 128]), ALU.mult, ["ones_f", "lfn"], [lfbt])
            yield
            fb, fbt = ps_alloc()
            for h in range(4):
                mm(fb[:, h * 128:h * 128 + cs], lfb[:cs, h * 128:(h + 1) * 128], triu_f[:cs, :cs], True, True,
                   [lfbt, "triu_f"], [fbt])
            fcp, fcpt = ps_alloc()
            mm(fcp[:cs, 0:4], triu_f[:cs, :cs], lfn[:cs, c, :], True, True, ["triu_f", "lfn"], [fcpt])
            sc, sct = ps_alloc()
            for h in range(4):
                mm(sc[:cs, h * 128:h * 128 + cs], mka(h, c0, c1), mqa(h, c0, c1), True, True,
                   [("mqp", 4 + h), ("mqp", h)], [sct])
            yield
            db, dbt = sm_next()
            stt(db[:cs, :], fcp[:cs, 0:4], LNS, ifg[:cs, c, 0:4], ALU.add, ALU.add, [fcpt, "ifg"], [dbt])
            ps_release(fcpt)
            EF, EFt = f32_next()
            act(v4(EF, cs), v4(fb, cs), AF.Exp, [fbt], [EFt], bias=LNS, scale=-1.0)
            act(GdD[:, :], fb[:, :].rearrange("p (h t) -> p h t", h=4)[:, :, cs - 1], AF.Exp, [fbt], ["GdD"], scale=-1.0)
            yield
            Dm, Dmt = f32_next()
            for h in range(4):
                act(Dm[:cs, h * 128:h * 128 + cs], fb[:cs, h * 128:h * 128 + cs], AF.Exp, [fbt, dbt], [Dmt],
                    bias=db[:cs, h:h + 1], scale=-1.0)
            wd, wdt = sm_next()
            tt("dve", wd[:cs, :], db[:cs, :], fb[:cs, :].rearrange("p (h t) -> p h t", h=4)[:, :, cs - 1], ALU.subtract,
               [dbt, fbt], [wdt])
            ps_release(fbt)
            tt(PL, v4(qfD, cs), mqk_pre[:, 0:4, 3 + c0:3 + c1], v4(EF, cs), ALU.mult,
               [("mqp", h) for h in range(4)] + [EFt], ["qfD"])
            yield
            tt(PL, v4(Dm[:cs, :], cs), v4(Dm[:cs, :], cs), triu_f[:cs, :cs].unsqueeze(1).to_broadcast([cs, 4, cs]),
               ALU.mult, [Dmt, "triu_f"], [Dmt])
            wst, wstt = sm_next()
            act(wst[:cs, :], wd[:cs, :], AF.Exp, [wdt], [wstt], bias=-LNS)
            ktp, ktpt = ps_alloc()
            for h in range(4):
                mm(ktp[:cs, h * 128:(h + 1) * 128], mka(h, c0, c1), ident_bf[:, :], True, True,
                   [("mqp", 4 + h), "ident_bf"], [ktpt])
            yield
            tt("dve", v4(qkDD[:cs, :], cs), v4(sc[:cs, :], cs), v4(Dm[:cs, :], cs), ALU.mult, [sct, Dmt], ["qkDD"])
            ps_release(sct)
            tt("dve", v4(kwD[:cs, :], 128), v4(ktp[:cs, :], 128), wst[:cs, :].unsqueeze(2).to_broadcast([cs, 4, 128]),
               ALU.mult, [ktpt, wstt], ["kwD"])
            ps_release(ktpt)
            yield

        def drain(*gens):
            gens = [g for g in gens if g is not None]
            while gens:
                for g in list(gens):
                    try:
                        next(g)
                    except StopIteration:
                        gens.remove(g)

        drain(prep_gla(0), prep_ml(0))
        for c in range(nch):
            c0, c1 = c * cs, (c + 1) * cs
            last = (c == nch - 1)
            ps_flush_auto()
            PG = [ps_alloc(), ps_alloc()]
            for h in range(4):
                mm(PG[h // 2][0][:, (h % 2) * 256:(h % 2) * 256 + 256], kTD[:cs, h * 128:(h + 1) * 128],
                   gv[:cs, c, h * 256:(h + 1) * 256], True, True, ["kTD", ("gv", c)], [PG[h // 2][1]])
            OG = [ps_alloc(), ps_alloc()]
            for h in range(4):
                for vc in range(2):
                    blk = (h % 2) * 2 + vc
                    dst = OG[h // 2][0][:, blk * 128:blk * 128 + cs]
                    mm(dst, gv[:cs, c, h * 256 + vc * 128:h * 256 + (vc + 1) * 128], scmD[:cs, h * 128:h * 128 + cs],
                       True, False, [("gv", c), "scmD"], [OG[h // 2][1]])
                    mm(dst, S_bf[:, h, vc * 128:(vc + 1) * 128], arena[:, AR_QE + h, c0:c1], False, True,
                       [("Sbf", h), ar(AR_QE + h)], [OG[h // 2][1]])
            for h in range(4):
                stt(S_st[l][:, h, :], S_st[l][:, h, :], A_t[:, h, c:c + 1],
                    PG[h // 2][0][:, (h % 2) * 256:(h % 2) * 256 + 256], ALU.mult, ALU.add,
                    [("S", l, h), ("A", h), PG[h // 2][1]], [("S", l, h)])
                if not last:
                    cp("act", S_bf[:, h, :], S_st[l][:, h, :], [("S", l, h)], [("Sbf", h)])
            ps_release(PG[0][1])
            ps_release(PG[1][1])
            PM = [ps_alloc(), ps_alloc()]
            for h in range(4):
                mm(PM[h // 2][0][:, (h % 2) * 256:(h % 2) * 256 + 256], kwD[:cs, h * 128:(h + 1) * 128],
                   mv[:cs, c, h * 256:(h + 1) * 256], True, True, ["kwD", ("mv", c)], [PM[h // 2][1]])
            npp, nppt = ps_alloc()
            for h in range(4):
                mm(npp[:, h:h + 1], kwD[:cs, h * 128:(h + 1) * 128], ones_bf[:cs, 0:1], True, True, ["kwD", "ones_bf"], [nppt])
            NM = [ps_alloc(), ps_alloc()]
            for h in range(4):
                for vc in range(2):
                    blk = (h % 2) * 2 + vc
                    dst = NM[h // 2][0][:, blk * 128:blk * 128 + cs]
                    mm(dst, mv[:cs, c, h * 256 + vc * 128:h * 256 + (vc + 1) * 128], qkDD[:cs, h * 128:h * 128 + cs],
                       True, False, [("mv", c), "qkDD"], [NM[h // 2][1]])
                    mm(dst, C_bf[:, h, vc * 128:(vc + 1) * 128], qfD[:, h * 128:h * 128 + cs], False, True,
                       [("Cbf", h), "qfD"], [NM[h // 2][1]])
            den, dent = ps_alloc()
            for h in range(4):
                mm(den[:, h * 128:h * 128 + cs], ones_bf[:cs, :], qkDD[:cs, h * 128:h * 128 + cs], True, False,
                   ["ones_bf", "qkDD"], [dent])
                mm(den[:, h * 128:h * 128 + cs], nb_bf[:, h * 128:(h + 1) * 128], qfD[:, h * 128:h * 128 + cs], False, True,
                   ["nb_bf", "qfD"], [dent])
            for h in range(4):
                stt(C_st[l][:, h, :], C_st[l][:, h, :], GdD[:, h:h + 1],
                    PM[h // 2][0][:, (h % 2) * 256:(h % 2) * 256 + 256], ALU.mult, ALU.add,
                    [("C", l, h), "GdD", PM[h // 2][1]], [("C", l, h)])
                if not last:
                    cp("act", C_bf[:, h, :], C_st[l][:, h, :], [("C", l, h)], [("Cbf", h)])
            ps_release(PM[0][1])
            ps_release(PM[1][1])
            ntmp, ntt = sm_next()
            tt("dve", ntmp[:, :], n_st[l][:, :], GdD[:, :], ALU.mult, [("n", l), "GdD"], [ntt])
            d0, d0t = f32_next()
            act(v4(d0, cs), v4(den, cs), AF.Abs, [dent], [d0t])
            ps_release(dent)
            tt("dve", n_st[l][:, :], npp[:, 0:4], ntmp[:, :], ALU.add, [nppt, ntt], [("n", l)])
            ps_release(nppt)
            if not last:
                nb_refresh()

            def ml_out():
                d1, d1t = f32_next()
                ts("dve", v4(d1, cs), v4(d0, cs), 1.0, ALU.max, [d0t], [d1t])
                yield
                lnd, lndt = f32_next()
                act(v4(lnd, cs), v4(d1, cs), AF.Ln, [d1t], [lndt])
                yield
                rden, rdent = f32_next()
                act(v4(rden, cs), v4(lnd, cs), AF.Exp, [lndt], [rdent], scale=-1.0)
                yield
                for b in range(2):
                    tt("dve", blk4(hbD[b]), blk4(NM[b][0]),
                       rden[:, :].rearrange("p (h t) -> p h t", h=4)[:, 2 * b:2 * b + 2, :cs].unsqueeze(2).to_broadcast(
                           [128, 2, 2, cs]), ALU.mult, [NM[b][1], rdent], [("hbD", b)])
                    ps_release(NM[b][1])
                    yield
                yield from rms_chain([(hbD[0], ("hbD", 0)), (hbD[1], ("hbD", 1))], AR_SO, (c0, c1), None)

            gG = rms_chain([(OG[0][0], OG[0][1]), (OG[1][0], OG[1][1])], AR_SR, (c0, c1), [OG[0][1], OG[1][1]])
            gM = ml_out()
            for _ in range(5):
                next(gG)
                next(gM)
            drain(gG, gM, prep_gla(c + 1) if not last else None, prep_ml(c + 1) if not last else None)
        ps_flush_auto()

        dbg("og_" + tag, arena[:, AR_SR:AR_SR + 8, :T], ALLAR)
        dbg("om_" + tag, arena[:, AR_SO:AR_SO + 8, :T], ALLAR)
        dbg("S_" + tag, S_st[l][:, :, :], [("S", l, h) for h in range(4)])
        dbg("C_" + tag, C_st[l][:, :, :], [("C", l, h) for h in range(4)])
        dbg("n_" + tag, n_st[l][:, :], [("n", l)])
        for br in range(2):
            src_base = AR_SR if br == 0 else AR_SO
            gate_base = AR_SGA if br == 0 else AR_SGB
            for half in range(2):
                sl, slt = next_slab(l, 17 + br * 2 + half)
                for dcl in range(4):
                    dc = half * 4 + dcl
                    ps, pt = ps_next()
                    for kc in range(8):
                        mm(ps[:, :T], sl[:, kc, dcl * 128:(dcl + 1) * 128], arena[:, src_base + kc, :T], kc == 0, kc == 7,
                           [slt, ar(src_base + kc)], [pt])
                    tt("dve", arena[:, gate_base + dc, :T], ps[:, :T], arena[:, gate_base + dc, :T], ALU.mult,
                       [pt, ar(gate_base + dc)], [ar(gate_base + dc)])
        for dc in range(8):
            tt(PL, arena[:, AR_SGA + dc, :T], arena[:, AR_SGA + dc, :T], arena[:, AR_SGB + dc, :T], ALU.add,
               [ar(AR_SGA + dc), ar(AR_SGB + dc)], [ar(AR_SGA + dc)])
        for half in range(2):
            sl, slt = next_slab(l, 21 + half)
            for dcl in range(4):
                dc = half * 4 + dcl
                ps, pt = ps_next()
                for kc in range(8):
                    mm(ps[:, :T], sl[:, kc, dcl * 128:(dcl + 1) * 128], arena[:, AR_SGA + kc, :T], kc == 0, kc == 7,
                       [slt, ar(AR_SGA + kc)], [pt])
                tt("dve", H[:, dc, :T], ps[:, :T], H[:, dc, :T], ALU.add, [pt, (hn, dc)], [(hn, dc)])

        dbg("hmix_" + tag, H[:, :, :T], [(hn, c) for c in range(8)])
        rms_to(pb + PP_NF, T, lambda c: (xn(c), [ar(AR_XN + c)]), H, hn)
        pend = None

        def ffn_tail(p):
            j_, accs_ = p
            ga, gat = b16_next()
            act(ga[:, :T], accs_[0][0][:, :T], AF.Silu, [accs_[0][1], "pp"], [gat],
                bias=pp[:, pb + PP_FB + j_:pb + PP_FB + j_ + 1])
            stt(arena[:, AR_AT + j_, :T], accs_[1][0][:, :T], pp[:, pb + PP_FB + 22 + j_:pb + PP_FB + 22 + j_ + 1],
                ga[:, :T], ALU.add, ALU.mult, [accs_[1][1], "pp", gat], [ar(AR_AT + j_)])

        for j in range(22):
            if j % 2 == 0:
                sl, slt = next_slab(l, 23 + j // 2)
            cb0 = (j % 2) * 256
            pg, pgt = ps_next()
            pu, put = ps_next()
            for kc in range(8):
                mm(pg[:, :T], sl[:, kc, cb0:cb0 + 128], xn(kc), kc == 0, kc == 7, [slt, ar(AR_XN + kc)], [pgt])
            for kc in range(8):
                mm(pu[:, :T], sl[:, kc, cb0 + 128:cb0 + 256], xn(kc), kc == 0, kc == 7, [slt, ar(AR_XN + kc)], [put])
            ui = ctr["ub"] % 2
            ctr["ub"] += 1
            ub = ubuf[ui]
            ubt = [("ub", ui, 0), ("ub", ui, 1)]
            cp(PL, ub[:, :, 0:2], u_hist[l][:, j, :, :], [("uh", l)], ubt)
            cp("act", ub[:, 0, 2:2 + T], pg[:, :T], [pgt], [ubt[0]])
            cp("act", ub[:, 1, 2:2 + T], pu[:, :T], [put], [ubt[1]])
            cp(PL, u_hist[l][:, j, :, :], ub[:, :, T:T + 2], ubt, [("uh", l)])
            accs = []
            for i in range(2):
                a_, at = f32_next()
                accs.append((a_, at))
                col = (j if i == 0 else 22 + j)
                act(a_[:, :T], ub[:, i, 0:T], AF.Identity, [ubt[i], "pp"], [at],
                    scale=pp[:, pb + PP_FW + col:pb + PP_FW + col + 1])
            if pend is not None:
                ffn_tail(pend)
            for tap in (1, 2):
                for i in range(2):
                    a_, at = accs[i]
                    col = (j if i == 0 else 22 + j)
                    stt(a_[:, :T], ub[:, i, tap:tap + T], pp[:, pb + PP_FW + tap * 44 + col:pb + PP_FW + tap * 44 + col + 1],
                        a_[:, :T], ALU.mult, ALU.add, [ubt[i], "pp", at], [at])
            pend = (j, accs)
        ffn_tail(pend)
        for dc in range(8):
            sl, slt = next_slab(l, 34 + dc)
            ps, pt = ps_next()
            for kc in range(22):
                mm(ps[:, :T], sl[:, kc, :], arena[:, AR_AT + kc, :T], kc == 0, kc == 21, [slt, ar(AR_AT + kc)], [pt])
            tt("dve", H[:, dc, :T], ps[:, :T], H[:, dc, :T], ALU.add, [pt, (hn, dc)], [(hn, dc)])
        dbg("at_" + tag, arena[:, AR_AT:AR_AT + 22, :T], ALLAR)
        dbg("hffn_" + tag, H[:, :, :T], [(hn, c) for c in range(8)])

    final_dma = []

    def load_tile(H, hn, T, src):
        HT = [(hn, c) for c in range(8)]
        if src[0] == "meta":
            dma("act", H[:, :, :T], metaT_d.rearrange("(c p) t -> p c t", p=128), (), HT, "xin")
        else:
            t0 = src[1]
            dma("act", H[:, :, :T], xT_d.rearrange("(c p) t -> p c t", p=128)[:, :, t0:t0 + T], (), HT, "xin")

    def store_tile(T, src):
        fcol = depth * PPL
        rms_to(fcol, T, lambda c: (hT[:, c, :T], [("hT", c)]), hT, "hT")
        t0 = src[1]
        HT = [("hT", c) for c in range(8)]
        return dma("act", outT_d.rearrange("(c p) t -> p c t", p=128)[:, :, t0:t0 + T], hT[:, :, :T], HT, (), "xout")

    Tm, csm, nchm, srcm = tiles[0]
    T1, cs1, nch1, src1 = tiles[1]
    load_tile(hTm, "hTm", Tm, srcm)
    load_tile(hT, "hT", T1, src1)
    for l in range(depth):
        tile_layer(l, Tm, csm, nchm, 0, hTm, "hTm")
        tile_layer(l, T1, cs1, nch1, 1, hT, "hT")
    final_dma = [store_tile(T1, src1)]
    for ti, (T, cs, nch, src) in enumerate(tiles):
        if ti < 2:
            continue
        load_tile(hT, "hT", T, src)
        for l in range(depth):
            tile_layer(l, T, cs, nch, ti, hT, "hT")
        final_dma = [store_tile(T, src)]
    P.emit(nc, es, {"act": final_dma + final_dbg})
    es.close()
    return nc


def _slab_img(Wblk):
    K, n = Wblk.shape
    kc = K // 128
    return np.ascontiguousarray(Wblk.reshape(kc, 128, n).transpose(1, 0, 2)).reshape(128, kc * n)


def pack_weights(depth, w_in, b_in, gla_w_gate, gla_b_gate, gla_norm, ml_conv_w, ml_conv_b, ml_norm,
                 w_branch_gla, w_branch_ml, w_out, norm_mix, norm_ffn, ffn_w_up, ffn_conv_w, ffn_conv_b,
                 ffn_w_down, norm_final):
    o_gq, o_gk, o_gv, o_gr, o_lr, o_mq, o_mk, o_mv, o_mo, o_mi, o_mf, o_ga, o_gb = (
        0, 512, 1024, 2048, 3072, 3088, 3600, 4112, 5136, 6160, 6164, 6168, 7192)
    fm_cols = []
    for h in range(4):
        fm_cols += list(range(o_gq + h * 128, o_gq + (h + 1) * 128))
        fm_cols += list(range(o_gk + h * 128, o_gk + (h + 1) * 128))
    fm_cols += list(range(o_gr, o_gr + 1024)) + list(range(o_mq, o_mq + 512)) + list(range(o_mk, o_mk + 512))
    fm_cols += list(range(o_mo, o_mo + 1024)) + list(range(o_ga, o_ga + 1024)) + list(range(o_gb, o_gb + 1024))
    fm_cols = np.array(fm_cols)
    small_cols = np.array(list(range(o_lr, o_lr + 16)) + list(range(o_mi, o_mi + 4)) + list(range(o_mf, o_mf + 4)))
    tm_cols = np.array(list(range(o_gv, o_gv + 1024)) + list(range(o_mv, o_mv + 1024)))
    up_cols = []
    for j in range(22):
        up_cols += list(range(j * 128, (j + 1) * 128)) + list(range(DFF + j * 128, DFF + (j + 1) * 128))
    up_cols = np.array(up_cols)

    w32 = np.zeros((depth, 128, PW), np.float32)
    pp = np.zeros((128, depth * PPL + 8), np.float32)
    p16 = np.zeros((16, depth * 512 + depth), np.float32)
    prow = np.zeros((4, 2048), np.float32)
    for l in range(depth):
        parts = [_slab_img(w_in[l][:, small_cols])]
        wf = w_in[l][:, fm_cols]
        for s in range(12):
            parts.append(_slab_img(wf[:, s * 512:(s + 1) * 512]))
        wt = w_in[l][:, tm_cols]
        for s in range(4):
            parts.append(_slab_img(wt[:, s * 512:(s + 1) * 512]))
        for W in (w_branch_gla[l], w_branch_ml[l], w_out[l]):
            for s in range(2):
                parts.append(_slab_img(W[:, s * 512:(s + 1) * 512]))
        wu = ffn_w_up[l][:, up_cols]
        for s in range(11):
            parts.append(_slab_img(wu[:, s * 512:(s + 1) * 512]))
        for dc in range(8):
            parts.append(_slab_img(ffn_w_down[l][:, dc * 128:(dc + 1) * 128]))
        w32[l] = np.concatenate(parts, axis=1)
        b = l * PPL
        pp[:, b + PP_NM:b + PP_NM + 8] = norm_mix[l].reshape(8, 128).T
        pp[:, b + PP_NF:b + PP_NF + 8] = norm_ffn[l].reshape(8, 128).T
        pp[:, b + PP_BFM:b + PP_BFM + 48] = b_in[l][fm_cols].reshape(48, 128).T
        pp[:, b + PP_NBG:b + PP_NBG + 4] = gla_b_gate[l].reshape(4, 128).T
        pp[:, b + PP_GN:b + PP_GN + 2] = gla_norm[l].reshape(2, 128).T
        pp[:, b + PP_MN:b + PP_MN + 8] = ml_norm[l].reshape(8, 128).T
        pp[:, b + PP_CW:b + PP_CW + 32] = ml_conv_w[l].reshape(4, 8, 128).transpose(2, 0, 1).reshape(128, 32)
        pp[:, b + PP_CB:b + PP_CB + 8] = ml_conv_b[l].reshape(8, 128).T
        pp[:, b + PP_FW:b + PP_FW + 132] = ffn_conv_w[l].reshape(3, 44, 128).transpose(2, 0, 1).reshape(128, 132)
        pp[:, b + PP_FB:b + PP_FB + 44] = ffn_conv_b[l].reshape(44, 128).T
        pp[:, b + PP_BIF:b + PP_BIF + 8] = np.broadcast_to(b_in[l][small_cols[16:24]][None, :], (128, 8))
        p16[:, l * 512:l * 512 + 512] = gla_w_gate[l]
        p16[:, depth * 512 + l] = b_in[l][small_cols[0:16]]
        prow[l, :] = b_in[l][tm_cols]
    pp[:, depth * PPL:depth * PPL + 8] = norm_final.reshape(8, 128).T
    hc = np.zeros((128, 768), np.float32)
    hc[:, 0:128] = np.eye(128, dtype=np.float32)
    hc[:, 128:256] = np.triu(np.ones((128, 128), np.float32))
    for l in range(4):
        hc[l, 256 + l * 128:256 + (l + 1) * 128] = 1.0
    return w32, pp, p16, prow, hc


def make_tiles(seq):
    tiles = [(NMETA, NMETA, 1, ("meta",))]
    for t0 in range(0, seq, 512):
        tiles.append((512, 128, 4, ("x", t0)))
    return tiles


def run_model(x, meta, depth, **w):
    B, seq, _ = x.shape
    w32, pp, p16, prow, hc = pack_weights(depth, **w)
    metaT = np.ascontiguousarray(np.asarray(meta, np.float32).T)
    nc = build_program(depth, make_tiles(seq), seq)
    in_maps = []
    for b in range(B):
        in_maps.append({"xT": np.ascontiguousarray(x[b].T), "metaT": metaT, "w32": w32, "pp": pp, "p16": p16,
                        "prow": prow, "hc": hc})
    res = run_bass_kernel_spmd(nc, in_maps, core_ids=list(range(B)))
    out = np.stack([np.ascontiguousarray(r["outT"].T) for r in res.results], axis=0)
    if DEBUG["on"]:
        DEBUG["res"] = res.results
    return out.astype(np.float32)


def kernel(x, meta, norm_mix, w_in, b_in, gla_w_gate, gla_b_gate, gla_norm, ml_conv_w, ml_conv_b, ml_norm,
           w_branch_gla, w_branch_ml, w_out, norm_ffn, ffn_w_up, ffn_conv_w, ffn_conv_b, ffn_w_down, norm_final):
    f = lambda a: np.asarray(a, np.float32)
    return run_model(f(x), f(meta), DEPTH, w_in=f(w_in), b_in=f(b_in), gla_w_gate=f(gla_w_gate),
                     gla_b_gate=f(gla_b_gate), gla_norm=f(gla_norm), ml_conv_w=f(ml_conv_w), ml_conv_b=f(ml_conv_b),
                     ml_norm=f(ml_norm), w_branch_gla=f(w_branch_gla), w_branch_ml=f(w_branch_ml), w_out=f(w_out),
                     norm_mix=f(norm_mix), norm_ffn=f(norm_ffn), ffn_w_up=f(ffn_w_up), ffn_conv_w=f(ffn_conv_w),
                     ffn_conv_b=f(ffn_conv_b), ffn_w_down=f(ffn_w_down), norm_final=f(norm_final))
```

```python
import math
from contextlib import ExitStack

import numpy as np
import concourse.bass as bass
import concourse.mybir as mybir
from concourse.bass_utils import run_bass_kernel_spmd

F32 = mybir.dt.float32
BF16 = mybir.dt.bfloat16
AF = mybir.ActivationFunctionType
ALU = mybir.AluOpType

D = 1024
DEPTH = 4
SEQ = 4096
NMETA = 16
DFF = 2816
NFC = 44
EPS = 1e-6
LNS = math.log(128.0 ** -0.5)
TAU = 16.0

PP_NM, PP_NF, PP_BFM, PP_NBG, PP_GN, PP_MN, PP_CW, PP_CB, PP_FW, PP_FB, PP_BIF = (
    0, 8, 16, 64, 68, 70, 78, 110, 118, 250, 294)
PPL = 302

SLABS = [(8, 24)] + [(8, 512)] * 12 + [(8, 512)] * 4 + [(8, 512)] * 6 + [(8, 512)] * 11 + [(22, 128)] * 8
SLAB_OFF = np.concatenate([[0], np.cumsum([k * n for k, n in SLABS])]).astype(int)
PW = int(SLAB_OFF[-1])
NSLAB = len(SLABS)
SLAB_MAX = 4096


class Prog:
    ENGS = ("pe", "act", "dve", "pool", "sp")
    SEM_LIMIT = 30000

    def __init__(self):
        self.ops = {e: [] for e in self.ENGS}
        self.state = {}
        self.know = {e: {} for e in self.ENGS}
        self.snap = {e: [] for e in self.ENGS}
        self.dsnap = {}
        self.dma_cnt = {}
        self.ring_gen = {}

    def _learn(self, eng, dep):
        k = self.know[eng]
        if dep[0] == "e":
            _, e2, pos = dep
            sn = self.snap[e2][pos - 1]
        else:
            sn = self.dsnap[dep]
        for key, v in sn.items():
            if k.get(key, 0) < v:
                k[key] = v
        key = (dep[0], dep[1])
        if k.get(key, 0) < dep[2]:
            k[key] = dep[2]

    RINGS = ("f32r", "b16r", "smr", "ps")

    def _norm(self, toks):
        out = []
        for t in toks:
            if isinstance(t, tuple) and len(t) == 3 and t[0] in self.RINGS:
                assert self.ring_gen.get(t[:2], 0) == t[2], f"stale ring buffer use {t} (current gen {self.ring_gen.get(t[:2])})"
                t = t[:2]
            out.append(t)
        return out

    def op(self, eng, fn, reads=(), writes=(), dma=None):
        reads = self._norm(reads)
        writes = self._norm(writes)
        extra = [t for t in reads if isinstance(t, tuple) and t[0] == "ps" and t not in writes]
        if extra:
            writes = list(writes) + extra
        deps = []
        for t in reads:
            st = self.state.get(t)
            if st is not None and st[0] is not None:
                deps.append((st[0], True))
        for t in writes:
            st = self.state.get(t)
            if st is not None:
                if st[0] is not None:
                    deps.append((st[0], False))
                for r in st[1].values():
                    deps.append((r, False))
        mypos = len(self.ops[eng]) + 1
        waits = []
        know = self.know[eng]
        for dep, is_raw in deps:
            key = (dep[0], dep[1])
            if dep[0] == "e" and dep[1] == eng:
                if eng in ("pe", "sp"):
                    continue
                if not is_raw or mypos - dep[2] > 1:
                    continue
            if know.get(key, 0) >= dep[2]:
                continue
            waits.append(dep)
            self._learn(eng, dep)
        best = {}
        for dpp in waits:
            key = (dpp[0], dpp[1])
            if key not in best or best[key][2] < dpp[2]:
                best[key] = dpp
        waits = list(best.values())
        for dpp in waits:
            if dpp[0] == "e":
                self.ops[dpp[1]][dpp[2] - 1]["sig"] = True
        rec = {"fn": fn, "waits": waits, "sig": False, "dma": None}
        self.ops[eng].append(rec)
        self.snap[eng].append(dict(know))
        if dma is not None:
            cnt = self.dma_cnt.get(dma, 0) + 16
            self.dma_cnt[dma] = cnt
            me = ("d", dma, cnt)
            rec["dma"] = dma
            self.dsnap[me] = dict(know)
        else:
            me = ("e", eng, mypos)
        for t in reads:
            st = self.state.setdefault(t, [None, {}])
            st[1][(me[0], me[1])] = me
        for t in writes:
            self.state[t] = [me, {}]
        return me

    def emit(self, nc, es, final_waits):
        semmap = {}
        nsem = {}
        for e in self.ENGS:
            cnt = 0
            idx = 0
            m = []
            for rec in self.ops[e]:
                if rec["sig"]:
                    if cnt >= self.SEM_LIMIT:
                        idx += 1
                        cnt = 0
                    cnt += 1
                m.append((idx, cnt))
            semmap[e] = m
            nsem[e] = idx + 1
        esems = {e: [es.enter_context(nc.semaphore(f"s_{e}{i}")) for i in range(nsem[e])] for e in self.ENGS}
        dsems = {k: es.enter_context(nc.semaphore(f"d_{k}")) for k in self.dma_cnt}
        block = es.enter_context(nc.Block())

        def run(engname, eh):
            m = semmap[engname]
            for i, rec in enumerate(self.ops[engname]):
                for dpp in rec["waits"]:
                    if dpp[0] == "e":
                        si, sv = semmap[dpp[1]][dpp[2] - 1]
                        eh.wait_ge(esems[dpp[1]][si], sv)
                    else:
                        eh.wait_ge(dsems[dpp[1]], dpp[2])
                ins = rec["fn"](eh)
                if rec["dma"] is not None:
                    ins.then_inc(dsems[rec["dma"]], 16)
                elif rec["sig"]:
                    ins.then_inc(esems[engname][m[i][0]], 1)
            for dpp in final_waits.get(engname, []):
                eh.wait_ge(dsems[dpp[1]], dpp[2])

        @block.tensor
        def _(t):
            run("pe", t)

        @block.scalar
        def _(a):
            run("act", a)

        @block.vector
        def _(v):
            run("dve", v)

        @block.gpsimd
        def _(g):
            run("pool", g)

        @block.sync
        def _(s):
            run("sp", s)


DEBUG = {"on": False, "names": []}


def build_program(depth, tiles, seq):
    nc = bass.Bass("TRN2", target_bir_lowering=False)
    DEBUG["names"] = []
    xT_d = nc.dram_tensor("xT", [D, seq], F32, kind="ExternalInput").ap()
    metaT_d = nc.dram_tensor("metaT", [D, NMETA], F32, kind="ExternalInput").ap()
    w32_d = nc.dram_tensor("w32", [depth, 128, PW], F32, kind="ExternalInput").ap()
    pp_d = nc.dram_tensor("pp", [128, depth * PPL + 8], F32, kind="ExternalInput").ap()
    p16_d = nc.dram_tensor("p16", [16, depth * 512 + depth], F32, kind="ExternalInput").ap()
    prow_d = nc.dram_tensor("prow", [4, 2048], F32, kind="ExternalInput").ap()
    hc_d = nc.dram_tensor("hc", [128, 768], F32, kind="ExternalInput").ap()
    outT_d = nc.dram_tensor("outT", [D, seq], F32, kind="ExternalOutput").ap()
    w16_d = nc.dram_tensor("w16", [depth, 128, PW], BF16, kind="Internal").ap()
    wg16_d = nc.dram_tensor("wg16", [depth, 16, 512], BF16, kind="Internal").ap()

    P = Prog()
    es = ExitStack()
    TM = 512

    def sb(name, shape, dt):
        return es.enter_context(nc.sbuf_tensor("sb_" + name, shape, dt))

    ident_bf = sb("ident_bf", [128, 128], BF16)
    ones_bf = sb("ones_bf", [128, 128], BF16)
    triu_bf = sb("triu_bf", [128, 128], BF16)
    triu_f = sb("triu_f", [128, 128], F32)
    ones_f = sb("ones_f", [128, 128], F32)
    pp = sb("pp", [128, depth * PPL + 8], F32)
    nbg = sb("nbg", [128, depth * 4], F32)
    wgl = sb("wgl", [16, 512], BF16)
    blr = sb("blr", [16, depth], F32)
    brow_bf = sb("brow_bf", [4, 2048], BF16)
    sel_bf = sb("sel_bf", [4, 512], BF16)
    hT = sb("hT", [128, 8, TM], F32)
    hTm = sb("hTm", [128, 8, NMETA], F32)
    NAR = 48
    arena = sb("arena", [128, NAR, TM], BF16)
    AR_XN, AR_SGA, AR_SR, AR_SO, AR_SGB, AR_QE, AR_KD = 0, 8, 16, 24, 32, 40, 44
    AR_AT = 16
    mqk_pre = sb("mqk_pre", [128, 8, TM + 3], BF16)
    NSB = 3
    slabbuf = [sb(f"slab{i}", [128, SLAB_MAX], BF16) for i in range(NSB)]
    S_st = [sb(f"S{l}", [128, 4, 256], F32) for l in range(depth)]
    C_st = [sb(f"C{l}", [128, 4, 256], F32) for l in range(depth)]
    n_st = [sb(f"n{l}", [128, 4], F32) for l in range(depth)]
    qk_hist = [sb(f"qkh{l}", [128, 8, 3], BF16) for l in range(depth)]
    u_hist = [sb(f"uh{l}", [128, 22, 2, 2], BF16) for l in range(depth)]
    S_bf = sb("S_bf", [128, 4, 256], BF16)
    C_bf = sb("C_bf", [128, 4, 256], BF16)
    nb_bf = sb("nb_bf", [128, 512], BF16)
    gv = sb("gv", [128, 4, 1024], BF16)
    mv = sb("mv", [128, 4, 1024], BF16)
    NF32 = 8
    f32r = [sb(f"f32r{i}", [128, 512], F32) for i in range(NF32)]
    NB16 = 6
    b16r = [sb(f"b16r{i}", [128, 512], BF16) for i in range(NB16)]
    ubuf = [sb(f"ubuf{i}", [128, 2, TM + 2], BF16) for i in range(2)]
    ifg = sb("ifg", [128, 4, 8], F32)
    lfn = sb("lfn", [128, 4, 4], F32)
    iftmp = sb("iftmp", [128, 4, 4], F32)
    A_t = sb("A_t", [128, 4, 4], F32)
    NSM = 8
    smr = [sb(f"smr{i}", [128, 4], F32) for i in range(NSM)]
    lrT = sb("lrT", [16, TM], BF16)
    scmD = sb("scmD", [128, 512], BF16)
    kTD = sb("kTD", [128, 512], BF16)
    qkDD = sb("qkDD", [128, 512], BF16)
    qfD = sb("qfD", [128, 512], BF16)
    kwD = sb("kwD", [128, 512], BF16)
    GdD = sb("GdD", [128, 4], F32)
    hbD = [sb(f"hbD{i}", [128, 512], BF16) for i in range(2)]
    NPS = 8
    psb = [es.enter_context(nc.psum_tensor(f"ps{i}", [128, 512], F32)) for i in range(NPS)]

    ctr = {"ps": 0, "f32": 0, "b16": 0, "sm": 0, "tp": 0, "ub": 0, "slab": 0}

    def _bump(name, i):
        g = P.ring_gen.get((name, i), 0) + 1
        P.ring_gen[(name, i)] = g
        return (name, i, g)

    ps_free_list = list(range(NPS))
    ps_live = {}

    def ps_next(auto=True):
        assert ps_free_list, "out of PSUM banks"
        i = ps_free_list.pop(0)
        tok = _bump("ps", i)
        ps_live[i] = tok
        if auto:
            ps_auto.append(i)
            while len(ps_auto) > PS_AUTO_DEPTH:
                j = ps_auto.pop(0)
                del ps_live[j]
                ps_free_list.append(j)
        return psb[i], tok

    ps_auto = []
    PS_AUTO_DEPTH = 7

    def ps_alloc():
        return ps_next(auto=False)

    def ps_release(tok):
        i = tok[1]
        assert ps_live.get(i) == tok, f"double free / stale psum {tok}"
        del ps_live[i]
        ps_free_list.append(i)

    def ps_flush_auto():
        while ps_auto:
            j = ps_auto.pop(0)
            del ps_live[j]
            ps_free_list.append(j)

    def f32_next():
        i = ctr["f32"] % NF32
        ctr["f32"] += 1
        return f32r[i], _bump("f32r", i)

    def b16_next():
        i = ctr["b16"] % NB16
        ctr["b16"] += 1
        return b16r[i], _bump("b16r", i)

    def sm_next():
        i = ctr["sm"] % NSM
        ctr["sm"] += 1
        return smr[i], _bump("smr", i)

    def ar(i):
        return ("ar", i)

    def mm(out, lhsT, rhs, start, stop, reads, writes):
        P.op("pe", lambda e: e.matmul(out, lhsT=lhsT, rhs=rhs, start=start, stop=stop), reads, writes)

    def tpose(out, in_, reads, writes):
        P.op("pe", lambda e: e.transpose(out, in_, ident_bf[:, :]), reads, writes)

    def act(out, in_, func, reads, writes, bias=None, scale=None):
        kw = {}
        if bias is not None:
            kw["bias"] = bias
        if scale is not None:
            kw["scale"] = scale
        P.op("act", lambda e: e.activation(out, in_, func, **kw), reads, writes)

    def tt(eng, out, in0, in1, op, reads, writes):
        P.op(eng, lambda e: e.tensor_tensor(out=out, in0=in0, in1=in1, op=op), reads, writes)

    def ts(eng, out, in0, s1, op0, reads, writes, s2=None, op1=None):
        if op1 is None:
            P.op(eng, lambda e: e.tensor_scalar(out=out, in0=in0, scalar1=s1, scalar2=None, op0=op0), reads, writes)
        else:
            P.op(eng, lambda e: e.tensor_scalar(out=out, in0=in0, scalar1=s1, scalar2=s2, op0=op0, op1=op1), reads, writes)

    def stt(out, in0, scalar, in1, op0, op1, reads, writes):
        P.op("dve", lambda e: e.scalar_tensor_tensor(out=out, in0=in0, scalar=scalar, in1=in1, op0=op0, op1=op1),
             reads, writes)

    def cp(eng, out, in_, reads, writes):
        if eng == "act":
            P.op(eng, lambda e: e.activation(out, in_, AF.Copy), reads, writes)
        else:
            P.op(eng, lambda e: e.tensor_copy(out=out, in_=in_), reads, writes)

    def scan(out, d0, d1, reads, writes):
        P.op("dve", lambda e: e.tensor_tensor_scan(out=out, data0=d0, data1=d1, initial=0.0, op0=ALU.mult,
                                                   op1=ALU.add), reads, writes)

    def recip(out, in_, reads, writes):
        P.op("dve", lambda e: e.reciprocal(out=out, in_=in_), reads, writes)

    def mset(eng, ap, val, writes):
        P.op(eng, lambda e: e.memset(ap, val), (), writes)

    def dma(eng, out, in_, reads, writes, sem, slow=False):
        if slow:
            return P.op(eng, lambda e: e.dma_start(out=out, in_=in_, allow_slow_non_contiguous=True), reads, writes,
                        dma=sem)
        return P.op(eng, lambda e: e.dma_start(out=out, in_=in_), reads, writes, dma=sem)

    def dbg(name, ap, reads):
        if not DEBUG["on"]:
            return
        shp = list(ap.shape)
        d = nc.dram_tensor("dbg_" + name, shp, ap.dtype, kind="ExternalOutput").ap()
        DEBUG["names"].append("dbg_" + name)
        final_dbg.append(dma("act", d, ap, reads, (), f"dbg{len(final_dbg)}", slow=True))

    final_dbg = []

    def v4(ap, n):
        return ap[:, :].rearrange("p (h t) -> p h t", h=4)[:, :, :n]

    W_PIECE = 8192
    GRP_END = [5, 17, 23, NSLAB]

    def slab_grp(si):
        return 0 if si < 5 else (1 if si < 17 else (2 if si < 23 else 3))

    for l in range(depth):
        dma("pool", wg16_d[l], p16_d[:, l * 512:(l + 1) * 512], (), ["wg16"], "prewg")
    for l in range(depth):
        g0 = 0
        for g in range(4):
            a = int(SLAB_OFF[g0])
            end = int(SLAB_OFF[GRP_END[g]])
            while a < end:
                b = min(end, a + W_PIECE)
                dma("pool", w16_d[l, :, a:b], w32_d[l, :, a:b], (), [("w16", l, g)], f"pre{l}_{g}")
                a = b
            g0 = GRP_END[g]
    cst_toks = ["pp", "triu_f", "blr"] + [("hT", c) for c in range(6)]
    dma("act", pp[:, :], pp_d, (), ["pp"], "cst")
    dma("act", triu_f[:, :], hc_d[:, 128:256], (), ["triu_f"], "cst")
    dma("act", blr[:, :], p16_d[:, depth * 512:depth * 512 + depth], (), ["blr"], "cst", slow=True)
    dma("act", hT[:, 0, 0:128], hc_d[:, 0:128], (), [("hT", 0)], "cst")
    dma("act", hT[0:4, 1, 0:512], hc_d[0:4, 256:768], (), [("hT", 1)], "cst")
    for q in range(4):
        last_cst = dma("act", hT[0:4, 2 + q, 0:512], prow_d[:, q * 512:(q + 1) * 512], (), [("hT", 2 + q)], "cst")
    for tk in cst_toks:
        P.state[tk][0] = last_cst
    mset("dve", ones_bf[:, :], 1.0, ["ones_bf"])
    mset("dve", ones_f[:, :], 1.0, ["ones_f"])
    cp("dve", ident_bf[:, :], hT[:, 0, 0:128], [("hT", 0)], ["ident_bf"])
    cp("dve", triu_bf[:, :], triu_f[:, :], ["triu_f"], ["triu_bf"])
    cp("dve", sel_bf[:, :], hT[0:4, 1, 0:512], [("hT", 1)], ["sel_bf"])
    for q in range(4):
        cp("dve", brow_bf[:, q * 512:(q + 1) * 512], hT[0:4, 2 + q, 0:512], [("hT", 2 + q)], ["brow_bf"])
    for l in range(depth):
        ts("dve", nbg[:, l * 4:(l + 1) * 4], pp[:, l * PPL + PP_NBG:l * PPL + PP_NBG + 4], -1.0, ALU.mult,
           ["pp"], ["nbg"])
    for l in range(depth):
        for h in range(4):
            mset("dve", S_st[l][:, h, :], 0.0, [("S", l, h)])
            mset("dve", C_st[l][:, h, :], 0.0, [("C", l, h)])
        mset("dve", n_st[l][:, :], 0.0, [("n", l)])
        mset("dve", qk_hist[l][:, :, :], 0.0, [("qkh", l)])
        mset("dve", u_hist[l][:, :, :, :], 0.0, [("uh", l)])

    slab_seq = {"n": 0}

    def next_slab(l, s):
        i = slab_seq["n"] % NSB
        slab_seq["n"] += 1
        kc, ncols = SLABS[s]
        n = kc * ncols
        off = int(SLAB_OFF[s])
        dma("sp", slabbuf[i][:, 0:n], w16_d[l, :, off:off + n], [("w16", l, slab_grp(s))], [("slab", i)], f"sl{i}")
        return slabbuf[i][:, 0:n].rearrange("p (k n) -> p k n", k=kc), ("slab", i)

    def rms_to(l_gcol, T, dst_fn, H, hn):
        for c in range(8):
            act(arena[:, AR_SGA + c, :T], H[:, c, :T], AF.Square, [(hn, c)], [ar(AR_SGA + c)])
        ss, sst = ps_next()
        for c in range(8):
            mm(ss[:, :T], ones_bf[:, :], arena[:, AR_SGA + c, :T], c == 0, c == 7, ["ones_bf", ar(AR_SGA + c)], [sst])
        lnv, lnt = f32_next()
        act(lnv[:, :T], ss[:, :T], AF.Ln, [sst], [lnt], bias=EPS, scale=1.0 / D)
        rstd, rstt = f32_next()
        act(rstd[:, :T], lnv[:, :T], AF.Exp, [lnt], [rstt], scale=-0.5)
        for c in range(8):
            o, otk = dst_fn(c)
            stt(o, H[:, c, :T], pp[:, l_gcol + c:l_gcol + c + 1], rstd[:, :T], ALU.mult, ALU.mult,
                [(hn, c), "pp", rstt], otk)

    def tile_layer(l, T, cs, nch, ti, H, hn):
        pb = l * PPL
        PL = "dve" if ti < 2 else "pool"
        tag = f"t{ti}l{l}"
        ALLAR = [ar(i) for i in range(NAR)]
        xn = lambda c: arena[:, AR_XN + c, :T]
        rms_to(pb + PP_NM, T, lambda c: (xn(c), [ar(AR_XN + c)]), H, hn)
        XN = [ar(AR_XN + c) for c in range(8)]
        dbg("xn_" + tag, arena[:, AR_XN:AR_XN + 8, :T], XN)

        dma("sp", wgl[:, :], wg16_d[l], ["wg16"], ["wgl"], "wgl")
        sl, slt = next_slab(l, 0)
        ps, pst_ = ps_next()
        for kc in range(8):
            mm(ps[:16, :T], sl[:, kc, 0:16], xn(kc), kc == 0, kc == 7, [slt, ar(AR_XN + kc)], [pst_])
        act(lrT[:, :T], ps[:16, :T], AF.Identity, [pst_, "blr"], ["lrT"], bias=blr[:, l:l + 1])
        ps2, ps2t = ps_next()
        for c in range(nch):
            for kc in range(8):
                mm(ps2[:cs, c * 8:(c + 1) * 8], arena[:, AR_XN + kc, c * cs:(c + 1) * cs], sl[:, kc, 16:24],
                   kc == 0, kc == 7, [slt, ar(AR_XN + kc)], [ps2t])
        tt("dve", ifg[:cs, :nch, :], ps2[:cs, 0:nch * 8].rearrange("p (c e) -> p c e", e=8),
           pp[:cs, pb + PP_BIF:pb + PP_BIF + 8].unsqueeze(1).to_broadcast([cs, nch, 8]), ALU.add,
           [ps2t, "pp"], ["ifg"])
        act(iftmp[:cs, :nch, :], ifg[:cs, :nch, 4:8], AF.Exp, ["ifg"], ["iftmp"], scale=-1.0)
        act(lfn[:cs, :nch, :], iftmp[:cs, :nch, :], AF.Ln, ["iftmp"], ["lfn"], bias=1.0)

        E_h = {}

        def gla_gates(h):
            gps, gpt = ps_next()
            mm(gps[:, :T], wgl[:, h * 128:(h + 1) * 128], lrT[:, :T], True, True, ["wgl", "lrT"], [gpt])
            e1, e1t = f32_next()
            act(e1[:, :T], gps[:, :T], AF.Exp, [gpt, "nbg"], [e1t], bias=nbg[:, l * 4 + h:l * 4 + h + 1], scale=-1.0)
            l1, l1t = f32_next()
            act(l1[:, :T], e1[:, :T], AF.Ln, [e1t], [l1t], bias=1.0)
            Lc, Lct = f32_next()
            for c in range(nch):
                scan(Lc[:, c * cs:(c + 1) * cs], ones_f[:, :cs], l1[:, c * cs:(c + 1) * cs], [l1t, "ones_f"], [Lct])
            act(A_t[:, h, :nch], Lc[:, :T].rearrange("p (c s) -> p c s", s=cs)[:, :, cs - 1], AF.Exp,
                [Lct], [("A", h)], scale=-1.0 / TAU)
            Eb, Ebt = f32_next()
            act(Eb[:, :T], Lc[:, :T], AF.Exp, [Lct], [Ebt], scale=-1.0 / TAU, bias=LNS)
            En, Ent = f32_next()
            act(En[:, :T], Lc[:, :T], AF.Exp, [Lct], [Ent], scale=1.0 / TAU)
            E_h[h] = (Eb, Ebt, En, Ent)

        def mlstm_conv():
            cp(PL, mqk_pre[:, :, 0:3], qk_hist[l][:, :, :], [("qkh", l)], [("mqp", j) for j in range(8)])
            cp(PL, qk_hist[l][:, :, :], mqk_pre[:, :, T:T + 3], [("mqp", j) for j in range(8)], [("qkh", l)])
            for j0 in range(0, 8, 4):
                js = list(range(j0, j0 + 4))
                accs = []
                for j in js:
                    a_, at = f32_next()
                    accs.append((a_, at))
                    act(a_[:, :T], mqk_pre[:, j, 0:T], AF.Identity, [("mqp", j), "pp"], [at],
                        scale=pp[:, pb + PP_CW + j:pb + PP_CW + j + 1])
                for tap in (1, 2, 3):
                    for k_, j in enumerate(js):
                        a_, at = accs[k_]
                        stt(a_[:, :T], mqk_pre[:, j, tap:tap + T],
                            pp[:, pb + PP_CW + tap * 8 + j:pb + PP_CW + tap * 8 + j + 1], a_[:, :T], ALU.mult, ALU.add,
                            [("mqp", j), "pp", at], [at])
                for k_, j in enumerate(js):
                    a_, at = accs[k_]
                    act(mqk_pre[:, j, 3:3 + T], a_[:, :T], AF.Silu, [at, "pp", ("mqp", j)], [("mqp", j)],
                        bias=pp[:, pb + PP_CB + j:pb + PP_CB + j + 1])

        fc = 0
        for s in range(1, 13):
            sl, slt = next_slab(l, s)
            for j in range(4):
                if fc < 8 and fc % 2 == 0:
                    gla_gates(fc // 2)
                ps, pt = ps_next()
                for kc in range(8):
                    mm(ps[:, :T], sl[:, kc, j * 128:(j + 1) * 128], xn(kc), kc == 0, kc == 7,
                       [slt, ar(AR_XN + kc)], [pt])
                bcol = pp[:, pb + PP_BFM + fc:pb + PP_BFM + fc + 1]
                if fc < 8:
                    h, isk = fc // 2, fc % 2
                    Eb, Ebt, En, Ent = E_h[h]
                    if not isk:
                        stt(arena[:, AR_QE + h, :T], ps[:, :T], bcol, Eb[:, :T], ALU.add, ALU.mult,
                            [pt, "pp", Ebt], [ar(AR_QE + h)])
                    else:
                        stt(arena[:, AR_KD + h, :T], ps[:, :T], bcol, En[:, :T], ALU.add, ALU.mult,
                            [pt, "pp", Ent], [ar(AR_KD + h)])
                elif fc < 16:
                    jj = fc - 8
                    act(arena[:, AR_SR + jj, :T], ps[:, :T], AF.Silu, [pt, "pp"], [ar(AR_SR + jj)], bias=bcol)
                    act(arena[:, AR_SR + jj, :T], arena[:, AR_SR + jj, :T], AF.Identity, [ar(AR_SR + jj), "pp"],
                        [ar(AR_SR + jj)], scale=pp[:, pb + PP_GN + jj % 2:pb + PP_GN + jj % 2 + 1])
                elif fc < 24:
                    jj = fc - 16
                    act(mqk_pre[:, jj, 3:3 + T], ps[:, :T], AF.Identity, [pt, "pp"], [("mqp", jj)], bias=bcol)
                elif fc < 32:
                    jj = fc - 24
                    act(arena[:, AR_SO + jj, :T], ps[:, :T], AF.Sigmoid, [pt, "pp"], [ar(AR_SO + jj)], bias=bcol)
                    act(arena[:, AR_SO + jj, :T], arena[:, AR_SO + jj, :T], AF.Identity, [ar(AR_SO + jj), "pp"],
                        [ar(AR_SO + jj)], scale=pp[:, pb + PP_MN + jj:pb + PP_MN + jj + 1])
                elif fc < 40:
                    jj = fc - 32
                    act(arena[:, AR_SGA + jj, :T], ps[:, :T], AF.Sigmoid, [pt, "pp"], [ar(AR_SGA + jj)], bias=bcol)
                else:
                    jj = fc - 40
                    act(arena[:, AR_SGB + jj, :T], ps[:, :T], AF.Sigmoid, [pt, "pp"], [ar(AR_SGB + jj)], bias=bcol)
                fc += 1
                if fc == 24:
                    mlstm_conv()

        if ti == 0 and l == 0:
            dbg("sel", sel_bf[:, :], ["sel_bf"])
            dbg("brow", brow_bf[:, :], ["brow_bf"])
            dbg("ident", ident_bf[:, :], ["ident_bf"])
            dbg("triu", triu_bf[:, :], ["triu_bf"])
        dbg("lfn_" + tag, lfn[:cs, :nch, :], ["lfn"])
        dbg("ifg_" + tag, ifg[:cs, :nch, :], ["ifg"])
        dbg("A_" + tag, A_t[:, :, :nch], [("A", h) for h in range(4)])
        dbg("fm_" + tag, arena[:, 8:NAR, :T], ALLAR)
        for q in range(4):
            sl, slt = next_slab(l, 13 + q)
            dst = gv if q < 2 else mv
            dname = "gv" if q < 2 else "mv"
            half = q % 2
            for c in range(nch):
                ps, pt = ps_next()
                for kc in range(8):
                    mm(ps[:cs, :], arena[:, AR_XN + kc, c * cs:(c + 1) * cs], sl[:, kc, :], kc == 0, False,
                       [slt, ar(AR_XN + kc)], [pt])
                mm(ps[:cs, :], sel_bf[:, l * 128:l * 128 + cs], brow_bf[:, q * 512:(q + 1) * 512], False, True,
                   ["sel_bf", "brow_bf"], [pt])
                cp("dve", dst[:cs, c, half * 512:(half + 1) * 512], ps[:cs, :], [pt], [(dname, c)])

        mqa = lambda h, a, b: mqk_pre[:, h, 3 + a:3 + b]
        mka = lambda h, a, b: mqk_pre[:, 4 + h, 3 + a:3 + b]
        dbg("gv_" + tag, gv[:cs, :nch, :], [("gv", c) for c in range(nch)])
        dbg("mv_" + tag, mv[:cs, :nch, :], [("mv", c) for c in range(nch)])
        dbg("mqk_" + tag, mqk_pre[:, :, 3:3 + T], [("mqp", j) for j in range(8)])

        cp("act", S_bf[:, :, :], S_st[l][:, :, :], [("S", l, h) for h in range(4)], [("Sbf", h) for h in range(4)])
        cp("act", C_bf[:, :, :], C_st[l][:, :, :], [("C", l, h) for h in range(4)], [("Cbf", h) for h in range(4)])
        ones3 = ones_f[:, :].unsqueeze(1).to_broadcast([128, 4, 128])

        def nb_refresh():
            tt(PL, nb_bf[:, :].rearrange("p (h m) -> p h m", h=4), ones3,
               n_st[l][:, :].unsqueeze(2).to_broadcast([128, 4, 128]), ALU.mult, [("n", l), "ones_f"], ["nb_bf"])

        nb_refresh()
        ps_flush_auto()
        blk4 = lambda ap: ap[:, :].rearrange("p (h v t) -> p h v t", h=2, v=2)[:, :, :, :cs]
        AR4 = lambda base: [ar(base + i) for i in range(4)]

        def rms_chain(srcs, dst_base, cols, release):
            sq = []
            for b in range(2):
                q_, qt = b16_next()
                act(v4(q_, cs), v4(srcs[b][0], cs), AF.Square, [srcs[b][1]], [qt])
                sq.append((q_, qt))
            yield
            ss, sst = ps_alloc()
            for h in range(4):
                for vc in range(2):
                    blk = (h % 2) * 2 + vc
                    mm(ss[:, h * 128:h * 128 + cs], ones_bf[:, :], sq[h // 2][0][:, blk * 128:blk * 128 + cs],
                       vc == 0, vc == 1, ["ones_bf", sq[h // 2][1]], [sst])
            yield
            lnv, lnt = f32_next()
            act(v4(lnv, cs), v4(ss, cs), AF.Ln, [sst], [lnt], bias=EPS, scale=1.0 / 256.0)
            ps_release(sst)
            yield
            rs, rst = f32_next()
            act(v4(rs, cs), v4(lnv, cs), AF.Exp, [lnt], [rst], scale=-0.5)
            yield
            for b in range(2):
                o1, o1t = f32_next()
                tt("dve", blk4(o1), blk4(srcs[b][0]),
                   rs[:, :].rearrange("p (h t) -> p h t", h=4)[:, 2 * b:2 * b + 2, :cs].unsqueeze(2).to_broadcast(
                       [128, 2, 2, cs]), ALU.mult, [srcs[b][1], rst], [o1t])
                if release:
                    ps_release(release[b])
                yield
                dsta = arena[:, dst_base + 4 * b:dst_base + 4 * b + 4, cols[0]:cols[1]]
                tt(PL, dsta, v4(o1, cs), dsta, ALU.mult, [o1t] + AR4(dst_base + 4 * b), AR4(dst_base + 4 * b))
                yield

        def prep_gla(c):
            c0, c1 = c * cs, (c + 1) * cs
            sc, sct = ps_alloc()
            for h in range(4):
                mm(sc[:cs, h * 128:h * 128 + cs], arena[:, AR_KD + h, c0:c1], arena[:, AR_QE + h, c0:c1], True, True,
                   [ar(AR_KD + h), ar(AR_QE + h)], [sct])
            kdl, kdlt = b16_next()
            tt(PL, v4(kdl, cs), arena[:, AR_KD:AR_KD + 4, c0:c1], A_t[:, :, c:c + 1].to_broadcast([128, 4, cs]),
               ALU.mult, AR4(AR_KD) + [("A", h) for h in range(4)], [kdlt])
            yield
            tt("dve", v4(scmD[:cs, :], cs), v4(sc[:cs, :], cs), triu_bf[:cs, :cs].unsqueeze(1).to_broadcast([cs, 4, cs]),
               ALU.mult, [sct, "triu_bf"], ["scmD"])
            ps_release(sct)
            ktp, ktpt = ps_alloc()
            for h in range(4):
                mm(ktp[:cs, h * 128:(h + 1) * 128], kdl[:, h * 128:h * 128 + cs], ident_bf[:, :], True, True,
                   [kdlt, "ident_bf"], [ktpt])
            yield
            cp("act", kTD[:cs, :], ktp[:cs, :], [ktpt], ["kTD"])
            ps_release(ktpt)
            yield

        def prep_ml(c):
            c0, c1 = c * cs, (c + 1) * cs
            lfb, lfbt = f32_next()
            tt(PL, v4(lfb[:cs, :], 128), ones_f[:cs, :].unsqueeze(1).to_broadcast([cs, 4, 128]),
               lfn[:cs, c, :].unsqueeze(2).to_broadcast([cs, 4, 128]), ALU.mult, ["ones_f", "lfn"], [lfbt])
            yield
            fb, fbt = ps_alloc()
            for h in range(4):
                mm(fb[:, h * 128:h * 128 + cs], lfb[:cs, h * 128:(h + 1) * 128], triu_f[:cs, :cs], True, True,
                   [lfbt, "triu_f"], [fbt])
            fcp, fcpt = ps_alloc()
            mm(fcp[:cs, 0:4], triu_f[:cs, :cs], lfn[:cs, c, :], True, True, ["triu_f", "lfn"], [fcpt])
            sc, sct = ps_alloc()
            for h in range(4):
                mm(sc[:cs, h * 128:h * 128 + cs], mka(h, c0, c1), mqa(h, c0, c1), True, True,
                   [("mqp", 4 + h), ("mqp", h)], [sct])
            yield
            db, dbt = sm_next()
            stt(db[:cs, :], fcp[:cs, 0:4], LNS, ifg[:cs, c, 0:4], ALU.add, ALU.add, [fcpt, "ifg"], [dbt])
            ps_release(fcpt)
            EF, EFt = f32_next()
            act(v4(EF, cs), v4(fb, cs), AF.Exp, [fbt], [EFt], bias=LNS, scale=-1.0)
            act(GdD[:, :], fb[:, :].rearrange("p (h t) -> p h t", h=4)[:, :, cs - 1], AF.Exp, [fbt], ["GdD"], scale=-1.0)
            yield
            Dm, Dmt = f32_next()
            for h in range(4):
                act(Dm[:cs, h * 128:h * 128 + cs], fb[:cs, h * 128:h * 128 + cs], AF.Exp, [fbt, dbt], [Dmt],
                    bias=db[:cs, h:h + 1], scale=-1.0)
            wd, wdt = sm_next()
            tt("dve", wd[:cs, :], db[:cs, :], fb[:cs, :].rearrange("p (h t) -> p h t", h=4)[:, :, cs - 1], ALU.subtract,
               [dbt, fbt], [wdt])
            ps_release(fbt)
            tt(PL, v4(qfD, cs), mqk_pre[:, 0:4, 3 + c0:3 + c1], v4(EF, cs), ALU.mult,
               [("mqp", h) for h in range(4)] + [EFt], ["qfD"])
            yield
            tt(PL, v4(Dm[:cs, :], cs), v4(Dm[:cs, :], cs), triu_f[:cs, :cs].unsqueeze(1).to_broadcast([cs, 4, cs]),
               ALU.mult, [Dmt, "triu_f"], [Dmt])
            wst, wstt = sm_next()
            act(wst[:cs, :], wd[:cs, :], AF.Exp, [wdt], [wstt], bias=-LNS)
            ktp, ktpt = ps_alloc()
            for h in range(4):
                mm(ktp[:cs, h * 128:(h + 1) * 128], mka(h, c0, c1), ident_bf[:, :], True, True,
                   [("mqp", 4 + h), "ident_bf"], [ktpt])
            yield
            tt("dve", v4(qkDD[:cs, :], cs), v4(sc[:cs, :], cs), v4(Dm[:cs, :], cs), ALU.mult, [sct, Dmt], ["qkDD"])
            ps_release(sct)
            tt("dve", v4(kwD[:cs, :], 128), v4(ktp[:cs, :], 128), wst[:cs, :].unsqueeze(2).to_broadcast([cs, 4, 128]),
               ALU.mult, [ktpt, wstt], ["kwD"])
            ps_release(ktpt)
            yield

        def drain(*gens):
            gens = [g for g in gens if g is not None]
            while gens:
                for g in list(gens):
                    try:
                        next(g)
                    except StopIteration:
                        gens.remove(g)

        drain(prep_gla(0), prep_ml(0))
        for c in range(nch):
            c0, c1 = c * cs, (c + 1) * cs
            last = (c == nch - 1)
            ps_flush_auto()
            PG = [ps_alloc(), ps_alloc()]
            for h in range(4):
                mm(PG[h // 2][0][:, (h % 2) * 256:(h % 2) * 256 + 256], kTD[:cs, h * 128:(h + 1) * 128],
                   gv[:cs, c, h * 256:(h + 1) * 256], True, True, ["kTD", ("gv", c)], [PG[h // 2][1]])
            OG = [ps_alloc(), ps_alloc()]
            for h in range(4):
                for vc in range(2):
                    blk = (h % 2) * 2 + vc
                    dst = OG[h // 2][0][:, blk * 128:blk * 128 + cs]
                    mm(dst, gv[:cs, c, h * 256 + vc * 128:h * 256 + (vc + 1) * 128], scmD[:cs, h * 128:h * 128 + cs],
                       True, False, [("gv", c), "scmD"], [OG[h // 2][1]])
                    mm(dst, S_bf[:, h, vc * 128:(vc + 1) * 128], arena[:, AR_QE + h, c0:c1], False, True,
                       [("Sbf", h), ar(AR_QE + h)], [OG[h // 2][1]])
            for h in range(4):
                stt(S_st[l][:, h, :], S_st[l][:, h, :], A_t[:, h, c:c + 1],
                    PG[h // 2][0][:, (h % 2) * 256:(h % 2) * 256 + 256], ALU.mult, ALU.add,
                    [("S", l, h), ("A", h), PG[h // 2][1]], [("S", l, h)])
                if not last:
                    cp("act", S_bf[:, h, :], S_st[l][:, h, :], [("S", l, h)], [("Sbf", h)])
            ps_release(PG[0][1])
            ps_release(PG[1][1])
            PM = [ps_alloc(), ps_alloc()]
            for h in range(4):
                mm(PM[h // 2][0][:, (h % 2) * 256:(h % 2) * 256 + 256], kwD[:cs, h * 128:(h + 1) * 128],
                   mv[:cs, c, h * 256:(h + 1) * 256], True, True, ["kwD", ("mv", c)], [PM[h // 2][1]])
            npp, nppt = ps_alloc()
            for h in range(4):
                mm(npp[:, h:h + 1], kwD[:cs, h * 128:(h + 1) * 128], ones_bf[:cs, 0:1], True, True, ["kwD", "ones_bf"], [nppt])
            NM = [ps_alloc(), ps_alloc()]
            for h in range(4):
                for vc in range(2):
                    blk = (h % 2) * 2 + vc
                    dst = NM[h // 2][0][:, blk * 128:blk * 128 + cs]
                    mm(dst, mv[:cs, c, h * 256 + vc * 128:h * 256 + (vc + 1) * 128], qkDD[:cs, h * 128:h * 128 + cs],
                       True, False, [("mv", c), "qkDD"], [NM[h // 2][1]])
                    mm(dst, C_bf[:, h, vc * 128:(vc + 1) * 128], qfD[:, h * 128:h * 128 + cs], False, True,
                       [("Cbf", h), "qfD"], [NM[h // 2][1]])
            den, dent = ps_alloc()
            for h in range(4):
                mm(den[:, h * 128:h * 128 + cs], ones_bf[:cs, :], qkDD[:cs, h * 128:h * 128 + cs], True, False,
                   ["ones_bf", "qkDD"], [dent])
                mm(den[:, h * 128:h * 128 + cs], nb_bf[:, h * 128:(h + 1) * 128], qfD[:, h * 128:h * 128 + cs], False, True,
                   ["nb_bf", "qfD"], [dent])
            for h in range(4):
                stt(C_st[l][:, h, :], C_st[l][:, h, :], GdD[:, h:h + 1],
                    PM[h // 2][0][:, (h % 2) * 256:(h % 2) * 256 + 256], ALU.mult, ALU.add,
                    [("C", l, h), "GdD", PM[h // 2][1]], [("C", l, h)])
                if not last:
                    cp("act", C_bf[:, h, :], C_st[l][:, h, :], [("C", l, h)], [("Cbf", h)])
            ps_release(PM[0][1])
            ps_release(PM[1][1])
            ntmp, ntt = sm_next()
            tt("dve", ntmp[:, :], n_st[l][:, :], GdD[:, :], ALU.mult, [("n", l), "GdD"], [ntt])
            d0, d0t = f32_next()
            act(v4(d0, cs), v4(den, cs), AF.Abs, [dent], [d0t])
            ps_release(dent)
            tt("dve", n_st[l][:, :], npp[:, 0:4], ntmp[:, :], ALU.add, [nppt, ntt], [("n", l)])
            ps_release(nppt)
            if not last:
                nb_refresh()

            def ml_out():
                d1, d1t = f32_next()
                ts("dve", v4(d1, cs), v4(d0, cs), 1.0, ALU.max, [d0t], [d1t])
                yield
                lnd, lndt = f32_next()
                act(v4(lnd, cs), v4(d1, cs), AF.Ln, [d1t], [lndt])
                yield
                rden, rdent = f32_next()
                act(v4(rden, cs), v4(lnd, cs), AF.Exp, [lndt], [rdent], scale=-1.0)
                yield
                for b in range(2):
                    tt("dve", blk4(hbD[b]), blk4(NM[b][0]),
                       rden[:, :].rearrange("p (h t) -> p h t", h=4)[:, 2 * b:2 * b + 2, :cs].unsqueeze(2).to_broadcast(
                           [128, 2, 2, cs]), ALU.mult, [NM[b][1], rdent], [("hbD", b)])
                    ps_release(NM[b][1])
                    yield
                yield from rms_chain([(hbD[0], ("hbD", 0)), (hbD[1], ("hbD", 1))], AR_SO, (c0, c1), None)

            gG = rms_chain([(OG[0][0], OG[0][1]), (OG[1][0], OG[1][1])], AR_SR, (c0, c1), [OG[0][1], OG[1][1]])
            gM = ml_out()
            for _ in range(5):
                next(gG)
                next(gM)
            drain(gG, gM, prep_gla(c + 1) if not last else None, prep_ml(c + 1) if not last else None)
        ps_flush_auto()

        dbg("og_" + tag, arena[:, AR_SR:AR_SR + 8, :T], ALLAR)
        dbg("om_" + tag, arena[:, AR_SO:AR_SO + 8, :T], ALLAR)
        dbg("S_" + tag, S_st[l][:, :, :], [("S", l, h) for h in range(4)])
        dbg("C_" + tag, C_st[l][:, :, :], [("C", l, h) for h in range(4)])
        dbg("n_" + tag, n_st[l][:, :], [("n", l)])
        for br in range(2):
            src_base = AR_SR if br == 0 else AR_SO
            gate_base = AR_SGA if br == 0 else AR_SGB
            for half in range(2):
                sl, slt = next_slab(l, 17 + br * 2 + half)
                for dcl in range(4):
                    dc = half * 4 + dcl
                    ps, pt = ps_next()
                    for kc in range(8):
                        mm(ps[:, :T], sl[:, kc, dcl * 128:(dcl + 1) * 128], arena[:, src_base + kc, :T], kc == 0, kc == 7,
                           [slt, ar(src_base + kc)], [pt])
                    tt("dve", arena[:, gate_base + dc, :T], ps[:, :T], arena[:, gate_base + dc, :T], ALU.mult,
                       [pt, ar(gate_base + dc)], [ar(gate_base + dc)])
        for dc in range(8):
            tt(PL, arena[:, AR_SGA + dc, :T], arena[:, AR_SGA + dc, :T], arena[:, AR_SGB + dc, :T], ALU.add,
               [ar(AR_SGA + dc), ar(AR_SGB + dc)], [ar(AR_SGA + dc)])
        for half in range(2):
            sl, slt = next_slab(l, 21 + half)
            for dcl in range(4):
                dc = half * 4 + dcl
                ps, pt = ps_next()
                for kc in range(8):
                    mm(ps[:, :T], sl[:, kc, dcl * 128:(dcl + 1) * 128], arena[:, AR_SGA + kc, :T], kc == 0, kc == 7,
                       [slt, ar(AR_SGA + kc)], [pt])
                tt("dve", H[:, dc, :T], ps[:, :T], H[:, dc, :T], ALU.add, [pt, (hn, dc)], [(hn, dc)])

        dbg("hmix_" + tag, H[:, :, :T], [(hn, c) for c in range(8)])
        rms_to(pb + PP_NF, T, lambda c: (xn(c), [ar(AR_XN + c)]), H, hn)
        pend = None

        def ffn_tail(p):
            j_, accs_ = p
            ga, gat = b16_next()
            act(ga[:, :T], accs_[0][0][:, :T], AF.Silu, [accs_[0][1], "pp"], [gat],
                bias=pp[:, pb + PP_FB + j_:pb + PP_FB + j_ + 1])
            stt(arena[:, AR_AT + j_, :T], accs_[1][0][:, :T], pp[:, pb + PP_FB + 22 + j_:pb + PP_FB + 22 + j_ + 1],
                ga[:, :T], ALU.add, ALU.mult, [accs_[1][1], "pp", gat], [ar(AR_AT + j_)])

        for j in range(22):
            if j % 2 == 0:
                sl, slt = next_slab(l, 23 + j // 2)
            cb0 = (j % 2) * 256
            pg, pgt = ps_next()
            pu, put = ps_next()
            for kc in range(8):
                mm(pg[:, :T], sl[:, kc, cb0:cb0 + 128], xn(kc), kc == 0, kc == 7, [slt, ar(AR_XN + kc)], [pgt])
            for kc in range(8):
                mm(pu[:, :T], sl[:, kc, cb0 + 128:cb0 + 256], xn(kc), kc == 0, kc == 7, [slt, ar(AR_XN + kc)], [put])
            ui = ctr["ub"] % 2
            ctr["ub"] += 1
            ub = ubuf[ui]
            ubt = [("ub", ui, 0), ("ub", ui, 1)]
            cp(PL, ub[:, :, 0:2], u_hist[l][:, j, :, :], [("uh", l)], ubt)
            cp("act", ub[:, 0, 2:2 + T], pg[:, :T], [pgt], [ubt[0]])
            cp("act", ub[:, 1, 2:2 + T], pu[:, :T], [put], [ubt[1]])
            cp(PL, u_hist[l][:, j, :, :], ub[:, :, T:T + 2], ubt, [("uh", l)])
            accs = []
            for i in range(2):
                a_, at = f32_next()
                accs.append((a_, at))
                col = (j if i == 0 else 22 + j)
                act(a_[:, :T], ub[:, i, 0:T], AF.Identity, [ubt[i], "pp"], [at],
                    scale=pp[:, pb + PP_FW + col:pb + PP_FW + col + 1])
            if pend is not None:
                ffn_tail(pend)
            for tap in (1, 2):
                for i in range(2):
                    a_, at = accs[i]
                    col = (j if i == 0 else 22 + j)
                    stt(a_[:, :T], ub[:, i, tap:tap + T], pp[:, pb + PP_FW + tap * 44 + col:pb + PP_FW + tap * 44 + col + 1],
                        a_[:, :T], ALU.mult, ALU.add, [ubt[i], "pp", at], [at])
            pend = (j, accs)
        ffn_tail(pend)
        for dc0 in range(0, 8, 2):
            grp = []
            for dc in (dc0, dc0 + 1):
                sl, slt = next_slab(l, 34 + dc)
                ps, pt = ps_next()
                grp.append((dc, sl, slt, ps, pt))
            for (dc, sl, slt, ps, pt) in grp:
                for kc in range(20):
                    mm(ps[:, :T], sl[:, kc, :], arena[:, AR_AT + kc, :T], kc == 0, False, [slt, ar(AR_AT + kc)], [pt])
            for (dc, sl, slt, ps, pt) in grp:
                for kc in (20, 21):
                    mm(ps[:, :T], sl[:, kc, :], arena[:, AR_AT + kc, :T], False, kc == 21, [slt, ar(AR_AT + kc)], [pt])
                tt("dve", H[:, dc, :T], ps[:, :T], H[:, dc, :T], ALU.add, [pt, (hn, dc)], [(hn, dc)])
        dbg("at_" + tag, arena[:, AR_AT:AR_AT + 22, :T], ALLAR)
        dbg("hffn_" + tag, H[:, :, :T], [(hn, c) for c in range(8)])

    final_dma = []

    def load_tile(H, hn, T, src):
        HT = [(hn, c) for c in range(8)]
        if src[0] == "meta":
            dma("act", H[:, :, :T], metaT_d.rearrange("(c p) t -> p c t", p=128), (), HT, "xin")
        else:
            t0 = src[1]
            dma("act", H[:, :, :T], xT_d.rearrange("(c p) t -> p c t", p=128)[:, :, t0:t0 + T], (), HT, "xin")

    def store_tile(T, src):
        fcol = depth * PPL
        rms_to(fcol, T, lambda c: (hT[:, c, :T], [("hT", c)]), hT, "hT")
        t0 = src[1]
        HT = [("hT", c) for c in range(8)]
        return dma("act", outT_d.rearrange("(c p) t -> p c t", p=128)[:, :, t0:t0 + T], hT[:, :, :T], HT, (), "xout")

    Tm, csm, nchm, srcm = tiles[0]
    T1, cs1, nch1, src1 = tiles[1]
    load_tile(hTm, "hTm", Tm, srcm)
    load_tile(hT, "hT", T1, src1)
    for l in range(depth):
        tile_layer(l, Tm, csm, nchm, 0, hTm, "hTm")
        tile_layer(l, T1, cs1, nch1, 1, hT, "hT")
    final_dma = [store_tile(T1, src1)]
    for ti, (T, cs, nch, src) in enumerate(tiles):
        if ti < 2:
            continue
        load_tile(hT, "hT", T, src)
        for l in range(depth):
            tile_layer(l, T, cs, nch, ti, hT, "hT")
        final_dma = [store_tile(T, src)]
    P.emit(nc, es, {"act": final_dma + final_dbg})
    es.close()
    return nc


def _slab_img(Wblk):
    K, n = Wblk.shape
    kc = K // 128
    return np.ascontiguousarray(Wblk.reshape(kc, 128, n).transpose(1, 0, 2)).reshape(128, kc * n)


def pack_weights(depth, w_in, b_in, gla_w_gate, gla_b_gate, gla_norm, ml_conv_w, ml_conv_b, ml_norm,
                 w_branch_gla, w_branch_ml, w_out, norm_mix, norm_ffn, ffn_w_up, ffn_conv_w, ffn_conv_b,
                 ffn_w_down, norm_final):
    o_gq, o_gk, o_gv, o_gr, o_lr, o_mq, o_mk, o_mv, o_mo, o_mi, o_mf, o_ga, o_gb = (
        0, 512, 1024, 2048, 3072, 3088, 3600, 4112, 5136, 6160, 6164, 6168, 7192)
    fm_cols = []
    for h in range(4):
        fm_cols += list(range(o_gq + h * 128, o_gq + (h + 1) * 128))
        fm_cols += list(range(o_gk + h * 128, o_gk + (h + 1) * 128))
    fm_cols += list(range(o_gr, o_gr + 1024)) + list(range(o_mq, o_mq + 512)) + list(range(o_mk, o_mk + 512))
    fm_cols += list(range(o_mo, o_mo + 1024)) + list(range(o_ga, o_ga + 1024)) + list(range(o_gb, o_gb + 1024))
    fm_cols = np.array(fm_cols)
    small_cols = np.array(list(range(o_lr, o_lr + 16)) + list(range(o_mi, o_mi + 4)) + list(range(o_mf, o_mf + 4)))
    tm_cols = np.array(list(range(o_gv, o_gv + 1024)) + list(range(o_mv, o_mv + 1024)))
    up_cols = []
    for j in range(22):
        up_cols += list(range(j * 128, (j + 1) * 128)) + list(range(DFF + j * 128, DFF + (j + 1) * 128))
    up_cols = np.array(up_cols)

    w32 = np.zeros((depth, 128, PW), np.float32)
    pp = np.zeros((128, depth * PPL + 8), np.float32)
    p16 = np.zeros((16, depth * 512 + depth), np.float32)
    prow = np.zeros((4, 2048), np.float32)
    for l in range(depth):
        parts = [_slab_img(w_in[l][:, small_cols])]
        wf = w_in[l][:, fm_cols]
        for s in range(12):
            parts.append(_slab_img(wf[:, s * 512:(s + 1) * 512]))
        wt = w_in[l][:, tm_cols]
        for s in range(4):
            parts.append(_slab_img(wt[:, s * 512:(s + 1) * 512]))
        for W in (w_branch_gla[l], w_branch_ml[l], w_out[l]):
            for s in range(2):
                parts.append(_slab_img(W[:, s * 512:(s + 1) * 512]))
        wu = ffn_w_up[l][:, up_cols]
        for s in range(11):
            parts.append(_slab_img(wu[:, s * 512:(s + 1) * 512]))
        for dc in range(8):
            parts.append(_slab_img(ffn_w_down[l][:, dc * 128:(dc + 1) * 128]))
        w32[l] = np.concatenate(parts, axis=1)
        b = l * PPL
        pp[:, b + PP_NM:b + PP_NM + 8] = norm_mix[l].reshape(8, 128).T
        pp[:, b + PP_NF:b + PP_NF + 8] = norm_ffn[l].reshape(8, 128).T
        pp[:, b + PP_BFM:b + PP_BFM + 48] = b_in[l][fm_cols].reshape(48, 128).T
        pp[:, b + PP_NBG:b + PP_NBG + 4] = gla_b_gate[l].reshape(4, 128).T
        pp[:, b + PP_GN:b + PP_GN + 2] = gla_norm[l].reshape(2, 128).T
        pp[:, b + PP_MN:b + PP_MN + 8] = ml_norm[l].reshape(8, 128).T
        pp[:, b + PP_CW:b + PP_CW + 32] = ml_conv_w[l].reshape(4, 8, 128).transpose(2, 0, 1).reshape(128, 32)
        pp[:, b + PP_CB:b + PP_CB + 8] = ml_conv_b[l].reshape(8, 128).T
        pp[:, b + PP_FW:b + PP_FW + 132] = ffn_conv_w[l].reshape(3, 44, 128).transpose(2, 0, 1).reshape(128, 132)
        pp[:, b + PP_FB:b + PP_FB + 44] = ffn_conv_b[l].reshape(44, 128).T
        pp[:, b + PP_BIF:b + PP_BIF + 8] = np.broadcast_to(b_in[l][small_cols[16:24]][None, :], (128, 8))
        p16[:, l * 512:l * 512 + 512] = gla_w_gate[l]
        p16[:, depth * 512 + l] = b_in[l][small_cols[0:16]]
        prow[l, :] = b_in[l][tm_cols]
    pp[:, depth * PPL:depth * PPL + 8] = norm_final.reshape(8, 128).T
    hc = np.zeros((128, 768), np.float32)
    hc[:, 0:128] = np.eye(128, dtype=np.float32)
    hc[:, 128:256] = np.triu(np.ones((128, 128), np.float32))
    for l in range(4):
        hc[l, 256 + l * 128:256 + (l + 1) * 128] = 1.0
    return w32, pp, p16, prow, hc


def make_tiles(seq):
    tiles = [(NMETA, NMETA, 1, ("meta",))]
    for t0 in range(0, seq, 512):
        tiles.append((512, 128, 4, ("x", t0)))
    return tiles


def run_model(x, meta, depth, **w):
    B, seq, _ = x.shape
    w32, pp, p16, prow, hc = pack_weights(depth, **w)
    metaT = np.ascontiguousarray(np.asarray(meta, np.float32).T)
    nc = build_program(depth, make_tiles(seq), seq)
    in_maps = []
    for b in range(B):
        in_maps.append({"xT": np.ascontiguousarray(x[b].T), "metaT": metaT, "w32": w32, "pp": pp, "p16": p16,
                        "prow": prow, "hc": hc})
    res = run_bass_kernel_spmd(nc, in_maps, core_ids=list(range(B)))
    out = np.stack([np.ascontiguousarray(r["outT"].T) for r in res.results], axis=0)
    if DEBUG["on"]:
        DEBUG["res"] = res.results
    return out.astype(np.float32)


def kernel(x, meta, norm_mix, w_in, b_in, gla_w_gate, gla_b_gate, gla_norm, ml_conv_w, ml_conv_b, ml_norm,
           w_branch_gla, w_branch_ml, w_out, norm_ffn, ffn_w_up, ffn_conv_w, ffn_conv_b, ffn_w_down, norm_final):
    f = lambda a: np.asarray(a, np.float32)
    return run_model(f(x), f(meta), DEPTH, w_in=f(w_in), b_in=f(b_in), gla_w_gate=f(gla_w_gate),
                     gla_b_gate=f(gla_b_gate), gla_norm=f(gla_norm), ml_conv_w=f(ml_conv_w), ml_conv_b=f(ml_conv_b),
                     ml_norm=f(ml_norm), w_branch_gla=f(w_branch_gla), w_branch_ml=f(w_branch_ml), w_out=f(w_out),
                     norm_mix=f(norm_mix), norm_ffn=f(norm_ffn), ffn_w_up=f(ffn_w_up), ffn_conv_w=f(ffn_conv_w),
                     ffn_conv_b=f(ffn_conv_b), ffn_w_down=f(ffn_w_down), norm_final=f(norm_final))
```
